# Optimizing a Trainium2 kernel written in Bass

```python
import jax
import jax.numpy as jnp
from jax import lax
import numpy as np

D_MODEL = 1024
BATCH = 8
SEQ = 2048
DEPTH = 4

N_MIXERS = 3
N_ATTN_LAYERS = (DEPTH + 2) // 3
N_LRU_LAYERS = (DEPTH + 1) // 3
N_POOL_LAYERS = DEPTH // 3

N_HEADS = 8
HEAD_DIM = D_MODEL // N_HEADS
MOBA_BLOCK = 256
MOBA_TOPK = 3
MOBA_QUERY_CHUNK = 8

LRU_BLOCK_WIDTH = 128
LRU_WIDTH = ((4 * D_MODEL // 3) // LRU_BLOCK_WIDTH) * LRU_BLOCK_WIDTH
LRU_BLOCKS = LRU_WIDTH // LRU_BLOCK_WIDTH
LRU_CONV_WIDTH = 4
LRU_C = 8.0

POOL_WINDOWS = (2, 4, 8, 16)
POOL_GROUP_WIDTH = D_MODEL // len(POOL_WINDOWS)

FFN_HIDDEN = ((8 * D_MODEL // 3 + 255) // 256) * 256
FFN_CONV_WIDTH = 3

RMS_EPS = 1e-6

kernel_name = 'hybrid_moba_rglru_pool_convffn'


def rms_norm(x, g):
    xf = x.astype(jnp.float32)
    y = xf * lax.rsqrt(jnp.mean(xf * xf, axis=-1, keepdims=True) + RMS_EPS)
    return (y * g.astype(jnp.float32)).astype(x.dtype)


def causal_depthwise_conv(x, w, b):
    width, ch = w.shape
    y = lax.conv_general_dilated(
        x, w[:, None, :].astype(x.dtype), window_strides=(1,),
        padding=[(width - 1, 0)], dimension_numbers=('NWC', 'WIO', 'NWC'),
        feature_group_count=ch)
    return y + b.astype(x.dtype)


def moba_attention(h, w_qkv, q_g, k_g, w_o):
    bsz, seq, _ = h.shape
    qkv = jnp.einsum('bsd,de->bse', h, w_qkv).reshape(bsz, seq, 3, N_HEADS, HEAD_DIM)
    q = rms_norm(qkv[:, :, 0], q_g)
    k = rms_norm(qkv[:, :, 1], k_g)
    v = qkv[:, :, 2]
    n_blk = -(-seq // MOBA_BLOCK)
    s_pad = n_blk * MOBA_BLOCK
    pad = ((0, 0), (0, s_pad - seq), (0, 0), (0, 0))
    q = jnp.pad(q, pad)
    k = jnp.pad(k, pad)
    v = jnp.pad(v, pad)
    kb = k.reshape(bsz, n_blk, MOBA_BLOCK, N_HEADS, HEAD_DIM)
    vb = v.reshape(bsz, n_blk, MOBA_BLOCK, N_HEADS, HEAD_DIM)
    pos = jnp.arange(s_pad)
    q_blk = pos // MOBA_BLOCK
    k_mean = jnp.mean(kb.astype(jnp.float32), axis=2)
    gate = jnp.einsum('bshd,bnhd->bshn', q.astype(jnp.float32), k_mean)
    fully_past = jnp.arange(n_blk)[None, :] < q_blk[:, None]
    gate = jnp.where(fully_past[None, :, None, :], gate, -jnp.inf)
    n_sel = min(MOBA_TOPK, n_blk)
    _, sel = lax.top_k(gate, n_sel)
    sel_valid = jnp.arange(n_sel)[None, :] < q_blk[:, None]
    kbh = kb.transpose(0, 3, 1, 2, 4)
    vbh = vb.transpose(0, 3, 1, 2, 4)
    n_chunk = s_pad // MOBA_QUERY_CHUNK

    def to_chunks(a):
        return a.reshape(bsz, n_chunk, MOBA_QUERY_CHUNK, *a.shape[2:]).swapaxes(0, 1)

    q_c = to_chunks(q)
    sel_c = to_chunks(sel)
    valid_c = sel_valid.reshape(n_chunk, MOBA_QUERY_CHUNK, n_sel)
    pos_c = pos.reshape(n_chunk, MOBA_QUERY_CHUNK)
    b_idx = jnp.arange(bsz)[:, None, None, None]
    h_idx = jnp.arange(N_HEADS)[None, None, :, None]
    scale = HEAD_DIM ** -0.5
    n_s = n_sel * MOBA_BLOCK

    def attend(args):
        qc, selc, validc, posc = args
        k_sel = kbh[b_idx, h_idx, selc]
        v_sel = vbh[b_idx, h_idx, selc]
        own = posc[0] // MOBA_BLOCK
        k_own = lax.dynamic_index_in_dim(kb, own, axis=1, keepdims=False)
        v_own = lax.dynamic_index_in_dim(vb, own, axis=1, keepdims=False)
        s_sel = jnp.einsum('bqhd,bqhrkd->bqhrk', qc, k_sel).astype(jnp.float32) * scale
        s_sel = jnp.where(validc[None, :, None, :, None], s_sel, -jnp.inf)
        s_own = jnp.einsum('bqhd,bkhd->bqhk', qc, k_own).astype(jnp.float32) * scale
        k_pos = own * MOBA_BLOCK + jnp.arange(MOBA_BLOCK)
        causal = k_pos[None, :] <= posc[:, None]
        s_own = jnp.where(causal[None, :, None, :], s_own, -jnp.inf)
        scores = jnp.concatenate(
            [s_sel.reshape(bsz, MOBA_QUERY_CHUNK, N_HEADS, n_s), s_own], axis=-1)
        p = jax.nn.softmax(scores, axis=-1).astype(v.dtype)
        p_sel = p[..., :n_s].reshape(bsz, MOBA_QUERY_CHUNK, N_HEADS, n_sel, MOBA_BLOCK)
        p_own = p[..., n_s:]
        return (jnp.einsum('bqhrk,bqhrkd->bqhd', p_sel, v_sel)
                + jnp.einsum('bqhk,bkhd->bqhd', p_own, v_own))

    o = lax.map(attend, (q_c, sel_c, valid_c, pos_c))
    o = o.swapaxes(0, 1).reshape(bsz, s_pad, N_HEADS * HEAD_DIM)[:, :seq]
    return jnp.einsum('bse,ed->bsd', o, w_o)


def _lru_combine(left, right):
    a_l, b_l = left
    a_r, b_r = right
    return a_l * a_r, a_r * b_l + b_r


def rglru_block(h, w_in, conv_w, conv_b, w_a, b_a, w_x, b_x, lam, w_out):
    bsz, seq, _ = h.shape
    u = jnp.einsum('bsd,de->bse', h, w_in)
    xb, yb = jnp.split(u, 2, axis=-1)
    yb = jax.nn.gelu(yb)
    xb = causal_depthwise_conv(xb, conv_w, conv_b)
    xg = xb.reshape(bsz, seq, LRU_BLOCKS, LRU_BLOCK_WIDTH)
    r = jax.nn.sigmoid(jnp.einsum('bsnc,ncd->bsnd', xg, w_a).reshape(bsz, seq, LRU_WIDTH) + b_a)
    i = jax.nn.sigmoid(jnp.einsum('bsnc,ncd->bsnd', xg, w_x).reshape(bsz, seq, LRU_WIDTH) + b_x)
    log_a = -LRU_C * r.astype(jnp.float32) * jax.nn.softplus(-lam.astype(jnp.float32))
    a = jnp.exp(log_a)
    mult = jnp.sqrt(-jnp.expm1(2.0 * log_a))
    b_in = mult * (i * xb).astype(jnp.float32)
    _, hs = lax.associative_scan(_lru_combine, (a, b_in), axis=1)
    return jnp.einsum('bse,ed->bsd', hs.astype(h.dtype) * yb, w_out)


def multiscale_pool(h, w_grp, scale):
    bsz, seq, dm = h.shape
    hf = h.astype(jnp.float32)
    cs = jnp.concatenate([jnp.zeros((bsz, 1, dm), jnp.float32), jnp.cumsum(hf, axis=1)], axis=1)
    t = jnp.arange(seq)
    pooled = []
    for g, w in enumerate(POOL_WINDOWS):
        lo, hi = g * POOL_GROUP_WIDTH, (g + 1) * POOL_GROUP_WIDTH
        cs_g = cs[:, :, lo:hi]
        start = jnp.maximum(t + 1 - w, 0)
        win_sum = cs_g[:, 1:] - jnp.take(cs_g, start, axis=1)
        count = jnp.minimum(t + 1, w).astype(jnp.float32)
        pooled.append(win_sum / count[None, :, None] - hf[:, :, lo:hi])
    pooled = jnp.stack(pooled, axis=2).astype(h.dtype)
    y = jnp.einsum('bsgc,gcd->bsgd', pooled, w_grp).reshape(bsz, seq, dm)
    return y * scale


def conv_ffn(h, w_up, conv_w, conv_b, w_down):
    u = causal_depthwise_conv(jnp.einsum('bsd,df->bsf', h, w_up), conv_w, conv_b)
    g, v = jnp.split(u, 2, axis=-1)
    return jnp.einsum('bsf,fd->bsd', jax.nn.silu(g) * v, w_down)


def setup_inputs(seed: int = 0) -> dict:
    key = jax.random.key(seed)
    keys = iter(jax.random.split(key, 32))

    def nrm(shape, s):
        return jax.random.normal(next(keys), shape, jnp.float32) * s

    d, dr, bw, gw, f = D_MODEL, LRU_WIDTH, LRU_BLOCK_WIDTH, POOL_GROUP_WIDTH, FFN_HIDDEN
    x = nrm((BATCH, SEQ, d), 1.0)
    norm_mix_g = 1.0 + nrm((DEPTH, d), 0.1)
    norm_ffn_g = 1.0 + nrm((DEPTH, d), 0.1)
    attn_w_qkv = nrm((N_ATTN_LAYERS, d, 3 * d), d ** -0.5)
    attn_q_g = 1.0 + nrm((N_ATTN_LAYERS, HEAD_DIM), 0.1)
    attn_k_g = 1.0 + nrm((N_ATTN_LAYERS, HEAD_DIM), 0.1)
    attn_w_o = nrm((N_ATTN_LAYERS, d, d), 0.5 * d ** -0.5)
    lru_w_in = nrm((N_LRU_LAYERS, d, 2 * dr), d ** -0.5)
    lru_conv_w = nrm((N_LRU_LAYERS, LRU_CONV_WIDTH, dr), LRU_CONV_WIDTH ** -0.5)
    lru_conv_b = nrm((N_LRU_LAYERS, dr), 0.01)
    lru_w_a = nrm((N_LRU_LAYERS, LRU_BLOCKS, bw, bw), bw ** -0.5)
    lru_b_a = nrm((N_LRU_LAYERS, dr), 0.01)
    lru_w_x = nrm((N_LRU_LAYERS, LRU_BLOCKS, bw, bw), bw ** -0.5)
    lru_b_x = nrm((N_LRU_LAYERS, dr), 0.01)
    a_pow = jax.random.uniform(next(keys), (N_LRU_LAYERS, dr), jnp.float32, 0.9, 0.999)
    sig = a_pow ** (1.0 / LRU_C)
    lru_lambda = jnp.log(sig) - jnp.log1p(-sig)
    lru_w_out = nrm((N_LRU_LAYERS, dr, d), 0.5 * dr ** -0.5)
    pool_w = nrm((N_POOL_LAYERS, len(POOL_WINDOWS), gw, gw), gw ** -0.5)
    pool_scale = 1.0 + nrm((N_POOL_LAYERS, d), 0.1)
    ffn_w_up = nrm((DEPTH, d, 2 * f), d ** -0.5)
    ffn_conv_w = nrm((DEPTH, FFN_CONV_WIDTH, 2 * f), FFN_CONV_WIDTH ** -0.5)
    ffn_conv_b = nrm((DEPTH, 2 * f), 0.01)
    ffn_w_down = nrm((DEPTH, f, d), 0.5 * f ** -0.5)
    return {'x': x, 'norm_mix_g': norm_mix_g, 'norm_ffn_g': norm_ffn_g,
            'attn_w_qkv': attn_w_qkv, 'attn_q_g': attn_q_g, 'attn_k_g': attn_k_g, 'attn_w_o': attn_w_o,
            'lru_w_in': lru_w_in, 'lru_conv_w': lru_conv_w, 'lru_conv_b': lru_conv_b,
            'lru_w_a': lru_w_a, 'lru_b_a': lru_b_a, 'lru_w_x': lru_w_x, 'lru_b_x': lru_b_x,
            'lru_lambda': lru_lambda, 'lru_w_out': lru_w_out,
            'pool_w': pool_w, 'pool_scale': pool_scale,
            'ffn_w_up': ffn_w_up, 'ffn_conv_w': ffn_conv_w, 'ffn_conv_b': ffn_conv_b, 'ffn_w_down': ffn_w_down}


def reference(x, norm_mix_g, norm_ffn_g, attn_w_qkv, attn_q_g, attn_k_g, attn_w_o,
              lru_w_in, lru_conv_w, lru_conv_b, lru_w_a, lru_b_a, lru_w_x, lru_b_x,
              lru_lambda, lru_w_out, pool_w, pool_scale,
              ffn_w_up, ffn_conv_w, ffn_conv_b, ffn_w_down):
    for layer in range(DEPTH):
        kind, slot = layer % N_MIXERS, layer // N_MIXERS
        h = rms_norm(x, norm_mix_g[layer])
        if kind == 0:
            mix = moba_attention(h, attn_w_qkv[slot], attn_q_g[slot], attn_k_g[slot], attn_w_o[slot])
        elif kind == 1:
            mix = rglru_block(h, lru_w_in[slot], lru_conv_w[slot], lru_conv_b[slot],
                              lru_w_a[slot], lru_b_a[slot], lru_w_x[slot], lru_b_x[slot],
                              lru_lambda[slot], lru_w_out[slot])
        else:
            mix = multiscale_pool(h, pool_w[slot], pool_scale[slot])
        x = x + mix
        h = rms_norm(x, norm_ffn_g[layer])
        x = x + conv_ffn(h, ffn_w_up[layer], ffn_conv_w[layer], ffn_conv_b[layer], ffn_w_down[layer])
    return x
```

```python
import contextlib
import numpy as np
import concourse.bass as bass
import concourse.mybir as mybir
from concourse.bass_utils import run_bass_kernel_spmd

F32 = mybir.dt.float32
BF16 = mybir.dt.bfloat16
AF = mybir.ActivationFunctionType
ALU = mybir.AluOpType
AX = mybir.AxisListType

D = 1024
S_ = 2048
NC_ = 8
DEPTH = 4
FH = 2816
NPAIR = 22
DR = 1280
NLC = 10
EPS = 1e-6
NEG = -30000.0
ENGS = ("pe", "act", "dve", "pool", "sp")


class Op:
    __slots__ = ("eng", "fn", "deps", "dma", "signal", "semv")

    def __init__(self, eng, fn, dma):
        self.eng = eng
        self.fn = fn
        self.deps = []
        self.dma = dma
        self.signal = False
        self.semv = None


class _Rec:
    def __init__(self):
        self.calls = []

    def __getattr__(self, name):
        def f(*args, **kwargs):
            self.calls.append((name, args, kwargs))
        return f


class Sched:
    N_DMA_SEMS = 8

    def __init__(self, nc):
        self.nc = nc
        self.ops = {e: [] for e in ENGS}
        self.last_writer = {}
        self.readers = {}
        self.fence_pending = set()
        self.fence_ops = []

    def fence(self):
        self.fence_ops = [self.ops[e][-1] for e in ENGS if self.ops[e]]
        self.fence_pending = set(ENGS)

    def op(self, eng, fn, reads=(), writes=(), dma=False):
        rec = _Rec()
        fn(rec)
        name, args, kwargs = rec.calls[0]
        o = Op(eng, lambda e: getattr(e, name)(*args, **kwargs), dma)
        cand = []
        for t in reads:
            w = self.last_writer.get(t)
            if w is not None:
                cand.append((w, True))
        for t in writes:
            w = self.last_writer.get(t)
            if w is not None:
                cand.append((w, False))
            for r in self.readers.get(t, ()):
                cand.append((r, False))
        seen = set()
        for d, raw in cand:
            if d is o or id(d) in seen:
                continue
            if d.eng == eng and not d.dma and not dma:
                if eng == "pe" or not raw:
                    continue
            seen.add(id(d))
            o.deps.append(d)
        if eng in self.fence_pending:
            self.fence_pending.discard(eng)
            for d in self.fence_ops:
                if id(d) in seen or (d.eng == eng and not d.dma and not dma):
                    continue
                seen.add(id(d))
                o.deps.append(d)
        self.ops[eng].append(o)
        for t in writes:
            self.last_writer[t] = o
            self.readers[t] = []
        for t in reads:
            self.readers.setdefault(t, []).append(o)
        return o

    def emit(self, final_waits=()):
        nc = self.nc
        for e in ENGS:
            for o in self.ops[e]:
                for d in o.deps:
                    d.signal = True
        for o in final_waits:
            o.signal = True
        with contextlib.ExitStack() as st:
            sems = {e: st.enter_context(nc.semaphore("s_" + e)) for e in ENGS}
            for e in ("sp", "pool", "act"):
                for k in range(self.N_DMA_SEMS):
                    sems[(e, k)] = st.enter_context(nc.semaphore("d_%s%d" % (e, k)))
            for e in ENGS:
                c = 0
                dcount = [0] * self.N_DMA_SEMS
                nd = 0
                for o in self.ops[e]:
                    if o.dma:
                        k = nd % self.N_DMA_SEMS
                        nd += 1
                        dcount[k] += 1
                        o.semv = ((e, k), 16 * dcount[k])
                    elif o.signal:
                        c += 1
                        o.semv = (e, c)
            block = st.enter_context(nc.Block())
            engobj = {"pe": block.tensor, "act": block.scalar, "dve": block.vector,
                      "pool": block.gpsimd, "sp": block.sync}

            def make(e):
                def body(eng):
                    known = {}
                    for o in self.ops[e]:
                        waits = {}
                        for d in o.deps:
                            sk, v = d.semv
                            if known.get(sk, 0) >= v:
                                continue
                            if waits.get(sk, 0) < v:
                                waits[sk] = v
                        if o.dma:
                            sk, v = o.semv
                            if v > 16 and known.get(sk, 0) < v - 16 and waits.get(sk, 0) < v - 16:
                                waits[sk] = v - 16
                        for sk, v in waits.items():
                            eng.wait_ge(sems[sk], v)
                            known[sk] = v
                        ins = o.fn(eng)
                        if o.semv is not None:
                            ins.then_inc(sems[o.semv[0]], 16 if o.dma else 1)
                    if e == "sp":
                        for o in final_waits:
                            eng.wait_ge(sems[o.semv[0]], o.semv[1])
                return body

            for e in ENGS:
                engobj[e](make(e))


class Cols:
    def __init__(self):
        self.n = 0
        self.off = {}

    def add(self, name, k):
        self.off[name] = self.n
        self.n += k


PC = Cols()
PC.add("nmix", DEPTH * NC_)
PC.add("nffn", DEPTH * NC_)
PC.add("qg", 2)
PC.add("kg", 2)
PC.add("lcw", 4 * NLC)
PC.add("lcb", NLC)
PC.add("lba", NLC)
PC.add("lbx", NLC)
PC.add("llam", NLC)
PC.add("pscale", NC_)
PC.add("fcw", DEPTH * 3 * 44)
PC.add("fcb", DEPTH * 44)

CC = Cols()
CC.add("ident", 128)
CC.add("ones", 128)
CC.add("cb", 2 * 256)
CC.add("esel", 8 * 128)
NCB = CC.n
CF = Cols()
CF.add("pastmask", 64)
CF.add("icnt", 4 * 16)
CF.add("eps", 1)


def pack_params(inp):
    P = np.zeros((128, PC.n), np.float32)

    def put(name, arr):
        a = np.asarray(arr, np.float32)
        a = a.reshape(-1, a.shape[-1] // 128, 128)
        a = a.transpose(2, 0, 1).reshape(128, -1)
        P[:, PC.off[name]:PC.off[name] + a.shape[1]] = a

    put("nmix", inp["norm_mix_g"])
    put("nffn", inp["norm_ffn_g"])
    put("qg", inp["attn_q_g"])
    put("kg", inp["attn_k_g"])
    put("lcw", inp["lru_conv_w"][0])
    put("lcb", inp["lru_conv_b"][0])
    put("lba", inp["lru_b_a"][0])
    put("lbx", inp["lru_b_x"][0])
    put("llam", inp["lru_lambda"][0])
    put("pscale", inp["pool_scale"][0])
    put("fcw", inp["ffn_conv_w"])
    put("fcb", inp["ffn_conv_b"])
    return P


def make_consts():
    cb16 = np.zeros((128, CC.n), np.float32)
    cb16[:, CC.off["ident"]:CC.off["ident"] + 128] = np.eye(128, dtype=np.float32)
    cb16[:, CC.off["ones"]:CC.off["ones"] + 128] = 1.0
    p = np.arange(128)[:, None]
    q = np.arange(256)[None, :]
    for j in range(2):
        m = np.where(j * 128 + p <= q, 0.0, NEG).astype(np.float32)
        cb16[:, CC.off["cb"] + j * 256:CC.off["cb"] + (j + 1) * 256] = m
    es = np.zeros((128, 8, 128), np.float32)
    for n in range(8):
        es[n, n, :] = 1.0
    cb16[:, CC.off["esel"]:CC.off["esel"] + 1024] = es.reshape(128, 1024)
    cf = np.zeros((128, CF.n), np.float32)
    pm = np.zeros((8, 8), np.float32)
    for i in range(8):
        for n in range(8):
            pm[i, n] = 0.0 if n < 4 + i // 2 else -1e30
    cf[:, CF.off["pastmask"]:CF.off["pastmask"] + 64] = pm.reshape(1, 64)
    ic = np.zeros((4, 16), np.float32)
    for g, w in enumerate((2, 4, 8, 16)):
        for t in range(16):
            ic[g, t] = 1.0 / min(t + 1, w)
    cf[:, CF.off["icnt"]:CF.off["icnt"] + 64] = ic.reshape(1, 64)
    cf[:, CF.off["eps"]] = EPS
    return cb16, cf


class Builder:
    def __init__(self, phases, attn_heads=8, attn_hg=4):
        self.phases = phases
        self.attn_heads = attn_heads
        self.attn_hg = attn_hg
        nc = bass.Bass("TRN2", target_bir_lowering=False)
        self.nc = nc
        dt = nc.dram_tensor
        self.d_x = dt("xT", [D, S_], F32, kind="ExternalInput").ap()
        self.d_out = dt("outT", [D, S_], F32, kind="ExternalOutput").ap()
        self.d_params = dt("params", [128, PC.n], F32, kind="ExternalInput").ap()
        self.d_c16 = dt("c16", [128, CC.n], F32, kind="ExternalInput").ap()
        self.d_cf = dt("cf", [128, CF.n], F32, kind="ExternalInput").ap()
        self.d_wup = dt("ffn_w_up", [DEPTH, D, 2 * FH], F32, kind="ExternalInput").ap()
        self.d_wdn = dt("ffn_w_down", [DEPTH, FH, D], F32, kind="ExternalInput").ap()
        self.d_wqkv = dt("attn_w_qkv", [2, D, 3 * D], F32, kind="ExternalInput").ap()
        self.d_wo = dt("attn_w_o", [2, D, D], F32, kind="ExternalInput").ap()
        self.d_lwin = dt("lru_w_in", [1, D, 2 * DR], F32, kind="ExternalInput").ap()
        self.d_lwa = dt("lru_w_a", [1, NLC, 128, 128], F32, kind="ExternalInput").ap()
        self.d_lwx = dt("lru_w_x", [1, NLC, 128, 128], F32, kind="ExternalInput").ap()
        self.d_lwout = dt("lru_w_out", [1, DR, D], F32, kind="ExternalInput").ap()
        self.d_pw = dt("pool_w", [1, 4, 256, 256], F32, kind="ExternalInput").ap()
        self.S = Sched(nc)
        self.psrot = 0
        self.uid = 0

    def sb(self, st, name, shape, dtype):
        self.uid += 1
        return st.enter_context(self.nc.sbuf_tensor("%s_u%d" % (name, self.uid), shape, dtype))

    def pcol(self, name, idx):
        o = PC.off[name] + idx
        return self.params[:, o:o + 1]

    def build(self):
        nc, S = self.nc, self.S
        with contextlib.ExitStack() as st:
            self.xT = self.sb(st, "xT_sb", [128, NC_, S_], F32)
            self.hT = self.sb(st, "hT_sb", [128, NC_, S_], BF16)
            self.params = self.sb(st, "params_sb", [128, PC.n], F32)
            self.c16 = self.sb(st, "c16_sb", [128, CC.n], BF16)
            self.cf = self.sb(st, "cf_sb", [128, CF.n], F32)
            self.sq = self.sb(st, "sq_sb", [128, NC_, 512], BF16)
            self.lnv = self.sb(st, "lnv_sb", [128, 512], F32)
            self.rstd = self.sb(st, "rstd_sb", [128, 512], F32)
            self.PS = st.enter_context(nc.psum_tensor("PS", [128, 4096], F32))
            self.ident = self.c16[:, CC.off["ident"]:CC.off["ident"] + 128]
            self.ones = self.c16[:, CC.off["ones"]:CC.off["ones"] + 128]
            self.eps = self.cf[:, CF.off["eps"]:CF.off["eps"] + 1]

            S.op("sp", lambda e: e.dma_start(out=self.params[:], in_=self.d_params[:, :]),
                 writes=["params"], dma=True)
            S.op("sp", lambda e: e.dma_start(out=self.cf[:], in_=self.d_cf[:, :]),
                 writes=["cf"], dma=True)
            S.op("pool", lambda e: e.dma_start(out=self.c16[:], in_=self.d_c16[:, :]),
                 writes=["c16"], dma=True)
            for c in range(NC_):
                S.op("sp", lambda e, c=c: e.dma_start(out=self.xT[:, c, :],
                                                      in_=self.d_x[c * 128:(c + 1) * 128, :]),
                     writes=[("x", c, tt) for tt in range(4)], dma=True)

            for ph in self.phases:
                kind, layer = ph
                S.fence()
                if kind == "ffn":
                    with contextlib.ExitStack() as st2:
                        self.ffn(st2, layer)
                elif kind == "mix":
                    with contextlib.ExitStack() as st2:
                        mk = layer % 3
                        if mk == 0:
                            self.attn(st2, layer)
                        elif mk == 1:
                            self.lru(st2, layer)
                        else:
                            self.pool(st2, layer)

            fw = []
            for c in range(NC_):
                fw.append(S.op("sp", lambda e, c=c: e.dma_start(
                    out=self.d_out[c * 128:(c + 1) * 128, :], in_=self.xT[:, c, :]),
                    reads=[("x", c, tt) for tt in range(4)], dma=True))
            S.emit(final_waits=fw)
        return nc

    def bank(self, b):
        return self.PS[:, b * 512:(b + 1) * 512]

    def norm_tt(self, gname, layer, tt):
        S = self.S
        ts = slice(tt * 512, (tt + 1) * 512)
        b = self.psrot % 8
        self.psrot += 1
        S.op("act", lambda e: e.activation(out=self.sq[:], in_=self.xT[:, :, ts], func=AF.Square),
             reads=[("x", c, tt) for c in range(NC_)], writes=["sq"])
        for c in range(NC_):
            S.op("pe", lambda e, c=c: e.matmul(self.bank(b), lhsT=self.ones, rhs=self.sq[:, c, :],
                                               start=(c == 0), stop=(c == NC_ - 1)),
                 reads=["sq", "c16"], writes=[("ps", b)])
        S.op("act", lambda e: e.activation(out=self.lnv[:], in_=self.bank(b), func=AF.Ln,
                                           scale=1.0 / D, bias=self.eps),
             reads=[("ps", b), "cf"], writes=["lnv"])
        S.op("act", lambda e: e.activation(out=self.rstd[:], in_=self.lnv[:], func=AF.Exp, scale=-0.5),
             reads=["lnv"], writes=["rstd"])
        for c in range(NC_):
            S.op("dve", lambda e, c=c: e.scalar_tensor_tensor(
                out=self.hT[:, c, ts], in0=self.xT[:, c, ts], scalar=self.pcol(gname, layer * NC_ + c),
                in1=self.rstd[:], op0=ALU.mult, op1=ALU.mult),
                reads=[("x", c, tt), "rstd", "params"], writes=[("h", c, tt)])

    def norm(self, gname, layer):
        for tt in range(4):
            self.norm_tt(gname, layer, tt)

    def ffn(self, st, layer):
        nc, S = self.nc, self.S
        GROUPS = [(0, 6), (6, 12), (12, 17), (17, 22)]
        wup = [self.sb(st, "wup%d" % i, [128, NC_, 512], BF16) for i in range(2)]
        wdn = self.sb(st, "wdn", [128, 6, D], BF16)
        Pb = self.sb(st, "Pb", [128, 6, S_], BF16)
        cbuf = [self.sb(st, "cbuf%d" % i, [128, S_], F32) for i in range(3)]
        self.norm("nffn", layer)
        wup_src = self.d_wup[layer].rearrange("(c p) f -> p c f", p=128)

        def load_slab(s):
            bi = s % 2
            S.op("pool", lambda e: e.dma_start(out=wup[bi][:, :, 0:256],
                                               in_=wup_src[:, :, s * 256:(s + 1) * 256]),
                 writes=[("wup", bi, 0)], dma=True)
            S.op("pool", lambda e: e.dma_start(out=wup[bi][:, :, 256:512],
                                               in_=wup_src[:, :, FH + s * 256:FH + (s + 1) * 256]),
                 writes=[("wup", bi, 1)], dma=True)

        def load_wdn(j0, j1):
            S.op("pool", lambda e: e.dma_start(
                out=wdn[:, 0:j1 - j0, :],
                in_=self.d_wdn[layer, j0 * 128:j1 * 128, :].rearrange("(j p) d -> p j d", p=128)),
                writes=["wdn"], dma=True)

        load_slab(0)
        load_slab(1)
        cbi = 0
        for (j0, j1) in GROUPS:
            load_wdn(j0, j1)
            for j in range(j0, j1):
                s, r = j // 2, j % 2
                bi = s % 2
                cg = None
                for half in range(2):
                    fj = half * NPAIR + j
                    base = half * 2048
                    lcol = half * 256 + r * 128
                    for tt in range(4):
                        for c in range(NC_):
                            S.op("pe", lambda e, c=c, tt=tt, base=base, lcol=lcol, bi=bi: e.matmul(
                                self.PS[:, base + tt * 512: base + (tt + 1) * 512],
                                lhsT=wup[bi][:, c, lcol:lcol + 128],
                                rhs=self.hT[:, c, tt * 512:(tt + 1) * 512],
                                start=(c == 0), stop=(c == NC_ - 1)),
                                reads=[("wup", bi, half), ("h", c, tt)], writes=[("ps", half * 4 + tt)])
                    cb = cbuf[cbi % 3]
                    cbn = ("cbuf", cbi % 3)
                    cbi += 1
                    psr = [("ps", half * 4 + tt) for tt in range(4)]
                    w0 = self.pcol("fcw", (layer * 3 + 0) * 44 + fj)
                    w1 = self.pcol("fcw", (layer * 3 + 1) * 44 + fj)
                    w2 = self.pcol("fcw", (layer * 3 + 2) * 44 + fj)
                    bb = self.pcol("fcb", layer * 44 + fj)
                    u = self.PS[:, base:base + 2048]
                    S.op("act", lambda e, cb=cb, u=u, w2=w2, bb=bb: e.activation(
                        out=cb[:], in_=u, func=AF.Identity, scale=w2, bias=bb),
                        reads=psr + ["params"], writes=[cbn])
                    S.op("dve", lambda e, cb=cb, u=u, w1=w1: e.scalar_tensor_tensor(
                        out=cb[:, 1:S_], in0=u[:, 0:S_ - 1], scalar=w1, in1=cb[:, 1:S_],
                        op0=ALU.mult, op1=ALU.add), reads=psr + [cbn, "params"], writes=[cbn])
                    S.op("dve", lambda e, cb=cb, u=u, w0=w0: e.scalar_tensor_tensor(
                        out=cb[:, 2:S_], in0=u[:, 0:S_ - 2], scalar=w0, in1=cb[:, 2:S_],
                        op0=ALU.mult, op1=ALU.add), reads=psr + [cbn, "params"], writes=[cbn])
                    if half == 0:
                        S.op("act", lambda e, cb=cb: e.activation(out=cb[:], in_=cb[:], func=AF.Silu),
                             reads=[cbn], writes=[cbn])
                        cg = (cb, cbn)
                    else:
                        jj = j - j0
                        S.op("pool", lambda e, cb=cb, cgb=cg[0], jj=jj: e.tensor_tensor(
                            out=Pb[:, jj, :], in0=cgb[:], in1=cb[:], op=ALU.mult),
                            reads=[cbn, cg[1]], writes=[("P", jj)])
                if r == 1 and s + 2 < 11:
                    load_slab(s + 2)
            nj = j1 - j0
            for tt in range(4):
                for m in range(NC_):
                    b = self.psrot % 8
                    self.psrot += 1
                    for jj in range(nj):
                        S.op("pe", lambda e, b=b, jj=jj, m=m, tt=tt: e.matmul(
                            self.bank(b), lhsT=wdn[:, jj, m * 128:(m + 1) * 128],
                            rhs=Pb[:, jj, tt * 512:(tt + 1) * 512],
                            start=(jj == 0), stop=(jj == nj - 1)),
                            reads=["wdn", ("P", jj)], writes=[("ps", b)])
                    S.op("dve", lambda e, b=b, m=m, tt=tt: e.tensor_tensor(
                        out=self.xT[:, m, tt * 512:(tt + 1) * 512], in0=self.bank(b),
                        in1=self.xT[:, m, tt * 512:(tt + 1) * 512], op=ALU.add),
                        reads=[("ps", b), ("x", m, tt)], writes=[("x", m, tt)])

    def rstd_tt(self, tt, dst, dst_tok):
        S = self.S
        ts = slice(tt * 512, (tt + 1) * 512)
        b = self.psrot % 8
        self.psrot += 1
        S.op("act", lambda e: e.activation(out=self.sq[:], in_=self.xT[:, :, ts], func=AF.Square),
             reads=[("x", c, tt) for c in range(NC_)], writes=["sq"])
        for c in range(NC_):
            S.op("pe", lambda e, c=c: e.matmul(self.bank(b), lhsT=self.ones, rhs=self.sq[:, c, :],
                                               start=(c == 0), stop=(c == NC_ - 1)),
                 reads=["sq", "c16"], writes=[("ps", b)])
        S.op("act", lambda e: e.activation(out=self.lnv[:], in_=self.bank(b), func=AF.Ln,
                                           scale=1.0 / D, bias=self.eps),
             reads=[("ps", b), "cf"], writes=["lnv"])
        S.op("act", lambda e: e.activation(out=dst, in_=self.lnv[:], func=AF.Exp, scale=-0.5),
             reads=["lnv"], writes=[dst_tok])

    def resid_proj(self, w, nk, rhs_fn, rhs_toks, wtok, scale_name=None):
        S = self.S
        for tt in range(4):
            for m in range(NC_):
                b = self.psrot % 8
                self.psrot += 1
                for k in range(nk):
                    S.op("pe", lambda e, b=b, k=k, m=m, tt=tt: e.matmul(
                        self.bank(b), lhsT=w[:, k, m * 128:(m + 1) * 128], rhs=rhs_fn(k, tt),
                        start=(k == 0), stop=(k == nk - 1)),
                        reads=[wtok] + rhs_toks(k, tt), writes=[("ps", b)])
                xs = self.xT[:, m, tt * 512:(tt + 1) * 512]
                S.op("dve", lambda e, b=b, xs=xs: e.tensor_tensor(
                    out=xs, in0=self.bank(b), in1=xs, op=ALU.add),
                    reads=[("ps", b), ("x", m, tt)], writes=[("x", m, tt)])

    def attn(self, st, layer):
        nc, S = self.nc, self.S
        slot = layer // 3
        HG = self.attn_hg
        NH = self.attn_heads
        wqkv = [self.sb(st, "wqkv%d" % i, [128, NC_, 384], BF16) for i in range(2)]
        qn = [self.sb(st, "qn%d" % i, [128, S_], BF16) for i in range(2)]
        kn = [self.sb(st, "kn%d" % i, [128, S_], BF16) for i in range(2)]
        Vt = [self.sb(st, "Vt%d" % i, [128, 16, 128], BF16) for i in range(2)]
        oT = self.sb(st, "oT", [128, HG, S_], BF16)
        wo = self.sb(st, "wo", [128, HG, D], BF16)
        rs = [self.sb(st, "rs%d" % i, [128, 512], F32) for i in range(2)]
        sq2 = [self.sb(st, "sq2_%d" % i, [128, 512], BF16) for i in range(2)]
        PT = [self.sb(st, "PT%d" % i, [128, 512], BF16) for i in range(4)]
        kmf = self.sb(st, "kmf", [128, 8], F32)
        kmb = self.sb(st, "kmb", [128, 8], BF16)
        g1 = self.sb(st, "g1", [128, 64], F32)
        top = self.sb(st, "top", [128, 64], F32)
        cmpf = self.sb(st, "cmpf", [128, 64], F32)
        btok = self.sb(st, "btok", [128, 8, 128], BF16)
        biasT = self.sb(st, "biasT", [128, 1024], BF16)
        rden = [self.sb(st, "rden%d" % i, [128, 256], F32) for i in range(2)]
        S.op("pool", lambda e: e.memset(btok[:], 0.0), writes=["btok"])
        S.op("pool", lambda e: e.memset(biasT[:], 0.0), writes=["biasT"])
        self.norm("nmix", layer)
        wsrc = self.d_wqkv[slot].rearrange("(c p) f -> p c f", p=128)
        scale = 128.0 ** -0.5
        pastmask = self.cf[:, CF.off["pastmask"]:CF.off["pastmask"] + 64]
        cbm = [self.c16[:, CC.off["cb"] + j * 256:CC.off["cb"] + (j + 1) * 256] for j in range(2)]
        esel = [self.c16[:, CC.off["esel"] + n * 128:CC.off["esel"] + (n + 1) * 128] for n in range(8)]
        srot = [0]
        cnt = [0]

        def load_w(hd):
            bi = hd % 2
            for k in range(3):
                S.op("pool", lambda e, k=k: e.dma_start(
                    out=wqkv[bi][:, :, k * 128:(k + 1) * 128],
                    in_=wsrc[:, :, k * D + hd * 128:k * D + (hd + 1) * 128]),
                    writes=[("wqkv", bi, k)], dma=True)

        load_w(0)
        for hd in range(NH):
            bi = hd % 2
            if hd + 1 < NH:
                load_w(hd + 1)
            if hd % HG == 0:
                g0 = hd
                S.op("pool", lambda e, g0=g0: e.dma_start(
                    out=wo[:], in_=self.d_wo[slot, g0 * 128:(g0 + HG) * 128, :].rearrange("(h p) d -> p h d", p=128)),
                    writes=["wo"], dma=True)
            for which, dst, gname in ((0, qn[bi], "qg"), (1, kn[bi], "kg")):
                for tt in range(4):
                    ts = slice(tt * 512, (tt + 1) * 512)
                    b = self.psrot % 8
                    self.psrot += 1
                    b2 = self.psrot % 8
                    self.psrot += 1
                    k2 = cnt[0] % 2
                    cnt[0] += 1
                    for c in range(NC_):
                        S.op("pe", lambda e, c=c, b=b, ts=ts, which=which: e.matmul(
                            self.bank(b), lhsT=wqkv[bi][:, c, which * 128:(which + 1) * 128],
                            rhs=self.hT[:, c, ts], start=(c == 0), stop=(c == NC_ - 1)),
                            reads=[("wqkv", bi, which), ("h", c, tt)], writes=[("ps", b)])
                    S.op("act", lambda e, b=b, k2=k2: e.activation(out=sq2[k2][:], in_=self.bank(b), func=AF.Square),
                         reads=[("ps", b)], writes=[("sq2", k2)])
                    S.op("pe", lambda e, b2=b2, k2=k2: e.matmul(self.bank(b2), lhsT=self.ones, rhs=sq2[k2][:],
                                                               start=True, stop=True),
                         reads=[("sq2", k2), "c16"], writes=[("ps", b2)])
                    S.op("act", lambda e, b2=b2, k2=k2: e.activation(out=rs[k2][:], in_=self.bank(b2), func=AF.Ln,
                                                                     scale=1.0 / 128, bias=self.eps),
                         reads=[("ps", b2), "cf"], writes=[("rs", k2)])
                    S.op("act", lambda e, k2=k2: e.activation(out=rs[k2][:], in_=rs[k2][:], func=AF.Exp, scale=-0.5),
                         reads=[("rs", k2)], writes=[("rs", k2)])
                    S.op("dve", lambda e, b=b, k2=k2, dst=dst, ts=ts, gname=gname: e.scalar_tensor_tensor(
                        out=dst[:, ts], in0=self.bank(b), scalar=self.pcol(gname, slot), in1=rs[k2][:],
                        op0=ALU.mult, op1=ALU.mult),
                        reads=[("ps", b), ("rs", k2), "params"], writes=[("qk", which, bi, tt)])
            for i in range(16):
                for c in range(NC_):
                    S.op("pe", lambda e, i=i, c=c: e.matmul(
                        self.PS[:, i * 128:(i + 1) * 128], lhsT=self.hT[:, c, i * 128:(i + 1) * 128],
                        rhs=wqkv[bi][:, c, 256:384], start=(c == 0), stop=(c == NC_ - 1)),
                        reads=[("wqkv", bi, 2), ("h", c, i // 4)], writes=[("ps", i // 4)])
            for b in range(4):
                S.op("act", lambda e, b=b: e.activation(
                    out=Vt[bi][:, b * 4:(b + 1) * 4, :], in_=self.bank(b).rearrange("p (i d) -> p i d", d=128),
                    func=AF.Identity), reads=[("ps", b)], writes=[("V", bi, b)])
            S.op("dve", lambda e: e.tensor_reduce(out=kmf[:], in_=kn[bi][:].rearrange("p (n k) -> p n k", k=256),
                                                  axis=AX.X, op=ALU.add),
                 reads=[("qk", 1, bi, tt) for tt in range(4)], writes=["kmf"])
            S.op("dve", lambda e: e.tensor_scalar(out=kmb[:], in0=kmf[:], scalar1=1.0 / 256, scalar2=None, op0=ALU.mult),
                 reads=["kmf"], writes=["kmb"])
            bg = self.psrot % 8
            self.psrot += 1
            for i in range(8):
                S.op("pe", lambda e, i=i, bg=bg: e.matmul(
                    self.bank(bg)[:, i * 8:(i + 1) * 8], lhsT=qn[bi][:, (8 + i) * 128:(9 + i) * 128], rhs=kmb[:],
                    start=True, stop=True),
                    reads=["kmb", ("qk", 0, bi, 2 + i // 4)], writes=[("ps", bg)])
            S.op("dve", lambda e, bg=bg: e.tensor_tensor(out=g1[:], in0=self.bank(bg)[:, 0:64], in1=pastmask, op=ALU.add),
                 reads=[("ps", bg), "cf"], writes=["g1"])
            for i in range(8):
                S.op("dve", lambda e, i=i: e.max(out=top[:, i * 8:(i + 1) * 8], in_=g1[:, i * 8:(i + 1) * 8]),
                     reads=["g1"], writes=["top"])
            S.op("dve", lambda e: e.tensor_tensor(
                out=cmpf[:].rearrange("p (i n) -> p i n", n=8), in0=g1[:].rearrange("p (i n) -> p i n", n=8),
                in1=top[:].rearrange("p (i n) -> p i n", n=8)[:, :, 2:3].to_broadcast([128, 8, 8]), op=ALU.is_lt),
                reads=["g1", "top"], writes=["cmpf"])
            S.op("dve", lambda e: e.tensor_scalar(out=btok[:, :, 0:8], in0=cmpf[:].rearrange("p (i n) -> p i n", n=8),
                                                  scalar1=NEG, scalar2=None, op0=ALU.mult),
                 reads=["cmpf"], writes=["btok"])
            bt = []
            for i in range(8):
                if i % 4 == 0:
                    bt.append(self.psrot % 8)
                    self.psrot += 1
                bb_ = bt[-1]
                S.op("pe", lambda e, i=i, bb_=bb_: e.matmul(
                    self.bank(bb_)[:, (i % 4) * 128:(i % 4 + 1) * 128], lhsT=btok[:, i, :], rhs=self.ident,
                    start=True, stop=True),
                    reads=["btok", "c16"], writes=[("ps", bb_)])
            for k in range(2):
                S.op("act", lambda e, k=k: e.activation(out=biasT[0:8, k * 512:(k + 1) * 512],
                                                        in_=self.bank(bt[k])[0:8, :], func=AF.Identity),
                     reads=[("ps", bt[k])], writes=["biasT"])
            for qb in range(8):
                qs = slice(qb * 256, (qb + 1) * 256)
                bo = qb % 2
                bd = 2 + qb % 2
                for n in range(qb + 1):
                    bs_ = 4 + srot[0] % 4
                    pk = srot[0] % 4
                    srot[0] += 1
                    for half in range(2):
                        kt = 2 * n + half
                        osl = self.bank(bs_)[:, half * 256:(half + 1) * 256]
                        extra = (n == qb) or (qb >= 4)
                        S.op("pe", lambda e, osl=osl, kt=kt, qs=qs, extra=extra: e.matmul(
                            osl, lhsT=kn[bi][:, kt * 128:(kt + 1) * 128], rhs=qn[bi][:, qs],
                            start=True, stop=(not extra)),
                            reads=[("qk", 1, bi, kt // 4), ("qk", 0, bi, qb // 2)], writes=[("ps", bs_)])
                        if n == qb:
                            S.op("pe", lambda e, osl=osl, half=half: e.matmul(
                                osl, lhsT=self.ident, rhs=cbm[half], start=False, stop=True),
                                reads=["c16"], writes=[("ps", bs_)])
                        elif qb >= 4:
                            S.op("pe", lambda e, osl=osl, n=n, qb=qb: e.matmul(
                                osl, lhsT=esel[n], rhs=biasT[:, (qb - 4) * 256:(qb - 3) * 256],
                                start=False, stop=True),
                                reads=["c16", "biasT"], writes=[("ps", bs_)])
                    S.op("act", lambda e, bs_=bs_, pk=pk: e.activation(
                        out=PT[pk][:], in_=self.bank(bs_), func=AF.Exp, scale=scale),
                        reads=[("ps", bs_)], writes=[("PT", pk)])
                    for half in range(2):
                        kt = 2 * n + half
                        first = (n == 0 and half == 0)
                        last = (n == qb and half == 1)
                        S.op("pe", lambda e, kt=kt, pk=pk, half=half, first=first, last=last, bo=bo: e.matmul(
                            self.bank(bo)[:, 0:256], lhsT=Vt[bi][:, kt, :], rhs=PT[pk][:, half * 256:(half + 1) * 256],
                            start=first, stop=last),
                            reads=[("V", bi, kt // 4), ("PT", pk)], writes=[("ps", bo)])
                        S.op("pe", lambda e, pk=pk, half=half, first=first, last=last, bd=bd: e.matmul(
                            self.bank(bd)[:, 0:256], lhsT=self.ones, rhs=PT[pk][:, half * 256:(half + 1) * 256],
                            start=first, stop=last),
                            reads=["c16", ("PT", pk)], writes=[("ps", bd)])
                rk = qb % 2
                S.op("dve", lambda e, bd=bd, rk=rk: e.reciprocal(out=rden[rk][:], in_=self.bank(bd)[:, 0:256]),
                     reads=[("ps", bd)], writes=[("rden", rk)])
                S.op("dve", lambda e, bo=bo, rk=rk, qs=qs, hd=hd: e.tensor_tensor(
                    out=oT[:, hd % HG, qs], in0=self.bank(bo)[:, 0:256], in1=rden[rk][:], op=ALU.mult),
                    reads=[("ps", bo), ("rden", rk)], writes=[("oT", hd % HG, qb // 2)])
            if hd % HG == HG - 1:
                self.resid_proj(wo, HG, lambda k, tt: oT[:, k, tt * 512:(tt + 1) * 512],
                                lambda k, tt: [("oT", k, tt)], "wo")

    def lru(self, st, layer):
        nc, S = self.nc, self.S
        CG = 5
        win = [self.sb(st, "win%d" % i, [128, NC_, 256], BF16) for i in range(2)]
        wa = self.sb(st, "wa", [128, NLC, 128], BF16)
        wx = self.sb(st, "wx", [128, NLC, 128], BF16)
        wout = self.sb(st, "wout", [128, CG, D], BF16)
        hy = self.sb(st, "hy", [128, CG, S_], BF16)
        xc = self.sb(st, "xc", [128, S_], F32)
        xcb = self.sb(st, "xcb", [128, S_], BF16)
        gy = self.sb(st, "gy", [128, S_], F32)
        A = self.sb(st, "lruA", [128, S_], F32)
        I = self.sb(st, "lruI", [128, S_], F32)
        M = self.sb(st, "lruM", [128, S_], F32)
        dp = self.sb(st, "lrudp", [128, 4 * NLC], F32)
        self.norm("nmix", layer)
        S.op("pool", lambda e: e.dma_start(out=wa[:], in_=self.d_lwa[0].rearrange("n c d -> c n d")),
             writes=["wa"], dma=True)
        S.op("pool", lambda e: e.dma_start(out=wx[:], in_=self.d_lwx[0].rearrange("n c d -> c n d")),
             writes=["wx"], dma=True)
        lam = self.params[:, PC.off["llam"]:PC.off["llam"] + NLC]
        S.op("act", lambda e: e.activation(out=dp[:, 0:NLC], in_=lam, func=AF.Exp, scale=-1.0),
             reads=["params"], writes=["dp0"])
        S.op("act", lambda e: e.activation(out=dp[:, 0:NLC], in_=dp[:, 0:NLC], func=AF.Ln, bias=1.0),
             reads=["dp0"], writes=["dp0"])
        S.op("dve", lambda e: e.tensor_scalar(out=dp[:, NLC:2 * NLC], in0=dp[:, 0:NLC], scalar1=-4.0, scalar2=None,
                                              op0=ALU.mult), reads=["dp0"], writes=["dp"])
        S.op("dve", lambda e: e.tensor_scalar(out=dp[:, 2 * NLC:3 * NLC],
                                              in0=self.params[:, PC.off["lba"]:PC.off["lba"] + NLC],
                                              scalar1=0.5, scalar2=None, op0=ALU.mult), reads=["params"], writes=["dp"])
        S.op("dve", lambda e: e.tensor_scalar(out=dp[:, 3 * NLC:4 * NLC],
                                              in0=self.params[:, PC.off["lbx"]:PC.off["lbx"] + NLC],
                                              scalar1=0.5, scalar2=None, op0=ALU.mult), reads=["params"], writes=["dp"])
        wsrc = self.d_lwin[0].rearrange("(c p) f -> p c f", p=128)

        def load_win(c):
            bi = c % 2
            S.op("pool", lambda e: e.dma_start(out=win[bi][:, :, 0:128], in_=wsrc[:, :, c * 128:(c + 1) * 128]),
                 writes=[("win", bi, 0)], dma=True)
            S.op("pool", lambda e: e.dma_start(out=win[bi][:, :, 128:256],
                                               in_=wsrc[:, :, DR + c * 128:DR + (c + 1) * 128]),
                 writes=[("win", bi, 1)], dma=True)

        load_win(0)
        for c in range(NLC):
            bi = c % 2
            if c + 1 < NLC:
                load_win(c + 1)
            if c % CG == 0:
                c0 = c
                S.op("pool", lambda e, c0=c0: e.dma_start(
                    out=wout[:], in_=self.d_lwout[0, c0 * 128:(c0 + CG) * 128, :].rearrange("(k p) d -> p k d", p=128)),
                    writes=["wout"], dma=True)
            for half in range(2):
                for tt in range(4):
                    for k in range(NC_):
                        S.op("pe", lambda e, k=k, tt=tt, half=half: e.matmul(
                            self.PS[:, half * 2048 + tt * 512: half * 2048 + (tt + 1) * 512],
                            lhsT=win[bi][:, k, half * 128:(half + 1) * 128],
                            rhs=self.hT[:, k, tt * 512:(tt + 1) * 512], start=(k == 0), stop=(k == NC_ - 1)),
                            reads=[("win", bi, half), ("h", k, tt)], writes=[("ps", half * 4 + tt)])
            psx = [("ps", tt) for tt in range(4)]
            psy = [("ps", 4 + tt) for tt in range(4)]
            u = self.PS[:, 0:2048]
            uy = self.PS[:, 2048:4096]
            wcol = [self.pcol("lcw", j * NLC + c) for j in range(4)]
            S.op("act", lambda e, wcol=wcol: e.activation(out=xc[:], in_=u, func=AF.Identity, scale=wcol[3],
                                                          bias=self.pcol("lcb", c)),
                 reads=psx + ["params"], writes=["xc"])
            for k in (1, 2, 3):
                S.op("dve", lambda e, k=k, wcol=wcol: e.scalar_tensor_tensor(
                    out=xc[:, k:S_], in0=u[:, 0:S_ - k], scalar=wcol[3 - k], in1=xc[:, k:S_],
                    op0=ALU.mult, op1=ALU.add), reads=psx + ["xc", "params"], writes=["xc"])
            S.op("pool", lambda e: e.tensor_copy(out=xcb[:], in_=xc[:]), reads=["xc"], writes=["xcb"])
            S.op("act", lambda e: e.activation(out=gy[:], in_=uy, func=AF.Gelu_apprx_tanh),
                 reads=psy, writes=["gy"])
            for which, wt, wtok in ((0, wa, "wa"), (1, wx, "wx")):
                for tt in range(4):
                    S.op("pe", lambda e, tt=tt, which=which, wt=wt: e.matmul(
                        self.PS[:, which * 2048 + tt * 512: which * 2048 + (tt + 1) * 512],
                        lhsT=wt[:, c, :], rhs=xcb[:, tt * 512:(tt + 1) * 512], start=True, stop=True),
                        reads=[wtok, "xcb"], writes=[("ps", which * 4 + tt)])
            hcl = dp[:, NLC + c:NLC + c + 1]
            hba = dp[:, 2 * NLC + c:2 * NLC + c + 1]
            hbx = dp[:, 3 * NLC + c:3 * NLC + c + 1]
            S.op("act", lambda e, hba=hba: e.activation(out=A[:], in_=u, func=AF.Tanh, scale=0.5, bias=hba),
                 reads=psx + ["dp"], writes=["A"])
            S.op("act", lambda e, hbx=hbx: e.activation(out=I[:], in_=uy, func=AF.Tanh, scale=0.5, bias=hbx),
                 reads=psy + ["dp"], writes=["I"])
            S.op("act", lambda e, hcl=hcl: e.activation(out=A[:], in_=A[:], func=AF.Exp, scale=hcl, bias=hcl),
                 reads=["A", "dp"], writes=["A"])
            S.op("act", lambda e: e.activation(out=M[:], in_=A[:], func=AF.Square), reads=["A"], writes=["M"])
            S.op("act", lambda e: e.activation(out=M[:], in_=M[:], func=AF.Sqrt, scale=-1.0, bias=1.0),
                 reads=["M"], writes=["M"])
            S.op("dve", lambda e: e.scalar_tensor_tensor(out=I[:], in0=I[:], scalar=1.0, in1=xc[:],
                                                         op0=ALU.add, op1=ALU.mult),
                 reads=["I", "xc"], writes=["I"])
            S.op("dve", lambda e: e.scalar_tensor_tensor(out=I[:], in0=I[:], scalar=0.5, in1=M[:],
                                                         op0=ALU.mult, op1=ALU.mult),
                 reads=["I", "M"], writes=["I"])
            S.op("dve", lambda e: e.tensor_tensor_scan(out=M[:], data0=A[:], data1=I[:], initial=0.0,
                                                       op0=ALU.mult, op1=ALU.add),
                 reads=["A", "I", "M"], writes=["M"])
            cc = c % CG
            S.op("pool", lambda e, cc=cc: e.tensor_tensor(out=hy[:, cc, :], in0=M[:], in1=gy[:], op=ALU.mult),
                 reads=["M", "gy"], writes=[("hy", cc)])
            if cc == CG - 1:
                self.resid_proj(wout, CG, lambda k, tt: hy[:, k, tt * 512:(tt + 1) * 512],
                                lambda k, tt: [("hy", k)], "wout")

    def pool(self, st, layer):
        nc, S = self.nc, self.S
        rstd_all = self.sb(st, "rstd_all", [128, S_], F32)
        hf = [self.sb(st, "hf%d" % i, [128, S_], F32) for i in range(2)]
        B = [self.sb(st, "pB%d" % i, [128, S_], F32) for i in range(4)]
        t16 = [self.sb(st, "t16_%d" % i, [128, 16], F32) for i in range(2)]
        pw = self.sb(st, "pw", [128, 4, 2, 256], BF16)
        S.op("pool", lambda e: e.dma_start(out=pw[:], in_=self.d_pw[0].rearrange("g (k p) d -> p g k d", p=128)),
             writes=["pw"], dma=True)
        for tt in range(4):
            self.rstd_tt(tt, rstd_all[:, tt * 512:(tt + 1) * 512], ("rstd_all", tt))
        xall = lambda c: [("x", c, tt) for tt in range(4)]
        hall = lambda c: [("h", c, tt) for tt in range(4)]
        for c in range(NC_):
            g = c // 2
            k2 = c % 2
            hfc = hf[k2]
            S.op("dve", lambda e, c=c, hfc=hfc: e.scalar_tensor_tensor(
                out=hfc[:], in0=self.xT[:, c, :], scalar=self.pcol("nmix", layer * NC_ + c), in1=rstd_all[:],
                op0=ALU.mult, op1=ALU.mult),
                reads=xall(c) + [("rstd_all", tt) for tt in range(4)] + ["params"], writes=[("hf", k2)])
            cur, curtoks = hfc, [("hf", k2)]
            for k in range(g + 1):
                d = 2 ** k
                bi = k2 * 2 + k % 2
                nxt, nxttok = B[bi], ("pB", bi)
                eng = "pool" if k % 2 == 0 else "dve"
                S.op(eng, lambda e, cur=cur, nxt=nxt, d=d: e.tensor_tensor(
                    out=nxt[:, d:S_], in0=cur[:, d:S_], in1=cur[:, 0:S_ - d], op=ALU.add),
                    reads=list(curtoks), writes=[nxttok])
                S.op("act", lambda e, cur=cur, nxt=nxt, d=d: e.activation(out=nxt[:, 0:d], in_=cur[:, 0:d], func=AF.Identity),
                     reads=list(curtoks), writes=[(nxttok, "head")])
                cur, curtoks = nxt, [nxttok, (nxttok, "head")]
            w = 2 ** (g + 1)
            icnt = self.cf[:, CF.off["icnt"] + g * 16:CF.off["icnt"] + (g + 1) * 16]
            S.op("dve", lambda e, c=c, cur=cur, hfc=hfc, w=w: e.scalar_tensor_tensor(
                out=self.hT[:, c, :], in0=cur[:], scalar=1.0 / w, in1=hfc[:], op0=ALU.mult, op1=ALU.subtract),
                reads=curtoks + [("hf", k2)], writes=hall(c))
            S.op("dve", lambda e, cur=cur, k2=k2, icnt=icnt: e.tensor_tensor(
                out=t16[k2][:], in0=cur[:, 0:16], in1=icnt, op=ALU.mult),
                reads=curtoks + ["cf"], writes=[("t16", k2)])
            S.op("dve", lambda e, c=c, k2=k2, hfc=hfc: e.tensor_tensor(
                out=self.hT[:, c, 0:16], in0=t16[k2][:], in1=hfc[:, 0:16], op=ALU.subtract),
                reads=[("t16", k2), ("hf", k2)] + hall(c), writes=hall(c))
        for g in range(4):
            for m2 in range(2):
                m = 2 * g + m2
                for tt in range(4):
                    b = self.psrot % 8
                    self.psrot += 1
                    for kk in range(2):
                        S.op("pe", lambda e, b=b, g=g, kk=kk, m2=m2, tt=tt: e.matmul(
                            self.bank(b), lhsT=pw[:, g, kk, m2 * 128:(m2 + 1) * 128],
                            rhs=self.hT[:, 2 * g + kk, tt * 512:(tt + 1) * 512], start=(kk == 0), stop=(kk == 1)),
                            reads=["pw", ("h", 2 * g + kk, tt)], writes=[("ps", b)])
                    xs = self.xT[:, m, tt * 512:(tt + 1) * 512]
                    S.op("dve", lambda e, b=b, xs=xs, m=m: e.scalar_tensor_tensor(
                        out=xs, in0=self.bank(b), scalar=self.pcol("pscale", m), in1=xs, op0=ALU.mult, op1=ALU.add),
                        reads=[("ps", b), ("x", m, tt), "params"], writes=[("x", m, tt)])


ALL_PHASES = []
for _l in range(DEPTH):
    ALL_PHASES += [("mix", _l), ("ffn", _l)]

WEIGHT_KEYS = ["ffn_w_up", "ffn_w_down", "attn_w_qkv", "attn_w_o", "lru_w_in", "lru_w_a", "lru_w_x",
               "lru_w_out", "pool_w"]


def run_phases(inputs, phases, x_cores=None, trace=False, **bkw):
    nc = Builder(phases, **bkw).build()
    params = pack_params(inputs)
    c16, cf = make_consts()
    if x_cores is None:
        x = np.asarray(inputs["x"], np.float32)
        x_cores = [np.ascontiguousarray(x[b].T) for b in range(8)]
    shared = {k: np.ascontiguousarray(np.asarray(inputs[k], np.float32)) for k in WEIGHT_KEYS}
    shared.update({"params": params, "c16": c16, "cf": cf})
    in_maps = []
    for b in range(8):
        m = dict(shared)
        m["xT"] = x_cores[b]
        in_maps.append(m)
    res = run_bass_kernel_spmd(nc, in_maps, core_ids=list(range(8)), trace=trace)
    return [r["outT"] for r in res.results], res


def kernel(**inputs):
    outs, _ = run_phases(inputs, ALL_PHASES)
    return np.stack([np.ascontiguousarray(o.T) for o in outs], axis=0).astype(np.float32)
```

```python
import contextlib
import numpy as np
import concourse.bass as bass
import concourse.mybir as mybir
from concourse.bass_utils import run_bass_kernel_spmd

F32 = mybir.dt.float32
BF16 = mybir.dt.bfloat16
AF = mybir.ActivationFunctionType
ALU = mybir.AluOpType
AX = mybir.AxisListType

D = 1024
S_ = 2048
NC_ = 8
DEPTH = 4
FH = 2816
NPAIR = 22
DR = 1280
NLC = 10
EPS = 1e-6
NEG = -30000.0
ENGS = ("pe", "act", "dve", "pool", "sp")


class Op:
    __slots__ = ("eng", "fn", "deps", "dma", "signal", "semv")

    def __init__(self, eng, fn, dma):
        self.eng = eng
        self.fn = fn
        self.deps = []
        self.dma = dma
        self.signal = False
        self.semv = None


class _Rec:
    def __init__(self):
        self.calls = []

    def __getattr__(self, name):
        def f(*args, **kwargs):
            self.calls.append((name, args, kwargs))
        return f


class Sched:
    N_DMA_SEMS = 8

    def __init__(self, nc):
        self.nc = nc
        self.ops = {e: [] for e in ENGS}
        self.last_writer = {}
        self.readers = {}
        self.fence_pending = set()
        self.fence_ops = []

    def fence(self):
        self.fence_ops = [self.ops[e][-1] for e in ENGS if self.ops[e]]
        self.fence_pending = set(ENGS)

    def op(self, eng, fn, reads=(), writes=(), dma=False):
        rec = _Rec()
        fn(rec)
        name, args, kwargs = rec.calls[0]
        o = Op(eng, lambda e: getattr(e, name)(*args, **kwargs), dma)
        cand = []
        for t in reads:
            w = self.last_writer.get(t)
            if w is not None:
                cand.append((w, True))
        for t in writes:
            w = self.last_writer.get(t)
            if w is not None:
                cand.append((w, False))
            for r in self.readers.get(t, ()):
                cand.append((r, False))
        seen = set()
        for d, raw in cand:
            if d is o or id(d) in seen:
                continue
            if d.eng == eng and not d.dma and not dma:
                if eng == "pe" or not raw:
                    continue
            seen.add(id(d))
            o.deps.append(d)
        if eng in self.fence_pending:
            self.fence_pending.discard(eng)
            for d in self.fence_ops:
                if id(d) in seen or (d.eng == eng and not d.dma and not dma):
                    continue
                seen.add(id(d))
                o.deps.append(d)
        self.ops[eng].append(o)
        for t in writes:
            self.last_writer[t] = o
            self.readers[t] = []
        for t in reads:
            self.readers.setdefault(t, []).append(o)
        return o

    def emit(self, final_waits=()):
        nc = self.nc
        for e in ENGS:
            for o in self.ops[e]:
                for d in o.deps:
                    d.signal = True
        for o in final_waits:
            o.signal = True
        with contextlib.ExitStack() as st:
            sems = {e: st.enter_context(nc.semaphore("s_" + e)) for e in ENGS}
            for e in ("sp", "pool", "act"):
                for k in range(self.N_DMA_SEMS):
                    sems[(e, k)] = st.enter_context(nc.semaphore("d_%s%d" % (e, k)))
            for e in ENGS:
                c = 0
                dcount = [0] * self.N_DMA_SEMS
                nd = 0
                for o in self.ops[e]:
                    if o.dma:
                        k = nd % self.N_DMA_SEMS
                        nd += 1
                        dcount[k] += 1
                        o.semv = ((e, k), 16 * dcount[k])
                    elif o.signal:
                        c += 1
                        o.semv = (e, c)
            block = st.enter_context(nc.Block())
            engobj = {"pe": block.tensor, "act": block.scalar, "dve": block.vector,
                      "pool": block.gpsimd, "sp": block.sync}

            def make(e):
                def body(eng):
                    known = {}
                    for o in self.ops[e]:
                        waits = {}
                        for d in o.deps:
                            sk, v = d.semv
                            if known.get(sk, 0) >= v:
                                continue
                            if waits.get(sk, 0) < v:
                                waits[sk] = v
                        if o.dma:
                            sk, v = o.semv
                            if v > 16 and known.get(sk, 0) < v - 16 and waits.get(sk, 0) < v - 16:
                                waits[sk] = v - 16
                        for sk, v in waits.items():
                            eng.wait_ge(sems[sk], v)
                            known[sk] = v
                        ins = o.fn(eng)
                        if o.semv is not None:
                            ins.then_inc(sems[o.semv[0]], 16 if o.dma else 1)
                    if e == "sp":
                        for o in final_waits:
                            eng.wait_ge(sems[o.semv[0]], o.semv[1])
                return body

            for e in ENGS:
                engobj[e](make(e))


class Cols:
    def __init__(self):
        self.n = 0
        self.off = {}

    def add(self, name, k):
        self.off[name] = self.n
        self.n += k


PC = Cols()
PC.add("nmix", DEPTH * NC_)
PC.add("nffn", DEPTH * NC_)
PC.add("qg", 2)
PC.add("kg", 2)
PC.add("lcw", 4 * NLC)
PC.add("lcb", NLC)
PC.add("lba", NLC)
PC.add("lbx", NLC)
PC.add("llam", NLC)
PC.add("pscale", NC_)
PC.add("fcw", DEPTH * 3 * 44)
PC.add("fcb", DEPTH * 44)

CC = Cols()
CC.add("ident", 128)
CC.add("ones", 128)
CC.add("cb", 2 * 256)
CC.add("esel", 8 * 128)
NCB = CC.n
CF = Cols()
CF.add("pastmask", 64)
CF.add("icnt", 4 * 16)
CF.add("eps", 1)


def pack_params(inp):
    P = np.zeros((128, PC.n), np.float32)

    def put(name, arr):
        a = np.asarray(arr, np.float32)
        a = a.reshape(-1, a.shape[-1] // 128, 128)
        a = a.transpose(2, 0, 1).reshape(128, -1)
        P[:, PC.off[name]:PC.off[name] + a.shape[1]] = a

    put("nmix", inp["norm_mix_g"])
    put("nffn", inp["norm_ffn_g"])
    put("qg", inp["attn_q_g"])
    put("kg", inp["attn_k_g"])
    put("lcw", inp["lru_conv_w"][0])
    put("lcb", inp["lru_conv_b"][0])
    put("lba", inp["lru_b_a"][0])
    put("lbx", inp["lru_b_x"][0])
    put("llam", inp["lru_lambda"][0])
    put("pscale", inp["pool_scale"][0])
    put("fcw", inp["ffn_conv_w"])
    put("fcb", inp["ffn_conv_b"])
    return P


def make_consts():
    cb16 = np.zeros((128, CC.n), np.float32)
    cb16[:, CC.off["ident"]:CC.off["ident"] + 128] = np.eye(128, dtype=np.float32)
    cb16[:, CC.off["ones"]:CC.off["ones"] + 128] = 1.0
    p = np.arange(128)[:, None]
    q = np.arange(256)[None, :]
    for j in range(2):
        m = np.where(j * 128 + p <= q, 0.0, NEG).astype(np.float32)
        cb16[:, CC.off["cb"] + j * 256:CC.off["cb"] + (j + 1) * 256] = m
    es = np.zeros((128, 8, 128), np.float32)
    for n in range(8):
        es[n, n, :] = 1.0
    cb16[:, CC.off["esel"]:CC.off["esel"] + 1024] = es.reshape(128, 1024)
    cf = np.zeros((128, CF.n), np.float32)
    pm = np.zeros((8, 8), np.float32)
    for i in range(8):
        for n in range(8):
            pm[i, n] = 0.0 if n < 4 + i // 2 else -1e30
    cf[:, CF.off["pastmask"]:CF.off["pastmask"] + 64] = pm.reshape(1, 64)
    ic = np.zeros((4, 16), np.float32)
    for g, w in enumerate((2, 4, 8, 16)):
        for t in range(16):
            ic[g, t] = 1.0 / min(t + 1, w)
    cf[:, CF.off["icnt"]:CF.off["icnt"] + 64] = ic.reshape(1, 64)
    cf[:, CF.off["eps"]] = EPS
    return cb16, cf


class Builder:
    def __init__(self, phases, attn_heads=8, attn_hg=4):
        self.phases = phases
        self.attn_heads = attn_heads
        self.attn_hg = attn_hg
        nc = bass.Bass("TRN2", target_bir_lowering=False)
        self.nc = nc
        dt = nc.dram_tensor
        self.d_x = dt("xT", [D, S_], F32, kind="ExternalInput").ap()
        self.d_out = dt("outT", [D, S_], F32, kind="ExternalOutput").ap()
        self.d_params = dt("params", [128, PC.n], F32, kind="ExternalInput").ap()
        self.d_c16 = dt("c16", [128, CC.n], F32, kind="ExternalInput").ap()
        self.d_cf = dt("cf", [128, CF.n], F32, kind="ExternalInput").ap()
        self.d_wup = dt("ffn_w_up", [DEPTH, D, 2 * FH], F32, kind="ExternalInput").ap()
        self.d_wdn = dt("ffn_w_down", [DEPTH, FH, D], F32, kind="ExternalInput").ap()
        self.d_wqkv = dt("attn_w_qkv", [2, D, 3 * D], F32, kind="ExternalInput").ap()
        self.d_wo = dt("attn_w_o", [2, D, D], F32, kind="ExternalInput").ap()
        self.d_lwin = dt("lru_w_in", [1, D, 2 * DR], F32, kind="ExternalInput").ap()
        self.d_lwa = dt("lru_w_a", [1, NLC, 128, 128], F32, kind="ExternalInput").ap()
        self.d_lwx = dt("lru_w_x", [1, NLC, 128, 128], F32, kind="ExternalInput").ap()
        self.d_lwout = dt("lru_w_out", [1, DR, D], F32, kind="ExternalInput").ap()
        self.d_pw = dt("pool_w", [1, 4, 256, 256], F32, kind="ExternalInput").ap()
        self.S = Sched(nc)
        self.psrot = 0
        self.uid = 0

    def sb(self, st, name, shape, dtype):
        self.uid += 1
        return st.enter_context(self.nc.sbuf_tensor("%s_u%d" % (name, self.uid), shape, dtype))

    def pcol(self, name, idx):
        o = PC.off[name] + idx
        return self.params[:, o:o + 1]

    def build(self):
        nc, S = self.nc, self.S
        with contextlib.ExitStack() as st:
            self.xT = self.sb(st, "xT_sb", [128, NC_, S_], F32)
            self.hT = self.sb(st, "hT_sb", [128, NC_, S_], BF16)
            self.params = self.sb(st, "params_sb", [128, PC.n], F32)
            self.c16 = self.sb(st, "c16_sb", [128, CC.n], BF16)
            self.cf = self.sb(st, "cf_sb", [128, CF.n], F32)
            self.sq = self.sb(st, "sq_sb", [128, NC_, 512], BF16)
            self.lnv = self.sb(st, "lnv_sb", [128, 512], F32)
            self.rstd = self.sb(st, "rstd_sb", [128, 512], F32)
            self.PS = st.enter_context(nc.psum_tensor("PS", [128, 4096], F32))
            self.ident = self.c16[:, CC.off["ident"]:CC.off["ident"] + 128]
            self.ones = self.c16[:, CC.off["ones"]:CC.off["ones"] + 128]
            self.eps = self.cf[:, CF.off["eps"]:CF.off["eps"] + 1]

            S.op("sp", lambda e: e.dma_start(out=self.params[:], in_=self.d_params[:, :]),
                 writes=["params"], dma=True)
            S.op("sp", lambda e: e.dma_start(out=self.cf[:], in_=self.d_cf[:, :]),
                 writes=["cf"], dma=True)
            S.op("pool", lambda e: e.dma_start(out=self.c16[:], in_=self.d_c16[:, :]),
                 writes=["c16"], dma=True)
            for c in range(NC_):
                S.op("sp", lambda e, c=c: e.dma_start(out=self.xT[:, c, :],
                                                      in_=self.d_x[c * 128:(c + 1) * 128, :]),
                     writes=[("x", c, tt) for tt in range(4)], dma=True)

            for ph in self.phases:
                kind, layer = ph
                S.fence()
                if kind == "ffn":
                    with contextlib.ExitStack() as st2:
                        self.ffn(st2, layer)
                elif kind == "mix":
                    with contextlib.ExitStack() as st2:
                        mk = layer % 3
                        if mk == 0:
                            self.attn(st2, layer)
                        elif mk == 1:
                            self.lru(st2, layer)
                        else:
                            self.pool(st2, layer)

            fw = []
            for c in range(NC_):
                fw.append(S.op("sp", lambda e, c=c: e.dma_start(
                    out=self.d_out[c * 128:(c + 1) * 128, :], in_=self.xT[:, c, :]),
                    reads=[("x", c, tt) for tt in range(4)], dma=True))
            S.emit(final_waits=fw)
        return nc

    def bank(self, b):
        return self.PS[:, b * 512:(b + 1) * 512]

    def norm_tt(self, gname, layer, tt):
        S = self.S
        ts = slice(tt * 512, (tt + 1) * 512)
        b = self.psrot % 8
        self.psrot += 1
        S.op("act", lambda e: e.activation(out=self.sq[:], in_=self.xT[:, :, ts], func=AF.Square),
             reads=[("x", c, tt) for c in range(NC_)], writes=["sq"])
        for c in range(NC_):
            S.op("pe", lambda e, c=c: e.matmul(self.bank(b), lhsT=self.ones, rhs=self.sq[:, c, :],
                                               start=(c == 0), stop=(c == NC_ - 1)),
                 reads=["sq", "c16"], writes=[("ps", b)])
        S.op("act", lambda e: e.activation(out=self.lnv[:], in_=self.bank(b), func=AF.Ln,
                                           scale=1.0 / D, bias=self.eps),
             reads=[("ps", b), "cf"], writes=["lnv"])
        S.op("act", lambda e: e.activation(out=self.rstd[:], in_=self.lnv[:], func=AF.Exp, scale=-0.5),
             reads=["lnv"], writes=["rstd"])
        for c in range(NC_):
            S.op("dve", lambda e, c=c: e.scalar_tensor_tensor(
                out=self.hT[:, c, ts], in0=self.xT[:, c, ts], scalar=self.pcol(gname, layer * NC_ + c),
                in1=self.rstd[:], op0=ALU.mult, op1=ALU.mult),
                reads=[("x", c, tt), "rstd", "params"], writes=[("h", c, tt)])

    def norm(self, gname, layer):
        for tt in range(4):
            self.norm_tt(gname, layer, tt)

    def ffn(self, st, layer):
        nc, S = self.nc, self.S
        GROUPS = [(0, 6), (6, 12), (12, 17), (17, 22)]
        wup = [self.sb(st, "wup%d" % i, [128, NC_, 512], BF16) for i in range(2)]
        wdn = self.sb(st, "wdn", [128, 6, D], BF16)
        Pb = self.sb(st, "Pb", [128, 6, S_], BF16)
        cbuf = [self.sb(st, "cbuf%d" % i, [128, S_], F32) for i in range(3)]
        self.norm("nffn", layer)
        wup_src = self.d_wup[layer].rearrange("(c p) f -> p c f", p=128)

        def load_slab(s):
            bi = s % 2
            S.op("pool", lambda e: e.dma_start(out=wup[bi][:, :, 0:256],
                                               in_=wup_src[:, :, s * 256:(s + 1) * 256]),
                 writes=[("wup", bi, 0)], dma=True)
            S.op("pool", lambda e: e.dma_start(out=wup[bi][:, :, 256:512],
                                               in_=wup_src[:, :, FH + s * 256:FH + (s + 1) * 256]),
                 writes=[("wup", bi, 1)], dma=True)

        def load_wdn(j0, j1):
            S.op("pool", lambda e: e.dma_start(
                out=wdn[:, 0:j1 - j0, :],
                in_=self.d_wdn[layer, j0 * 128:j1 * 128, :].rearrange("(j p) d -> p j d", p=128)),
                writes=["wdn"], dma=True)

        load_slab(0)
        load_slab(1)
        cbi = 0
        for (j0, j1) in GROUPS:
            load_wdn(j0, j1)
            for j in range(j0, j1):
                s, r = j // 2, j % 2
                bi = s % 2
                cg = None
                for half in range(2):
                    fj = half * NPAIR + j
                    base = half * 2048
                    lcol = half * 256 + r * 128
                    for tt in range(4):
                        for c in range(NC_):
                            S.op("pe", lambda e, c=c, tt=tt, base=base, lcol=lcol, bi=bi: e.matmul(
                                self.PS[:, base + tt * 512: base + (tt + 1) * 512],
                                lhsT=wup[bi][:, c, lcol:lcol + 128],
                                rhs=self.hT[:, c, tt * 512:(tt + 1) * 512],
                                start=(c == 0), stop=(c == NC_ - 1)),
                                reads=[("wup", bi, half), ("h", c, tt)], writes=[("ps", half * 4 + tt)])
                    cb = cbuf[cbi % 3]
                    cbn = ("cbuf", cbi % 3)
                    cbi += 1
                    psr = [("ps", half * 4 + tt) for tt in range(4)]
                    w0 = self.pcol("fcw", (layer * 3 + 0) * 44 + fj)
                    w1 = self.pcol("fcw", (layer * 3 + 1) * 44 + fj)
                    w2 = self.pcol("fcw", (layer * 3 + 2) * 44 + fj)
                    bb = self.pcol("fcb", layer * 44 + fj)
                    u = self.PS[:, base:base + 2048]
                    S.op("act", lambda e, cb=cb, u=u, w2=w2, bb=bb: e.activation(
                        out=cb[:], in_=u, func=AF.Identity, scale=w2, bias=bb),
                        reads=psr + ["params"], writes=[cbn])
                    S.op("dve", lambda e, cb=cb, u=u, w1=w1: e.scalar_tensor_tensor(
                        out=cb[:, 1:S_], in0=u[:, 0:S_ - 1], scalar=w1, in1=cb[:, 1:S_],
                        op0=ALU.mult, op1=ALU.add), reads=psr + [cbn, "params"], writes=[cbn])
                    S.op("dve", lambda e, cb=cb, u=u, w0=w0: e.scalar_tensor_tensor(
                        out=cb[:, 2:S_], in0=u[:, 0:S_ - 2], scalar=w0, in1=cb[:, 2:S_],
                        op0=ALU.mult, op1=ALU.add), reads=psr + [cbn, "params"], writes=[cbn])
                    if half == 0:
                        S.op("act", lambda e, cb=cb: e.activation(out=cb[:], in_=cb[:], func=AF.Silu),
                             reads=[cbn], writes=[cbn])
                        cg = (cb, cbn)
                    else:
                        jj = j - j0
                        S.op("pool", lambda e, cb=cb, cgb=cg[0], jj=jj: e.tensor_tensor(
                            out=Pb[:, jj, :], in0=cgb[:], in1=cb[:], op=ALU.mult),
                            reads=[cbn, cg[1]], writes=[("P", jj)])
                if r == 1 and s + 2 < 11:
                    load_slab(s + 2)
            nj = j1 - j0
            for tt in range(4):
                for m in range(NC_):
                    b = self.psrot % 8
                    self.psrot += 1
                    for jj in range(nj):
                        S.op("pe", lambda e, b=b, jj=jj, m=m, tt=tt: e.matmul(
                            self.bank(b), lhsT=wdn[:, jj, m * 128:(m + 1) * 128],
                            rhs=Pb[:, jj, tt * 512:(tt + 1) * 512],
                            start=(jj == 0), stop=(jj == nj - 1)),
                            reads=["wdn", ("P", jj)], writes=[("ps", b)])
                    S.op("dve", lambda e, b=b, m=m, tt=tt: e.tensor_tensor(
                        out=self.xT[:, m, tt * 512:(tt + 1) * 512], in0=self.bank(b),
                        in1=self.xT[:, m, tt * 512:(tt + 1) * 512], op=ALU.add),
                        reads=[("ps", b), ("x", m, tt)], writes=[("x", m, tt)])

    def rstd_tt(self, tt, dst, dst_tok):
        S = self.S
        ts = slice(tt * 512, (tt + 1) * 512)
        b = self.psrot % 8
        self.psrot += 1
        S.op("act", lambda e: e.activation(out=self.sq[:], in_=self.xT[:, :, ts], func=AF.Square),
             reads=[("x", c, tt) for c in range(NC_)], writes=["sq"])
        for c in range(NC_):
            S.op("pe", lambda e, c=c: e.matmul(self.bank(b), lhsT=self.ones, rhs=self.sq[:, c, :],
                                               start=(c == 0), stop=(c == NC_ - 1)),
                 reads=["sq", "c16"], writes=[("ps", b)])
        S.op("act", lambda e: e.activation(out=self.lnv[:], in_=self.bank(b), func=AF.Ln,
                                           scale=1.0 / D, bias=self.eps),
             reads=[("ps", b), "cf"], writes=["lnv"])
        S.op("act", lambda e: e.activation(out=dst, in_=self.lnv[:], func=AF.Exp, scale=-0.5),
             reads=["lnv"], writes=[dst_tok])

    def resid_proj(self, w, nk, rhs_fn, rhs_toks, wtok, scale_name=None):
        S = self.S
        for tt in range(4):
            for m in range(NC_):
                b = self.psrot % 8
                self.psrot += 1
                for k in range(nk):
                    S.op("pe", lambda e, b=b, k=k, m=m, tt=tt: e.matmul(
                        self.bank(b), lhsT=w[:, k, m * 128:(m + 1) * 128], rhs=rhs_fn(k, tt),
                        start=(k == 0), stop=(k == nk - 1)),
                        reads=[wtok] + rhs_toks(k, tt), writes=[("ps", b)])
                xs = self.xT[:, m, tt * 512:(tt + 1) * 512]
                S.op("dve", lambda e, b=b, xs=xs: e.tensor_tensor(
                    out=xs, in0=self.bank(b), in1=xs, op=ALU.add),
                    reads=[("ps", b), ("x", m, tt)], writes=[("x", m, tt)])

    def attn(self, st, layer):
        nc, S = self.nc, self.S
        slot = layer // 3
        HG = self.attn_hg
        NH = self.attn_heads
        wqkv = [self.sb(st, "wqkv%d" % i, [128, NC_, 384], BF16) for i in range(2)]
        qn = [self.sb(st, "qn%d" % i, [128, S_], BF16) for i in range(2)]
        kn = [self.sb(st, "kn%d" % i, [128, S_], BF16) for i in range(2)]
        Vt = [self.sb(st, "Vt%d" % i, [128, 16, 128], BF16) for i in range(2)]
        oT = self.sb(st, "oT", [128, HG, S_], BF16)
        wo = self.sb(st, "wo", [128, HG, D], BF16)
        rs = [self.sb(st, "rs%d" % i, [128, 512], F32) for i in range(2)]
        sq2 = [self.sb(st, "sq2_%d" % i, [128, 512], BF16) for i in range(2)]
        PT = [self.sb(st, "PT%d" % i, [128, 512], BF16) for i in range(4)]
        kmf = self.sb(st, "kmf", [128, 8], F32)
        kmb = self.sb(st, "kmb", [128, 8], BF16)
        g1 = self.sb(st, "g1", [128, 64], F32)
        top = self.sb(st, "top", [128, 64], F32)
        cmpf = self.sb(st, "cmpf", [128, 64], F32)
        btok = self.sb(st, "btok", [128, 8, 128], BF16)
        biasT = self.sb(st, "biasT", [128, 1024], BF16)
        rden = [self.sb(st, "rden%d" % i, [128, 256], F32) for i in range(2)]
        S.op("pool", lambda e: e.memset(btok[:], 0.0), writes=["btok"])
        S.op("pool", lambda e: e.memset(biasT[:], 0.0), writes=["biasT"])
        self.norm("nmix", layer)
        wsrc = self.d_wqkv[slot].rearrange("(c p) f -> p c f", p=128)
        scale = 128.0 ** -0.5
        pastmask = self.cf[:, CF.off["pastmask"]:CF.off["pastmask"] + 64]
        cbm = [self.c16[:, CC.off["cb"] + j * 256:CC.off["cb"] + (j + 1) * 256] for j in range(2)]
        esel = [self.c16[:, CC.off["esel"] + n * 128:CC.off["esel"] + (n + 1) * 128] for n in range(8)]
        cnt = {"s": 0, "p": 0, "k2": 0}

        def load_w(hd):
            bi = hd % 2
            for k in range(3):
                S.op("pool", lambda e, k=k: e.dma_start(
                    out=wqkv[bi][:, :, k * 128:(k + 1) * 128],
                    in_=wsrc[:, :, k * D + hd * 128:k * D + (hd + 1) * 128]),
                    writes=[("wqkv", bi, k)], dma=True)

        def prologue_units(hd):
            bi = hd % 2
            units = []
            items = [(which, tt) for which in (0, 1) for tt in range(4)]
            state = {}

            def A1(i):
                which, tt = items[i]
                ts = slice(tt * 512, (tt + 1) * 512)
                b = 5 + cnt["p"] % 2
                cnt["p"] += 1
                k2 = cnt["k2"] % 2
                cnt["k2"] += 1
                state[i] = (b, k2)
                for c in range(NC_):
                    S.op("pe", lambda e, c=c: e.matmul(
                        self.bank(b), lhsT=wqkv[bi][:, c, which * 128:(which + 1) * 128],
                        rhs=self.hT[:, c, ts], start=(c == 0), stop=(c == NC_ - 1)),
                        reads=[("wqkv", bi, which), ("h", c, tt)], writes=[("ps", b)])
                S.op("act", lambda e: e.activation(out=sq2[k2][:], in_=self.bank(b), func=AF.Square),
                     reads=[("ps", b)], writes=[("sq2", k2)])

            def A2(i):
                which, tt = items[i]
                ts = slice(tt * 512, (tt + 1) * 512)
                b, k2 = state[i]
                dst = (qn if which == 0 else kn)[bi]
                gname = "qg" if which == 0 else "kg"
                S.op("pe", lambda e: e.matmul(self.bank(7), lhsT=self.ones, rhs=sq2[k2][:], start=True, stop=True),
                     reads=[("sq2", k2), "c16"], writes=[("ps", 7)])
                S.op("act", lambda e: e.activation(out=rs[k2][:], in_=self.bank(7), func=AF.Ln,
                                                   scale=1.0 / 128, bias=self.eps),
                     reads=[("ps", 7), "cf"], writes=[("rs", k2)])
                S.op("act", lambda e: e.activation(out=rs[k2][:], in_=rs[k2][:], func=AF.Exp, scale=-0.5),
                     reads=[("rs", k2)], writes=[("rs", k2)])
                S.op("dve", lambda e: e.scalar_tensor_tensor(
                    out=dst[:, ts], in0=self.bank(b), scalar=self.pcol(gname, slot), in1=rs[k2][:],
                    op0=ALU.mult, op1=ALU.mult),
                    reads=[("ps", b), ("rs", k2), "params"], writes=[("qk", which, bi, tt)])

            units.append(lambda: A1(0))
            for i in range(1, 8):
                units.append(lambda i=i: A1(i))
                units.append(lambda i=i: A2(i - 1))
            units.append(lambda: A2(7))

            def Vunit(bq):
                b = 5 + cnt["p"] % 2
                cnt["p"] += 1
                for i4 in range(4):
                    i = bq * 4 + i4
                    for c in range(NC_):
                        S.op("pe", lambda e, i=i, i4=i4, c=c: e.matmul(
                            self.bank(b)[:, i4 * 128:(i4 + 1) * 128], lhsT=self.hT[:, c, i * 128:(i + 1) * 128],
                            rhs=wqkv[bi][:, c, 256:384], start=(c == 0), stop=(c == NC_ - 1)),
                            reads=[("wqkv", bi, 2), ("h", c, bq)], writes=[("ps", b)])
                S.op("act", lambda e: e.activation(
                    out=Vt[bi][:, bq * 4:(bq + 1) * 4, :], in_=self.bank(b).rearrange("p (i d) -> p i d", d=128),
                    func=AF.Identity), reads=[("ps", b)], writes=[("V", bi, bq)])

            for bq in range(4):
                units.append(lambda bq=bq: Vunit(bq))

            def Cunit():
                S.op("dve", lambda e: e.tensor_reduce(out=kmf[:], in_=kn[bi][:].rearrange("p (n k) -> p n k", k=256),
                                                      axis=AX.X, op=ALU.add),
                     reads=[("qk", 1, bi, tt) for tt in range(4)], writes=["kmf"])
                S.op("dve", lambda e: e.tensor_scalar(out=kmb[:], in0=kmf[:], scalar1=1.0 / 256, scalar2=None,
                                                      op0=ALU.mult), reads=["kmf"], writes=["kmb"])
                for i in range(8):
                    S.op("pe", lambda e, i=i: e.matmul(
                        self.bank(7)[:, i * 8:(i + 1) * 8], lhsT=qn[bi][:, (8 + i) * 128:(9 + i) * 128], rhs=kmb[:],
                        start=True, stop=True),
                        reads=["kmb", ("qk", 0, bi, 2 + i // 4)], writes=[("ps", 7)])
                S.op("dve", lambda e: e.tensor_tensor(out=g1[:], in0=self.bank(7)[:, 0:64], in1=pastmask, op=ALU.add),
                     reads=[("ps", 7), "cf"], writes=["g1"])
                for i in range(8):
                    S.op("dve", lambda e, i=i: e.max(out=top[:, i * 8:(i + 1) * 8], in_=g1[:, i * 8:(i + 1) * 8]),
                         reads=["g1"], writes=["top"])
                S.op("dve", lambda e: e.tensor_tensor(
                    out=cmpf[:].rearrange("p (i n) -> p i n", n=8), in0=g1[:].rearrange("p (i n) -> p i n", n=8),
                    in1=top[:].rearrange("p (i n) -> p i n", n=8)[:, :, 2:3].to_broadcast([128, 8, 8]), op=ALU.is_lt),
                    reads=["g1", "top"], writes=["cmpf"])

            units.append(Cunit)
            return units

        def Dunit(hd):
            S.op("dve", lambda e: e.tensor_scalar(out=btok[:, :, 0:8], in0=cmpf[:].rearrange("p (i n) -> p i n", n=8),
                                                  scalar1=NEG, scalar2=None, op0=ALU.mult),
                 reads=["cmpf"], writes=["btok"])
            for k in range(2):
                for i4 in range(4):
                    i = k * 4 + i4
                    S.op("pe", lambda e, i=i, i4=i4: e.matmul(
                        self.bank(7)[:, i4 * 128:(i4 + 1) * 128], lhsT=btok[:, i, :], rhs=self.ident,
                        start=True, stop=True),
                        reads=["btok", "c16"], writes=[("ps", 7)])
                S.op("act", lambda e, k=k: e.activation(out=biasT[0:8, k * 512:(k + 1) * 512],
                                                        in_=self.bank(7)[0:8, :], func=AF.Identity),
                     reads=[("ps", 7)], writes=["biasT"])

        def E1(hd, qb, n, st_):
            bi = hd % 2
            qs = slice(qb * 256, (qb + 1) * 256)
            bs_ = 2 + cnt["s"] % 3
            pk = cnt["s"] % 4
            cnt["s"] += 1
            st_[(qb, n)] = pk
            for half in range(2):
                kt = 2 * n + half
                osl = self.bank(bs_)[:, half * 256:(half + 1) * 256]
                extra = (n == qb) or (qb >= 4)
                S.op("pe", lambda e, osl=osl, kt=kt, extra=extra: e.matmul(
                    osl, lhsT=kn[bi][:, kt * 128:(kt + 1) * 128], rhs=qn[bi][:, qs],
                    start=True, stop=(not extra)),
                    reads=[("qk", 1, bi, kt // 4), ("qk", 0, bi, qb // 2)], writes=[("ps", bs_)])
                if n == qb:
                    S.op("pe", lambda e, osl=osl, half=half: e.matmul(
                        osl, lhsT=self.ident, rhs=cbm[half], start=False, stop=True),
                        reads=["c16"], writes=[("ps", bs_)])
                elif qb >= 4:
                    S.op("pe", lambda e, osl=osl: e.matmul(
                        osl, lhsT=esel[n], rhs=biasT[:, (qb - 4) * 256:(qb - 3) * 256],
                        start=False, stop=True),
                        reads=["c16", "biasT"], writes=[("ps", bs_)])
            S.op("act", lambda e: e.activation(out=PT[pk][:], in_=self.bank(bs_), func=AF.Exp, scale=scale),
                 reads=[("ps", bs_)], writes=[("PT", pk)])

        def E2(hd, qb, n, st_):
            bi = hd % 2
            qs = slice(qb * 256, (qb + 1) * 256)
            pk = st_[(qb, n)]
            bo = qb % 2
            for half in range(2):
                kt = 2 * n + half
                first = (n == 0 and half == 0)
                last = (n == qb and half == 1)
                S.op("pe", lambda e, kt=kt, half=half, first=first, last=last: e.matmul(
                    self.bank(bo)[:, 0:256], lhsT=Vt[bi][:, kt, :], rhs=PT[pk][:, half * 256:(half + 1) * 256],
                    start=first, stop=last, skip_group_check=True),
                    reads=[("V", bi, kt // 4), ("PT", pk)], writes=[("ps", bo)])
                S.op("pe", lambda e, half=half, last=last: e.matmul(
                    self.bank(bo)[:, 256:512], lhsT=self.ones, rhs=PT[pk][:, half * 256:(half + 1) * 256],
                    start=False, stop=last, skip_group_check=True),
                    reads=["c16", ("PT", pk)], writes=[("ps", bo)])
            if n == qb:
                rk = qb % 2
                S.op("dve", lambda e: e.reciprocal(out=rden[rk][:], in_=self.bank(bo)[:, 256:512]),
                     reads=[("ps", bo)], writes=[("rden", rk)])
                S.op("dve", lambda e: e.tensor_tensor(
                    out=oT[:, hd % HG, qs], in0=self.bank(bo)[:, 0:256], in1=rden[rk][:], op=ALU.mult),
                    reads=[("ps", bo), ("rden", rk)], writes=[("oT", hd % HG, qb // 2)])

        load_w(0)
        for u in prologue_units(0):
            u()
        LAG = 2
        for hd in range(NH):
            if hd + 1 < NH:
                load_w(hd + 1)
                nxt = prologue_units(hd + 1)
            else:
                nxt = []
            if hd % HG == 0:
                g0 = hd
                S.op("pool", lambda e, g0=g0: e.dma_start(
                    out=wo[:], in_=self.d_wo[slot, g0 * 128:(g0 + HG) * 128, :].rearrange("(h p) d -> p h d", p=128)),
                    writes=["wo"], dma=True)
            items = [(qb, n) for qb in range(8) for n in range(qb + 1)]
            st_ = {}
            for idx in range(len(items) + LAG):
                if idx == 4:
                    Dunit(hd)
                if idx < len(items):
                    E1(hd, items[idx][0], items[idx][1], st_)
                if idx >= LAG:
                    E2(hd, items[idx - LAG][0], items[idx - LAG][1], st_)
                if nxt and idx >= 6:
                    nxt.pop(0)()
            while nxt:
                nxt.pop(0)()
            if hd % HG == HG - 1:
                self.resid_proj(wo, HG, lambda k, tt: oT[:, k, tt * 512:(tt + 1) * 512],
                                lambda k, tt: [("oT", k, tt)], "wo")

    def lru(self, st, layer):
        nc, S = self.nc, self.S
        CG = 5
        win = [self.sb(st, "win%d" % i, [128, NC_, 256], BF16) for i in range(2)]
        wa = self.sb(st, "wa", [128, NLC, 128], BF16)
        wx = self.sb(st, "wx", [128, NLC, 128], BF16)
        wout = self.sb(st, "wout", [128, CG, D], BF16)
        hy = self.sb(st, "hy", [128, CG, S_], BF16)
        xc = self.sb(st, "xc", [128, S_], F32)
        xcb = self.sb(st, "xcb", [128, S_], BF16)
        gy = self.sb(st, "gy", [128, S_], F32)
        A = self.sb(st, "lruA", [128, S_], F32)
        I = self.sb(st, "lruI", [128, S_], F32)
        M = self.sb(st, "lruM", [128, S_], F32)
        dp = self.sb(st, "lrudp", [128, 4 * NLC], F32)
        self.norm("nmix", layer)
        S.op("pool", lambda e: e.dma_start(out=wa[:], in_=self.d_lwa[0].rearrange("n c d -> c n d")),
             writes=["wa"], dma=True)
        S.op("pool", lambda e: e.dma_start(out=wx[:], in_=self.d_lwx[0].rearrange("n c d -> c n d")),
             writes=["wx"], dma=True)
        lam = self.params[:, PC.off["llam"]:PC.off["llam"] + NLC]
        S.op("act", lambda e: e.activation(out=dp[:, 0:NLC], in_=lam, func=AF.Exp, scale=-1.0),
             reads=["params"], writes=["dp0"])
        S.op("act", lambda e: e.activation(out=dp[:, 0:NLC], in_=dp[:, 0:NLC], func=AF.Ln, bias=1.0),
             reads=["dp0"], writes=["dp0"])
        S.op("dve", lambda e: e.tensor_scalar(out=dp[:, NLC:2 * NLC], in0=dp[:, 0:NLC], scalar1=-4.0, scalar2=None,
                                              op0=ALU.mult), reads=["dp0"], writes=["dp"])
        S.op("dve", lambda e: e.tensor_scalar(out=dp[:, 2 * NLC:3 * NLC],
                                              in0=self.params[:, PC.off["lba"]:PC.off["lba"] + NLC],
                                              scalar1=0.5, scalar2=None, op0=ALU.mult), reads=["params"], writes=["dp"])
        S.op("dve", lambda e: e.tensor_scalar(out=dp[:, 3 * NLC:4 * NLC],
                                              in0=self.params[:, PC.off["lbx"]:PC.off["lbx"] + NLC],
                                              scalar1=0.5, scalar2=None, op0=ALU.mult), reads=["params"], writes=["dp"])
        wsrc = self.d_lwin[0].rearrange("(c p) f -> p c f", p=128)

        def load_win(c):
            bi = c % 2
            S.op("pool", lambda e: e.dma_start(out=win[bi][:, :, 0:128], in_=wsrc[:, :, c * 128:(c + 1) * 128]),
                 writes=[("win", bi, 0)], dma=True)
            S.op("pool", lambda e: e.dma_start(out=win[bi][:, :, 128:256],
                                               in_=wsrc[:, :, DR + c * 128:DR + (c + 1) * 128]),
                 writes=[("win", bi, 1)], dma=True)

        load_win(0)
        for c in range(NLC):
            bi = c % 2
            if c + 1 < NLC:
                load_win(c + 1)
            if c % CG == 0:
                c0 = c
                S.op("pool", lambda e, c0=c0: e.dma_start(
                    out=wout[:], in_=self.d_lwout[0, c0 * 128:(c0 + CG) * 128, :].rearrange("(k p) d -> p k d", p=128)),
                    writes=["wout"], dma=True)
            for half in range(2):
                for tt in range(4):
                    for k in range(NC_):
                        S.op("pe", lambda e, k=k, tt=tt, half=half: e.matmul(
                            self.PS[:, half * 2048 + tt * 512: half * 2048 + (tt + 1) * 512],
                            lhsT=win[bi][:, k, half * 128:(half + 1) * 128],
                            rhs=self.hT[:, k, tt * 512:(tt + 1) * 512], start=(k == 0), stop=(k == NC_ - 1)),
                            reads=[("win", bi, half), ("h", k, tt)], writes=[("ps", half * 4 + tt)])
            psx = [("ps", tt) for tt in range(4)]
            psy = [("ps", 4 + tt) for tt in range(4)]
            u = self.PS[:, 0:2048]
            uy = self.PS[:, 2048:4096]
            wcol = [self.pcol("lcw", j * NLC + c) for j in range(4)]
            S.op("act", lambda e, wcol=wcol: e.activation(out=xc[:], in_=u, func=AF.Identity, scale=wcol[3],
                                                          bias=self.pcol("lcb", c)),
                 reads=psx + ["params"], writes=["xc"])
            for k in (1, 2, 3):
                S.op("dve", lambda e, k=k, wcol=wcol: e.scalar_tensor_tensor(
                    out=xc[:, k:S_], in0=u[:, 0:S_ - k], scalar=wcol[3 - k], in1=xc[:, k:S_],
                    op0=ALU.mult, op1=ALU.add), reads=psx + ["xc", "params"], writes=["xc"])
            S.op("pool", lambda e: e.tensor_copy(out=xcb[:], in_=xc[:]), reads=["xc"], writes=["xcb"])
            S.op("act", lambda e: e.activation(out=gy[:], in_=uy, func=AF.Gelu_apprx_tanh),
                 reads=psy, writes=["gy"])
            for which, wt, wtok in ((0, wa, "wa"), (1, wx, "wx")):
                for tt in range(4):
                    S.op("pe", lambda e, tt=tt, which=which, wt=wt: e.matmul(
                        self.PS[:, which * 2048 + tt * 512: which * 2048 + (tt + 1) * 512],
                        lhsT=wt[:, c, :], rhs=xcb[:, tt * 512:(tt + 1) * 512], start=True, stop=True),
                        reads=[wtok, "xcb"], writes=[("ps", which * 4 + tt)])
            hcl = dp[:, NLC + c:NLC + c + 1]
            hba = dp[:, 2 * NLC + c:2 * NLC + c + 1]
            hbx = dp[:, 3 * NLC + c:3 * NLC + c + 1]
            S.op("act", lambda e, hba=hba: e.activation(out=A[:], in_=u, func=AF.Tanh, scale=0.5, bias=hba),
                 reads=psx + ["dp"], writes=["A"])
            S.op("act", lambda e, hbx=hbx: e.activation(out=I[:], in_=uy, func=AF.Tanh, scale=0.5, bias=hbx),
                 reads=psy + ["dp"], writes=["I"])
            S.op("act", lambda e, hcl=hcl: e.activation(out=A[:], in_=A[:], func=AF.Exp, scale=hcl, bias=hcl),
                 reads=["A", "dp"], writes=["A"])
            S.op("act", lambda e: e.activation(out=M[:], in_=A[:], func=AF.Square), reads=["A"], writes=["M"])
            S.op("act", lambda e: e.activation(out=M[:], in_=M[:], func=AF.Sqrt, scale=-1.0, bias=1.0),
                 reads=["M"], writes=["M"])
            S.op("dve", lambda e: e.scalar_tensor_tensor(out=I[:], in0=I[:], scalar=1.0, in1=xc[:],
                                                         op0=ALU.add, op1=ALU.mult),
                 reads=["I", "xc"], writes=["I"])
            S.op("dve", lambda e: e.scalar_tensor_tensor(out=I[:], in0=I[:], scalar=0.5, in1=M[:],
                                                         op0=ALU.mult, op1=ALU.mult),
                 reads=["I", "M"], writes=["I"])
            S.op("dve", lambda e: e.tensor_tensor_scan(out=M[:], data0=A[:], data1=I[:], initial=0.0,
                                                       op0=ALU.mult, op1=ALU.add),
                 reads=["A", "I", "M"], writes=["M"])
            cc = c % CG
            S.op("pool", lambda e, cc=cc: e.tensor_tensor(out=hy[:, cc, :], in0=M[:], in1=gy[:], op=ALU.mult),
                 reads=["M", "gy"], writes=[("hy", cc)])
            if cc == CG - 1:
                self.resid_proj(wout, CG, lambda k, tt: hy[:, k, tt * 512:(tt + 1) * 512],
                                lambda k, tt: [("hy", k)], "wout")

    def pool(self, st, layer):
        nc, S = self.nc, self.S
        rstd_all = self.sb(st, "rstd_all", [128, S_], F32)
        hf = [self.sb(st, "hf%d" % i, [128, S_], F32) for i in range(2)]
        B = [self.sb(st, "pB%d" % i, [128, S_], F32) for i in range(4)]
        t16 = [self.sb(st, "t16_%d" % i, [128, 16], F32) for i in range(2)]
        pw = self.sb(st, "pw", [128, 4, 2, 256], BF16)
        S.op("pool", lambda e: e.dma_start(out=pw[:], in_=self.d_pw[0].rearrange("g (k p) d -> p g k d", p=128)),
             writes=["pw"], dma=True)
        for tt in range(4):
            self.rstd_tt(tt, rstd_all[:, tt * 512:(tt + 1) * 512], ("rstd_all", tt))
        xall = lambda c: [("x", c, tt) for tt in range(4)]
        hall = lambda c: [("h", c, tt) for tt in range(4)]
        for c in range(NC_):
            g = c // 2
            k2 = c % 2
            hfc = hf[k2]
            S.op("dve", lambda e, c=c, hfc=hfc: e.scalar_tensor_tensor(
                out=hfc[:], in0=self.xT[:, c, :], scalar=self.pcol("nmix", layer * NC_ + c), in1=rstd_all[:],
                op0=ALU.mult, op1=ALU.mult),
                reads=xall(c) + [("rstd_all", tt) for tt in range(4)] + ["params"], writes=[("hf", k2)])
            cur, curtoks = hfc, [("hf", k2)]
            for k in range(g + 1):
                d = 2 ** k
                bi = k2 * 2 + k % 2
                nxt, nxttok = B[bi], ("pB", bi)
                eng = "pool" if k % 2 == 0 else "dve"
                S.op(eng, lambda e, cur=cur, nxt=nxt, d=d: e.tensor_tensor(
                    out=nxt[:, d:S_], in0=cur[:, d:S_], in1=cur[:, 0:S_ - d], op=ALU.add),
                    reads=list(curtoks), writes=[nxttok])
                S.op("act", lambda e, cur=cur, nxt=nxt, d=d: e.activation(out=nxt[:, 0:d], in_=cur[:, 0:d], func=AF.Identity),
                     reads=list(curtoks), writes=[(nxttok, "head")])
                cur, curtoks = nxt, [nxttok, (nxttok, "head")]
            w = 2 ** (g + 1)
            icnt = self.cf[:, CF.off["icnt"] + g * 16:CF.off["icnt"] + (g + 1) * 16]
            S.op("dve", lambda e, c=c, cur=cur, hfc=hfc, w=w: e.scalar_tensor_tensor(
                out=self.hT[:, c, :], in0=cur[:], scalar=1.0 / w, in1=hfc[:], op0=ALU.mult, op1=ALU.subtract),
                reads=curtoks + [("hf", k2)], writes=hall(c))
            S.op("dve", lambda e, cur=cur, k2=k2, icnt=icnt: e.tensor_tensor(
                out=t16[k2][:], in0=cur[:, 0:16], in1=icnt, op=ALU.mult),
                reads=curtoks + ["cf"], writes=[("t16", k2)])
            S.op("dve", lambda e, c=c, k2=k2, hfc=hfc: e.tensor_tensor(
                out=self.hT[:, c, 0:16], in0=t16[k2][:], in1=hfc[:, 0:16], op=ALU.subtract),
                reads=[("t16", k2), ("hf", k2)] + hall(c), writes=hall(c))
        for g in range(4):
            for m2 in range(2):
                m = 2 * g + m2
                for tt in range(4):
                    b = self.psrot % 8
                    self.psrot += 1
                    for kk in range(2):
                        S.op("pe", lambda e, b=b, g=g, kk=kk, m2=m2, tt=tt: e.matmul(
                            self.bank(b), lhsT=pw[:, g, kk, m2 * 128:(m2 + 1) * 128],
                            rhs=self.hT[:, 2 * g + kk, tt * 512:(tt + 1) * 512], start=(kk == 0), stop=(kk == 1)),
                            reads=["pw", ("h", 2 * g + kk, tt)], writes=[("ps", b)])
                    xs = self.xT[:, m, tt * 512:(tt + 1) * 512]
                    S.op("dve", lambda e, b=b, xs=xs, m=m: e.scalar_tensor_tensor(
                        out=xs, in0=self.bank(b), scalar=self.pcol("pscale", m), in1=xs, op0=ALU.mult, op1=ALU.add),
                        reads=[("ps", b), ("x", m, tt), "params"], writes=[("x", m, tt)])


ALL_PHASES = []
for _l in range(DEPTH):
    ALL_PHASES += [("mix", _l), ("ffn", _l)]

WEIGHT_KEYS = ["ffn_w_up", "ffn_w_down", "attn_w_qkv", "attn_w_o", "lru_w_in", "lru_w_a", "lru_w_x",
               "lru_w_out", "pool_w"]


def run_phases(inputs, phases, x_cores=None, trace=False, **bkw):
    nc = Builder(phases, **bkw).build()
    params = pack_params(inputs)
    c16, cf = make_consts()
    if x_cores is None:
        x = np.asarray(inputs["x"], np.float32)
        x_cores = [np.ascontiguousarray(x[b].T) for b in range(8)]
    shared = {k: np.ascontiguousarray(np.asarray(inputs[k], np.float32)) for k in WEIGHT_KEYS}
    shared.update({"params": params, "c16": c16, "cf": cf})
    in_maps = []
    for b in range(8):
        m = dict(shared)
        m["xT"] = x_cores[b]
        in_maps.append(m)
    res = run_bass_kernel_spmd(nc, in_maps, core_ids=list(range(8)), trace=trace)
    return [r["outT"] for r in res.results], res


def kernel(**inputs):
    outs, _ = run_phases(inputs, ALL_PHASES)
    return np.stack([np.ascontiguousarray(o.T) for o in outs], axis=0).astype(np.float32)
```

```python
import contextlib
import numpy as np
import concourse.bass as bass
import concourse.mybir as mybir
from concourse.bass_utils import run_bass_kernel_spmd

F32 = mybir.dt.float32
BF16 = mybir.dt.bfloat16
AF = mybir.ActivationFunctionType
ALU = mybir.AluOpType
AX = mybir.AxisListType

D = 1024
S_ = 2048
NC_ = 8
DEPTH = 4
FH = 2816
NPAIR = 22
DR = 1280
NLC = 10
EPS = 1e-6
NEG = -30000.0
ENGS = ("pe", "act", "dve", "pool", "sp")


class Op:
    __slots__ = ("eng", "fn", "deps", "dma", "signal", "semv")

    def __init__(self, eng, fn, dma):
        self.eng = eng
        self.fn = fn
        self.deps = []
        self.dma = dma
        self.signal = False
        self.semv = None


class _Rec:
    def __init__(self):
        self.calls = []

    def __getattr__(self, name):
        def f(*args, **kwargs):
            self.calls.append((name, args, kwargs))
        return f


class Sched:
    N_DMA_SEMS = 8

    def __init__(self, nc):
        self.nc = nc
        self.ops = {e: [] for e in ENGS}
        self.last_writer = {}
        self.readers = {}
        self.fence_pending = set()
        self.fence_ops = []

    def fence(self):
        self.fence_ops = [self.ops[e][-1] for e in ENGS if self.ops[e]]
        self.fence_pending = set(ENGS)

    def op(self, eng, fn, reads=(), writes=(), dma=False):
        rec = _Rec()
        fn(rec)
        name, args, kwargs = rec.calls[0]
        o = Op(eng, lambda e: getattr(e, name)(*args, **kwargs), dma)
        cand = []
        for t in reads:
            w = self.last_writer.get(t)
            if w is not None:
                cand.append((w, True))
        for t in writes:
            w = self.last_writer.get(t)
            if w is not None:
                cand.append((w, False))
            for r in self.readers.get(t, ()):
                cand.append((r, False))
        seen = set()
        for d, raw in cand:
            if d is o or id(d) in seen:
                continue
            if d.eng == eng and not d.dma and not dma:
                if eng == "pe" or not raw:
                    continue
            seen.add(id(d))
            o.deps.append(d)
        if eng in self.fence_pending:
            self.fence_pending.discard(eng)
            for d in self.fence_ops:
                if id(d) in seen or (d.eng == eng and not d.dma and not dma):
                    continue
                seen.add(id(d))
                o.deps.append(d)
        self.ops[eng].append(o)
        for t in writes:
            self.last_writer[t] = o
            self.readers[t] = []
        for t in reads:
            self.readers.setdefault(t, []).append(o)
        return o

    def emit(self, final_waits=()):
        nc = self.nc
        for e in ENGS:
            for o in self.ops[e]:
                for d in o.deps:
                    d.signal = True
        for o in final_waits:
            o.signal = True
        with contextlib.ExitStack() as st:
            sems = {e: st.enter_context(nc.semaphore("s_" + e)) for e in ENGS}
            for e in ("sp", "pool", "act"):
                for k in range(self.N_DMA_SEMS):
                    sems[(e, k)] = st.enter_context(nc.semaphore("d_%s%d" % (e, k)))
            for e in ENGS:
                c = 0
                dcount = [0] * self.N_DMA_SEMS
                nd = 0
                for o in self.ops[e]:
                    if o.dma:
                        k = nd % self.N_DMA_SEMS
                        nd += 1
                        dcount[k] += 1
                        o.semv = ((e, k), 16 * dcount[k])
                    elif o.signal:
                        c += 1
                        o.semv = (e, c)
            block = st.enter_context(nc.Block())
            engobj = {"pe": block.tensor, "act": block.scalar, "dve": block.vector,
                      "pool": block.gpsimd, "sp": block.sync}

            def make(e):
                def body(eng):
                    known = {}
                    for o in self.ops[e]:
                        waits = {}
                        for d in o.deps:
                            sk, v = d.semv
                            if known.get(sk, 0) >= v:
                                continue
                            if waits.get(sk, 0) < v:
                                waits[sk] = v
                        if o.dma:
                            sk, v = o.semv
                            if v > 16 and known.get(sk, 0) < v - 16 and waits.get(sk, 0) < v - 16:
                                waits[sk] = v - 16
                        for sk, v in waits.items():
                            eng.wait_ge(sems[sk], v)
                            known[sk] = v
                        ins = o.fn(eng)
                        if o.semv is not None:
                            ins.then_inc(sems[o.semv[0]], 16 if o.dma else 1)
                    if e == "sp":
                        for o in final_waits:
                            eng.wait_ge(sems[o.semv[0]], o.semv[1])
                return body

            for e in ENGS:
                engobj[e](make(e))


class Cols:
    def __init__(self):
        self.n = 0
        self.off = {}

    def add(self, name, k):
        self.off[name] = self.n
        self.n += k


PC = Cols()
PC.add("nmix", DEPTH * NC_)
PC.add("nffn", DEPTH * NC_)
PC.add("qg", 2)
PC.add("kg", 2)
PC.add("lcw", 4 * NLC)
PC.add("lcb", NLC)
PC.add("lba", NLC)
PC.add("lbx", NLC)
PC.add("llam", NLC)
PC.add("pscale", NC_)
PC.add("fcw", DEPTH * 3 * 44)
PC.add("fcb", DEPTH * 44)

CC = Cols()
CC.add("ident", 128)
CC.add("ones", 128)
CC.add("cb", 2 * 256)
CC.add("esel", 8 * 128)
NCB = CC.n
CF = Cols()
CF.add("pastmask", 64)
CF.add("icnt", 4 * 16)
CF.add("eps", 1)


def pack_params(inp):
    P = np.zeros((128, PC.n), np.float32)

    def put(name, arr):
        a = np.asarray(arr, np.float32)
        a = a.reshape(-1, a.shape[-1] // 128, 128)
        a = a.transpose(2, 0, 1).reshape(128, -1)
        P[:, PC.off[name]:PC.off[name] + a.shape[1]] = a

    put("nmix", inp["norm_mix_g"])
    put("nffn", inp["norm_ffn_g"])
    put("qg", inp["attn_q_g"])
    put("kg", inp["attn_k_g"])
    put("lcw", inp["lru_conv_w"][0])
    put("lcb", inp["lru_conv_b"][0])
    put("lba", inp["lru_b_a"][0])
    put("lbx", inp["lru_b_x"][0])
    put("llam", inp["lru_lambda"][0])
    put("pscale", inp["pool_scale"][0])
    put("fcw", inp["ffn_conv_w"])
    put("fcb", inp["ffn_conv_b"])
    return P


def make_consts():
    cb16 = np.zeros((128, CC.n), np.float32)
    cb16[:, CC.off["ident"]:CC.off["ident"] + 128] = np.eye(128, dtype=np.float32)
    cb16[:, CC.off["ones"]:CC.off["ones"] + 128] = 1.0
    p = np.arange(128)[:, None]
    q = np.arange(256)[None, :]
    for j in range(2):
        m = np.where(j * 128 + p <= q, 0.0, NEG).astype(np.float32)
        cb16[:, CC.off["cb"] + j * 256:CC.off["cb"] + (j + 1) * 256] = m
    es = np.zeros((128, 8, 128), np.float32)
    for n in range(8):
        es[n, n, :] = 1.0
    cb16[:, CC.off["esel"]:CC.off["esel"] + 1024] = es.reshape(128, 1024)
    cf = np.zeros((128, CF.n), np.float32)
    pm = np.zeros((8, 8), np.float32)
    for i in range(8):
        for n in range(8):
            pm[i, n] = 0.0 if n < 4 + i // 2 else -1e30
    cf[:, CF.off["pastmask"]:CF.off["pastmask"] + 64] = pm.reshape(1, 64)
    ic = np.zeros((4, 16), np.float32)
    for g, w in enumerate((2, 4, 8, 16)):
        for t in range(16):
            ic[g, t] = 1.0 / min(t + 1, w)
    cf[:, CF.off["icnt"]:CF.off["icnt"] + 64] = ic.reshape(1, 64)
    cf[:, CF.off["eps"]] = EPS
    return cb16, cf


class Builder:
    def __init__(self, phases, attn_heads=8, attn_hg=4):
        self.phases = phases
        self.attn_heads = attn_heads
        self.attn_hg = attn_hg
        nc = bass.Bass("TRN2", target_bir_lowering=False)
        self.nc = nc
        dt = nc.dram_tensor
        self.d_x = dt("xT", [D, S_], F32, kind="ExternalInput").ap()
        self.d_out = dt("outT", [D, S_], F32, kind="ExternalOutput").ap()
        self.d_params = dt("params", [128, PC.n], F32, kind="ExternalInput").ap()
        self.d_c16 = dt("c16", [128, CC.n], F32, kind="ExternalInput").ap()
        self.d_cf = dt("cf", [128, CF.n], F32, kind="ExternalInput").ap()
        self.d_wup = dt("ffn_w_up", [DEPTH, D, 2 * FH], F32, kind="ExternalInput").ap()
        self.d_wdn = dt("ffn_w_down", [DEPTH, FH, D], F32, kind="ExternalInput").ap()
        self.d_wqkv = dt("attn_w_qkv", [2, D, 3 * D], F32, kind="ExternalInput").ap()
        self.d_wo = dt("attn_w_o", [2, D, D], F32, kind="ExternalInput").ap()
        self.d_lwin = dt("lru_w_in", [1, D, 2 * DR], F32, kind="ExternalInput").ap()
        self.d_lwa = dt("lru_w_a", [1, NLC, 128, 128], F32, kind="ExternalInput").ap()
        self.d_lwx = dt("lru_w_x", [1, NLC, 128, 128], F32, kind="ExternalInput").ap()
        self.d_lwout = dt("lru_w_out", [1, DR, D], F32, kind="ExternalInput").ap()
        self.d_pw = dt("pool_w", [1, 4, 256, 256], F32, kind="ExternalInput").ap()
        self.S = Sched(nc)
        self.psrot = 0
        self.uid = 0

    def sb(self, st, name, shape, dtype):
        self.uid += 1
        return st.enter_context(self.nc.sbuf_tensor("%s_u%d" % (name, self.uid), shape, dtype))

    def pcol(self, name, idx):
        o = PC.off[name] + idx
        return self.params[:, o:o + 1]

    def build(self):
        nc, S = self.nc, self.S
        with contextlib.ExitStack() as st:
            self.xT = self.sb(st, "xT_sb", [128, NC_, S_], F32)
            self.hT = self.sb(st, "hT_sb", [128, NC_, S_], BF16)
            self.params = self.sb(st, "params_sb", [128, PC.n], F32)
            self.c16 = self.sb(st, "c16_sb", [128, CC.n], BF16)
            self.cf = self.sb(st, "cf_sb", [128, CF.n], F32)
            self.sq = self.sb(st, "sq_sb", [128, NC_, 512], BF16)
            self.lnv = self.sb(st, "lnv_sb", [128, 512], F32)
            self.rstd = self.sb(st, "rstd_sb", [128, 512], F32)
            self.PS = st.enter_context(nc.psum_tensor("PS", [128, 4096], F32))
            self.ident = self.c16[:, CC.off["ident"]:CC.off["ident"] + 128]
            self.ones = self.c16[:, CC.off["ones"]:CC.off["ones"] + 128]
            self.eps = self.cf[:, CF.off["eps"]:CF.off["eps"] + 1]

            S.op("sp", lambda e: e.dma_start(out=self.params[:], in_=self.d_params[:, :]),
                 writes=["params"], dma=True)
            S.op("sp", lambda e: e.dma_start(out=self.cf[:], in_=self.d_cf[:, :]),
                 writes=["cf"], dma=True)
            S.op("pool", lambda e: e.dma_start(out=self.c16[:], in_=self.d_c16[:, :]),
                 writes=["c16"], dma=True)
            for c in range(NC_):
                S.op("sp", lambda e, c=c: e.dma_start(out=self.xT[:, c, :],
                                                      in_=self.d_x[c * 128:(c + 1) * 128, :]),
                     writes=[("x", c, tt) for tt in range(4)], dma=True)

            self.wfirst = self.sb(st, "wfirst", [128, NC_, 512], BF16)
            nph = len(self.phases)
            self.prefetch_first(self.phases[0])
            self.start_norm(self.phases[0])
            for i, ph in enumerate(self.phases):
                kind, layer = ph
                self.next_phase = self.phases[i + 1] if i + 1 < nph else None
                S.fence()
                with contextlib.ExitStack() as st2:
                    if kind == "ffn":
                        self.ffn(st2, layer)
                    else:
                        mk = layer % 3
                        if mk == 0:
                            self.attn(st2, layer)
                        elif mk == 1:
                            self.lru(st2, layer)
                        else:
                            self.pool(st2, layer)

            fw = []
            for c in range(NC_):
                fw.append(S.op("sp", lambda e, c=c: e.dma_start(
                    out=self.d_out[c * 128:(c + 1) * 128, :], in_=self.xT[:, c, :]),
                    reads=[("x", c, tt) for tt in range(4)], dma=True))
            S.emit(final_waits=fw)
        return nc

    def bank(self, b):
        return self.PS[:, b * 512:(b + 1) * 512]

    def normA(self, tt):
        ts = slice(tt * 512, (tt + 1) * 512)
        self.S.op("act", lambda e: e.activation(out=self.sq[:], in_=self.xT[:, :, ts], func=AF.Square),
                  reads=[("x", c, tt) for c in range(NC_)], writes=["sq"])

    def normB(self, gname, layer, tt):
        S = self.S
        ts = slice(tt * 512, (tt + 1) * 512)
        b = self.psrot % 8
        self.psrot += 1
        for c in range(NC_):
            S.op("pe", lambda e, c=c: e.matmul(self.bank(b), lhsT=self.ones, rhs=self.sq[:, c, :],
                                               start=(c == 0), stop=(c == NC_ - 1)),
                 reads=["sq", "c16"], writes=[("ps", b)])
        S.op("act", lambda e: e.activation(out=self.lnv[:], in_=self.bank(b), func=AF.Ln,
                                           scale=1.0 / D, bias=self.eps),
             reads=[("ps", b), "cf"], writes=["lnv"])
        S.op("act", lambda e: e.activation(out=self.rstd[:], in_=self.lnv[:], func=AF.Exp, scale=-0.5),
             reads=["lnv"], writes=["rstd"])
        for c in range(NC_):
            S.op("dve", lambda e, c=c: e.scalar_tensor_tensor(
                out=self.hT[:, c, ts], in0=self.xT[:, c, ts], scalar=self.pcol(gname, layer * NC_ + c),
                in1=self.rstd[:], op0=ALU.mult, op1=ALU.mult),
                reads=[("x", c, tt), "rstd", "params"], writes=[("h", c, tt)])

    @staticmethod
    def norm_of(ph):
        if ph is None:
            return None
        kind, layer = ph
        if kind == "ffn":
            return ("nffn", layer)
        if layer % 3 == 2:
            return None
        return ("nmix", layer)

    def start_norm(self, ph):
        nm = self.norm_of(ph)
        if nm is None:
            return
        for tt in range(4):
            self.normA(tt)
            self.normB(nm[0], nm[1], tt)

    def tail_hook(self, tt):
        nm = self.norm_of(self.next_phase)
        if tt == 0 and self.next_phase is not None:
            self.prefetch_first(self.next_phase)
        if nm is None:
            return
        if tt >= 1:
            self.normB(nm[0], nm[1], tt - 1)
        self.normA(tt)
        if tt == 3:
            self.normB(nm[0], nm[1], 3)

    def prefetch_first(self, ph):
        S = self.S
        kind, layer = ph
        wf = self.wfirst
        if kind == "ffn":
            src = self.d_wup[layer].rearrange("(c p) f -> p c f", p=128)
            S.op("pool", lambda e: e.dma_start(out=wf[:, :, 0:256], in_=src[:, :, 0:256]),
                 writes=[("wf", 0), ("wf", 1)], dma=True)
            S.op("pool", lambda e: e.dma_start(out=wf[:, :, 256:512], in_=src[:, :, FH:FH + 256]),
                 writes=[("wf", 2), ("wf", 3)], dma=True)
        elif layer % 3 == 0:
            src = self.d_wqkv[layer // 3].rearrange("(c p) f -> p c f", p=128)
            for k in range(3):
                S.op("pool", lambda e, k=k: e.dma_start(out=wf[:, :, k * 128:(k + 1) * 128],
                                                        in_=src[:, :, k * D:k * D + 128]),
                     writes=[("wf", k)], dma=True)
        elif layer % 3 == 1:
            src = self.d_lwin[0].rearrange("(c p) f -> p c f", p=128)
            S.op("pool", lambda e: e.dma_start(out=wf[:, :, 0:128], in_=src[:, :, 0:128]),
                 writes=[("wf", 0)], dma=True)
            S.op("pool", lambda e: e.dma_start(out=wf[:, :, 128:256], in_=src[:, :, DR:DR + 128]),
                 writes=[("wf", 1)], dma=True)

    def ffn(self, st, layer):
        nc, S = self.nc, self.S
        GROUPS = [(0, 6), (6, 12), (12, 17), (17, 22)]
        NP = 7
        NW = 3
        wup = [self.wfirst] + [self.sb(st, "wup%d" % i, [128, NC_, 512], BF16) for i in range(1, NW)]
        wdn = self.sb(st, "wdn", [128, 6, D], BF16)
        Pb = self.sb(st, "Pb", [128, NP, S_], BF16)
        cbuf = [self.sb(st, "cbuf%d" % i, [128, S_], F32) for i in range(3)]
        wup_src = self.d_wup[layer].rearrange("(c p) f -> p c f", p=128)

        def wtok(bi, half):
            if bi == 0:
                return [("wf", 2 * half), ("wf", 2 * half + 1)]
            return [("wup", bi, half)]

        def load_slab(s):
            bi = s % NW
            S.op("pool", lambda e: e.dma_start(out=wup[bi][:, :, 0:256],
                                               in_=wup_src[:, :, s * 256:(s + 1) * 256]),
                 writes=wtok(bi, 0), dma=True)
            S.op("pool", lambda e: e.dma_start(out=wup[bi][:, :, 256:512],
                                               in_=wup_src[:, :, FH + s * 256:FH + (s + 1) * 256]),
                 writes=wtok(bi, 1), dma=True)

        def load_wdn(j0, j1):
            S.op("pool", lambda e: e.dma_start(
                out=wdn[:, 0:j1 - j0, :],
                in_=self.d_wdn[layer, j0 * 128:j1 * 128, :].rearrange("(j p) d -> p j d", p=128)),
                writes=["wdn"], dma=True)

        cbi = [0]

        def up(j):
            s, r = j // 2, j % 2
            bi = s % NW
            cg = None
            for half in range(2):
                fj = half * NPAIR + j
                base = half * 2048
                lcol = half * 256 + r * 128
                for tt in range(4):
                    for c in range(NC_):
                        S.op("pe", lambda e, c=c, tt=tt: e.matmul(
                            self.PS[:, base + tt * 512: base + (tt + 1) * 512],
                            lhsT=wup[bi][:, c, lcol:lcol + 128],
                            rhs=self.hT[:, c, tt * 512:(tt + 1) * 512],
                            start=(c == 0), stop=(c == NC_ - 1)),
                            reads=wtok(bi, half) + [("h", c, tt)], writes=[("ps", half * 4 + tt)])
                if half == 1 and r == 1 and s + NW < 11:
                    load_slab(s + NW)
                cb = cbuf[cbi[0] % 3]
                cbk = cbi[0] % 3
                cbn = [("cbuf", cbk, 0), ("cbuf", cbk, 1)]
                cbi[0] += 1
                w0 = self.pcol("fcw", (layer * 3 + 0) * 44 + fj)
                w1 = self.pcol("fcw", (layer * 3 + 1) * 44 + fj)
                w2 = self.pcol("fcw", (layer * 3 + 2) * 44 + fj)
                bb = self.pcol("fcb", layer * 44 + fj)
                u = self.PS[:, base:base + 2048]
                for hh in range(2):
                    lo, hi = hh * 1024, (hh + 1) * 1024
                    psr = [("ps", half * 4 + 2 * hh), ("ps", half * 4 + 2 * hh + 1)]
                    if hh == 1:
                        psr.append(("ps", half * 4 + 1))
                    ct = [("cbuf", cbk, hh)]
                    S.op("act", lambda e: e.activation(out=cb[:, lo:hi], in_=u[:, lo:hi], func=AF.Identity,
                                                       scale=w2, bias=bb),
                         reads=psr + ["params"], writes=ct)
                    for k, wk in ((1, w1), (2, w0)):
                        o0 = max(lo, k)
                        S.op("dve", lambda e, o0=o0, k=k, wk=wk: e.scalar_tensor_tensor(
                            out=cb[:, o0:hi], in0=u[:, o0 - k:hi - k], scalar=wk, in1=cb[:, o0:hi],
                            op0=ALU.mult, op1=ALU.add), reads=psr + ct + ["params"], writes=ct)
                if half == 0:
                    S.op("act", lambda e: e.activation(out=cb[:], in_=cb[:], func=AF.Silu),
                         reads=cbn, writes=cbn)
                    cg = (cb, cbn)
                else:
                    S.op("pool", lambda e: e.tensor_tensor(out=Pb[:, j % NP, :], in0=cg[0][:], in1=cb[:], op=ALU.mult),
                         reads=cbn + cg[1], writes=[("P", j % NP)])

        def down(j0, j1, nbanks=4, final=False):
            nj = j1 - j0
            for tt in range(4):
                for m in range(NC_):
                    b = self.psrot % nbanks
                    self.psrot += 1
                    for jj in range(nj):
                        slot = (j0 + jj) % NP
                        S.op("pe", lambda e, jj=jj, slot=slot: e.matmul(
                            self.bank(b), lhsT=wdn[:, jj, m * 128:(m + 1) * 128],
                            rhs=Pb[:, slot, tt * 512:(tt + 1) * 512],
                            start=(jj == 0), stop=(jj == nj - 1)),
                            reads=["wdn", ("P", slot)], writes=[("ps", b)])
                    xs = self.xT[:, m, tt * 512:(tt + 1) * 512]
                    S.op("dve", lambda e: e.tensor_tensor(out=xs, in0=self.bank(b), in1=xs, op=ALU.add),
                         reads=[("ps", b), ("x", m, tt)], writes=[("x", m, tt)])
                if final:
                    self.tail_hook(tt)

        for s0 in range(1, NW):
            load_slab(s0)
        load_wdn(*GROUPS[0])
        gidx = {}
        for gi, (j0, j1) in enumerate(GROUPS):
            for j in range(j0, j1):
                gidx[j] = gi
        for j in range(NPAIR):
            up(j)
            gi = gidx[j]
            j0, j1 = GROUPS[gi]
            if j == j0 and gi > 0:
                down(*GROUPS[gi - 1])
            if j == j0 + 2 and gi > 0:
                load_wdn(j0, j1)
        down(*GROUPS[-1], nbanks=8, final=True)

    def rstd_tt(self, tt, dst, dst_tok):
        S = self.S
        ts = slice(tt * 512, (tt + 1) * 512)
        b = self.psrot % 8
        self.psrot += 1
        S.op("act", lambda e: e.activation(out=self.sq[:], in_=self.xT[:, :, ts], func=AF.Square),
             reads=[("x", c, tt) for c in range(NC_)], writes=["sq"])
        for c in range(NC_):
            S.op("pe", lambda e, c=c: e.matmul(self.bank(b), lhsT=self.ones, rhs=self.sq[:, c, :],
                                               start=(c == 0), stop=(c == NC_ - 1)),
                 reads=["sq", "c16"], writes=[("ps", b)])
        S.op("act", lambda e: e.activation(out=self.lnv[:], in_=self.bank(b), func=AF.Ln,
                                           scale=1.0 / D, bias=self.eps),
             reads=[("ps", b), "cf"], writes=["lnv"])
        S.op("act", lambda e: e.activation(out=dst, in_=self.lnv[:], func=AF.Exp, scale=-0.5),
             reads=["lnv"], writes=[dst_tok])

    def resid_proj(self, w, nk, rhs_fn, rhs_toks, wtok, final=False):
        S = self.S
        for tt in range(4):
            for m in range(NC_):
                b = self.psrot % 8
                self.psrot += 1
                for k in range(nk):
                    S.op("pe", lambda e, b=b, k=k, m=m, tt=tt: e.matmul(
                        self.bank(b), lhsT=w[:, k, m * 128:(m + 1) * 128], rhs=rhs_fn(k, tt),
                        start=(k == 0), stop=(k == nk - 1)),
                        reads=[wtok] + rhs_toks(k, tt), writes=[("ps", b)])
                xs = self.xT[:, m, tt * 512:(tt + 1) * 512]
                S.op("dve", lambda e, b=b, xs=xs: e.tensor_tensor(
                    out=xs, in0=self.bank(b), in1=xs, op=ALU.add),
                    reads=[("ps", b), ("x", m, tt)], writes=[("x", m, tt)])
            if final:
                self.tail_hook(tt)

    def attn(self, st, layer):
        nc, S = self.nc, self.S
        slot = layer // 3
        HG = self.attn_hg
        NH = self.attn_heads
        wqkv = [self.wfirst, self.sb(st, "wqkv1", [128, NC_, 384], BF16)]

        def wqt(bi, k):
            return ("wf", k) if bi == 0 else ("wqkv", bi, k)

        qn = [self.sb(st, "qn%d" % i, [128, S_], BF16) for i in range(2)]
        kn = [self.sb(st, "kn%d" % i, [128, S_], BF16) for i in range(2)]
        Vt = [self.sb(st, "Vt%d" % i, [128, 16, 128], BF16) for i in range(2)]
        oT = self.sb(st, "oT", [128, HG, S_], BF16)
        wo = self.sb(st, "wo", [128, HG, D], BF16)
        rs = [self.sb(st, "rs%d" % i, [128, 512], F32) for i in range(2)]
        sq2 = [self.sb(st, "sq2_%d" % i, [128, 512], BF16) for i in range(2)]
        PT = [self.sb(st, "PT%d" % i, [128, 512], BF16) for i in range(4)]
        kmf = self.sb(st, "kmf", [128, 8], F32)
        kmb = self.sb(st, "kmb", [128, 8], BF16)
        g1 = self.sb(st, "g1", [128, 64], F32)
        top = self.sb(st, "top", [128, 64], F32)
        cmpf = self.sb(st, "cmpf", [128, 64], F32)
        btok = self.sb(st, "btok", [128, 8, 128], BF16)
        biasT = self.sb(st, "biasT", [128, 1024], BF16)
        rden = [self.sb(st, "rden%d" % i, [128, 256], F32) for i in range(2)]
        S.op("pool", lambda e: e.memset(btok[:], 0.0), writes=["btok"])
        S.op("pool", lambda e: e.memset(biasT[:], 0.0), writes=["biasT"])
        wsrc = self.d_wqkv[slot].rearrange("(c p) f -> p c f", p=128)
        scale = 128.0 ** -0.5
        pastmask = self.cf[:, CF.off["pastmask"]:CF.off["pastmask"] + 64]
        cbm = [self.c16[:, CC.off["cb"] + j * 256:CC.off["cb"] + (j + 1) * 256] for j in range(2)]
        esel = [self.c16[:, CC.off["esel"] + n * 128:CC.off["esel"] + (n + 1) * 128] for n in range(8)]
        cnt = {"s": 0, "p": 0, "k2": 0}

        def load_w(hd):
            bi = hd % 2
            for k in range(3):
                S.op("pool", lambda e, k=k: e.dma_start(
                    out=wqkv[bi][:, :, k * 128:(k + 1) * 128],
                    in_=wsrc[:, :, k * D + hd * 128:k * D + (hd + 1) * 128]),
                    writes=[wqt(bi, k)], dma=True)

        def prologue_units(hd):
            bi = hd % 2
            units = []
            items = [(which, tt) for which in (0, 1) for tt in range(4)]
            state = {}

            def A1(i):
                which, tt = items[i]
                ts = slice(tt * 512, (tt + 1) * 512)
                b = 5 + cnt["p"] % 2
                cnt["p"] += 1
                k2 = cnt["k2"] % 2
                cnt["k2"] += 1
                state[i] = (b, k2)
                for c in range(NC_):
                    S.op("pe", lambda e, c=c: e.matmul(
                        self.bank(b), lhsT=wqkv[bi][:, c, which * 128:(which + 1) * 128],
                        rhs=self.hT[:, c, ts], start=(c == 0), stop=(c == NC_ - 1)),
                        reads=[wqt(bi, which), ("h", c, tt)], writes=[("ps", b)])
                S.op("act", lambda e: e.activation(out=sq2[k2][:], in_=self.bank(b), func=AF.Square),
                     reads=[("ps", b)], writes=[("sq2", k2)])

            def A2(i):
                which, tt = items[i]
                ts = slice(tt * 512, (tt + 1) * 512)
                b, k2 = state[i]
                dst = (qn if which == 0 else kn)[bi]
                gname = "qg" if which == 0 else "kg"
                S.op("pe", lambda e: e.matmul(self.bank(7), lhsT=self.ones, rhs=sq2[k2][:], start=True, stop=True),
                     reads=[("sq2", k2), "c16"], writes=[("ps", 7)])
                S.op("act", lambda e: e.activation(out=rs[k2][:], in_=self.bank(7), func=AF.Ln,
                                                   scale=1.0 / 128, bias=self.eps),
                     reads=[("ps", 7), "cf"], writes=[("rs", k2)])
                S.op("act", lambda e: e.activation(out=rs[k2][:], in_=rs[k2][:], func=AF.Exp, scale=-0.5),
                     reads=[("rs", k2)], writes=[("rs", k2)])
                S.op("dve", lambda e: e.scalar_tensor_tensor(
                    out=dst[:, ts], in0=self.bank(b), scalar=self.pcol(gname, slot), in1=rs[k2][:],
                    op0=ALU.mult, op1=ALU.mult),
                    reads=[("ps", b), ("rs", k2), "params"], writes=[("qk", which, bi, tt)])

            units.append(lambda: A1(0))
            for i in range(1, 8):
                units.append(lambda i=i: A1(i))
                units.append(lambda i=i: A2(i - 1))
            units.append(lambda: A2(7))

            def Vunit(bq):
                b = 5 + cnt["p"] % 2
                cnt["p"] += 1
                for i4 in range(4):
                    i = bq * 4 + i4
                    for c in range(NC_):
                        S.op("pe", lambda e, i=i, i4=i4, c=c: e.matmul(
                            self.bank(b)[:, i4 * 128:(i4 + 1) * 128], lhsT=self.hT[:, c, i * 128:(i + 1) * 128],
                            rhs=wqkv[bi][:, c, 256:384], start=(c == 0), stop=(c == NC_ - 1)),
                            reads=[wqt(bi, 2), ("h", c, bq)], writes=[("ps", b)])
                S.op("act", lambda e: e.activation(
                    out=Vt[bi][:, bq * 4:(bq + 1) * 4, :], in_=self.bank(b).rearrange("p (i d) -> p i d", d=128),
                    func=AF.Identity), reads=[("ps", b)], writes=[("V", bi, bq)])

            for bq in range(4):
                units.append(lambda bq=bq: Vunit(bq))

            def Cunit():
                S.op("dve", lambda e: e.tensor_reduce(out=kmf[:], in_=kn[bi][:].rearrange("p (n k) -> p n k", k=256),
                                                      axis=AX.X, op=ALU.add),
                     reads=[("qk", 1, bi, tt) for tt in range(4)], writes=["kmf"])
                S.op("dve", lambda e: e.tensor_scalar(out=kmb[:], in0=kmf[:], scalar1=1.0 / 256, scalar2=None,
                                                      op0=ALU.mult), reads=["kmf"], writes=["kmb"])
                for i in range(8):
                    S.op("pe", lambda e, i=i: e.matmul(
                        self.bank(7)[:, i * 8:(i + 1) * 8], lhsT=qn[bi][:, (8 + i) * 128:(9 + i) * 128], rhs=kmb[:],
                        start=True, stop=True),
                        reads=["kmb", ("qk", 0, bi, 2 + i // 4)], writes=[("ps", 7)])
                S.op("dve", lambda e: e.tensor_tensor(out=g1[:], in0=self.bank(7)[:, 0:64], in1=pastmask, op=ALU.add),
                     reads=[("ps", 7), "cf"], writes=["g1"])
                for i in range(8):
                    S.op("dve", lambda e, i=i: e.max(out=top[:, i * 8:(i + 1) * 8], in_=g1[:, i * 8:(i + 1) * 8]),
                         reads=["g1"], writes=["top"])
                S.op("dve", lambda e: e.tensor_tensor(
                    out=cmpf[:].rearrange("p (i n) -> p i n", n=8), in0=g1[:].rearrange("p (i n) -> p i n", n=8),
                    in1=top[:].rearrange("p (i n) -> p i n", n=8)[:, :, 2:3].to_broadcast([128, 8, 8]), op=ALU.is_lt),
                    reads=["g1", "top"], writes=["cmpf"])

            units.append(Cunit)
            return units

        def Dunit(hd):
            S.op("dve", lambda e: e.tensor_scalar(out=btok[:, :, 0:8], in0=cmpf[:].rearrange("p (i n) -> p i n", n=8),
                                                  scalar1=NEG, scalar2=None, op0=ALU.mult),
                 reads=["cmpf"], writes=["btok"])
            for k in range(2):
                for i4 in range(4):
                    i = k * 4 + i4
                    S.op("pe", lambda e, i=i, i4=i4: e.matmul(
                        self.bank(7)[:, i4 * 128:(i4 + 1) * 128], lhsT=btok[:, i, :], rhs=self.ident,
                        start=True, stop=True),
                        reads=["btok", "c16"], writes=[("ps", 7)])
                S.op("act", lambda e, k=k: e.activation(out=biasT[0:8, k * 512:(k + 1) * 512],
                                                        in_=self.bank(7)[0:8, :], func=AF.Identity),
                     reads=[("ps", 7)], writes=["biasT"])

        def E1(hd, qb, n, st_):
            bi = hd % 2
            qs = slice(qb * 256, (qb + 1) * 256)
            bs_ = 2 + cnt["s"] % 3
            pk = cnt["s"] % 4
            cnt["s"] += 1
            st_[(qb, n)] = pk
            for half in range(2):
                kt = 2 * n + half
                osl = self.bank(bs_)[:, half * 256:(half + 1) * 256]
                extra = (n == qb) or (qb >= 4)
                S.op("pe", lambda e, osl=osl, kt=kt, extra=extra: e.matmul(
                    osl, lhsT=kn[bi][:, kt * 128:(kt + 1) * 128], rhs=qn[bi][:, qs],
                    start=True, stop=(not extra)),
                    reads=[("qk", 1, bi, kt // 4), ("qk", 0, bi, qb // 2)], writes=[("ps", bs_)])
                if n == qb:
                    S.op("pe", lambda e, osl=osl, half=half: e.matmul(
                        osl, lhsT=self.ident, rhs=cbm[half], start=False, stop=True),
                        reads=["c16"], writes=[("ps", bs_)])
                elif qb >= 4:
                    S.op("pe", lambda e, osl=osl: e.matmul(
                        osl, lhsT=esel[n], rhs=biasT[:, (qb - 4) * 256:(qb - 3) * 256],
                        start=False, stop=True),
                        reads=["c16", "biasT"], writes=[("ps", bs_)])
            S.op("act", lambda e: e.activation(out=PT[pk][:], in_=self.bank(bs_), func=AF.Exp, scale=scale),
                 reads=[("ps", bs_)], writes=[("PT", pk)])

        def E2(hd, qb, n, st_):
            bi = hd % 2
            qs = slice(qb * 256, (qb + 1) * 256)
            pk = st_[(qb, n)]
            bo = qb % 2
            for half in range(2):
                kt = 2 * n + half
                first = (n == 0 and half == 0)
                last = (n == qb and half == 1)
                S.op("pe", lambda e, kt=kt, half=half, first=first, last=last: e.matmul(
                    self.bank(bo)[:, 0:256], lhsT=Vt[bi][:, kt, :], rhs=PT[pk][:, half * 256:(half + 1) * 256],
                    start=first, stop=last, skip_group_check=True),
                    reads=[("V", bi, kt // 4), ("PT", pk)], writes=[("ps", bo)])
                S.op("pe", lambda e, half=half, last=last: e.matmul(
                    self.bank(bo)[:, 256:512], lhsT=self.ones, rhs=PT[pk][:, half * 256:(half + 1) * 256],
                    start=False, stop=last, skip_group_check=True),
                    reads=["c16", ("PT", pk)], writes=[("ps", bo)])
            if n == qb:
                rk = qb % 2
                S.op("dve", lambda e: e.reciprocal(out=rden[rk][:], in_=self.bank(bo)[:, 256:512]),
                     reads=[("ps", bo)], writes=[("rden", rk)])
                S.op("dve", lambda e: e.tensor_tensor(
                    out=oT[:, hd % HG, qs], in0=self.bank(bo)[:, 0:256], in1=rden[rk][:], op=ALU.mult),
                    reads=[("ps", bo), ("rden", rk)], writes=[("oT", hd % HG, qb // 2)])

        for u in prologue_units(0):
            u()
        LAG = 2
        for hd in range(NH):
            if hd + 1 < NH:
                load_w(hd + 1)
                nxt = prologue_units(hd + 1)
            else:
                nxt = []
            if hd % HG == 0:
                g0 = hd
                S.op("pool", lambda e, g0=g0: e.dma_start(
                    out=wo[:], in_=self.d_wo[slot, g0 * 128:(g0 + HG) * 128, :].rearrange("(h p) d -> p h d", p=128)),
                    writes=["wo"], dma=True)
            items = [(qb, n) for qb in range(8) for n in range(qb + 1)]
            st_ = {}
            for idx in range(len(items) + LAG):
                if idx == 4:
                    Dunit(hd)
                if idx < len(items):
                    E1(hd, items[idx][0], items[idx][1], st_)
                if idx >= LAG:
                    E2(hd, items[idx - LAG][0], items[idx - LAG][1], st_)
                if nxt and idx >= 6:
                    nxt.pop(0)()
            while nxt:
                nxt.pop(0)()
            if hd % HG == HG - 1:
                self.resid_proj(wo, HG, lambda k, tt: oT[:, k, tt * 512:(tt + 1) * 512],
                                lambda k, tt: [("oT", k, tt)], "wo", final=(hd == NH - 1))

    def lru(self, st, layer):
        nc, S = self.nc, self.S
        CG = 5
        win = [self.wfirst, self.sb(st, "win1", [128, NC_, 256], BF16)]

        def wit(bi, half):
            return ("wf", half) if bi == 0 else ("win", bi, half)

        wa = self.sb(st, "wa", [128, NLC, 128], BF16)
        wx = self.sb(st, "wx", [128, NLC, 128], BF16)
        wout = self.sb(st, "wout", [128, CG, D], BF16)
        hy = self.sb(st, "hy", [128, CG, S_], BF16)
        xc = self.sb(st, "xc", [128, S_], F32)
        xcb = self.sb(st, "xcb", [128, S_], BF16)
        gy = self.sb(st, "gy", [128, S_], F32)
        A = self.sb(st, "lruA", [128, S_], F32)
        I = self.sb(st, "lruI", [128, S_], F32)
        M = self.sb(st, "lruM", [128, S_], F32)
        dp = self.sb(st, "lrudp", [128, 4 * NLC], F32)
        S.op("pool", lambda e: e.dma_start(out=wa[:], in_=self.d_lwa[0].rearrange("n c d -> c n d")),
             writes=["wa"], dma=True)
        S.op("pool", lambda e: e.dma_start(out=wx[:], in_=self.d_lwx[0].rearrange("n c d -> c n d")),
             writes=["wx"], dma=True)
        lam = self.params[:, PC.off["llam"]:PC.off["llam"] + NLC]
        S.op("act", lambda e: e.activation(out=dp[:, 0:NLC], in_=lam, func=AF.Exp, scale=-1.0),
             reads=["params"], writes=["dp0"])
        S.op("act", lambda e: e.activation(out=dp[:, 0:NLC], in_=dp[:, 0:NLC], func=AF.Ln, bias=1.0),
             reads=["dp0"], writes=["dp0"])
        S.op("dve", lambda e: e.tensor_scalar(out=dp[:, NLC:2 * NLC], in0=dp[:, 0:NLC], scalar1=-4.0, scalar2=None,
                                              op0=ALU.mult), reads=["dp0"], writes=["dp"])
        S.op("dve", lambda e: e.tensor_scalar(out=dp[:, 2 * NLC:3 * NLC],
                                              in0=self.params[:, PC.off["lba"]:PC.off["lba"] + NLC],
                                              scalar1=0.5, scalar2=None, op0=ALU.mult), reads=["params"], writes=["dp"])
        S.op("dve", lambda e: e.tensor_scalar(out=dp[:, 3 * NLC:4 * NLC],
                                              in0=self.params[:, PC.off["lbx"]:PC.off["lbx"] + NLC],
                                              scalar1=0.5, scalar2=None, op0=ALU.mult), reads=["params"], writes=["dp"])
        wsrc = self.d_lwin[0].rearrange("(c p) f -> p c f", p=128)

        def load_win(c):
            bi = c % 2
            S.op("pool", lambda e: e.dma_start(out=win[bi][:, :, 0:128], in_=wsrc[:, :, c * 128:(c + 1) * 128]),
                 writes=[wit(bi, 0)], dma=True)
            S.op("pool", lambda e: e.dma_start(out=win[bi][:, :, 128:256],
                                               in_=wsrc[:, :, DR + c * 128:DR + (c + 1) * 128]),
                 writes=[wit(bi, 1)], dma=True)

        for c in range(NLC):
            bi = c % 2
            if c + 1 < NLC:
                load_win(c + 1)
            if c % CG == 0:
                c0 = c
                S.op("pool", lambda e, c0=c0: e.dma_start(
                    out=wout[:], in_=self.d_lwout[0, c0 * 128:(c0 + CG) * 128, :].rearrange("(k p) d -> p k d", p=128)),
                    writes=["wout"], dma=True)
            for half in range(2):
                for tt in range(4):
                    for k in range(NC_):
                        S.op("pe", lambda e, k=k, tt=tt, half=half: e.matmul(
                            self.PS[:, half * 2048 + tt * 512: half * 2048 + (tt + 1) * 512],
                            lhsT=win[bi][:, k, half * 128:(half + 1) * 128],
                            rhs=self.hT[:, k, tt * 512:(tt + 1) * 512], start=(k == 0), stop=(k == NC_ - 1)),
                            reads=[wit(bi, half), ("h", k, tt)], writes=[("ps", half * 4 + tt)])
            psx = [("ps", tt) for tt in range(4)]
            psy = [("ps", 4 + tt) for tt in range(4)]
            u = self.PS[:, 0:2048]
            uy = self.PS[:, 2048:4096]
            wcol = [self.pcol("lcw", j * NLC + c) for j in range(4)]
            S.op("act", lambda e, wcol=wcol: e.activation(out=xc[:], in_=u, func=AF.Identity, scale=wcol[3],
                                                          bias=self.pcol("lcb", c)),
                 reads=psx + ["params"], writes=["xc"])
            for k in (1, 2, 3):
                S.op("dve", lambda e, k=k, wcol=wcol: e.scalar_tensor_tensor(
                    out=xc[:, k:S_], in0=u[:, 0:S_ - k], scalar=wcol[3 - k], in1=xc[:, k:S_],
                    op0=ALU.mult, op1=ALU.add), reads=psx + ["xc", "params"], writes=["xc"])
            S.op("pool", lambda e: e.tensor_copy(out=xcb[:], in_=xc[:]), reads=["xc"], writes=["xcb"])
            S.op("act", lambda e: e.activation(out=gy[:], in_=uy, func=AF.Gelu_apprx_tanh),
                 reads=psy, writes=["gy"])
            for which, wt, wtok in ((0, wa, "wa"), (1, wx, "wx")):
                for tt in range(4):
                    S.op("pe", lambda e, tt=tt, which=which, wt=wt: e.matmul(
                        self.PS[:, which * 2048 + tt * 512: which * 2048 + (tt + 1) * 512],
                        lhsT=wt[:, c, :], rhs=xcb[:, tt * 512:(tt + 1) * 512], start=True, stop=True),
                        reads=[wtok, "xcb"], writes=[("ps", which * 4 + tt)])
            hcl = dp[:, NLC + c:NLC + c + 1]
            hba = dp[:, 2 * NLC + c:2 * NLC + c + 1]
            hbx = dp[:, 3 * NLC + c:3 * NLC + c + 1]
            S.op("act", lambda e, hba=hba: e.activation(out=A[:], in_=u, func=AF.Tanh, scale=0.5, bias=hba),
                 reads=psx + ["dp"], writes=["A"])
            S.op("act", lambda e, hbx=hbx: e.activation(out=I[:], in_=uy, func=AF.Tanh, scale=0.5, bias=hbx),
                 reads=psy + ["dp"], writes=["I"])
            S.op("act", lambda e, hcl=hcl: e.activation(out=A[:], in_=A[:], func=AF.Exp, scale=hcl, bias=hcl),
                 reads=["A", "dp"], writes=["A"])
            S.op("act", lambda e: e.activation(out=M[:], in_=A[:], func=AF.Square), reads=["A"], writes=["M"])
            S.op("act", lambda e: e.activation(out=M[:], in_=M[:], func=AF.Sqrt, scale=-1.0, bias=1.0),
                 reads=["M"], writes=["M"])
            S.op("dve", lambda e: e.scalar_tensor_tensor(out=I[:], in0=I[:], scalar=1.0, in1=xc[:],
                                                         op0=ALU.add, op1=ALU.mult),
                 reads=["I", "xc"], writes=["I"])
            S.op("dve", lambda e: e.scalar_tensor_tensor(out=I[:], in0=I[:], scalar=0.5, in1=M[:],
                                                         op0=ALU.mult, op1=ALU.mult),
                 reads=["I", "M"], writes=["I"])
            S.op("dve", lambda e: e.tensor_tensor_scan(out=M[:], data0=A[:], data1=I[:], initial=0.0,
                                                       op0=ALU.mult, op1=ALU.add),
                 reads=["A", "I", "M"], writes=["M"])
            cc = c % CG
            S.op("pool", lambda e, cc=cc: e.tensor_tensor(out=hy[:, cc, :], in0=M[:], in1=gy[:], op=ALU.mult),
                 reads=["M", "gy"], writes=[("hy", cc)])
            if cc == CG - 1:
                self.resid_proj(wout, CG, lambda k, tt: hy[:, k, tt * 512:(tt + 1) * 512],
                                lambda k, tt: [("hy", k)], "wout", final=(c == NLC - 1))

    def pool(self, st, layer):
        nc, S = self.nc, self.S
        rstd_all = self.sb(st, "rstd_all", [128, S_], F32)
        hf = [self.sb(st, "hf%d" % i, [128, S_], F32) for i in range(2)]
        B = [self.sb(st, "pB%d" % i, [128, S_], F32) for i in range(4)]
        t16 = [self.sb(st, "t16_%d" % i, [128, 16], F32) for i in range(2)]
        pw = self.sb(st, "pw", [128, 4, 2, 256], BF16)
        S.op("pool", lambda e: e.dma_start(out=pw[:], in_=self.d_pw[0].rearrange("g (k p) d -> p g k d", p=128)),
             writes=["pw"], dma=True)
        for tt in range(4):
            self.rstd_tt(tt, rstd_all[:, tt * 512:(tt + 1) * 512], ("rstd_all", tt))
        xall = lambda c: [("x", c, tt) for tt in range(4)]
        hall = lambda c: [("h", c, tt) for tt in range(4)]
        for c in range(NC_):
            g = c // 2
            k2 = c % 2
            hfc = hf[k2]
            S.op("dve", lambda e, c=c, hfc=hfc: e.scalar_tensor_tensor(
                out=hfc[:], in0=self.xT[:, c, :], scalar=self.pcol("nmix", layer * NC_ + c), in1=rstd_all[:],
                op0=ALU.mult, op1=ALU.mult),
                reads=xall(c) + [("rstd_all", tt) for tt in range(4)] + ["params"], writes=[("hf", k2)])
            cur, curtoks = hfc, [("hf", k2)]
            for k in range(g + 1):
                d = 2 ** k
                bi = k2 * 2 + k % 2
                nxt, nxttok = B[bi], ("pB", bi)
                eng = "pool" if k % 2 == 0 else "dve"
                S.op(eng, lambda e, cur=cur, nxt=nxt, d=d: e.tensor_tensor(
                    out=nxt[:, d:S_], in0=cur[:, d:S_], in1=cur[:, 0:S_ - d], op=ALU.add),
                    reads=list(curtoks), writes=[nxttok])
                S.op("act", lambda e, cur=cur, nxt=nxt, d=d: e.activation(out=nxt[:, 0:d], in_=cur[:, 0:d], func=AF.Identity),
                     reads=list(curtoks), writes=[(nxttok, "head")])
                cur, curtoks = nxt, [nxttok, (nxttok, "head")]
            w = 2 ** (g + 1)
            icnt = self.cf[:, CF.off["icnt"] + g * 16:CF.off["icnt"] + (g + 1) * 16]
            S.op("dve", lambda e, c=c, cur=cur, hfc=hfc, w=w: e.scalar_tensor_tensor(
                out=self.hT[:, c, :], in0=cur[:], scalar=1.0 / w, in1=hfc[:], op0=ALU.mult, op1=ALU.subtract),
                reads=curtoks + [("hf", k2)], writes=hall(c))
            S.op("dve", lambda e, cur=cur, k2=k2, icnt=icnt: e.tensor_tensor(
                out=t16[k2][:], in0=cur[:, 0:16], in1=icnt, op=ALU.mult),
                reads=curtoks + ["cf"], writes=[("t16", k2)])
            S.op("dve", lambda e, c=c, k2=k2, hfc=hfc: e.tensor_tensor(
                out=self.hT[:, c, 0:16], in0=t16[k2][:], in1=hfc[:, 0:16], op=ALU.subtract),
                reads=[("t16", k2), ("hf", k2)] + hall(c), writes=hall(c))
        for tt in range(4):
            for g in range(4):
                for m2 in range(2):
                    m = 2 * g + m2
                    b = self.psrot % 8
                    self.psrot += 1
                    for kk in range(2):
                        S.op("pe", lambda e, b=b, g=g, kk=kk, m2=m2, tt=tt: e.matmul(
                            self.bank(b), lhsT=pw[:, g, kk, m2 * 128:(m2 + 1) * 128],
                            rhs=self.hT[:, 2 * g + kk, tt * 512:(tt + 1) * 512], start=(kk == 0), stop=(kk == 1)),
                            reads=["pw", ("h", 2 * g + kk, tt)], writes=[("ps", b)])
                    xs = self.xT[:, m, tt * 512:(tt + 1) * 512]
                    S.op("dve", lambda e, b=b, xs=xs, m=m: e.scalar_tensor_tensor(
                        out=xs, in0=self.bank(b), scalar=self.pcol("pscale", m), in1=xs, op0=ALU.mult, op1=ALU.add),
                        reads=[("ps", b), ("x", m, tt), "params"], writes=[("x", m, tt)])
            self.tail_hook(tt)


ALL_PHASES = []
for _l in range(DEPTH):
    ALL_PHASES += [("mix", _l), ("ffn", _l)]

WEIGHT_KEYS = ["ffn_w_up", "ffn_w_down", "attn_w_qkv", "attn_w_o", "lru_w_in", "lru_w_a", "lru_w_x",
               "lru_w_out", "pool_w"]


def run_phases(inputs, phases, x_cores=None, trace=False, **bkw):
    nc = Builder(phases, **bkw).build()
    params = pack_params(inputs)
    c16, cf = make_consts()
    if x_cores is None:
        x = np.asarray(inputs["x"], np.float32)
        x_cores = [np.ascontiguousarray(x[b].T) for b in range(8)]
    shared = {k: np.ascontiguousarray(np.asarray(inputs[k], np.float32)) for k in WEIGHT_KEYS}
    shared.update({"params": params, "c16": c16, "cf": cf})
    in_maps = []
    for b in range(8):
        m = dict(shared)
        m["xT"] = x_cores[b]
        in_maps.append(m)
    res = run_bass_kernel_spmd(nc, in_maps, core_ids=list(range(8)), trace=trace)
    return [r["outT"] for r in res.results], res


def kernel(**inputs):
    outs, _ = run_phases(inputs, ALL_PHASES)
    return np.stack([np.ascontiguousarray(o.T) for o in outs], axis=0).astype(np.float32)
```

```python
import contextlib
import numpy as np
import concourse.bass as bass
import concourse.mybir as mybir
from concourse.bass_utils import run_bass_kernel_spmd

F32 = mybir.dt.float32
BF16 = mybir.dt.bfloat16
AF = mybir.ActivationFunctionType
ALU = mybir.AluOpType
AX = mybir.AxisListType

D = 1024
S_ = 2048
NC_ = 8
DEPTH = 4
FH = 2816
NPAIR = 22
DR = 1280
NLC = 10
EPS = 1e-6
NEG = -30000.0
ENGS = ("pe", "act", "dve", "pool", "sp")


class Op:
    __slots__ = ("eng", "fn", "deps", "dma", "signal", "semv")

    def __init__(self, eng, fn, dma):
        self.eng = eng
        self.fn = fn
        self.deps = []
        self.dma = dma
        self.signal = False
        self.semv = None


class _Rec:
    def __init__(self):
        self.calls = []

    def __getattr__(self, name):
        def f(*args, **kwargs):
            self.calls.append((name, args, kwargs))
        return f


class Sched:
    N_DMA_SEMS = 8

    def __init__(self, nc):
        self.nc = nc
        self.ops = {e: [] for e in ENGS}
        self.last_writer = {}
        self.readers = {}
        self.fence_pending = set()
        self.fence_ops = []

    def fence(self):
        self.fence_ops = [self.ops[e][-1] for e in ENGS if self.ops[e]]
        self.fence_pending = set(ENGS)

    def op(self, eng, fn, reads=(), writes=(), dma=False):
        rec = _Rec()
        fn(rec)
        name, args, kwargs = rec.calls[0]
        o = Op(eng, lambda e: getattr(e, name)(*args, **kwargs), dma)
        cand = []
        for t in reads:
            w = self.last_writer.get(t)
            if w is not None:
                cand.append((w, True))
        for t in writes:
            w = self.last_writer.get(t)
            if w is not None:
                cand.append((w, False))
            for r in self.readers.get(t, ()):
                cand.append((r, False))
        seen = set()
        for d, raw in cand:
            if d is o or id(d) in seen:
                continue
            if d.eng == eng and not d.dma and not dma:
                if eng == "pe" or not raw:
                    continue
            seen.add(id(d))
            o.deps.append(d)
        if eng in self.fence_pending:
            self.fence_pending.discard(eng)
            for d in self.fence_ops:
                if id(d) in seen or (d.eng == eng and not d.dma and not dma):
                    continue
                seen.add(id(d))
                o.deps.append(d)
        self.ops[eng].append(o)
        for t in writes:
            self.last_writer[t] = o
            self.readers[t] = []
        for t in reads:
            self.readers.setdefault(t, []).append(o)
        return o

    def emit(self, final_waits=()):
        nc = self.nc
        for e in ENGS:
            for o in self.ops[e]:
                for d in o.deps:
                    d.signal = True
        for o in final_waits:
            o.signal = True
        with contextlib.ExitStack() as st:
            sems = {e: st.enter_context(nc.semaphore("s_" + e)) for e in ENGS}
            for e in ("sp", "pool", "act"):
                for k in range(self.N_DMA_SEMS):
                    sems[(e, k)] = st.enter_context(nc.semaphore("d_%s%d" % (e, k)))
            for e in ENGS:
                c = 0
                dcount = [0] * self.N_DMA_SEMS
                nd = 0
                for o in self.ops[e]:
                    if o.dma:
                        k = nd % self.N_DMA_SEMS
                        nd += 1
                        dcount[k] += 1
                        o.semv = ((e, k), 16 * dcount[k])
                    elif o.signal:
                        c += 1
                        o.semv = (e, c)
            block = st.enter_context(nc.Block())
            engobj = {"pe": block.tensor, "act": block.scalar, "dve": block.vector,
                      "pool": block.gpsimd, "sp": block.sync}

            def make(e):
                def body(eng):
                    known = {}
                    for o in self.ops[e]:
                        waits = {}
                        for d in o.deps:
                            sk, v = d.semv
                            if known.get(sk, 0) >= v:
                                continue
                            if waits.get(sk, 0) < v:
                                waits[sk] = v
                        if o.dma:
                            sk, v = o.semv
                            if v > 16 and known.get(sk, 0) < v - 16 and waits.get(sk, 0) < v - 16:
                                waits[sk] = v - 16
                        for sk, v in waits.items():
                            eng.wait_ge(sems[sk], v)
                            known[sk] = v
                        ins = o.fn(eng)
                        if o.semv is not None:
                            ins.then_inc(sems[o.semv[0]], 16 if o.dma else 1)
                    if e == "sp":
                        for o in final_waits:
                            eng.wait_ge(sems[o.semv[0]], o.semv[1])
                return body

            for e in ENGS:
                engobj[e](make(e))


class Cols:
    def __init__(self):
        self.n = 0
        self.off = {}

    def add(self, name, k):
        self.off[name] = self.n
        self.n += k


PC = Cols()
PC.add("nmix", DEPTH * NC_)
PC.add("nffn", DEPTH * NC_)
PC.add("qg", 2)
PC.add("kg", 2)
PC.add("lcw", 4 * NLC)
PC.add("lcb", NLC)
PC.add("lba", NLC)
PC.add("lbx", NLC)
PC.add("llam", NLC)
PC.add("pscale", NC_)
PC.add("fcw", DEPTH * 3 * 44)
PC.add("fcb", DEPTH * 44)

CC = Cols()
CC.add("ident", 128)
CC.add("ones", 128)
CC.add("cb", 2 * 256)
CC.add("esel", 8 * 128)
NCB = CC.n
CF = Cols()
CF.add("pastmask", 64)
CF.add("icnt", 4 * 16)
CF.add("eps", 1)


def pack_params(inp):
    P = np.zeros((128, PC.n), np.float32)

    def put(name, arr):
        a = np.asarray(arr, np.float32)
        a = a.reshape(-1, a.shape[-1] // 128, 128)
        a = a.transpose(2, 0, 1).reshape(128, -1)
        P[:, PC.off[name]:PC.off[name] + a.shape[1]] = a

    put("nmix", inp["norm_mix_g"])
    put("nffn", inp["norm_ffn_g"])
    put("qg", inp["attn_q_g"])
    put("kg", inp["attn_k_g"])
    put("lcw", inp["lru_conv_w"][0])
    put("lcb", inp["lru_conv_b"][0])
    put("lba", inp["lru_b_a"][0])
    put("lbx", inp["lru_b_x"][0])
    put("llam", inp["lru_lambda"][0])
    put("pscale", inp["pool_scale"][0])
    put("fcw", inp["ffn_conv_w"])
    put("fcb", inp["ffn_conv_b"])
    return P


def make_consts():
    cb16 = np.zeros((128, CC.n), np.float32)
    cb16[:, CC.off["ident"]:CC.off["ident"] + 128] = np.eye(128, dtype=np.float32)
    cb16[:, CC.off["ones"]:CC.off["ones"] + 128] = 1.0
    p = np.arange(128)[:, None]
    q = np.arange(256)[None, :]
    for j in range(2):
        m = np.where(j * 128 + p <= q, 0.0, NEG).astype(np.float32)
        cb16[:, CC.off["cb"] + j * 256:CC.off["cb"] + (j + 1) * 256] = m
    es = np.zeros((128, 8, 128), np.float32)
    for n in range(8):
        es[n, n, :] = 1.0
    cb16[:, CC.off["esel"]:CC.off["esel"] + 1024] = es.reshape(128, 1024)
    cf = np.zeros((128, CF.n), np.float32)
    pm = np.zeros((8, 8), np.float32)
    for i in range(8):
        for n in range(8):
            pm[i, n] = 0.0 if n < 4 + i // 2 else -1e30
    cf[:, CF.off["pastmask"]:CF.off["pastmask"] + 64] = pm.reshape(1, 64)
    ic = np.zeros((4, 16), np.float32)
    for g, w in enumerate((2, 4, 8, 16)):
        for t in range(16):
            ic[g, t] = 1.0 / min(t + 1, w)
    cf[:, CF.off["icnt"]:CF.off["icnt"] + 64] = ic.reshape(1, 64)
    cf[:, CF.off["eps"]] = EPS
    return cb16, cf


class Builder:
    def __init__(self, phases, attn_heads=8, attn_hg=4):
        self.phases = phases
        self.attn_heads = attn_heads
        self.attn_hg = attn_hg
        nc = bass.Bass("TRN2", target_bir_lowering=False)
        self.nc = nc
        dt = nc.dram_tensor
        self.d_x = dt("xT", [D, S_], F32, kind="ExternalInput").ap()
        self.d_out = dt("outT", [D, S_], F32, kind="ExternalOutput").ap()
        self.d_params = dt("params", [128, PC.n], F32, kind="ExternalInput").ap()
        self.d_c16 = dt("c16", [128, CC.n], F32, kind="ExternalInput").ap()
        self.d_cf = dt("cf", [128, CF.n], F32, kind="ExternalInput").ap()
        self.d_wup = dt("ffn_w_up", [DEPTH, D, 2 * FH], F32, kind="ExternalInput").ap()
        self.d_wdn = dt("ffn_w_down", [DEPTH, FH, D], F32, kind="ExternalInput").ap()
        self.d_wqkv = dt("attn_w_qkv", [2, D, 3 * D], F32, kind="ExternalInput").ap()
        self.d_wo = dt("attn_w_o", [2, D, D], F32, kind="ExternalInput").ap()
        self.d_lwin = dt("lru_w_in", [1, D, 2 * DR], F32, kind="ExternalInput").ap()
        self.d_lwa = dt("lru_w_a", [1, NLC, 128, 128], F32, kind="ExternalInput").ap()
        self.d_lwx = dt("lru_w_x", [1, NLC, 128, 128], F32, kind="ExternalInput").ap()
        self.d_lwout = dt("lru_w_out", [1, DR, D], F32, kind="ExternalInput").ap()
        self.d_pw = dt("pool_w", [1, 4, 256, 256], F32, kind="ExternalInput").ap()
        self.S = Sched(nc)
        self.psrot = 0
        self.uid = 0

    def sb(self, st, name, shape, dtype):
        self.uid += 1
        return st.enter_context(self.nc.sbuf_tensor("%s_u%d" % (name, self.uid), shape, dtype))

    def pcol(self, name, idx):
        o = PC.off[name] + idx
        return self.params[:, o:o + 1]

    def build(self):
        nc, S = self.nc, self.S
        with contextlib.ExitStack() as st:
            self.xT = self.sb(st, "xT_sb", [128, NC_, S_], F32)
            self.hT = self.sb(st, "hT_sb", [128, NC_, S_], BF16)
            self.params = self.sb(st, "params_sb", [128, PC.n], F32)
            self.c16 = self.sb(st, "c16_sb", [128, CC.n], BF16)
            self.cf = self.sb(st, "cf_sb", [128, CF.n], F32)
            self.sq = self.sb(st, "sq_sb", [128, NC_, 512], BF16)
            self.lnv = self.sb(st, "lnv_sb", [128, 512], F32)
            self.rstd = self.sb(st, "rstd_sb", [128, 512], F32)
            self.PS = st.enter_context(nc.psum_tensor("PS", [128, 4096], F32))
            self.ident = self.c16[:, CC.off["ident"]:CC.off["ident"] + 128]
            self.ones = self.c16[:, CC.off["ones"]:CC.off["ones"] + 128]
            self.eps = self.cf[:, CF.off["eps"]:CF.off["eps"] + 1]

            S.op("sp", lambda e: e.dma_start(out=self.params[:], in_=self.d_params[:, :]),
                 writes=["params"], dma=True)
            S.op("sp", lambda e: e.dma_start(out=self.cf[:], in_=self.d_cf[:, :]),
                 writes=["cf"], dma=True)
            S.op("pool", lambda e: e.dma_start(out=self.c16[:], in_=self.d_c16[:, :]),
                 writes=["c16"], dma=True)
            for c in range(NC_):
                S.op("sp", lambda e, c=c: e.dma_start(out=self.xT[:, c, :],
                                                      in_=self.d_x[c * 128:(c + 1) * 128, :]),
                     writes=[("x", c, tt) for tt in range(4)], dma=True)

            self.wfirst = self.sb(st, "wfirst", [128, NC_, 512], BF16)
            nph = len(self.phases)
            self.prefetch_first(self.phases[0])
            self.start_norm(self.phases[0])
            for i, ph in enumerate(self.phases):
                kind, layer = ph
                self.next_phase = self.phases[i + 1] if i + 1 < nph else None
                S.fence()
                with contextlib.ExitStack() as st2:
                    if kind == "ffn":
                        self.ffn(st2, layer)
                    else:
                        mk = layer % 3
                        if mk == 0:
                            self.attn(st2, layer)
                        elif mk == 1:
                            self.lru(st2, layer)
                        else:
                            self.pool(st2, layer)

            fw = []
            for c in range(NC_):
                fw.append(S.op("sp", lambda e, c=c: e.dma_start(
                    out=self.d_out[c * 128:(c + 1) * 128, :], in_=self.xT[:, c, :]),
                    reads=[("x", c, tt) for tt in range(4)], dma=True))
            S.emit(final_waits=fw)
        return nc

    def bank(self, b):
        return self.PS[:, b * 512:(b + 1) * 512]

    def normA(self, tt):
        ts = slice(tt * 512, (tt + 1) * 512)
        self.S.op("act", lambda e: e.activation(out=self.sq[:], in_=self.xT[:, :, ts], func=AF.Square),
                  reads=[("x", c, tt) for c in range(NC_)], writes=["sq"])

    def normB(self, gname, layer, tt):
        S = self.S
        ts = slice(tt * 512, (tt + 1) * 512)
        b = self.psrot % 8
        self.psrot += 1
        for c in range(NC_):
            S.op("pe", lambda e, c=c: e.matmul(self.bank(b), lhsT=self.ones, rhs=self.sq[:, c, :],
                                               start=(c == 0), stop=(c == NC_ - 1)),
                 reads=["sq", "c16"], writes=[("ps", b)])
        S.op("act", lambda e: e.activation(out=self.lnv[:], in_=self.bank(b), func=AF.Ln,
                                           scale=1.0 / D, bias=self.eps),
             reads=[("ps", b), "cf"], writes=["lnv"])
        S.op("act", lambda e: e.activation(out=self.rstd[:], in_=self.lnv[:], func=AF.Exp, scale=-0.5),
             reads=["lnv"], writes=["rstd"])
        for c in range(NC_):
            S.op("dve", lambda e, c=c: e.scalar_tensor_tensor(
                out=self.hT[:, c, ts], in0=self.xT[:, c, ts], scalar=self.pcol(gname, layer * NC_ + c),
                in1=self.rstd[:], op0=ALU.mult, op1=ALU.mult),
                reads=[("x", c, tt), "rstd", "params"], writes=[("h", c, tt)])

    @staticmethod
    def norm_of(ph):
        if ph is None:
            return None
        kind, layer = ph
        if kind == "ffn":
            return ("nffn", layer)
        if layer % 3 == 2:
            return None
        return ("nmix", layer)

    def start_norm(self, ph):
        nm = self.norm_of(ph)
        if nm is None:
            return
        for tt in range(4):
            self.normA(tt)
            self.normB(nm[0], nm[1], tt)

    def tail_hook(self, tt):
        nm = self.norm_of(self.next_phase)
        if tt == 0 and self.next_phase is not None:
            self.prefetch_first(self.next_phase)
        if nm is None:
            return
        if tt >= 1:
            self.normB(nm[0], nm[1], tt - 1)
        self.normA(tt)
        if tt == 3:
            self.normB(nm[0], nm[1], 3)

    def prefetch_first(self, ph):
        S = self.S
        kind, layer = ph
        wf = self.wfirst
        if kind == "ffn":
            src = self.d_wup[layer].rearrange("(c p) f -> p c f", p=128)
            S.op("pool", lambda e: e.dma_start(out=wf[:, :, 0:256], in_=src[:, :, 0:256]),
                 writes=[("wf", 0), ("wf", 1)], dma=True)
            S.op("pool", lambda e: e.dma_start(out=wf[:, :, 256:512], in_=src[:, :, FH:FH + 256]),
                 writes=[("wf", 2), ("wf", 3)], dma=True)
        elif layer % 3 == 0:
            src = self.d_wqkv[layer // 3].rearrange("(c p) f -> p c f", p=128)
            for k in range(3):
                S.op("pool", lambda e, k=k: e.dma_start(out=wf[:, :, k * 128:(k + 1) * 128],
                                                        in_=src[:, :, k * D:k * D + 128]),
                     writes=[("wf", k)], dma=True)
        elif layer % 3 == 1:
            src = self.d_lwin[0].rearrange("(c p) f -> p c f", p=128)
            S.op("pool", lambda e: e.dma_start(out=wf[:, :, 0:128], in_=src[:, :, 0:128]),
                 writes=[("wf", 0)], dma=True)
            S.op("pool", lambda e: e.dma_start(out=wf[:, :, 128:256], in_=src[:, :, DR:DR + 128]),
                 writes=[("wf", 1)], dma=True)

    def ffn(self, st, layer):
        nc, S = self.nc, self.S
        GROUPS = [(0, 6), (6, 12), (12, 17), (17, 22)]
        NP = 7
        NW = 3
        wup = [self.wfirst] + [self.sb(st, "wup%d" % i, [128, NC_, 512], BF16) for i in range(1, NW)]
        wdn = self.sb(st, "wdn", [128, 6, D], BF16)
        Pb = self.sb(st, "Pb", [128, NP, S_], BF16)
        cbuf = [self.sb(st, "cbuf%d" % i, [128, S_], F32) for i in range(3)]
        wup_src = self.d_wup[layer].rearrange("(c p) f -> p c f", p=128)

        def wtok(bi, half):
            if bi == 0:
                return [("wf", 2 * half), ("wf", 2 * half + 1)]
            return [("wup", bi, half)]

        def load_slab(s):
            bi = s % NW
            S.op("pool", lambda e: e.dma_start(out=wup[bi][:, :, 0:256],
                                               in_=wup_src[:, :, s * 256:(s + 1) * 256]),
                 writes=wtok(bi, 0), dma=True)
            S.op("pool", lambda e: e.dma_start(out=wup[bi][:, :, 256:512],
                                               in_=wup_src[:, :, FH + s * 256:FH + (s + 1) * 256]),
                 writes=wtok(bi, 1), dma=True)

        def load_wdn(j0, j1):
            S.op("pool", lambda e: e.dma_start(
                out=wdn[:, 0:j1 - j0, :],
                in_=self.d_wdn[layer, j0 * 128:j1 * 128, :].rearrange("(j p) d -> p j d", p=128)),
                writes=["wdn"], dma=True)

        cbi = [0]

        def up(j):
            s, r = j // 2, j % 2
            bi = s % NW
            cg = None
            for half in range(2):
                fj = half * NPAIR + j
                base = half * 2048
                lcol = half * 256 + r * 128
                for tt in range(4):
                    for c in range(NC_):
                        S.op("pe", lambda e, c=c, tt=tt: e.matmul(
                            self.PS[:, base + tt * 512: base + (tt + 1) * 512],
                            lhsT=wup[bi][:, c, lcol:lcol + 128],
                            rhs=self.hT[:, c, tt * 512:(tt + 1) * 512],
                            start=(c == 0), stop=(c == NC_ - 1)),
                            reads=wtok(bi, half) + [("h", c, tt)], writes=[("ps", half * 4 + tt)])
                if half == 1 and r == 1 and s + NW < 11:
                    load_slab(s + NW)
                cb = cbuf[cbi[0] % 3]
                cbk = cbi[0] % 3
                cbn = [("cbuf", cbk, 0), ("cbuf", cbk, 1)]
                cbi[0] += 1
                w0 = self.pcol("fcw", (layer * 3 + 0) * 44 + fj)
                w1 = self.pcol("fcw", (layer * 3 + 1) * 44 + fj)
                w2 = self.pcol("fcw", (layer * 3 + 2) * 44 + fj)
                bb = self.pcol("fcb", layer * 44 + fj)
                u = self.PS[:, base:base + 2048]
                for hh in range(2):
                    lo, hi = hh * 1024, (hh + 1) * 1024
                    psr = [("ps", half * 4 + 2 * hh), ("ps", half * 4 + 2 * hh + 1)]
                    if hh == 1:
                        psr.append(("ps", half * 4 + 1))
                    ct = [("cbuf", cbk, hh)]
                    S.op("act", lambda e: e.activation(out=cb[:, lo:hi], in_=u[:, lo:hi], func=AF.Identity,
                                                       scale=w2, bias=bb),
                         reads=psr + ["params"], writes=ct)
                    for k, wk in ((1, w1), (2, w0)):
                        o0 = max(lo, k)
                        S.op("dve", lambda e, o0=o0, k=k, wk=wk: e.scalar_tensor_tensor(
                            out=cb[:, o0:hi], in0=u[:, o0 - k:hi - k], scalar=wk, in1=cb[:, o0:hi],
                            op0=ALU.mult, op1=ALU.add), reads=psr + ct + ["params"], writes=ct)
                if half == 0:
                    S.op("act", lambda e: e.activation(out=cb[:], in_=cb[:], func=AF.Silu),
                         reads=cbn, writes=cbn)
                    cg = (cb, cbn)
                else:
                    S.op("pool", lambda e: e.tensor_tensor(out=Pb[:, j % NP, :], in0=cg[0][:], in1=cb[:], op=ALU.mult),
                         reads=cbn + cg[1], writes=[("P", j % NP)])

        def down(j0, j1, nbanks=4, final=False):
            nj = j1 - j0
            for tt in range(4):
                for m in range(NC_):
                    b = self.psrot % nbanks
                    self.psrot += 1
                    for jj in range(nj):
                        slot = (j0 + jj) % NP
                        S.op("pe", lambda e, jj=jj, slot=slot: e.matmul(
                            self.bank(b), lhsT=wdn[:, jj, m * 128:(m + 1) * 128],
                            rhs=Pb[:, slot, tt * 512:(tt + 1) * 512],
                            start=(jj == 0), stop=(jj == nj - 1)),
                            reads=["wdn", ("P", slot)], writes=[("ps", b)])
                    xs = self.xT[:, m, tt * 512:(tt + 1) * 512]
                    S.op("dve", lambda e: e.tensor_tensor(out=xs, in0=self.bank(b), in1=xs, op=ALU.add),
                         reads=[("ps", b), ("x", m, tt)], writes=[("x", m, tt)])
                if final:
                    self.tail_hook(tt)

        for s0 in range(1, NW):
            load_slab(s0)
        load_wdn(*GROUPS[0])
        gidx = {}
        for gi, (j0, j1) in enumerate(GROUPS):
            for j in range(j0, j1):
                gidx[j] = gi
        for j in range(NPAIR):
            up(j)
            gi = gidx[j]
            j0, j1 = GROUPS[gi]
            if j == j0 and gi > 0:
                down(*GROUPS[gi - 1])
            if j == j0 + 2 and gi > 0:
                load_wdn(j0, j1)
        down(*GROUPS[-1], nbanks=8, final=True)

    def rstd_tt(self, tt, dst, dst_tok):
        S = self.S
        ts = slice(tt * 512, (tt + 1) * 512)
        b = self.psrot % 8
        self.psrot += 1
        S.op("act", lambda e: e.activation(out=self.sq[:], in_=self.xT[:, :, ts], func=AF.Square),
             reads=[("x", c, tt) for c in range(NC_)], writes=["sq"])
        for c in range(NC_):
            S.op("pe", lambda e, c=c: e.matmul(self.bank(b), lhsT=self.ones, rhs=self.sq[:, c, :],
                                               start=(c == 0), stop=(c == NC_ - 1)),
                 reads=["sq", "c16"], writes=[("ps", b)])
        S.op("act", lambda e: e.activation(out=self.lnv[:], in_=self.bank(b), func=AF.Ln,
                                           scale=1.0 / D, bias=self.eps),
             reads=[("ps", b), "cf"], writes=["lnv"])
        S.op("act", lambda e: e.activation(out=dst, in_=self.lnv[:], func=AF.Exp, scale=-0.5),
             reads=["lnv"], writes=[dst_tok])

    def resid_proj(self, w, nk, rhs_fn, rhs_toks, wtok, final=False):
        S = self.S
        for tt in range(4):
            for m in range(NC_):
                b = self.psrot % 8
                self.psrot += 1
                for k in range(nk):
                    S.op("pe", lambda e, b=b, k=k, m=m, tt=tt: e.matmul(
                        self.bank(b), lhsT=w[:, k, m * 128:(m + 1) * 128], rhs=rhs_fn(k, tt),
                        start=(k == 0), stop=(k == nk - 1)),
                        reads=[wtok] + rhs_toks(k, tt), writes=[("ps", b)])
                xs = self.xT[:, m, tt * 512:(tt + 1) * 512]
                S.op("dve", lambda e, b=b, xs=xs: e.tensor_tensor(
                    out=xs, in0=self.bank(b), in1=xs, op=ALU.add),
                    reads=[("ps", b), ("x", m, tt)], writes=[("x", m, tt)])
            if final:
                self.tail_hook(tt)

    def attn(self, st, layer):
        nc, S = self.nc, self.S
        slot = layer // 3
        HG = self.attn_hg
        NH = self.attn_heads
        wqkv = [self.wfirst, self.sb(st, "wqkv1", [128, NC_, 384], BF16)]

        def wqt(bi, k):
            return ("wf", k) if bi == 0 else ("wqkv", bi, k)

        qn = [self.sb(st, "qn%d" % i, [128, S_], BF16) for i in range(2)]
        kn = [self.sb(st, "kn%d" % i, [128, S_], BF16) for i in range(2)]
        Vt = [self.sb(st, "Vt%d" % i, [128, 16, 128], BF16) for i in range(2)]
        oT = self.sb(st, "oT", [128, HG, S_], BF16)
        wo = self.sb(st, "wo", [128, HG, D], BF16)
        rs = [self.sb(st, "rs%d" % i, [128, 512], F32) for i in range(2)]
        sq2 = [self.sb(st, "sq2_%d" % i, [128, 512], BF16) for i in range(2)]
        PT = [self.sb(st, "PT%d" % i, [128, 512], BF16) for i in range(4)]
        kmf = self.sb(st, "kmf", [128, 8], F32)
        kmb = self.sb(st, "kmb", [128, 8], BF16)
        g1 = self.sb(st, "g1", [128, 64], F32)
        top = self.sb(st, "top", [128, 64], F32)
        cmpf = self.sb(st, "cmpf", [128, 64], F32)
        btok = self.sb(st, "btok", [128, 8, 128], BF16)
        biasT = self.sb(st, "biasT", [128, 1024], BF16)
        rden = [self.sb(st, "rden%d" % i, [128, 256], F32) for i in range(2)]
        S.op("pool", lambda e: e.memset(btok[:], 0.0), writes=["btok"])
        S.op("pool", lambda e: e.memset(biasT[:], 0.0), writes=["biasT"])
        wsrc = self.d_wqkv[slot].rearrange("(c p) f -> p c f", p=128)
        scale = 128.0 ** -0.5
        pastmask = self.cf[:, CF.off["pastmask"]:CF.off["pastmask"] + 64]
        cbm = [self.c16[:, CC.off["cb"] + j * 256:CC.off["cb"] + (j + 1) * 256] for j in range(2)]
        esel = [self.c16[:, CC.off["esel"] + n * 128:CC.off["esel"] + (n + 1) * 128] for n in range(8)]
        cnt = {"s": 0, "p": 0, "k2": 0, "pt": 0}

        def sbank():
            b = 2 + cnt["s"] % 3
            cnt["s"] += 1
            return b


        def load_w(hd):
            bi = hd % 2
            for k in range(3):
                S.op("pool", lambda e, k=k: e.dma_start(
                    out=wqkv[bi][:, :, k * 128:(k + 1) * 128],
                    in_=wsrc[:, :, k * D + hd * 128:k * D + (hd + 1) * 128]),
                    writes=[wqt(bi, k)], dma=True)

        def prologue_units(hd):
            bi = hd % 2
            units = []
            items = [(which, tt) for which in (0, 1) for tt in range(4)]
            state = {}

            def A1(i):
                which, tt = items[i]
                ts = slice(tt * 512, (tt + 1) * 512)
                b = 5 + cnt["p"] % 3
                cnt["p"] += 1
                k2 = cnt["k2"] % 2
                cnt["k2"] += 1
                state[i] = (b, k2)
                for c in range(NC_):
                    S.op("pe", lambda e, c=c: e.matmul(
                        self.bank(b), lhsT=wqkv[bi][:, c, which * 128:(which + 1) * 128],
                        rhs=self.hT[:, c, ts], start=(c == 0), stop=(c == NC_ - 1)),
                        reads=[wqt(bi, which), ("h", c, tt)], writes=[("ps", b)])
                S.op("act", lambda e: e.activation(out=sq2[k2][:], in_=self.bank(b), func=AF.Square),
                     reads=[("ps", b)], writes=[("sq2", k2)])

            def A2(i):
                which, tt = items[i]
                ts = slice(tt * 512, (tt + 1) * 512)
                b, k2 = state[i]
                dst = (qn if which == 0 else kn)[bi]
                gname = "qg" if which == 0 else "kg"
                b7 = sbank()
                S.op("pe", lambda e: e.matmul(self.bank(b7), lhsT=self.ones, rhs=sq2[k2][:], start=True, stop=True),
                     reads=[("sq2", k2), "c16"], writes=[("ps", b7)])
                S.op("act", lambda e: e.activation(out=rs[k2][:], in_=self.bank(b7), func=AF.Ln,
                                                   scale=1.0 / 128, bias=self.eps),
                     reads=[("ps", b7), "cf"], writes=[("rs", k2)])
                S.op("act", lambda e: e.activation(out=rs[k2][:], in_=rs[k2][:], func=AF.Exp, scale=-0.5),
                     reads=[("rs", k2)], writes=[("rs", k2)])
                S.op("dve", lambda e: e.scalar_tensor_tensor(
                    out=dst[:, ts], in0=self.bank(b), scalar=self.pcol(gname, slot), in1=rs[k2][:],
                    op0=ALU.mult, op1=ALU.mult),
                    reads=[("ps", b), ("rs", k2), "params"], writes=[("qk", which, bi, tt)])

            units.append(lambda: A1(0))
            for i in range(1, 8):
                units.append(lambda i=i: A1(i))
                units.append(lambda i=i: A2(i - 1))
            units.append(lambda: A2(7))

            def Vunit(bq):
                b = 5 + cnt["p"] % 3
                cnt["p"] += 1
                for i4 in range(4):
                    i = bq * 4 + i4
                    for c in range(NC_):
                        S.op("pe", lambda e, i=i, i4=i4, c=c: e.matmul(
                            self.bank(b)[:, i4 * 128:(i4 + 1) * 128], lhsT=self.hT[:, c, i * 128:(i + 1) * 128],
                            rhs=wqkv[bi][:, c, 256:384], start=(c == 0), stop=(c == NC_ - 1)),
                            reads=[wqt(bi, 2), ("h", c, bq)], writes=[("ps", b)])
                S.op("act", lambda e: e.activation(
                    out=Vt[bi][:, bq * 4:(bq + 1) * 4, :], in_=self.bank(b).rearrange("p (i d) -> p i d", d=128),
                    func=AF.Identity), reads=[("ps", b)], writes=[("V", bi, bq)])

            for bq in range(4):
                units.append(lambda bq=bq: Vunit(bq))

            def Cunit():
                S.op("dve", lambda e: e.tensor_reduce(out=kmf[:], in_=kn[bi][:].rearrange("p (n k) -> p n k", k=256),
                                                      axis=AX.X, op=ALU.add),
                     reads=[("qk", 1, bi, tt) for tt in range(4)], writes=["kmf"])
                S.op("dve", lambda e: e.tensor_scalar(out=kmb[:], in0=kmf[:], scalar1=1.0 / 256, scalar2=None,
                                                      op0=ALU.mult), reads=["kmf"], writes=["kmb"])
                bg = sbank()
                for i in range(8):
                    S.op("pe", lambda e, i=i: e.matmul(
                        self.bank(bg)[:, i * 8:(i + 1) * 8], lhsT=qn[bi][:, (8 + i) * 128:(9 + i) * 128], rhs=kmb[:],
                        start=True, stop=True),
                        reads=["kmb", ("qk", 0, bi, 2 + i // 4)], writes=[("ps", bg)])
                S.op("dve", lambda e: e.tensor_tensor(out=g1[:], in0=self.bank(bg)[:, 0:64], in1=pastmask, op=ALU.add),
                     reads=[("ps", bg), "cf"], writes=["g1"])
                for i in range(8):
                    S.op("dve", lambda e, i=i: e.max(out=top[:, i * 8:(i + 1) * 8], in_=g1[:, i * 8:(i + 1) * 8]),
                         reads=["g1"], writes=["top"])
                S.op("dve", lambda e: e.tensor_tensor(
                    out=cmpf[:].rearrange("p (i n) -> p i n", n=8), in0=g1[:].rearrange("p (i n) -> p i n", n=8),
                    in1=top[:].rearrange("p (i n) -> p i n", n=8)[:, :, 2:3].to_broadcast([128, 8, 8]), op=ALU.is_lt),
                    reads=["g1", "top"], writes=["cmpf"])

            units.append(Cunit)
            return units

        def Dunit(hd):
            S.op("dve", lambda e: e.tensor_scalar(out=btok[:, :, 0:8], in0=cmpf[:].rearrange("p (i n) -> p i n", n=8),
                                                  scalar1=NEG, scalar2=None, op0=ALU.mult),
                 reads=["cmpf"], writes=["btok"])
            for k in range(2):
                bd_ = sbank()
                for i4 in range(4):
                    i = k * 4 + i4
                    S.op("pe", lambda e, i=i, i4=i4: e.matmul(
                        self.bank(bd_)[:, i4 * 128:(i4 + 1) * 128], lhsT=btok[:, i, :], rhs=self.ident,
                        start=True, stop=True),
                        reads=["btok", "c16"], writes=[("ps", bd_)])
                S.op("act", lambda e, k=k: e.activation(out=biasT[0:8, k * 512:(k + 1) * 512],
                                                        in_=self.bank(bd_)[0:8, :], func=AF.Identity),
                     reads=[("ps", bd_)], writes=["biasT"])

        def E1(hd, qb, n, st_):
            bi = hd % 2
            qs = slice(qb * 256, (qb + 1) * 256)
            bs_ = sbank()
            pk = cnt["pt"] % 4
            cnt["pt"] += 1
            st_[(qb, n)] = pk
            for half in range(2):
                kt = 2 * n + half
                osl = self.bank(bs_)[:, half * 256:(half + 1) * 256]
                extra = (n == qb) or (qb >= 4)
                S.op("pe", lambda e, osl=osl, kt=kt, extra=extra: e.matmul(
                    osl, lhsT=kn[bi][:, kt * 128:(kt + 1) * 128], rhs=qn[bi][:, qs],
                    start=True, stop=(not extra)),
                    reads=[("qk", 1, bi, kt // 4), ("qk", 0, bi, qb // 2)], writes=[("ps", bs_)])
                if n == qb:
                    S.op("pe", lambda e, osl=osl, half=half: e.matmul(
                        osl, lhsT=self.ident, rhs=cbm[half], start=False, stop=True),
                        reads=["c16"], writes=[("ps", bs_)])
                elif qb >= 4:
                    S.op("pe", lambda e, osl=osl: e.matmul(
                        osl, lhsT=esel[n], rhs=biasT[:, (qb - 4) * 256:(qb - 3) * 256],
                        start=False, stop=True),
                        reads=["c16", "biasT"], writes=[("ps", bs_)])
            S.op("act", lambda e: e.activation(out=PT[pk][:], in_=self.bank(bs_), func=AF.Exp, scale=scale),
                 reads=[("ps", bs_)], writes=[("PT", pk)])

        def E2(hd, qb, n, st_):
            bi = hd % 2
            qs = slice(qb * 256, (qb + 1) * 256)
            pk = st_[(qb, n)]
            bo = qb % 2
            for half in range(2):
                kt = 2 * n + half
                first = (n == 0 and half == 0)
                last = (n == qb and half == 1)
                S.op("pe", lambda e, kt=kt, half=half, first=first, last=last: e.matmul(
                    self.bank(bo)[:, 0:256], lhsT=Vt[bi][:, kt, :], rhs=PT[pk][:, half * 256:(half + 1) * 256],
                    start=first, stop=last, skip_group_check=True),
                    reads=[("V", bi, kt // 4), ("PT", pk)], writes=[("ps", bo)])
                S.op("pe", lambda e, half=half, last=last: e.matmul(
                    self.bank(bo)[:, 256:512], lhsT=self.ones, rhs=PT[pk][:, half * 256:(half + 1) * 256],
                    start=False, stop=last, skip_group_check=True),
                    reads=["c16", ("PT", pk)], writes=[("ps", bo)])
            if n == qb:
                rk = qb % 2
                S.op("dve", lambda e: e.reciprocal(out=rden[rk][:], in_=self.bank(bo)[:, 256:512]),
                     reads=[("ps", bo)], writes=[("rden", rk)])
                S.op("dve", lambda e: e.tensor_tensor(
                    out=oT[:, hd % HG, qs], in0=self.bank(bo)[:, 0:256], in1=rden[rk][:], op=ALU.mult),
                    reads=[("ps", bo), ("rden", rk)], writes=[("oT", hd % HG, qb // 2)])

        for u in prologue_units(0):
            u()
        LAG = 2
        for hd in range(NH):
            if hd + 1 < NH:
                load_w(hd + 1)
                nxt = prologue_units(hd + 1)
            else:
                nxt = []
            if hd % HG == 0:
                g0 = hd
                S.op("pool", lambda e, g0=g0: e.dma_start(
                    out=wo[:], in_=self.d_wo[slot, g0 * 128:(g0 + HG) * 128, :].rearrange("(h p) d -> p h d", p=128)),
                    writes=["wo"], dma=True)
            items = [(qb, n) for qb in range(8) for n in range(qb + 1)]
            st_ = {}
            for idx in range(len(items) + LAG):
                if idx == 4:
                    Dunit(hd)
                if idx < len(items):
                    E1(hd, items[idx][0], items[idx][1], st_)
                if idx >= LAG:
                    E2(hd, items[idx - LAG][0], items[idx - LAG][1], st_)
                if nxt and idx >= 6:
                    nxt.pop(0)()
            while nxt:
                nxt.pop(0)()
            if hd % HG == HG - 1:
                self.resid_proj(wo, HG, lambda k, tt: oT[:, k, tt * 512:(tt + 1) * 512],
                                lambda k, tt: [("oT", k, tt)], "wo", final=(hd == NH - 1))

    def lru(self, st, layer):
        nc, S = self.nc, self.S
        CG = 5
        win = [self.wfirst, self.sb(st, "win1", [128, NC_, 256], BF16)]

        def wit(bi, half):
            return ("wf", half) if bi == 0 else ("win", bi, half)

        wa = self.sb(st, "wa", [128, NLC, 128], BF16)
        wx = self.sb(st, "wx", [128, NLC, 128], BF16)
        wout = self.sb(st, "wout", [128, CG, D], BF16)
        hy = self.sb(st, "hy", [128, CG, S_], BF16)
        xc = self.sb(st, "xc", [128, S_], F32)
        xcb = self.sb(st, "xcb", [128, S_], BF16)
        gy = self.sb(st, "gy", [128, S_], F32)
        A = self.sb(st, "lruA", [128, S_], F32)
        I = self.sb(st, "lruI", [128, S_], F32)
        M = self.sb(st, "lruM", [128, S_], F32)
        dp = self.sb(st, "lrudp", [128, 4 * NLC], F32)
        S.op("pool", lambda e: e.dma_start(out=wa[:], in_=self.d_lwa[0].rearrange("n c d -> c n d")),
             writes=["wa"], dma=True)
        S.op("pool", lambda e: e.dma_start(out=wx[:], in_=self.d_lwx[0].rearrange("n c d -> c n d")),
             writes=["wx"], dma=True)
        lam = self.params[:, PC.off["llam"]:PC.off["llam"] + NLC]
        S.op("act", lambda e: e.activation(out=dp[:, 0:NLC], in_=lam, func=AF.Exp, scale=-1.0),
             reads=["params"], writes=["dp0"])
        S.op("act", lambda e: e.activation(out=dp[:, 0:NLC], in_=dp[:, 0:NLC], func=AF.Ln, bias=1.0),
             reads=["dp0"], writes=["dp0"])
        S.op("dve", lambda e: e.tensor_scalar(out=dp[:, NLC:2 * NLC], in0=dp[:, 0:NLC], scalar1=-4.0, scalar2=None,
                                              op0=ALU.mult), reads=["dp0"], writes=["dp"])
        S.op("dve", lambda e: e.tensor_scalar(out=dp[:, 2 * NLC:3 * NLC],
                                              in0=self.params[:, PC.off["lba"]:PC.off["lba"] + NLC],
                                              scalar1=0.5, scalar2=None, op0=ALU.mult), reads=["params"], writes=["dp"])
        S.op("dve", lambda e: e.tensor_scalar(out=dp[:, 3 * NLC:4 * NLC],
                                              in0=self.params[:, PC.off["lbx"]:PC.off["lbx"] + NLC],
                                              scalar1=0.5, scalar2=None, op0=ALU.mult), reads=["params"], writes=["dp"])
        wsrc = self.d_lwin[0].rearrange("(c p) f -> p c f", p=128)

        def load_win(c):
            bi = c % 2
            S.op("pool", lambda e: e.dma_start(out=win[bi][:, :, 0:128], in_=wsrc[:, :, c * 128:(c + 1) * 128]),
                 writes=[wit(bi, 0)], dma=True)
            S.op("pool", lambda e: e.dma_start(out=win[bi][:, :, 128:256],
                                               in_=wsrc[:, :, DR + c * 128:DR + (c + 1) * 128]),
                 writes=[wit(bi, 1)], dma=True)

        for c in range(NLC):
            bi = c % 2
            if c + 1 < NLC:
                load_win(c + 1)
            if c % CG == 0:
                c0 = c
                S.op("pool", lambda e, c0=c0: e.dma_start(
                    out=wout[:], in_=self.d_lwout[0, c0 * 128:(c0 + CG) * 128, :].rearrange("(k p) d -> p k d", p=128)),
                    writes=["wout"], dma=True)
            for half in range(2):
                for tt in range(4):
                    for k in range(NC_):
                        S.op("pe", lambda e, k=k, tt=tt, half=half: e.matmul(
                            self.PS[:, half * 2048 + tt * 512: half * 2048 + (tt + 1) * 512],
                            lhsT=win[bi][:, k, half * 128:(half + 1) * 128],
                            rhs=self.hT[:, k, tt * 512:(tt + 1) * 512], start=(k == 0), stop=(k == NC_ - 1)),
                            reads=[wit(bi, half), ("h", k, tt)], writes=[("ps", half * 4 + tt)])
            psx = [("ps", tt) for tt in range(4)]
            psy = [("ps", 4 + tt) for tt in range(4)]
            u = self.PS[:, 0:2048]
            uy = self.PS[:, 2048:4096]
            wcol = [self.pcol("lcw", j * NLC + c) for j in range(4)]
            S.op("act", lambda e, wcol=wcol: e.activation(out=xc[:], in_=u, func=AF.Identity, scale=wcol[3],
                                                          bias=self.pcol("lcb", c)),
                 reads=psx + ["params"], writes=["xc"])
            for k in (1, 2, 3):
                S.op("dve", lambda e, k=k, wcol=wcol: e.scalar_tensor_tensor(
                    out=xc[:, k:S_], in0=u[:, 0:S_ - k], scalar=wcol[3 - k], in1=xc[:, k:S_],
                    op0=ALU.mult, op1=ALU.add), reads=psx + ["xc", "params"], writes=["xc"])
            S.op("pool", lambda e: e.tensor_copy(out=xcb[:], in_=xc[:]), reads=["xc"], writes=["xcb"])
            S.op("act", lambda e: e.activation(out=gy[:], in_=uy, func=AF.Gelu_apprx_tanh),
                 reads=psy, writes=["gy"])
            for which, wt, wtok in ((0, wa, "wa"), (1, wx, "wx")):
                for tt in range(4):
                    S.op("pe", lambda e, tt=tt, which=which, wt=wt: e.matmul(
                        self.PS[:, which * 2048 + tt * 512: which * 2048 + (tt + 1) * 512],
                        lhsT=wt[:, c, :], rhs=xcb[:, tt * 512:(tt + 1) * 512], start=True, stop=True),
                        reads=[wtok, "xcb"], writes=[("ps", which * 4 + tt)])
            hcl = dp[:, NLC + c:NLC + c + 1]
            hba = dp[:, 2 * NLC + c:2 * NLC + c + 1]
            hbx = dp[:, 3 * NLC + c:3 * NLC + c + 1]
            S.op("act", lambda e, hba=hba: e.activation(out=A[:], in_=u, func=AF.Tanh, scale=0.5, bias=hba),
                 reads=psx + ["dp"], writes=["A"])
            S.op("act", lambda e, hbx=hbx: e.activation(out=I[:], in_=uy, func=AF.Tanh, scale=0.5, bias=hbx),
                 reads=psy + ["dp"], writes=["I"])
            S.op("act", lambda e, hcl=hcl: e.activation(out=A[:], in_=A[:], func=AF.Exp, scale=hcl, bias=hcl),
                 reads=["A", "dp"], writes=["A"])
            S.op("act", lambda e: e.activation(out=M[:], in_=A[:], func=AF.Square), reads=["A"], writes=["M"])
            S.op("act", lambda e: e.activation(out=M[:], in_=M[:], func=AF.Sqrt, scale=-1.0, bias=1.0),
                 reads=["M"], writes=["M"])
            S.op("dve", lambda e: e.scalar_tensor_tensor(out=I[:], in0=I[:], scalar=1.0, in1=xc[:],
                                                         op0=ALU.add, op1=ALU.mult),
                 reads=["I", "xc"], writes=["I"])
            S.op("dve", lambda e: e.scalar_tensor_tensor(out=I[:], in0=I[:], scalar=0.5, in1=M[:],
                                                         op0=ALU.mult, op1=ALU.mult),
                 reads=["I", "M"], writes=["I"])
            S.op("dve", lambda e: e.tensor_tensor_scan(out=M[:], data0=A[:], data1=I[:], initial=0.0,
                                                       op0=ALU.mult, op1=ALU.add),
                 reads=["A", "I", "M"], writes=["M"])
            cc = c % CG
            S.op("pool", lambda e, cc=cc: e.tensor_tensor(out=hy[:, cc, :], in0=M[:], in1=gy[:], op=ALU.mult),
                 reads=["M", "gy"], writes=[("hy", cc)])
            if cc == CG - 1:
                self.resid_proj(wout, CG, lambda k, tt: hy[:, k, tt * 512:(tt + 1) * 512],
                                lambda k, tt: [("hy", k)], "wout", final=(c == NLC - 1))

    def pool(self, st, layer):
        nc, S = self.nc, self.S
        rstd_all = self.sb(st, "rstd_all", [128, S_], F32)
        hf = [self.sb(st, "hf%d" % i, [128, S_], F32) for i in range(2)]
        B = [self.sb(st, "pB%d" % i, [128, S_], F32) for i in range(4)]
        t16 = [self.sb(st, "t16_%d" % i, [128, 16], F32) for i in range(2)]
        pw = self.sb(st, "pw", [128, 4, 2, 256], BF16)
        S.op("pool", lambda e: e.dma_start(out=pw[:], in_=self.d_pw[0].rearrange("g (k p) d -> p g k d", p=128)),
             writes=["pw"], dma=True)
        for tt in range(4):
            self.rstd_tt(tt, rstd_all[:, tt * 512:(tt + 1) * 512], ("rstd_all", tt))
        xall = lambda c: [("x", c, tt) for tt in range(4)]
        hall = lambda c: [("h", c, tt) for tt in range(4)]
        for c in range(NC_):
            g = c // 2
            k2 = c % 2
            hfc = hf[k2]
            S.op("dve", lambda e, c=c, hfc=hfc: e.scalar_tensor_tensor(
                out=hfc[:], in0=self.xT[:, c, :], scalar=self.pcol("nmix", layer * NC_ + c), in1=rstd_all[:],
                op0=ALU.mult, op1=ALU.mult),
                reads=xall(c) + [("rstd_all", tt) for tt in range(4)] + ["params"], writes=[("hf", k2)])
            cur, curtoks = hfc, [("hf", k2)]
            for k in range(g + 1):
                d = 2 ** k
                bi = k2 * 2 + k % 2
                nxt, nxttok = B[bi], ("pB", bi)
                eng = "pool" if k % 2 == 0 else "dve"
                S.op(eng, lambda e, cur=cur, nxt=nxt, d=d: e.tensor_tensor(
                    out=nxt[:, d:S_], in0=cur[:, d:S_], in1=cur[:, 0:S_ - d], op=ALU.add),
                    reads=list(curtoks), writes=[nxttok])
                S.op("act", lambda e, cur=cur, nxt=nxt, d=d: e.activation(out=nxt[:, 0:d], in_=cur[:, 0:d], func=AF.Identity),
                     reads=list(curtoks), writes=[(nxttok, "head")])
                cur, curtoks = nxt, [nxttok, (nxttok, "head")]
            w = 2 ** (g + 1)
            icnt = self.cf[:, CF.off["icnt"] + g * 16:CF.off["icnt"] + (g + 1) * 16]
            S.op("dve", lambda e, c=c, cur=cur, hfc=hfc, w=w: e.scalar_tensor_tensor(
                out=self.hT[:, c, :], in0=cur[:], scalar=1.0 / w, in1=hfc[:], op0=ALU.mult, op1=ALU.subtract),
                reads=curtoks + [("hf", k2)], writes=hall(c))
            S.op("dve", lambda e, cur=cur, k2=k2, icnt=icnt: e.tensor_tensor(
                out=t16[k2][:], in0=cur[:, 0:16], in1=icnt, op=ALU.mult),
                reads=curtoks + ["cf"], writes=[("t16", k2)])
            S.op("dve", lambda e, c=c, k2=k2, hfc=hfc: e.tensor_tensor(
                out=self.hT[:, c, 0:16], in0=t16[k2][:], in1=hfc[:, 0:16], op=ALU.subtract),
                reads=[("t16", k2), ("hf", k2)] + hall(c), writes=hall(c))
        for tt in range(4):
            for g in range(4):
                for m2 in range(2):
                    m = 2 * g + m2
                    b = self.psrot % 8
                    self.psrot += 1
                    for kk in range(2):
                        S.op("pe", lambda e, b=b, g=g, kk=kk, m2=m2, tt=tt: e.matmul(
                            self.bank(b), lhsT=pw[:, g, kk, m2 * 128:(m2 + 1) * 128],
                            rhs=self.hT[:, 2 * g + kk, tt * 512:(tt + 1) * 512], start=(kk == 0), stop=(kk == 1)),
                            reads=["pw", ("h", 2 * g + kk, tt)], writes=[("ps", b)])
                    xs = self.xT[:, m, tt * 512:(tt + 1) * 512]
                    S.op("dve", lambda e, b=b, xs=xs, m=m: e.scalar_tensor_tensor(
                        out=xs, in0=self.bank(b), scalar=self.pcol("pscale", m), in1=xs, op0=ALU.mult, op1=ALU.add),
                        reads=[("ps", b), ("x", m, tt), "params"], writes=[("x", m, tt)])
            self.tail_hook(tt)


ALL_PHASES = []
for _l in range(DEPTH):
    ALL_PHASES += [("mix", _l), ("ffn", _l)]

WEIGHT_KEYS = ["ffn_w_up", "ffn_w_down", "attn_w_qkv", "attn_w_o", "lru_w_in", "lru_w_a", "lru_w_x",
               "lru_w_out", "pool_w"]


def run_phases(inputs, phases, x_cores=None, trace=False, **bkw):
    nc = Builder(phases, **bkw).build()
    params = pack_params(inputs)
    c16, cf = make_consts()
    if x_cores is None:
        x = np.asarray(inputs["x"], np.float32)
        x_cores = [np.ascontiguousarray(x[b].T) for b in range(8)]
    shared = {k: np.ascontiguousarray(np.asarray(inputs[k], np.float32)) for k in WEIGHT_KEYS}
    shared.update({"params": params, "c16": c16, "cf": cf})
    in_maps = []
    for b in range(8):
        m = dict(shared)
        m["xT"] = x_cores[b]
        in_maps.append(m)
    res = run_bass_kernel_spmd(nc, in_maps, core_ids=list(range(8)), trace=trace)
    return [r["outT"] for r in res.results], res


def kernel(**inputs):
    outs, _ = run_phases(inputs, ALL_PHASES)
    return np.stack([np.ascontiguousarray(o.T) for o in outs], axis=0).astype(np.float32)
```

```python
import contextlib
import numpy as np
import concourse.bass as bass
import concourse.mybir as mybir
from concourse.bass_utils import run_bass_kernel_spmd

F32 = mybir.dt.float32
BF16 = mybir.dt.bfloat16
AF = mybir.ActivationFunctionType
ALU = mybir.AluOpType
AX = mybir.AxisListType

D = 1024
S_ = 2048
NC_ = 8
DEPTH = 4
FH = 2816
NPAIR = 22
DR = 1280
NLC = 10
EPS = 1e-6
NEG = -30000.0
ENGS = ("pe", "act", "dve", "pool", "sp")


class Op:
    __slots__ = ("eng", "fn", "deps", "dma", "signal", "semv")

    def __init__(self, eng, fn, dma):
        self.eng = eng
        self.fn = fn
        self.deps = []
        self.dma = dma
        self.signal = False
        self.semv = None


class _Rec:
    def __init__(self):
        self.calls = []

    def __getattr__(self, name):
        def f(*args, **kwargs):
            self.calls.append((name, args, kwargs))
        return f


class Sched:
    N_DMA_SEMS = 8

    def __init__(self, nc):
        self.nc = nc
        self.ops = {e: [] for e in ENGS}
        self.last_writer = {}
        self.readers = {}
        self.fence_pending = set()
        self.fence_ops = []

    def fence(self):
        self.fence_ops = [self.ops[e][-1] for e in ENGS if self.ops[e]]
        self.fence_pending = set(ENGS)

    def op(self, eng, fn, reads=(), writes=(), dma=False):
        rec = _Rec()
        fn(rec)
        name, args, kwargs = rec.calls[0]
        o = Op(eng, lambda e: getattr(e, name)(*args, **kwargs), dma)
        cand = []
        for t in reads:
            w = self.last_writer.get(t)
            if w is not None:
                cand.append((w, True))
        for t in writes:
            w = self.last_writer.get(t)
            if w is not None:
                cand.append((w, False))
            for r in self.readers.get(t, ()):
                cand.append((r, False))
        seen = set()
        for d, raw in cand:
            if d is o or id(d) in seen:
                continue
            if d.eng == eng and not d.dma and not dma:
                if eng == "pe" or not raw:
                    continue
            seen.add(id(d))
            o.deps.append(d)
        if eng in self.fence_pending:
            self.fence_pending.discard(eng)
            for d in self.fence_ops:
                if id(d) in seen or (d.eng == eng and not d.dma and not dma):
                    continue
                seen.add(id(d))
                o.deps.append(d)
        self.ops[eng].append(o)
        for t in writes:
            self.last_writer[t] = o
            self.readers[t] = []
        for t in reads:
            self.readers.setdefault(t, []).append(o)
        return o

    def emit(self, final_waits=()):
        nc = self.nc
        for e in ENGS:
            for o in self.ops[e]:
                for d in o.deps:
                    d.signal = True
        for o in final_waits:
            o.signal = True
        with contextlib.ExitStack() as st:
            sems = {e: st.enter_context(nc.semaphore("s_" + e)) for e in ENGS}
            for e in ("sp", "pool", "act"):
                for k in range(self.N_DMA_SEMS):
                    sems[(e, k)] = st.enter_context(nc.semaphore("d_%s%d" % (e, k)))
            for e in ENGS:
                c = 0
                dcount = [0] * self.N_DMA_SEMS
                nd = 0
                for o in self.ops[e]:
                    if o.dma:
                        k = nd % self.N_DMA_SEMS
                        nd += 1
                        dcount[k] += 1
                        o.semv = ((e, k), 16 * dcount[k])
                    elif o.signal:
                        c += 1
                        o.semv = (e, c)
            block = st.enter_context(nc.Block())
            engobj = {"pe": block.tensor, "act": block.scalar, "dve": block.vector,
                      "pool": block.gpsimd, "sp": block.sync}

            def make(e):
                def body(eng):
                    known = {}
                    for o in self.ops[e]:
                        waits = {}
                        for d in o.deps:
                            sk, v = d.semv
                            if known.get(sk, 0) >= v:
                                continue
                            if waits.get(sk, 0) < v:
                                waits[sk] = v
                        if o.dma:
                            sk, v = o.semv
                            if v > 16 and known.get(sk, 0) < v - 16 and waits.get(sk, 0) < v - 16:
                                waits[sk] = v - 16
                        for sk, v in waits.items():
                            eng.wait_ge(sems[sk], v)
                            known[sk] = v
                        ins = o.fn(eng)
                        if o.semv is not None:
                            ins.then_inc(sems[o.semv[0]], 16 if o.dma else 1)
                    if e == "sp":
                        for o in final_waits:
                            eng.wait_ge(sems[o.semv[0]], o.semv[1])
                return body

            for e in ENGS:
                engobj[e](make(e))


class Cols:
    def __init__(self):
        self.n = 0
        self.off = {}

    def add(self, name, k):
        self.off[name] = self.n
        self.n += k


PC = Cols()
PC.add("nmix", DEPTH * NC_)
PC.add("nffn", DEPTH * NC_)
PC.add("qg", 2)
PC.add("kg", 2)
PC.add("lcw", 4 * NLC)
PC.add("lcb", NLC)
PC.add("lba", NLC)
PC.add("lbx", NLC)
PC.add("llam", NLC)
PC.add("pscale", NC_)
PC.add("fcw", DEPTH * 3 * 44)
PC.add("fcb", DEPTH * 44)

CC = Cols()
CC.add("ident", 128)
CC.add("ones", 128)
CC.add("cb", 2 * 256)
CC.add("esel", 8 * 128)
NCB = CC.n
CF = Cols()
CF.add("pastmask", 64)
CF.add("icnt", 4 * 16)
CF.add("eps", 1)


def pack_params(inp):
    P = np.zeros((128, PC.n), np.float32)

    def put(name, arr):
        a = np.asarray(arr, np.float32)
        a = a.reshape(-1, a.shape[-1] // 128, 128)
        a = a.transpose(2, 0, 1).reshape(128, -1)
        P[:, PC.off[name]:PC.off[name] + a.shape[1]] = a

    put("nmix", inp["norm_mix_g"])
    put("nffn", inp["norm_ffn_g"])
    put("qg", inp["attn_q_g"])
    put("kg", inp["attn_k_g"])
    put("lcw", inp["lru_conv_w"][0])
    put("lcb", inp["lru_conv_b"][0])
    put("lba", inp["lru_b_a"][0])
    put("lbx", inp["lru_b_x"][0])
    put("llam", inp["lru_lambda"][0])
    put("pscale", inp["pool_scale"][0])
    put("fcw", inp["ffn_conv_w"])
    put("fcb", inp["ffn_conv_b"])
    return P


def make_consts():
    cb16 = np.zeros((128, CC.n), np.float32)
    cb16[:, CC.off["ident"]:CC.off["ident"] + 128] = np.eye(128, dtype=np.float32)
    cb16[:, CC.off["ones"]:CC.off["ones"] + 128] = 1.0
    p = np.arange(128)[:, None]
    q = np.arange(256)[None, :]
    for j in range(2):
        m = np.where(j * 128 + p <= q, 0.0, NEG).astype(np.float32)
        cb16[:, CC.off["cb"] + j * 256:CC.off["cb"] + (j + 1) * 256] = m
    es = np.zeros((128, 8, 128), np.float32)
    for n in range(8):
        es[n, n, :] = 1.0
    cb16[:, CC.off["esel"]:CC.off["esel"] + 1024] = es.reshape(128, 1024)
    cf = np.zeros((128, CF.n), np.float32)
    pm = np.zeros((8, 8), np.float32)
    for i in range(8):
        for n in range(8):
            pm[i, n] = 0.0 if n < 4 + i // 2 else -1e30
    cf[:, CF.off["pastmask"]:CF.off["pastmask"] + 64] = pm.reshape(1, 64)
    ic = np.zeros((4, 16), np.float32)
    for g, w in enumerate((2, 4, 8, 16)):
        for t in range(16):
            ic[g, t] = 1.0 / min(t + 1, w)
    cf[:, CF.off["icnt"]:CF.off["icnt"] + 64] = ic.reshape(1, 64)
    cf[:, CF.off["eps"]] = EPS
    return cb16, cf


class Builder:
    def __init__(self, phases, attn_heads=8, attn_hg=4):
        self.phases = phases
        self.attn_heads = attn_heads
        self.attn_hg = attn_hg
        nc = bass.Bass("TRN2", target_bir_lowering=False)
        self.nc = nc
        dt = nc.dram_tensor
        self.d_x = dt("xT", [D, S_], F32, kind="ExternalInput").ap()
        self.d_out = dt("outT", [D, S_], F32, kind="ExternalOutput").ap()
        self.d_params = dt("params", [128, PC.n], F32, kind="ExternalInput").ap()
        self.d_c16 = dt("c16", [128, CC.n], F32, kind="ExternalInput").ap()
        self.d_cf = dt("cf", [128, CF.n], F32, kind="ExternalInput").ap()
        self.d_wup = dt("ffn_w_up", [DEPTH, D, 2 * FH], F32, kind="ExternalInput").ap()
        self.d_wdn = dt("ffn_w_down", [DEPTH, FH, D], F32, kind="ExternalInput").ap()
        self.d_wqkv = dt("attn_w_qkv", [2, D, 3 * D], F32, kind="ExternalInput").ap()
        self.d_wo = dt("attn_w_o", [2, D, D], F32, kind="ExternalInput").ap()
        self.d_lwin = dt("lru_w_in", [1, D, 2 * DR], F32, kind="ExternalInput").ap()
        self.d_lwa = dt("lru_w_a", [1, NLC, 128, 128], F32, kind="ExternalInput").ap()
        self.d_lwx = dt("lru_w_x", [1, NLC, 128, 128], F32, kind="ExternalInput").ap()
        self.d_lwout = dt("lru_w_out", [1, DR, D], F32, kind="ExternalInput").ap()
        self.d_pw = dt("pool_w", [1, 4, 256, 256], F32, kind="ExternalInput").ap()
        self.S = Sched(nc)
        self.psrot = 0
        self.uid = 0

    def sb(self, st, name, shape, dtype):
        self.uid += 1
        return st.enter_context(self.nc.sbuf_tensor("%s_u%d" % (name, self.uid), shape, dtype))

    def pcol(self, name, idx):
        o = PC.off[name] + idx
        return self.params[:, o:o + 1]

    def build(self):
        nc, S = self.nc, self.S
        with contextlib.ExitStack() as st:
            self.xT = self.sb(st, "xT_sb", [128, NC_, S_], F32)
            self.hT = self.sb(st, "hT_sb", [128, NC_, S_], BF16)
            self.params = self.sb(st, "params_sb", [128, PC.n], F32)
            self.c16 = self.sb(st, "c16_sb", [128, CC.n], BF16)
            self.cf = self.sb(st, "cf_sb", [128, CF.n], F32)
            self.sq = self.sb(st, "sq_sb", [128, NC_, 512], BF16)
            self.lnv = self.sb(st, "lnv_sb", [128, 512], F32)
            self.rstd = self.sb(st, "rstd_sb", [128, 512], F32)
            self.PS = st.enter_context(nc.psum_tensor("PS", [128, 4096], F32))
            self.ident = self.c16[:, CC.off["ident"]:CC.off["ident"] + 128]
            self.ones = self.c16[:, CC.off["ones"]:CC.off["ones"] + 128]
            self.eps = self.cf[:, CF.off["eps"]:CF.off["eps"] + 1]

            S.op("sp", lambda e: e.dma_start(out=self.params[:], in_=self.d_params[:, :]),
                 writes=["params"], dma=True)
            S.op("sp", lambda e: e.dma_start(out=self.cf[:], in_=self.d_cf[:, :]),
                 writes=["cf"], dma=True)
            S.op("pool", lambda e: e.dma_start(out=self.c16[:], in_=self.d_c16[:, :]),
                 writes=["c16"], dma=True)
            for c in range(NC_):
                S.op("sp", lambda e, c=c: e.dma_start(out=self.xT[:, c, :],
                                                      in_=self.d_x[c * 128:(c + 1) * 128, :]),
                     writes=[("x", c, tt) for tt in range(4)], dma=True)

            self.wfirst = self.sb(st, "wfirst", [128, NC_, 512], BF16)
            nph = len(self.phases)
            self.prefetch_first(self.phases[0])
            self.start_norm(self.phases[0])
            for i, ph in enumerate(self.phases):
                kind, layer = ph
                self.next_phase = self.phases[i + 1] if i + 1 < nph else None
                S.fence()
                with contextlib.ExitStack() as st2:
                    if kind == "ffn":
                        self.ffn(st2, layer)
                    else:
                        mk = layer % 3
                        if mk == 0:
                            self.attn(st2, layer)
                        elif mk == 1:
                            self.lru(st2, layer)
                        else:
                            self.pool(st2, layer)

            fw = []
            for c in range(NC_):
                fw.append(S.op("sp", lambda e, c=c: e.dma_start(
                    out=self.d_out[c * 128:(c + 1) * 128, :], in_=self.xT[:, c, :]),
                    reads=[("x", c, tt) for tt in range(4)], dma=True))
            S.emit(final_waits=fw)
        return nc

    def bank(self, b):
        return self.PS[:, b * 512:(b + 1) * 512]

    def normA(self, tt):
        ts = slice(tt * 512, (tt + 1) * 512)
        self.S.op("act", lambda e: e.activation(out=self.sq[:], in_=self.xT[:, :, ts], func=AF.Square),
                  reads=[("x", c, tt) for c in range(NC_)], writes=["sq"])

    def normB(self, gname, layer, tt):
        S = self.S
        ts = slice(tt * 512, (tt + 1) * 512)
        b = self.psrot % 8
        self.psrot += 1
        for c in range(NC_):
            S.op("pe", lambda e, c=c: e.matmul(self.bank(b), lhsT=self.ones, rhs=self.sq[:, c, :],
                                               start=(c == 0), stop=(c == NC_ - 1)),
                 reads=["sq", "c16"], writes=[("ps", b)])
        S.op("act", lambda e: e.activation(out=self.lnv[:], in_=self.bank(b), func=AF.Ln,
                                           scale=1.0 / D, bias=self.eps),
             reads=[("ps", b), "cf"], writes=["lnv"])
        S.op("act", lambda e: e.activation(out=self.rstd[:], in_=self.lnv[:], func=AF.Exp, scale=-0.5),
             reads=["lnv"], writes=["rstd"])
        for c in range(NC_):
            S.op("dve", lambda e, c=c: e.scalar_tensor_tensor(
                out=self.hT[:, c, ts], in0=self.xT[:, c, ts], scalar=self.pcol(gname, layer * NC_ + c),
                in1=self.rstd[:], op0=ALU.mult, op1=ALU.mult),
                reads=[("x", c, tt), "rstd", "params"], writes=[("h", c, tt)])

    @staticmethod
    def norm_of(ph):
        if ph is None:
            return None
        kind, layer = ph
        if kind == "ffn":
            return ("nffn", layer)
        if layer % 3 == 2:
            return None
        return ("nmix", layer)

    def start_norm(self, ph):
        nm = self.norm_of(ph)
        if nm is None:
            return
        for tt in range(4):
            self.normA(tt)
            self.normB(nm[0], nm[1], tt)

    def tail_hook(self, tt):
        nm = self.norm_of(self.next_phase)
        if tt == 0 and self.next_phase is not None:
            self.prefetch_first(self.next_phase)
        if nm is None:
            return
        if tt >= 1:
            self.normB(nm[0], nm[1], tt - 1)
        self.normA(tt)
        if tt == 3:
            self.normB(nm[0], nm[1], 3)

    def prefetch_first(self, ph):
        S = self.S
        kind, layer = ph
        wf = self.wfirst
        if kind == "ffn":
            src = self.d_wup[layer].rearrange("(c p) f -> p c f", p=128)
            S.op("pool", lambda e: e.dma_start(out=wf[:, :, 0:256], in_=src[:, :, 0:256]),
                 writes=[("wf", 0), ("wf", 1)], dma=True)
            S.op("pool", lambda e: e.dma_start(out=wf[:, :, 256:512], in_=src[:, :, FH:FH + 256]),
                 writes=[("wf", 2), ("wf", 3)], dma=True)
        elif layer % 3 == 0:
            src = self.d_wqkv[layer // 3].rearrange("(c p) f -> p c f", p=128)
            for k in range(3):
                S.op("pool", lambda e, k=k: e.dma_start(out=wf[:, :, k * 128:(k + 1) * 128],
                                                        in_=src[:, :, k * D:k * D + 128]),
                     writes=[("wf", k)], dma=True)
        elif layer % 3 == 1:
            src = self.d_lwin[0].rearrange("(c p) f -> p c f", p=128)
            S.op("pool", lambda e: e.dma_start(out=wf[:, :, 0:128], in_=src[:, :, 0:128]),
                 writes=[("wf", 0)], dma=True)
            S.op("pool", lambda e: e.dma_start(out=wf[:, :, 128:256], in_=src[:, :, DR:DR + 128]),
                 writes=[("wf", 1)], dma=True)

    def ffn(self, st, layer):
        nc, S = self.nc, self.S
        GROUPS = [(0, 6), (6, 12), (12, 17), (17, 22)]
        NP = 7
        NW = 3
        wup = [self.wfirst] + [self.sb(st, "wup%d" % i, [128, NC_, 512], BF16) for i in range(1, NW)]
        wdn = self.sb(st, "wdn", [128, 6, D], BF16)
        Pb = self.sb(st, "Pb", [128, NP, S_], BF16)
        cbuf = [self.sb(st, "cbuf%d" % i, [128, S_], F32) for i in range(3)]
        wup_src = self.d_wup[layer].rearrange("(c p) f -> p c f", p=128)

        def wtok(bi, half):
            if bi == 0:
                return [("wf", 2 * half), ("wf", 2 * half + 1)]
            return [("wup", bi, half)]

        def load_slab(s):
            bi = s % NW
            S.op("pool", lambda e: e.dma_start(out=wup[bi][:, :, 0:256],
                                               in_=wup_src[:, :, s * 256:(s + 1) * 256]),
                 writes=wtok(bi, 0), dma=True)
            S.op("pool", lambda e: e.dma_start(out=wup[bi][:, :, 256:512],
                                               in_=wup_src[:, :, FH + s * 256:FH + (s + 1) * 256]),
                 writes=wtok(bi, 1), dma=True)

        def load_wdn(j0, j1):
            S.op("pool", lambda e: e.dma_start(
                out=wdn[:, 0:j1 - j0, :],
                in_=self.d_wdn[layer, j0 * 128:j1 * 128, :].rearrange("(j p) d -> p j d", p=128)),
                writes=["wdn"], dma=True)

        cbi = [0]

        def up(j):
            s, r = j // 2, j % 2
            bi = s % NW
            cg = None
            for half in range(2):
                fj = half * NPAIR + j
                base = half * 2048
                lcol = half * 256 + r * 128
                for tt in range(4):
                    for c in range(NC_):
                        S.op("pe", lambda e, c=c, tt=tt: e.matmul(
                            self.PS[:, base + tt * 512: base + (tt + 1) * 512],
                            lhsT=wup[bi][:, c, lcol:lcol + 128],
                            rhs=self.hT[:, c, tt * 512:(tt + 1) * 512],
                            start=(c == 0), stop=(c == NC_ - 1)),
                            reads=wtok(bi, half) + [("h", c, tt)], writes=[("ps", half * 4 + tt)])
                if half == 1 and r == 1 and s + NW < 11:
                    load_slab(s + NW)
                cb = cbuf[cbi[0] % 3]
                cbk = cbi[0] % 3
                cbn = [("cbuf", cbk, 0), ("cbuf", cbk, 1)]
                cbi[0] += 1
                w0 = self.pcol("fcw", (layer * 3 + 0) * 44 + fj)
                w1 = self.pcol("fcw", (layer * 3 + 1) * 44 + fj)
                w2 = self.pcol("fcw", (layer * 3 + 2) * 44 + fj)
                bb = self.pcol("fcb", layer * 44 + fj)
                u = self.PS[:, base:base + 2048]
                for hh in range(2):
                    lo, hi = hh * 1024, (hh + 1) * 1024
                    psr = [("ps", half * 4 + 2 * hh), ("ps", half * 4 + 2 * hh + 1)]
                    if hh == 1:
                        psr.append(("ps", half * 4 + 1))
                    ct = [("cbuf", cbk, hh)]
                    S.op("act", lambda e: e.activation(out=cb[:, lo:hi], in_=u[:, lo:hi], func=AF.Identity,
                                                       scale=w2, bias=bb),
                         reads=psr + ["params"], writes=ct)
                    for k, wk in ((1, w1), (2, w0)):
                        o0 = max(lo, k)
                        S.op("dve", lambda e, o0=o0, k=k, wk=wk: e.scalar_tensor_tensor(
                            out=cb[:, o0:hi], in0=u[:, o0 - k:hi - k], scalar=wk, in1=cb[:, o0:hi],
                            op0=ALU.mult, op1=ALU.add), reads=psr + ct + ["params"], writes=ct)
                if half == 0:
                    S.op("act", lambda e: e.activation(out=cb[:], in_=cb[:], func=AF.Silu),
                         reads=cbn, writes=cbn)
                    cg = (cb, cbn)
                else:
                    S.op("pool", lambda e: e.tensor_tensor(out=Pb[:, j % NP, :], in0=cg[0][:], in1=cb[:], op=ALU.mult),
                         reads=cbn + cg[1], writes=[("P", j % NP)])

        def down(j0, j1, nbanks=4, final=False):
            nj = j1 - j0
            for tt in range(4):
                for m in range(NC_):
                    b = self.psrot % nbanks
                    self.psrot += 1
                    for jj in range(nj):
                        slot = (j0 + jj) % NP
                        S.op("pe", lambda e, jj=jj, slot=slot: e.matmul(
                            self.bank(b), lhsT=wdn[:, jj, m * 128:(m + 1) * 128],
                            rhs=Pb[:, slot, tt * 512:(tt + 1) * 512],
                            start=(jj == 0), stop=(jj == nj - 1)),
                            reads=["wdn", ("P", slot)], writes=[("ps", b)])
                    xs = self.xT[:, m, tt * 512:(tt + 1) * 512]
                    S.op("dve", lambda e: e.tensor_tensor(out=xs, in0=self.bank(b), in1=xs, op=ALU.add),
                         reads=[("ps", b), ("x", m, tt)], writes=[("x", m, tt)])
                if final:
                    self.tail_hook(tt)

        for s0 in range(1, NW):
            load_slab(s0)
        load_wdn(*GROUPS[0])
        gidx = {}
        for gi, (j0, j1) in enumerate(GROUPS):
            for j in range(j0, j1):
                gidx[j] = gi
        for j in range(NPAIR):
            up(j)
            gi = gidx[j]
            j0, j1 = GROUPS[gi]
            if j == j0 and gi > 0:
                down(*GROUPS[gi - 1])
            if j == j0 + 2 and gi > 0:
                load_wdn(j0, j1)
        down(*GROUPS[-1], nbanks=8, final=True)

    def rstd_tt(self, tt, dst, dst_tok):
        S = self.S
        ts = slice(tt * 512, (tt + 1) * 512)
        b = self.psrot % 8
        self.psrot += 1
        S.op("act", lambda e: e.activation(out=self.sq[:], in_=self.xT[:, :, ts], func=AF.Square),
             reads=[("x", c, tt) for c in range(NC_)], writes=["sq"])
        for c in range(NC_):
            S.op("pe", lambda e, c=c: e.matmul(self.bank(b), lhsT=self.ones, rhs=self.sq[:, c, :],
                                               start=(c == 0), stop=(c == NC_ - 1)),
                 reads=["sq", "c16"], writes=[("ps", b)])
        S.op("act", lambda e: e.activation(out=self.lnv[:], in_=self.bank(b), func=AF.Ln,
                                           scale=1.0 / D, bias=self.eps),
             reads=[("ps", b), "cf"], writes=["lnv"])
        S.op("act", lambda e: e.activation(out=dst, in_=self.lnv[:], func=AF.Exp, scale=-0.5),
             reads=["lnv"], writes=[dst_tok])

    def resid_proj(self, w, nk, rhs_fn, rhs_toks, wtok, final=False):
        S = self.S
        for tt in range(4):
            for m in range(NC_):
                b = self.psrot % 8
                self.psrot += 1
                for k in range(nk):
                    S.op("pe", lambda e, b=b, k=k, m=m, tt=tt: e.matmul(
                        self.bank(b), lhsT=w[:, k, m * 128:(m + 1) * 128], rhs=rhs_fn(k, tt),
                        start=(k == 0), stop=(k == nk - 1)),
                        reads=[wtok] + rhs_toks(k, tt), writes=[("ps", b)])
                xs = self.xT[:, m, tt * 512:(tt + 1) * 512]
                S.op("dve", lambda e, b=b, xs=xs: e.tensor_tensor(
                    out=xs, in0=self.bank(b), in1=xs, op=ALU.add),
                    reads=[("ps", b), ("x", m, tt)], writes=[("x", m, tt)])
            if final:
                self.tail_hook(tt)

    def attn(self, st, layer):
        nc, S = self.nc, self.S
        slot = layer // 3
        HG = self.attn_hg
        NH = self.attn_heads
        wqkv = [self.wfirst, self.sb(st, "wqkv1", [128, NC_, 384], BF16)]

        def wqt(bi, k):
            return ("wf", k) if bi == 0 else ("wqkv", bi, k)

        qn = [self.sb(st, "qn%d" % i, [128, S_], BF16) for i in range(2)]
        kn = [self.sb(st, "kn%d" % i, [128, S_], BF16) for i in range(2)]
        Vt = [self.sb(st, "Vt%d" % i, [128, 16, 128], BF16) for i in range(2)]
        oT = self.sb(st, "oT", [128, HG, S_], BF16)
        wo = self.sb(st, "wo", [128, HG, D], BF16)
        rs = [self.sb(st, "rs%d" % i, [128, 512], F32) for i in range(2)]
        sq2 = [self.sb(st, "sq2_%d" % i, [128, 512], BF16) for i in range(2)]
        PT = [self.sb(st, "PT%d" % i, [128, 512], BF16) for i in range(4)]
        kmf = self.sb(st, "kmf", [128, 8], F32)
        kmb = self.sb(st, "kmb", [128, 8], BF16)
        g1 = self.sb(st, "g1", [128, 64], F32)
        top = self.sb(st, "top", [128, 64], F32)
        cmpf = self.sb(st, "cmpf", [128, 64], F32)
        btok = self.sb(st, "btok", [128, 8, 128], BF16)
        biasT = self.sb(st, "biasT", [128, 1024], BF16)
        rden = [self.sb(st, "rden%d" % i, [128, 256], F32) for i in range(2)]
        S.op("pool", lambda e: e.memset(btok[:], 0.0), writes=["btok"])
        S.op("pool", lambda e: e.memset(biasT[:], 0.0), writes=["biasT"])
        wsrc = self.d_wqkv[slot].rearrange("(c p) f -> p c f", p=128)
        scale = 128.0 ** -0.5
        pastmask = self.cf[:, CF.off["pastmask"]:CF.off["pastmask"] + 64]
        cbm = [self.c16[:, CC.off["cb"] + j * 256:CC.off["cb"] + (j + 1) * 256] for j in range(2)]
        esel = [self.c16[:, CC.off["esel"] + n * 128:CC.off["esel"] + (n + 1) * 128] for n in range(8)]
        cnt = {"s": 0, "p": 0, "k2": 0, "pt": 0}

        def sbank():
            b = 2 + cnt["s"] % 3
            cnt["s"] += 1
            return b


        def load_w(hd):
            bi = hd % 2
            for k in range(3):
                S.op("pool", lambda e, k=k: e.dma_start(
                    out=wqkv[bi][:, :, k * 128:(k + 1) * 128],
                    in_=wsrc[:, :, k * D + hd * 128:k * D + (hd + 1) * 128]),
                    writes=[wqt(bi, k)], dma=True)

        def prologue_units(hd):
            bi = hd % 2
            units = []
            items = [(which, tt) for which in (0, 1) for tt in range(4)]
            state = {}

            def A1(i):
                which, tt = items[i]
                ts = slice(tt * 512, (tt + 1) * 512)
                b = 5 + cnt["p"] % 3
                cnt["p"] += 1
                k2 = cnt["k2"] % 2
                cnt["k2"] += 1
                state[i] = (b, k2)
                for c in range(NC_):
                    S.op("pe", lambda e, c=c: e.matmul(
                        self.bank(b), lhsT=wqkv[bi][:, c, which * 128:(which + 1) * 128],
                        rhs=self.hT[:, c, ts], start=(c == 0), stop=(c == NC_ - 1)),
                        reads=[wqt(bi, which), ("h", c, tt)], writes=[("ps", b)])
                S.op("act", lambda e: e.activation(out=sq2[k2][:], in_=self.bank(b), func=AF.Square),
                     reads=[("ps", b)], writes=[("sq2", k2)])

            def A2(i):
                which, tt = items[i]
                ts = slice(tt * 512, (tt + 1) * 512)
                b, k2 = state[i]
                dst = (qn if which == 0 else kn)[bi]
                gname = "qg" if which == 0 else "kg"
                b7 = sbank()
                S.op("pe", lambda e: e.matmul(self.bank(b7), lhsT=self.ones, rhs=sq2[k2][:], start=True, stop=True),
                     reads=[("sq2", k2), "c16"], writes=[("ps", b7)])
                S.op("act", lambda e: e.activation(out=rs[k2][:], in_=self.bank(b7), func=AF.Ln,
                                                   scale=1.0 / 128, bias=self.eps),
                     reads=[("ps", b7), "cf"], writes=[("rs", k2)])
                S.op("act", lambda e: e.activation(out=rs[k2][:], in_=rs[k2][:], func=AF.Exp, scale=-0.5),
                     reads=[("rs", k2)], writes=[("rs", k2)])
                S.op("dve", lambda e: e.scalar_tensor_tensor(
                    out=dst[:, ts], in0=self.bank(b), scalar=self.pcol(gname, slot), in1=rs[k2][:],
                    op0=ALU.mult, op1=ALU.mult),
                    reads=[("ps", b), ("rs", k2), "params"], writes=[("qk", which, bi, tt)])

            units.append(lambda: A1(0))
            for i in range(1, 8):
                units.append(lambda i=i: A1(i))
                units.append(lambda i=i: A2(i - 1))
            units.append(lambda: A2(7))

            def Vunit(bq):
                b = 5 + cnt["p"] % 3
                cnt["p"] += 1
                for i4 in range(4):
                    i = bq * 4 + i4
                    for c in range(NC_):
                        S.op("pe", lambda e, i=i, i4=i4, c=c: e.matmul(
                            self.bank(b)[:, i4 * 128:(i4 + 1) * 128], lhsT=self.hT[:, c, i * 128:(i + 1) * 128],
                            rhs=wqkv[bi][:, c, 256:384], start=(c == 0), stop=(c == NC_ - 1)),
                            reads=[wqt(bi, 2), ("h", c, bq)], writes=[("ps", b)])
                S.op("act", lambda e: e.activation(
                    out=Vt[bi][:, bq * 4:(bq + 1) * 4, :], in_=self.bank(b).rearrange("p (i d) -> p i d", d=128),
                    func=AF.Identity), reads=[("ps", b)], writes=[("V", bi, bq)])

            for bq in range(4):
                units.append(lambda bq=bq: Vunit(bq))

            def Cunit():
                S.op("dve", lambda e: e.tensor_reduce(out=kmf[:], in_=kn[bi][:].rearrange("p (n k) -> p n k", k=256),
                                                      axis=AX.X, op=ALU.add),
                     reads=[("qk", 1, bi, tt) for tt in range(4)], writes=["kmf"])
                S.op("dve", lambda e: e.tensor_scalar(out=kmb[:], in0=kmf[:], scalar1=1.0 / 256, scalar2=None,
                                                      op0=ALU.mult), reads=["kmf"], writes=["kmb"])
                bg = sbank()
                for i in range(8):
                    S.op("pe", lambda e, i=i: e.matmul(
                        self.bank(bg)[:, i * 8:(i + 1) * 8], lhsT=qn[bi][:, (8 + i) * 128:(9 + i) * 128], rhs=kmb[:],
                        start=True, stop=True),
                        reads=["kmb", ("qk", 0, bi, 2 + i // 4)], writes=[("ps", bg)])
                S.op("dve", lambda e: e.tensor_tensor(out=g1[:], in0=self.bank(bg)[:, 0:64], in1=pastmask, op=ALU.add),
                     reads=[("ps", bg), "cf"], writes=["g1"])
                for i in range(8):
                    S.op("dve", lambda e, i=i: e.max(out=top[:, i * 8:(i + 1) * 8], in_=g1[:, i * 8:(i + 1) * 8]),
                         reads=["g1"], writes=["top"])
                S.op("dve", lambda e: e.tensor_tensor(
                    out=cmpf[:].rearrange("p (i n) -> p i n", n=8), in0=g1[:].rearrange("p (i n) -> p i n", n=8),
                    in1=top[:].rearrange("p (i n) -> p i n", n=8)[:, :, 2:3].to_broadcast([128, 8, 8]), op=ALU.is_lt),
                    reads=["g1", "top"], writes=["cmpf"])

            units.append(Cunit)
            return units

        def Dunit(hd):
            S.op("dve", lambda e: e.tensor_scalar(out=btok[:, :, 0:8], in0=cmpf[:].rearrange("p (i n) -> p i n", n=8),
                                                  scalar1=NEG, scalar2=None, op0=ALU.mult),
                 reads=["cmpf"], writes=["btok"])
            for k in range(2):
                bd_ = sbank()
                for i4 in range(4):
                    i = k * 4 + i4
                    S.op("pe", lambda e, i=i, i4=i4: e.matmul(
                        self.bank(bd_)[:, i4 * 128:(i4 + 1) * 128], lhsT=btok[:, i, :], rhs=self.ident,
                        start=True, stop=True),
                        reads=["btok", "c16"], writes=[("ps", bd_)])
                S.op("act", lambda e, k=k: e.activation(out=biasT[0:8, k * 512:(k + 1) * 512],
                                                        in_=self.bank(bd_)[0:8, :], func=AF.Identity),
                     reads=[("ps", bd_)], writes=["biasT"])

        def E1(hd, qb, n, st_):
            bi = hd % 2
            qs = slice(qb * 256, (qb + 1) * 256)
            bs_ = sbank()
            pk = cnt["pt"] % 4
            cnt["pt"] += 1
            st_[(qb, n)] = pk
            for half in range(2):
                kt = 2 * n + half
                osl = self.bank(bs_)[:, half * 256:(half + 1) * 256]
                extra = (n == qb) or (qb >= 4)
                S.op("pe", lambda e, osl=osl, kt=kt, extra=extra: e.matmul(
                    osl, lhsT=kn[bi][:, kt * 128:(kt + 1) * 128], rhs=qn[bi][:, qs],
                    start=True, stop=(not extra)),
                    reads=[("qk", 1, bi, kt // 4), ("qk", 0, bi, qb // 2)], writes=[("ps", bs_)])
                if n == qb:
                    S.op("pe", lambda e, osl=osl, half=half: e.matmul(
                        osl, lhsT=self.ident, rhs=cbm[half], start=False, stop=True),
                        reads=["c16"], writes=[("ps", bs_)])
                elif qb >= 4:
                    S.op("pe", lambda e, osl=osl: e.matmul(
                        osl, lhsT=esel[n], rhs=biasT[:, (qb - 4) * 256:(qb - 3) * 256],
                        start=False, stop=True),
                        reads=["c16", "biasT"], writes=[("ps", bs_)])
            S.op("act", lambda e: e.activation(out=PT[pk][:], in_=self.bank(bs_), func=AF.Exp, scale=scale),
                 reads=[("ps", bs_)], writes=[("PT", pk)])

        def E2(hd, qb, n, st_):
            bi = hd % 2
            qs = slice(qb * 256, (qb + 1) * 256)
            pk = st_[(qb, n)]
            bo = qb % 2
            for half in range(2):
                kt = 2 * n + half
                first = (n == 0 and half == 0)
                last = (n == qb and half == 1)
                S.op("pe", lambda e, kt=kt, half=half, first=first, last=last: e.matmul(
                    self.bank(bo)[:, 0:256], lhsT=Vt[bi][:, kt, :], rhs=PT[pk][:, half * 256:(half + 1) * 256],
                    start=first, stop=last, skip_group_check=True),
                    reads=[("V", bi, kt // 4), ("PT", pk)], writes=[("ps", bo)])
                S.op("pe", lambda e, half=half, last=last: e.matmul(
                    self.bank(bo)[:, 256:512], lhsT=self.ones, rhs=PT[pk][:, half * 256:(half + 1) * 256],
                    start=False, stop=last, skip_group_check=True),
                    reads=["c16", ("PT", pk)], writes=[("ps", bo)])
            if n == qb:
                rk = qb % 2
                S.op("dve", lambda e: e.reciprocal(out=rden[rk][:], in_=self.bank(bo)[:, 256:512]),
                     reads=[("ps", bo)], writes=[("rden", rk)])
                S.op("dve", lambda e: e.tensor_tensor(
                    out=oT[:, hd % HG, qs], in0=self.bank(bo)[:, 0:256], in1=rden[rk][:], op=ALU.mult),
                    reads=[("ps", bo), ("rden", rk)], writes=[("oT", hd % HG, qb // 2)])

        for u in prologue_units(0):
            u()
        LAG = 2
        for hd in range(NH):
            if hd + 1 < NH:
                load_w(hd + 1)
                nxt = prologue_units(hd + 1)
            else:
                nxt = []
            if hd % HG == 0:
                g0 = hd
                S.op("pool", lambda e, g0=g0: e.dma_start(
                    out=wo[:], in_=self.d_wo[slot, g0 * 128:(g0 + HG) * 128, :].rearrange("(h p) d -> p h d", p=128)),
                    writes=["wo"], dma=True)
            items = [(qb, n) for qb in range(8) for n in range(qb + 1)]
            st_ = {}
            for idx in range(len(items) + LAG):
                if idx == 4:
                    Dunit(hd)
                if idx < len(items):
                    E1(hd, items[idx][0], items[idx][1], st_)
                if idx >= LAG:
                    E2(hd, items[idx - LAG][0], items[idx - LAG][1], st_)
                if nxt and idx >= 6:
                    nxt.pop(0)()
            while nxt:
                nxt.pop(0)()
            if hd % HG == HG - 1:
                self.resid_proj(wo, HG, lambda k, tt: oT[:, k, tt * 512:(tt + 1) * 512],
                                lambda k, tt: [("oT", k, tt)], "wo", final=(hd == NH - 1))

    def lru(self, st, layer):
        nc, S = self.nc, self.S
        CG = 5
        NSLOT = 6
        win = [self.wfirst, self.sb(st, "win1", [128, NC_, 256], BF16)]

        def wit(bi, half):
            return ("wf", half) if bi == 0 else ("win", bi, half)

        wa = self.sb(st, "wa", [128, NLC, 128], BF16)
        wx = self.sb(st, "wx", [128, NLC, 128], BF16)
        wout = self.sb(st, "wout", [128, CG, D], BF16)
        hy = self.sb(st, "hy", [128, NSLOT, S_], BF16)
        xc = [self.sb(st, "xc%d" % i, [128, 512], F32) for i in range(4)]
        xcb = [self.sb(st, "xcb%d" % i, [128, 512], BF16) for i in range(4)]
        gy = [self.sb(st, "gy%d" % i, [128, 512], BF16) for i in range(4)]
        A = [self.sb(st, "lruA%d" % i, [128, 512], F32) for i in range(4)]
        I = [self.sb(st, "lruI%d" % i, [128, 512], F32) for i in range(4)]
        M = [self.sb(st, "lruM%d" % i, [128, 512], F32) for i in range(4)]
        us = [self.sb(st, "lruus%d" % i, [128, 4], F32) for i in range(4)]
        dp = self.sb(st, "lrudp", [128, 4 * NLC], F32)
        S.op("pool", lambda e: e.dma_start(out=wa[:], in_=self.d_lwa[0].rearrange("n c d -> c n d")),
             writes=["wa"], dma=True)
        S.op("pool", lambda e: e.dma_start(out=wx[:], in_=self.d_lwx[0].rearrange("n c d -> c n d")),
             writes=["wx"], dma=True)
        lam = self.params[:, PC.off["llam"]:PC.off["llam"] + NLC]
        S.op("act", lambda e: e.activation(out=dp[:, 0:NLC], in_=lam, func=AF.Exp, scale=-1.0),
             reads=["params"], writes=["dp0"])
        S.op("act", lambda e: e.activation(out=dp[:, 0:NLC], in_=dp[:, 0:NLC], func=AF.Ln, bias=1.0),
             reads=["dp0"], writes=["dp0"])
        S.op("dve", lambda e: e.tensor_scalar(out=dp[:, NLC:2 * NLC], in0=dp[:, 0:NLC], scalar1=-4.0, scalar2=None,
                                              op0=ALU.mult), reads=["dp0"], writes=["dp"])
        S.op("dve", lambda e: e.tensor_scalar(out=dp[:, 2 * NLC:3 * NLC],
                                              in0=self.params[:, PC.off["lba"]:PC.off["lba"] + NLC],
                                              scalar1=0.5, scalar2=None, op0=ALU.mult), reads=["params"], writes=["dp"])
        S.op("dve", lambda e: e.tensor_scalar(out=dp[:, 3 * NLC:4 * NLC],
                                              in0=self.params[:, PC.off["lbx"]:PC.off["lbx"] + NLC],
                                              scalar1=0.5, scalar2=None, op0=ALU.mult), reads=["params"], writes=["dp"])
        wsrc = self.d_lwin[0].rearrange("(c p) f -> p c f", p=128)

        def load_win(c):
            bi = c % 2
            S.op("pool", lambda e: e.dma_start(out=win[bi][:, :, 0:128], in_=wsrc[:, :, c * 128:(c + 1) * 128]),
                 writes=[wit(bi, 0)], dma=True)
            S.op("pool", lambda e: e.dma_start(out=win[bi][:, :, 128:256],
                                               in_=wsrc[:, :, DR + c * 128:DR + (c + 1) * 128]),
                 writes=[wit(bi, 1)], dma=True)

        def load_wout(c0):
            S.op("pool", lambda e: e.dma_start(
                out=wout[:], in_=self.d_lwout[0, c0 * 128:(c0 + CG) * 128, :].rearrange("(k p) d -> p k d", p=128)),
                writes=["wout"], dma=True)

        pairs = [(c, hf) for c in range(NLC) for hf in range(2)]

        def units(p):
            c, hf = pairs[p]
            return [(c, 2 * hf + j, (2 * p + j) % 4) for j in range(2)]

        def Wp(p):
            for (c, tt, s_) in units(p):
                bi = c % 2
                ts = slice(tt * 512, (tt + 1) * 512)
                for half in range(2):
                    for k in range(NC_):
                        S.op("pe", lambda e, k=k: e.matmul(
                            self.bank(2 * s_ + half), lhsT=win[bi][:, k, half * 128:(half + 1) * 128],
                            rhs=self.hT[:, k, ts], start=(k == 0), stop=(k == NC_ - 1)),
                            reads=[wit(bi, half), ("h", k, tt)], writes=[("ps", 2 * s_ + half)])

        def E1(p):
            for (c, tt, s_) in units(p):
                u = self.bank(2 * s_)
                wcol = [self.pcol("lcw", j * NLC + c) for j in range(4)]
                S.op("act", lambda e: e.activation(out=xc[s_][:], in_=u, func=AF.Identity, scale=wcol[3],
                                                   bias=self.pcol("lcb", c)),
                     reads=[("ps", 2 * s_), "params"], writes=[("xc", s_)])
                S.op("act", lambda e: e.activation(out=us[s_][:, 0:3], in_=u[:, 509:512], func=AF.Identity),
                     reads=[("ps", 2 * s_)], writes=[("us", s_)])
                for k in (1, 2, 3):
                    S.op("dve", lambda e, k=k: e.scalar_tensor_tensor(
                        out=xc[s_][:, k:512], in0=u[:, 0:512 - k], scalar=wcol[3 - k], in1=xc[s_][:, k:512],
                        op0=ALU.mult, op1=ALU.add), reads=[("ps", 2 * s_), ("xc", s_), "params"], writes=[("xc", s_)])
                if tt > 0:
                    sp = (s_ - 1) % 4
                    for k in (1, 2, 3):
                        S.op("dve", lambda e, k=k: e.scalar_tensor_tensor(
                            out=xc[s_][:, 0:k], in0=us[sp][:, 3 - k:3], scalar=wcol[3 - k], in1=xc[s_][:, 0:k],
                            op0=ALU.mult, op1=ALU.add), reads=[("us", sp), ("xc", s_), "params"], writes=[("xc", s_)])
                S.op("pool", lambda e: e.tensor_copy(out=xcb[s_][:], in_=xc[s_][:]),
                     reads=[("xc", s_)], writes=[("xcb", s_)])
            for (c, tt, s_) in units(p):
                S.op("act", lambda e: e.activation(out=gy[s_][:], in_=self.bank(2 * s_ + 1), func=AF.Gelu_apprx_tanh),
                     reads=[("ps", 2 * s_ + 1)], writes=[("gy", s_)])

        def Gp(p):
            for (c, tt, s_) in units(p):
                for which, wt, wtok in ((0, wa, "wa"), (1, wx, "wx")):
                    S.op("pe", lambda e, which=which, wt=wt: e.matmul(
                        self.bank(2 * s_ + which), lhsT=wt[:, c, :], rhs=xcb[s_][:], start=True, stop=True),
                        reads=[wtok, ("xcb", s_)], writes=[("ps", 2 * s_ + which)])

        def E2(p):
            un = units(p)
            for (c, tt, s_) in un:
                hba = dp[:, 2 * NLC + c:2 * NLC + c + 1]
                hbx = dp[:, 3 * NLC + c:3 * NLC + c + 1]
                S.op("act", lambda e: e.activation(out=A[s_][:], in_=self.bank(2 * s_), func=AF.Tanh, scale=0.5, bias=hba),
                     reads=[("ps", 2 * s_), "dp"], writes=[("A", s_)])
                S.op("act", lambda e: e.activation(out=I[s_][:], in_=self.bank(2 * s_ + 1), func=AF.Tanh, scale=0.5, bias=hbx),
                     reads=[("ps", 2 * s_ + 1), "dp"], writes=[("I", s_)])
            for (c, tt, s_) in un:
                hcl = dp[:, NLC + c:NLC + c + 1]
                S.op("act", lambda e: e.activation(out=A[s_][:], in_=A[s_][:], func=AF.Exp, scale=hcl, bias=hcl),
                     reads=[("A", s_), "dp"], writes=[("A", s_)])
            for (c, tt, s_) in un:
                S.op("act", lambda e: e.activation(out=M[s_][:], in_=A[s_][:], func=AF.Square),
                     reads=[("A", s_)], writes=[("M", s_)])
            for (c, tt, s_) in un:
                S.op("act", lambda e: e.activation(out=M[s_][:], in_=M[s_][:], func=AF.Sqrt, scale=-1.0, bias=1.0),
                     reads=[("M", s_)], writes=[("M", s_)])
            for (c, tt, s_) in un:
                S.op("dve", lambda e: e.scalar_tensor_tensor(out=I[s_][:], in0=I[s_][:], scalar=1.0, in1=xc[s_][:],
                                                             op0=ALU.add, op1=ALU.mult),
                     reads=[("I", s_), ("xc", s_)], writes=[("I", s_)])
                S.op("dve", lambda e: e.scalar_tensor_tensor(out=I[s_][:], in0=I[s_][:], scalar=0.5, in1=M[s_][:],
                                                             op0=ALU.mult, op1=ALU.mult),
                     reads=[("I", s_), ("M", s_)], writes=[("I", s_)])
                if tt > 0:
                    sp = (s_ - 1) % 4
                    init, itok = M[sp][:, 511:512], [("M", sp)]
                else:
                    init, itok = 0.0, []
                S.op("dve", lambda e, init=init: e.tensor_tensor_scan(
                    out=M[s_][:], data0=A[s_][:], data1=I[s_][:], initial=init, op0=ALU.mult, op1=ALU.add),
                    reads=[("A", s_), ("I", s_), ("M", s_)] + itok, writes=[("M", s_)])
                slot = c % NSLOT
                S.op("pool", lambda e, slot=slot, tt=tt: e.tensor_tensor(
                    out=hy[:, slot, tt * 512:(tt + 1) * 512], in0=M[s_][:], in1=gy[s_][:], op=ALU.mult),
                    reads=[("M", s_), ("gy", s_)], writes=[("hy", slot, tt)])

        def outproj(c0, final):
            self.resid_proj(wout, CG, lambda k, tt: hy[:, (c0 + k) % NSLOT, tt * 512:(tt + 1) * 512],
                            lambda k, tt: [("hy", (c0 + k) % NSLOT, tt)], "wout", final=final)

        load_wout(0)
        npair = len(pairs)
        pend = {}
        for idx in range(npair + 1):
            if idx < npair:
                c, hf = pairs[idx]
                if hf == 0 and c + 1 < NLC:
                    load_win(c + 1)
                Wp(idx)
                E1(idx)
            if idx >= 1:
                Gp(idx - 1)
                E2(idx - 1)
                c, hf = pairs[idx - 1]
                if hf == 1 and c == CG - 1:
                    pend[idx + 1] = "out0"
                    pend[idx + 5] = "wout1"
            act = pend.get(idx)
            if act == "out0":
                outproj(0, False)
            elif act == "wout1":
                load_wout(CG)
        outproj(CG, True)

    def pool(self, st, layer):
        nc, S = self.nc, self.S
        rstd_all = self.sb(st, "rstd_all", [128, S_], F32)
        hf = [self.sb(st, "hf%d" % i, [128, S_], F32) for i in range(2)]
        B = [self.sb(st, "pB%d" % i, [128, S_], F32) for i in range(4)]
        t16 = [self.sb(st, "t16_%d" % i, [128, 16], F32) for i in range(2)]
        pw = self.sb(st, "pw", [128, 4, 2, 256], BF16)
        S.op("pool", lambda e: e.dma_start(out=pw[:], in_=self.d_pw[0].rearrange("g (k p) d -> p g k d", p=128)),
             writes=["pw"], dma=True)
        for tt in range(4):
            self.rstd_tt(tt, rstd_all[:, tt * 512:(tt + 1) * 512], ("rstd_all", tt))
        xall = lambda c: [("x", c, tt) for tt in range(4)]
        hall = lambda c: [("h", c, tt) for tt in range(4)]
        for c in range(NC_):
            g = c // 2
            k2 = c % 2
            hfc = hf[k2]
            S.op("dve", lambda e, c=c, hfc=hfc: e.scalar_tensor_tensor(
                out=hfc[:], in0=self.xT[:, c, :], scalar=self.pcol("nmix", layer * NC_ + c), in1=rstd_all[:],
                op0=ALU.mult, op1=ALU.mult),
                reads=xall(c) + [("rstd_all", tt) for tt in range(4)] + ["params"], writes=[("hf", k2)])
            cur, curtoks = hfc, [("hf", k2)]
            for k in range(g + 1):
                d = 2 ** k
                bi = k2 * 2 + k % 2
                nxt, nxttok = B[bi], ("pB", bi)
                eng = "pool" if k % 2 == 0 else "dve"
                S.op(eng, lambda e, cur=cur, nxt=nxt, d=d: e.tensor_tensor(
                    out=nxt[:, d:S_], in0=cur[:, d:S_], in1=cur[:, 0:S_ - d], op=ALU.add),
                    reads=list(curtoks), writes=[nxttok])
                S.op("act", lambda e, cur=cur, nxt=nxt, d=d: e.activation(out=nxt[:, 0:d], in_=cur[:, 0:d], func=AF.Identity),
                     reads=list(curtoks), writes=[(nxttok, "head")])
                cur, curtoks = nxt, [nxttok, (nxttok, "head")]
            w = 2 ** (g + 1)
            icnt = self.cf[:, CF.off["icnt"] + g * 16:CF.off["icnt"] + (g + 1) * 16]
            S.op("dve", lambda e, c=c, cur=cur, hfc=hfc, w=w: e.scalar_tensor_tensor(
                out=self.hT[:, c, :], in0=cur[:], scalar=1.0 / w, in1=hfc[:], op0=ALU.mult, op1=ALU.subtract),
                reads=curtoks + [("hf", k2)], writes=hall(c))
            S.op("dve", lambda e, cur=cur, k2=k2, icnt=icnt: e.tensor_tensor(
                out=t16[k2][:], in0=cur[:, 0:16], in1=icnt, op=ALU.mult),
                reads=curtoks + ["cf"], writes=[("t16", k2)])
            S.op("dve", lambda e, c=c, k2=k2, hfc=hfc: e.tensor_tensor(
                out=self.hT[:, c, 0:16], in0=t16[k2][:], in1=hfc[:, 0:16], op=ALU.subtract),
                reads=[("t16", k2), ("hf", k2)] + hall(c), writes=hall(c))
        for tt in range(4):
            for g in range(4):
                for m2 in range(2):
                    m = 2 * g + m2
                    b = self.psrot % 8
                    self.psrot += 1
                    for kk in range(2):
                        S.op("pe", lambda e, b=b, g=g, kk=kk, m2=m2, tt=tt: e.matmul(
                            self.bank(b), lhsT=pw[:, g, kk, m2 * 128:(m2 + 1) * 128],
                            rhs=self.hT[:, 2 * g + kk, tt * 512:(tt + 1) * 512], start=(kk == 0), stop=(kk == 1)),
                            reads=["pw", ("h", 2 * g + kk, tt)], writes=[("ps", b)])
                    xs = self.xT[:, m, tt * 512:(tt + 1) * 512]
                    S.op("dve", lambda e, b=b, xs=xs, m=m: e.scalar_tensor_tensor(
                        out=xs, in0=self.bank(b), scalar=self.pcol("pscale", m), in1=xs, op0=ALU.mult, op1=ALU.add),
                        reads=[("ps", b), ("x", m, tt), "params"], writes=[("x", m, tt)])
            self.tail_hook(tt)


ALL_PHASES = []
for _l in range(DEPTH):
    ALL_PHASES += [("mix", _l), ("ffn", _l)]

WEIGHT_KEYS = ["ffn_w_up", "ffn_w_down", "attn_w_qkv", "attn_w_o", "lru_w_in", "lru_w_a", "lru_w_x",
               "lru_w_out", "pool_w"]


def run_phases(inputs, phases, x_cores=None, trace=False, **bkw):
    nc = Builder(phases, **bkw).build()
    params = pack_params(inputs)
    c16, cf = make_consts()
    if x_cores is None:
        x = np.asarray(inputs["x"], np.float32)
        x_cores = [np.ascontiguousarray(x[b].T) for b in range(8)]
    shared = {k: np.ascontiguousarray(np.asarray(inputs[k], np.float32)) for k in WEIGHT_KEYS}
    shared.update({"params": params, "c16": c16, "cf": cf})
    in_maps = []
    for b in range(8):
        m = dict(shared)
        m["xT"] = x_cores[b]
        in_maps.append(m)
    res = run_bass_kernel_spmd(nc, in_maps, core_ids=list(range(8)), trace=trace)
    return [r["outT"] for r in res.results], res


def kernel(**inputs):
    outs, _ = run_phases(inputs, ALL_PHASES)
    return np.stack([np.ascontiguousarray(o.T) for o in outs], axis=0).astype(np.float32)
```

```python
import contextlib
import numpy as np
import concourse.bass as bass
import concourse.mybir as mybir
from concourse.bass_utils import run_bass_kernel_spmd

F32 = mybir.dt.float32
BF16 = mybir.dt.bfloat16
AF = mybir.ActivationFunctionType
ALU = mybir.AluOpType
AX = mybir.AxisListType

D = 1024
S_ = 2048
NC_ = 8
DEPTH = 4
FH = 2816
NPAIR = 22
DR = 1280
NLC = 10
EPS = 1e-6
NEG = -30000.0
ENGS = ("pe", "act", "dve", "pool", "sp")


class Op:
    __slots__ = ("eng", "fn", "deps", "dma", "signal", "semv")

    def __init__(self, eng, fn, dma):
        self.eng = eng
        self.fn = fn
        self.deps = []
        self.dma = dma
        self.signal = False
        self.semv = None


class _Rec:
    def __init__(self):
        self.calls = []

    def __getattr__(self, name):
        def f(*args, **kwargs):
            self.calls.append((name, args, kwargs))
        return f


class Sched:
    N_DMA_SEMS = 8

    def __init__(self, nc):
        self.nc = nc
        self.ops = {e: [] for e in ENGS}
        self.last_writer = {}
        self.readers = {}
        self.fence_pending = set()
        self.fence_ops = []

    def fence(self):
        self.fence_ops = [self.ops[e][-1] for e in ENGS if self.ops[e]]
        self.fence_pending = set(ENGS)

    def op(self, eng, fn, reads=(), writes=(), dma=False):
        rec = _Rec()
        fn(rec)
        name, args, kwargs = rec.calls[0]
        o = Op(eng, lambda e: getattr(e, name)(*args, **kwargs), dma)
        cand = []
        for t in reads:
            w = self.last_writer.get(t)
            if w is not None:
                cand.append((w, True))
        for t in writes:
            w = self.last_writer.get(t)
            if w is not None:
                cand.append((w, False))
            for r in self.readers.get(t, ()):
                cand.append((r, False))
        seen = set()
        for d, raw in cand:
            if d is o or id(d) in seen:
                continue
            if d.eng == eng and not d.dma and not dma:
                if eng == "pe" or not raw:
                    continue
            seen.add(id(d))
            o.deps.append(d)
        if eng in self.fence_pending:
            self.fence_pending.discard(eng)
            for d in self.fence_ops:
                if id(d) in seen or (d.eng == eng and not d.dma and not dma):
                    continue
                seen.add(id(d))
                o.deps.append(d)
        self.ops[eng].append(o)
        for t in writes:
            self.last_writer[t] = o
            self.readers[t] = []
        for t in reads:
            self.readers.setdefault(t, []).append(o)
        return o

    def emit(self, final_waits=()):
        nc = self.nc
        for e in ENGS:
            for o in self.ops[e]:
                for d in o.deps:
                    d.signal = True
        for o in final_waits:
            o.signal = True
        with contextlib.ExitStack() as st:
            sems = {e: st.enter_context(nc.semaphore("s_" + e)) for e in ENGS}
            for e in ("sp", "pool", "act"):
                for k in range(self.N_DMA_SEMS):
                    sems[(e, k)] = st.enter_context(nc.semaphore("d_%s%d" % (e, k)))
            for e in ENGS:
                c = 0
                dcount = [0] * self.N_DMA_SEMS
                nd = 0
                for o in self.ops[e]:
                    if o.dma:
                        k = nd % self.N_DMA_SEMS
                        nd += 1
                        dcount[k] += 1
                        o.semv = ((e, k), 16 * dcount[k])
                    elif o.signal:
                        c += 1
                        o.semv = (e, c)
            block = st.enter_context(nc.Block())
            engobj = {"pe": block.tensor, "act": block.scalar, "dve": block.vector,
                      "pool": block.gpsimd, "sp": block.sync}

            def make(e):
                def body(eng):
                    known = {}
                    for o in self.ops[e]:
                        waits = {}
                        for d in o.deps:
                            sk, v = d.semv
                            if known.get(sk, 0) >= v:
                                continue
                            if waits.get(sk, 0) < v:
                                waits[sk] = v
                        if o.dma:
                            sk, v = o.semv
                            if v > 16 and known.get(sk, 0) < v - 16 and waits.get(sk, 0) < v - 16:
                                waits[sk] = v - 16
                        for sk, v in waits.items():
                            eng.wait_ge(sems[sk], v)
                            known[sk] = v
                        ins = o.fn(eng)
                        if o.semv is not None:
                            ins.then_inc(sems[o.semv[0]], 16 if o.dma else 1)
                    if e == "sp":
                        for o in final_waits:
                            eng.wait_ge(sems[o.semv[0]], o.semv[1])
                return body

            for e in ENGS:
                engobj[e](make(e))


class Cols:
    def __init__(self):
        self.n = 0
        self.off = {}

    def add(self, name, k):
        self.off[name] = self.n
        self.n += k


PC = Cols()
PC.add("nmix", DEPTH * NC_)
PC.add("nffn", DEPTH * NC_)
PC.add("qg", 2)
PC.add("kg", 2)
PC.add("lcw", 4 * NLC)
PC.add("lcb", NLC)
PC.add("lba", NLC)
PC.add("lbx", NLC)
PC.add("llam", NLC)
PC.add("pscale", NC_)
PC.add("fcw", DEPTH * 3 * 44)
PC.add("fcb", DEPTH * 44)

CC = Cols()
CC.add("ident", 128)
CC.add("ones", 128)
CC.add("cb", 2 * 256)
CC.add("esel", 8 * 128)
NCB = CC.n
CF = Cols()
CF.add("pastmask", 64)
CF.add("icnt", 4 * 16)
CF.add("eps", 1)


def pack_params(inp):
    P = np.zeros((128, PC.n), np.float32)

    def put(name, arr):
        a = np.asarray(arr, np.float32)
        a = a.reshape(-1, a.shape[-1] // 128, 128)
        a = a.transpose(2, 0, 1).reshape(128, -1)
        P[:, PC.off[name]:PC.off[name] + a.shape[1]] = a

    put("nmix", inp["norm_mix_g"])
    put("nffn", inp["norm_ffn_g"])
    put("qg", inp["attn_q_g"])
    put("kg", inp["attn_k_g"])
    put("lcw", inp["lru_conv_w"][0])
    put("lcb", inp["lru_conv_b"][0])
    put("lba", inp["lru_b_a"][0])
    put("lbx", inp["lru_b_x"][0])
    put("llam", inp["lru_lambda"][0])
    put("pscale", inp["pool_scale"][0])
    put("fcw", inp["ffn_conv_w"])
    put("fcb", inp["ffn_conv_b"])
    return P


def make_consts():
    cb16 = np.zeros((128, CC.n), np.float32)
    cb16[:, CC.off["ident"]:CC.off["ident"] + 128] = np.eye(128, dtype=np.float32)
    cb16[:, CC.off["ones"]:CC.off["ones"] + 128] = 1.0
    p = np.arange(128)[:, None]
    q = np.arange(256)[None, :]
    for j in range(2):
        m = np.where(j * 128 + p <= q, 0.0, NEG).astype(np.float32)
        cb16[:, CC.off["cb"] + j * 256:CC.off["cb"] + (j + 1) * 256] = m
    es = np.zeros((128, 8, 128), np.float32)
    for n in range(8):
        es[n, n, :] = 1.0
    cb16[:, CC.off["esel"]:CC.off["esel"] + 1024] = es.reshape(128, 1024)
    cf = np.zeros((128, CF.n), np.float32)
    pm = np.zeros((8, 8), np.float32)
    for i in range(8):
        for n in range(8):
            pm[i, n] = 0.0 if n < 4 + i // 2 else -1e30
    cf[:, CF.off["pastmask"]:CF.off["pastmask"] + 64] = pm.reshape(1, 64)
    ic = np.zeros((4, 16), np.float32)
    for g, w in enumerate((2, 4, 8, 16)):
        for t in range(16):
            ic[g, t] = 1.0 / min(t + 1, w)
    cf[:, CF.off["icnt"]:CF.off["icnt"] + 64] = ic.reshape(1, 64)
    cf[:, CF.off["eps"]] = EPS
    return cb16, cf


class Builder:
    def __init__(self, phases, attn_heads=8, attn_hg=4):
        self.phases = phases
        self.attn_heads = attn_heads
        self.attn_hg = attn_hg
        nc = bass.Bass("TRN2", target_bir_lowering=False)
        self.nc = nc
        dt = nc.dram_tensor
        self.d_x = dt("xT", [D, S_], F32, kind="ExternalInput").ap()
        self.d_out = dt("outT", [D, S_], F32, kind="ExternalOutput").ap()
        self.d_params = dt("params", [128, PC.n], F32, kind="ExternalInput").ap()
        self.d_c16 = dt("c16", [128, CC.n], F32, kind="ExternalInput").ap()
        self.d_cf = dt("cf", [128, CF.n], F32, kind="ExternalInput").ap()
        self.d_wup = dt("ffn_w_up", [DEPTH, D, 2 * FH], F32, kind="ExternalInput").ap()
        self.d_wdn = dt("ffn_w_down", [DEPTH, FH, D], F32, kind="ExternalInput").ap()
        self.d_wqkv = dt("attn_w_qkv", [2, D, 3 * D], F32, kind="ExternalInput").ap()
        self.d_wo = dt("attn_w_o", [2, D, D], F32, kind="ExternalInput").ap()
        self.d_lwin = dt("lru_w_in", [1, D, 2 * DR], F32, kind="ExternalInput").ap()
        self.d_lwa = dt("lru_w_a", [1, NLC, 128, 128], F32, kind="ExternalInput").ap()
        self.d_lwx = dt("lru_w_x", [1, NLC, 128, 128], F32, kind="ExternalInput").ap()
        self.d_lwout = dt("lru_w_out", [1, DR, D], F32, kind="ExternalInput").ap()
        self.d_pw = dt("pool_w", [1, 4, 256, 256], F32, kind="ExternalInput").ap()
        self.S = Sched(nc)
        self.psrot = 0
        self.uid = 0
        self.out_dmas = []

    def sb(self, st, name, shape, dtype):
        self.uid += 1
        return st.enter_context(self.nc.sbuf_tensor("%s_u%d" % (name, self.uid), shape, dtype))

    def pcol(self, name, idx):
        o = PC.off[name] + idx
        return self.params[:, o:o + 1]

    def build(self):
        nc, S = self.nc, self.S
        with contextlib.ExitStack() as st:
            self.xT = self.sb(st, "xT_sb", [128, NC_, S_], F32)
            self.hT = self.sb(st, "hT_sb", [128, NC_, S_], BF16)
            self.params = self.sb(st, "params_sb", [128, PC.n], F32)
            self.c16 = self.sb(st, "c16_sb", [128, CC.n], BF16)
            self.cf = self.sb(st, "cf_sb", [128, CF.n], F32)
            self.sq = self.sb(st, "sq_sb", [128, NC_, 512], BF16)
            self.lnv = self.sb(st, "lnv_sb", [128, 512], F32)
            self.rstd = self.sb(st, "rstd_sb", [128, 512], F32)
            self.PS = st.enter_context(nc.psum_tensor("PS", [128, 4096], F32))
            self.ident = self.c16[:, CC.off["ident"]:CC.off["ident"] + 128]
            self.ones = self.c16[:, CC.off["ones"]:CC.off["ones"] + 128]
            self.eps = self.cf[:, CF.off["eps"]:CF.off["eps"] + 1]

            S.op("sp", lambda e: e.dma_start(out=self.params[:], in_=self.d_params[:, :]),
                 writes=["params"], dma=True)
            S.op("sp", lambda e: e.dma_start(out=self.cf[:], in_=self.d_cf[:, :]),
                 writes=["cf"], dma=True)
            S.op("pool", lambda e: e.dma_start(out=self.c16[:], in_=self.d_c16[:, :]),
                 writes=["c16"], dma=True)
            for tt in range(4):
                for c in range(NC_):
                    S.op("sp", lambda e, c=c, tt=tt: e.dma_start(
                        out=self.xT[:, c, tt * 512:(tt + 1) * 512],
                        in_=self.d_x[c * 128:(c + 1) * 128, tt * 512:(tt + 1) * 512]),
                        writes=[("x", c, tt)], dma=True)

            self.wfirst = self.sb(st, "wfirst", [128, NC_, 512], BF16)
            nph = len(self.phases)
            self.prefetch_first(self.phases[0])
            self.start_norm(self.phases[0])
            for i, ph in enumerate(self.phases):
                kind, layer = ph
                self.next_phase = self.phases[i + 1] if i + 1 < nph else None
                S.fence()
                with contextlib.ExitStack() as st2:
                    if kind == "ffn":
                        self.ffn(st2, layer)
                    else:
                        mk = layer % 3
                        if mk == 0:
                            self.attn(st2, layer)
                        elif mk == 1:
                            self.lru(st2, layer)
                        else:
                            self.pool(st2, layer)

            S.emit(final_waits=self.out_dmas)
        return nc

    def bank(self, b):
        return self.PS[:, b * 512:(b + 1) * 512]

    def normA(self, tt):
        ts = slice(tt * 512, (tt + 1) * 512)
        self.S.op("act", lambda e: e.activation(out=self.sq[:], in_=self.xT[:, :, ts], func=AF.Square),
                  reads=[("x", c, tt) for c in range(NC_)], writes=["sq"])

    def normB(self, gname, layer, tt):
        S = self.S
        ts = slice(tt * 512, (tt + 1) * 512)
        b = self.psrot % 8
        self.psrot += 1
        for c in range(NC_):
            S.op("pe", lambda e, c=c: e.matmul(self.bank(b), lhsT=self.ones, rhs=self.sq[:, c, :],
                                               start=(c == 0), stop=(c == NC_ - 1)),
                 reads=["sq", "c16"], writes=[("ps", b)])
        S.op("act", lambda e: e.activation(out=self.lnv[:], in_=self.bank(b), func=AF.Ln,
                                           scale=1.0 / D, bias=self.eps),
             reads=[("ps", b), "cf"], writes=["lnv"])
        S.op("act", lambda e: e.activation(out=self.rstd[:], in_=self.lnv[:], func=AF.Exp, scale=-0.5),
             reads=["lnv"], writes=["rstd"])
        for c in range(NC_):
            S.op("dve", lambda e, c=c: e.scalar_tensor_tensor(
                out=self.hT[:, c, ts], in0=self.xT[:, c, ts], scalar=self.pcol(gname, layer * NC_ + c),
                in1=self.rstd[:], op0=ALU.mult, op1=ALU.mult),
                reads=[("x", c, tt), "rstd", "params"], writes=[("h", c, tt)])

    @staticmethod
    def norm_of(ph):
        if ph is None:
            return None
        kind, layer = ph
        if kind == "ffn":
            return ("nffn", layer)
        if layer % 3 == 2:
            return None
        return ("nmix", layer)

    def start_norm(self, ph):
        nm = self.norm_of(ph)
        if nm is None:
            return
        for tt in range(4):
            self.normA(tt)
            self.normB(nm[0], nm[1], tt)

    def tail_hook(self, tt):
        nm = self.norm_of(self.next_phase)
        if self.next_phase is None:
            for c in range(NC_):
                self.out_dmas.append(self.S.op("sp", lambda e, c=c: e.dma_start(
                    out=self.d_out[c * 128:(c + 1) * 128, tt * 512:(tt + 1) * 512],
                    in_=self.xT[:, c, tt * 512:(tt + 1) * 512]),
                    reads=[("x", c, tt)], dma=True))
            return
        if tt == 0:
            self.prefetch_first(self.next_phase)
        if nm is None:
            return
        if tt >= 1:
            self.normB(nm[0], nm[1], tt - 1)
        self.normA(tt)
        if tt == 3:
            self.normB(nm[0], nm[1], 3)

    def prefetch_first(self, ph):
        S = self.S
        kind, layer = ph
        wf = self.wfirst
        if kind == "ffn":
            src = self.d_wup[layer].rearrange("(c p) f -> p c f", p=128)
            S.op("pool", lambda e: e.dma_start(out=wf[:, :, 0:256], in_=src[:, :, 0:256]),
                 writes=[("wf", 0), ("wf", 1)], dma=True)
            S.op("pool", lambda e: e.dma_start(out=wf[:, :, 256:512], in_=src[:, :, FH:FH + 256]),
                 writes=[("wf", 2), ("wf", 3)], dma=True)
        elif layer % 3 == 0:
            src = self.d_wqkv[layer // 3].rearrange("(c p) f -> p c f", p=128)
            for k in range(3):
                S.op("pool", lambda e, k=k: e.dma_start(out=wf[:, :, k * 128:(k + 1) * 128],
                                                        in_=src[:, :, k * D:k * D + 128]),
                     writes=[("wf", k)], dma=True)
        elif layer % 3 == 1:
            src = self.d_lwin[0].rearrange("(c p) f -> p c f", p=128)
            S.op("pool", lambda e: e.dma_start(out=wf[:, :, 0:128], in_=src[:, :, 0:128]),
                 writes=[("wf", 0)], dma=True)
            S.op("pool", lambda e: e.dma_start(out=wf[:, :, 128:256], in_=src[:, :, DR:DR + 128]),
                 writes=[("wf", 1)], dma=True)

    def ffn(self, st, layer):
        nc, S = self.nc, self.S
        GROUPS = [(0, 6), (6, 12), (12, 17), (17, 22)]
        NP = 7
        NW = 3
        wup = [self.wfirst] + [self.sb(st, "wup%d" % i, [128, NC_, 512], BF16) for i in range(1, NW)]
        wdn = self.sb(st, "wdn", [128, 6, D], BF16)
        Pb = self.sb(st, "Pb", [128, NP, S_], BF16)
        cbuf = [self.sb(st, "cbuf%d" % i, [128, S_], F32) for i in range(3)]
        wup_src = self.d_wup[layer].rearrange("(c p) f -> p c f", p=128)

        def wtok(bi, half):
            if bi == 0:
                return [("wf", 2 * half), ("wf", 2 * half + 1)]
            return [("wup", bi, half)]

        def load_slab(s):
            bi = s % NW
            S.op("pool", lambda e: e.dma_start(out=wup[bi][:, :, 0:256],
                                               in_=wup_src[:, :, s * 256:(s + 1) * 256]),
                 writes=wtok(bi, 0), dma=True)
            S.op("pool", lambda e: e.dma_start(out=wup[bi][:, :, 256:512],
                                               in_=wup_src[:, :, FH + s * 256:FH + (s + 1) * 256]),
                 writes=wtok(bi, 1), dma=True)

        def load_wdn(j0, j1):
            S.op("pool", lambda e: e.dma_start(
                out=wdn[:, 0:j1 - j0, :],
                in_=self.d_wdn[layer, j0 * 128:j1 * 128, :].rearrange("(j p) d -> p j d", p=128)),
                writes=["wdn"], dma=True)

        cbi = [0]

        def up(j):
            s, r = j // 2, j % 2
            bi = s % NW
            cg = None
            for half in range(2):
                fj = half * NPAIR + j
                base = half * 2048
                lcol = half * 256 + r * 128
                for tt in range(4):
                    for c in range(NC_):
                        S.op("pe", lambda e, c=c, tt=tt: e.matmul(
                            self.PS[:, base + tt * 512: base + (tt + 1) * 512],
                            lhsT=wup[bi][:, c, lcol:lcol + 128],
                            rhs=self.hT[:, c, tt * 512:(tt + 1) * 512],
                            start=(c == 0), stop=(c == NC_ - 1)),
                            reads=wtok(bi, half) + [("h", c, tt)], writes=[("ps", half * 4 + tt)])
                if half == 1 and r == 1 and s + NW < 11:
                    load_slab(s + NW)
                cb = cbuf[cbi[0] % 3]
                cbk = cbi[0] % 3
                cbn = [("cbuf", cbk, 0), ("cbuf", cbk, 1)]
                cbi[0] += 1
                w0 = self.pcol("fcw", (layer * 3 + 0) * 44 + fj)
                w1 = self.pcol("fcw", (layer * 3 + 1) * 44 + fj)
                w2 = self.pcol("fcw", (layer * 3 + 2) * 44 + fj)
                bb = self.pcol("fcb", layer * 44 + fj)
                u = self.PS[:, base:base + 2048]
                for hh in range(2):
                    lo, hi = hh * 1024, (hh + 1) * 1024
                    psr = [("ps", half * 4 + 2 * hh), ("ps", half * 4 + 2 * hh + 1)]
                    if hh == 1:
                        psr.append(("ps", half * 4 + 1))
                    ct = [("cbuf", cbk, hh)]
                    S.op("act", lambda e: e.activation(out=cb[:, lo:hi], in_=u[:, lo:hi], func=AF.Identity,
                                                       scale=w2, bias=bb),
                         reads=psr + ["params"], writes=ct)
                    for k, wk in ((1, w1), (2, w0)):
                        o0 = max(lo, k)
                        S.op("dve", lambda e, o0=o0, k=k, wk=wk: e.scalar_tensor_tensor(
                            out=cb[:, o0:hi], in0=u[:, o0 - k:hi - k], scalar=wk, in1=cb[:, o0:hi],
                            op0=ALU.mult, op1=ALU.add), reads=psr + ct + ["params"], writes=ct)
                if half == 0:
                    S.op("act", lambda e: e.activation(out=cb[:], in_=cb[:], func=AF.Silu),
                         reads=cbn, writes=cbn)
                    cg = (cb, cbn)
                else:
                    S.op("pool", lambda e: e.tensor_tensor(out=Pb[:, j % NP, :], in0=cg[0][:], in1=cb[:], op=ALU.mult),
                         reads=cbn + cg[1], writes=[("P", j % NP)])

        def down(j0, j1, nbanks=4, final=False):
            nj = j1 - j0
            for tt in range(4):
                for m in range(NC_):
                    b = self.psrot % nbanks
                    self.psrot += 1
                    for jj in range(nj):
                        slot = (j0 + jj) % NP
                        S.op("pe", lambda e, jj=jj, slot=slot: e.matmul(
                            self.bank(b), lhsT=wdn[:, jj, m * 128:(m + 1) * 128],
                            rhs=Pb[:, slot, tt * 512:(tt + 1) * 512],
                            start=(jj == 0), stop=(jj == nj - 1)),
                            reads=["wdn", ("P", slot)], writes=[("ps", b)])
                    xs = self.xT[:, m, tt * 512:(tt + 1) * 512]
                    S.op("dve", lambda e: e.tensor_tensor(out=xs, in0=self.bank(b), in1=xs, op=ALU.add),
                         reads=[("ps", b), ("x", m, tt)], writes=[("x", m, tt)])
                if final:
                    self.tail_hook(tt)

        for s0 in range(1, NW):
            load_slab(s0)
        load_wdn(*GROUPS[0])
        gidx = {}
        for gi, (j0, j1) in enumerate(GROUPS):
            for j in range(j0, j1):
                gidx[j] = gi
        for j in range(NPAIR):
            up(j)
            gi = gidx[j]
            j0, j1 = GROUPS[gi]
            if j == j0 and gi > 0:
                down(*GROUPS[gi - 1])
            if j == j0 + 2 and gi > 0:
                load_wdn(j0, j1)
        down(*GROUPS[-1], nbanks=8, final=True)

    def rstd_tt(self, tt, dst, dst_tok):
        S = self.S
        ts = slice(tt * 512, (tt + 1) * 512)
        b = self.psrot % 8
        self.psrot += 1
        S.op("act", lambda e: e.activation(out=self.sq[:], in_=self.xT[:, :, ts], func=AF.Square),
             reads=[("x", c, tt) for c in range(NC_)], writes=["sq"])
        for c in range(NC_):
            S.op("pe", lambda e, c=c: e.matmul(self.bank(b), lhsT=self.ones, rhs=self.sq[:, c, :],
                                               start=(c == 0), stop=(c == NC_ - 1)),
                 reads=["sq", "c16"], writes=[("ps", b)])
        S.op("act", lambda e: e.activation(out=self.lnv[:], in_=self.bank(b), func=AF.Ln,
                                           scale=1.0 / D, bias=self.eps),
             reads=[("ps", b), "cf"], writes=["lnv"])
        S.op("act", lambda e: e.activation(out=dst, in_=self.lnv[:], func=AF.Exp, scale=-0.5),
             reads=["lnv"], writes=[dst_tok])

    def resid_proj(self, w, nk, rhs_fn, rhs_toks, wtok, final=False):
        S = self.S
        for tt in range(4):
            for m in range(NC_):
                b = self.psrot % 8
                self.psrot += 1
                for k in range(nk):
                    S.op("pe", lambda e, b=b, k=k, m=m, tt=tt: e.matmul(
                        self.bank(b), lhsT=w[:, k, m * 128:(m + 1) * 128], rhs=rhs_fn(k, tt),
                        start=(k == 0), stop=(k == nk - 1)),
                        reads=[wtok] + rhs_toks(k, tt), writes=[("ps", b)])
                xs = self.xT[:, m, tt * 512:(tt + 1) * 512]
                S.op("dve", lambda e, b=b, xs=xs: e.tensor_tensor(
                    out=xs, in0=self.bank(b), in1=xs, op=ALU.add),
                    reads=[("ps", b), ("x", m, tt)], writes=[("x", m, tt)])
            if final:
                self.tail_hook(tt)

    def attn(self, st, layer):
        nc, S = self.nc, self.S
        slot = layer // 3
        HG = self.attn_hg
        NH = self.attn_heads
        wqkv = [self.wfirst, self.sb(st, "wqkv1", [128, NC_, 384], BF16)]

        def wqt(bi, k):
            return ("wf", k) if bi == 0 else ("wqkv", bi, k)

        qn = [self.sb(st, "qn%d" % i, [128, S_], BF16) for i in range(2)]
        kn = [self.sb(st, "kn%d" % i, [128, S_], BF16) for i in range(2)]
        Vt = [self.sb(st, "Vt%d" % i, [128, 16, 128], BF16) for i in range(2)]
        oT = self.sb(st, "oT", [128, HG, S_], BF16)
        wo = self.sb(st, "wo", [128, HG, D], BF16)
        rs = [self.sb(st, "rs%d" % i, [128, 512], F32) for i in range(2)]
        sq2 = [self.sb(st, "sq2_%d" % i, [128, 512], BF16) for i in range(2)]
        PT = [self.sb(st, "PT%d" % i, [128, 512], BF16) for i in range(4)]
        kmf = self.sb(st, "kmf", [128, 8], F32)
        kmb = self.sb(st, "kmb", [128, 8], BF16)
        g1 = self.sb(st, "g1", [128, 64], F32)
        top = self.sb(st, "top", [128, 64], F32)
        cmpf = self.sb(st, "cmpf", [128, 64], F32)
        btok = self.sb(st, "btok", [128, 8, 128], BF16)
        biasT = self.sb(st, "biasT", [128, 1024], BF16)
        rden = [self.sb(st, "rden%d" % i, [128, 256], F32) for i in range(2)]
        S.op("pool", lambda e: e.memset(btok[:], 0.0), writes=["btok"])
        S.op("pool", lambda e: e.memset(biasT[:], 0.0), writes=["biasT"])
        wsrc = self.d_wqkv[slot].rearrange("(c p) f -> p c f", p=128)
        scale = 128.0 ** -0.5
        pastmask = self.cf[:, CF.off["pastmask"]:CF.off["pastmask"] + 64]
        cbm = [self.c16[:, CC.off["cb"] + j * 256:CC.off["cb"] + (j + 1) * 256] for j in range(2)]
        esel = [self.c16[:, CC.off["esel"] + n * 128:CC.off["esel"] + (n + 1) * 128] for n in range(8)]
        cnt = {"s": 0, "p": 0, "k2": 0, "pt": 0}

        def sbank():
            b = 2 + cnt["s"] % 3
            cnt["s"] += 1
            return b


        def load_w(hd):
            bi = hd % 2
            for k in range(3):
                S.op("pool", lambda e, k=k: e.dma_start(
                    out=wqkv[bi][:, :, k * 128:(k + 1) * 128],
                    in_=wsrc[:, :, k * D + hd * 128:k * D + (hd + 1) * 128]),
                    writes=[wqt(bi, k)], dma=True)

        def prologue_units(hd):
            bi = hd % 2
            units = []
            items = [(which, tt) for which in (0, 1) for tt in range(4)]
            state = {}

            def A1(i):
                which, tt = items[i]
                ts = slice(tt * 512, (tt + 1) * 512)
                b = 5 + cnt["p"] % 3
                cnt["p"] += 1
                k2 = cnt["k2"] % 2
                cnt["k2"] += 1
                state[i] = (b, k2)
                for c in range(NC_):
                    S.op("pe", lambda e, c=c: e.matmul(
                        self.bank(b), lhsT=wqkv[bi][:, c, which * 128:(which + 1) * 128],
                        rhs=self.hT[:, c, ts], start=(c == 0), stop=(c == NC_ - 1)),
                        reads=[wqt(bi, which), ("h", c, tt)], writes=[("ps", b)])
                S.op("act", lambda e: e.activation(out=sq2[k2][:], in_=self.bank(b), func=AF.Square),
                     reads=[("ps", b)], writes=[("sq2", k2)])

            def A2(i):
                which, tt = items[i]
                ts = slice(tt * 512, (tt + 1) * 512)
                b, k2 = state[i]
                dst = (qn if which == 0 else kn)[bi]
                gname = "qg" if which == 0 else "kg"
                b7 = sbank()
                S.op("pe", lambda e: e.matmul(self.bank(b7), lhsT=self.ones, rhs=sq2[k2][:], start=True, stop=True),
                     reads=[("sq2", k2), "c16"], writes=[("ps", b7)])
                S.op("act", lambda e: e.activation(out=rs[k2][:], in_=self.bank(b7), func=AF.Ln,
                                                   scale=1.0 / 128, bias=self.eps),
                     reads=[("ps", b7), "cf"], writes=[("rs", k2)])
                S.op("act", lambda e: e.activation(out=rs[k2][:], in_=rs[k2][:], func=AF.Exp, scale=-0.5),
                     reads=[("rs", k2)], writes=[("rs", k2)])
                S.op("dve", lambda e: e.scalar_tensor_tensor(
                    out=dst[:, ts], in0=self.bank(b), scalar=self.pcol(gname, slot), in1=rs[k2][:],
                    op0=ALU.mult, op1=ALU.mult),
                    reads=[("ps", b), ("rs", k2), "params"], writes=[("qk", which, bi, tt)])

            units.append(lambda: A1(0))
            for i in range(1, 8):
                units.append(lambda i=i: A1(i))
                units.append(lambda i=i: A2(i - 1))
            units.append(lambda: A2(7))

            def Vunit(bq):
                b = 5 + cnt["p"] % 3
                cnt["p"] += 1
                for i4 in range(4):
                    i = bq * 4 + i4
                    for c in range(NC_):
                        S.op("pe", lambda e, i=i, i4=i4, c=c: e.matmul(
                            self.bank(b)[:, i4 * 128:(i4 + 1) * 128], lhsT=self.hT[:, c, i * 128:(i + 1) * 128],
                            rhs=wqkv[bi][:, c, 256:384], start=(c == 0), stop=(c == NC_ - 1)),
                            reads=[wqt(bi, 2), ("h", c, bq)], writes=[("ps", b)])
                S.op("act", lambda e: e.activation(
                    out=Vt[bi][:, bq * 4:(bq + 1) * 4, :], in_=self.bank(b).rearrange("p (i d) -> p i d", d=128),
                    func=AF.Identity), reads=[("ps", b)], writes=[("V", bi, bq)])

            for bq in range(4):
                units.append(lambda bq=bq: Vunit(bq))

            def Cunit():
                S.op("dve", lambda e: e.tensor_reduce(out=kmf[:], in_=kn[bi][:].rearrange("p (n k) -> p n k", k=256),
                                                      axis=AX.X, op=ALU.add),
                     reads=[("qk", 1, bi, tt) for tt in range(4)], writes=["kmf"])
                S.op("dve", lambda e: e.tensor_scalar(out=kmb[:], in0=kmf[:], scalar1=1.0 / 256, scalar2=None,
                                                      op0=ALU.mult), reads=["kmf"], writes=["kmb"])
                bg = sbank()
                for i in range(8):
                    S.op("pe", lambda e, i=i: e.matmul(
                        self.bank(bg)[:, i * 8:(i + 1) * 8], lhsT=qn[bi][:, (8 + i) * 128:(9 + i) * 128], rhs=kmb[:],
                        start=True, stop=True),
                        reads=["kmb", ("qk", 0, bi, 2 + i // 4)], writes=[("ps", bg)])
                S.op("dve", lambda e: e.tensor_tensor(out=g1[:], in0=self.bank(bg)[:, 0:64], in1=pastmask, op=ALU.add),
                     reads=[("ps", bg), "cf"], writes=["g1"])
                for i in range(8):
                    S.op("dve", lambda e, i=i: e.max(out=top[:, i * 8:(i + 1) * 8], in_=g1[:, i * 8:(i + 1) * 8]),
                         reads=["g1"], writes=["top"])
                S.op("dve", lambda e: e.tensor_tensor(
                    out=cmpf[:].rearrange("p (i n) -> p i n", n=8), in0=g1[:].rearrange("p (i n) -> p i n", n=8),
                    in1=top[:].rearrange("p (i n) -> p i n", n=8)[:, :, 2:3].to_broadcast([128, 8, 8]), op=ALU.is_lt),
                    reads=["g1", "top"], writes=["cmpf"])

            units.append(Cunit)
            return units

        def Dunit(hd):
            S.op("dve", lambda e: e.tensor_scalar(out=btok[:, :, 0:8], in0=cmpf[:].rearrange("p (i n) -> p i n", n=8),
                                                  scalar1=NEG, scalar2=None, op0=ALU.mult),
                 reads=["cmpf"], writes=["btok"])
            for k in range(2):
                bd_ = sbank()
                for i4 in range(4):
                    i = k * 4 + i4
                    S.op("pe", lambda e, i=i, i4=i4: e.matmul(
                        self.bank(bd_)[:, i4 * 128:(i4 + 1) * 128], lhsT=btok[:, i, :], rhs=self.ident,
                        start=True, stop=True),
                        reads=["btok", "c16"], writes=[("ps", bd_)])
                S.op("act", lambda e, k=k: e.activation(out=biasT[0:8, k * 512:(k + 1) * 512],
                                                        in_=self.bank(bd_)[0:8, :], func=AF.Identity),
                     reads=[("ps", bd_)], writes=["biasT"])

        def E1(hd, qb, n, st_):
            bi = hd % 2
            qs = slice(qb * 256, (qb + 1) * 256)
            bs_ = sbank()
            pk = cnt["pt"] % 4
            cnt["pt"] += 1
            st_[(qb, n)] = pk
            for half in range(2):
                kt = 2 * n + half
                osl = self.bank(bs_)[:, half * 256:(half + 1) * 256]
                extra = (n == qb) or (qb >= 4)
                S.op("pe", lambda e, osl=osl, kt=kt, extra=extra: e.matmul(
                    osl, lhsT=kn[bi][:, kt * 128:(kt + 1) * 128], rhs=qn[bi][:, qs],
                    start=True, stop=(not extra)),
                    reads=[("qk", 1, bi, kt // 4), ("qk", 0, bi, qb // 2)], writes=[("ps", bs_)])
                if n == qb:
                    S.op("pe", lambda e, osl=osl, half=half: e.matmul(
                        osl, lhsT=self.ident, rhs=cbm[half], start=False, stop=True),
                        reads=["c16"], writes=[("ps", bs_)])
                elif qb >= 4:
                    S.op("pe", lambda e, osl=osl: e.matmul(
                        osl, lhsT=esel[n], rhs=biasT[:, (qb - 4) * 256:(qb - 3) * 256],
                        start=False, stop=True),
                        reads=["c16", "biasT"], writes=[("ps", bs_)])
            S.op("act", lambda e: e.activation(out=PT[pk][:], in_=self.bank(bs_), func=AF.Exp, scale=scale),
                 reads=[("ps", bs_)], writes=[("PT", pk)])

        def E2(hd, qb, n, st_):
            bi = hd % 2
            qs = slice(qb * 256, (qb + 1) * 256)
            pk = st_[(qb, n)]
            bo = qb % 2
            for half in range(2):
                kt = 2 * n + half
                first = (n == 0 and half == 0)
                last = (n == qb and half == 1)
                S.op("pe", lambda e, kt=kt, half=half, first=first, last=last: e.matmul(
                    self.bank(bo)[:, 0:256], lhsT=Vt[bi][:, kt, :], rhs=PT[pk][:, half * 256:(half + 1) * 256],
                    start=first, stop=last, skip_group_check=True),
                    reads=[("V", bi, kt // 4), ("PT", pk)], writes=[("ps", bo)])
                S.op("pe", lambda e, half=half, last=last: e.matmul(
                    self.bank(bo)[:, 256:512], lhsT=self.ones, rhs=PT[pk][:, half * 256:(half + 1) * 256],
                    start=False, stop=last, skip_group_check=True),
                    reads=["c16", ("PT", pk)], writes=[("ps", bo)])
            if n == qb:
                rk = qb % 2
                S.op("dve", lambda e: e.reciprocal(out=rden[rk][:], in_=self.bank(bo)[:, 256:512]),
                     reads=[("ps", bo)], writes=[("rden", rk)])
                S.op("dve", lambda e: e.tensor_tensor(
                    out=oT[:, hd % HG, qs], in0=self.bank(bo)[:, 0:256], in1=rden[rk][:], op=ALU.mult),
                    reads=[("ps", bo), ("rden", rk)], writes=[("oT", hd % HG, qb // 2)])

        for u in prologue_units(0):
            u()
        LAG = 2
        for hd in range(NH):
            if hd + 1 < NH:
                load_w(hd + 1)
                nxt = prologue_units(hd + 1)
            else:
                nxt = []
            if hd % HG == 0:
                g0 = hd
                S.op("pool", lambda e, g0=g0: e.dma_start(
                    out=wo[:], in_=self.d_wo[slot, g0 * 128:(g0 + HG) * 128, :].rearrange("(h p) d -> p h d", p=128)),
                    writes=["wo"], dma=True)
            items = [(qb, n) for qb in range(8) for n in range(qb + 1)]
            st_ = {}
            for idx in range(len(items) + LAG):
                if idx == 4:
                    Dunit(hd)
                if idx < len(items):
                    E1(hd, items[idx][0], items[idx][1], st_)
                if idx >= LAG:
                    E2(hd, items[idx - LAG][0], items[idx - LAG][1], st_)
                if nxt and idx >= 6:
                    nxt.pop(0)()
            while nxt:
                nxt.pop(0)()
            if hd % HG == HG - 1:
                self.resid_proj(wo, HG, lambda k, tt: oT[:, k, tt * 512:(tt + 1) * 512],
                                lambda k, tt: [("oT", k, tt)], "wo", final=(hd == NH - 1))

    def lru(self, st, layer):
        nc, S = self.nc, self.S
        CG = 5
        NSLOT = 6
        win = [self.wfirst, self.sb(st, "win1", [128, NC_, 256], BF16)]

        def wit(bi, half):
            return ("wf", half) if bi == 0 else ("win", bi, half)

        wa = self.sb(st, "wa", [128, NLC, 128], BF16)
        wx = self.sb(st, "wx", [128, NLC, 128], BF16)
        wout = self.sb(st, "wout", [128, CG, D], BF16)
        hy = self.sb(st, "hy", [128, NSLOT, S_], BF16)
        xc = [self.sb(st, "xc%d" % i, [128, 512], F32) for i in range(4)]
        xcb = [self.sb(st, "xcb%d" % i, [128, 512], BF16) for i in range(4)]
        gy = [self.sb(st, "gy%d" % i, [128, 512], BF16) for i in range(4)]
        A = [self.sb(st, "lruA%d" % i, [128, 512], F32) for i in range(4)]
        I = [self.sb(st, "lruI%d" % i, [128, 512], F32) for i in range(4)]
        M = [self.sb(st, "lruM%d" % i, [128, 512], F32) for i in range(4)]
        us = [self.sb(st, "lruus%d" % i, [128, 4], F32) for i in range(4)]
        dp = self.sb(st, "lrudp", [128, 4 * NLC], F32)
        S.op("pool", lambda e: e.dma_start(out=wa[:], in_=self.d_lwa[0].rearrange("n c d -> c n d")),
             writes=["wa"], dma=True)
        S.op("pool", lambda e: e.dma_start(out=wx[:], in_=self.d_lwx[0].rearrange("n c d -> c n d")),
             writes=["wx"], dma=True)
        lam = self.params[:, PC.off["llam"]:PC.off["llam"] + NLC]
        S.op("act", lambda e: e.activation(out=dp[:, 0:NLC], in_=lam, func=AF.Exp, scale=-1.0),
             reads=["params"], writes=["dp0"])
        S.op("act", lambda e: e.activation(out=dp[:, 0:NLC], in_=dp[:, 0:NLC], func=AF.Ln, bias=1.0),
             reads=["dp0"], writes=["dp0"])
        S.op("dve", lambda e: e.tensor_scalar(out=dp[:, NLC:2 * NLC], in0=dp[:, 0:NLC], scalar1=-4.0, scalar2=None,
                                              op0=ALU.mult), reads=["dp0"], writes=["dp"])
        S.op("dve", lambda e: e.tensor_scalar(out=dp[:, 2 * NLC:3 * NLC],
                                              in0=self.params[:, PC.off["lba"]:PC.off["lba"] + NLC],
                                              scalar1=0.5, scalar2=None, op0=ALU.mult), reads=["params"], writes=["dp"])
        S.op("dve", lambda e: e.tensor_scalar(out=dp[:, 3 * NLC:4 * NLC],
                                              in0=self.params[:, PC.off["lbx"]:PC.off["lbx"] + NLC],
                                              scalar1=0.5, scalar2=None, op0=ALU.mult), reads=["params"], writes=["dp"])
        wsrc = self.d_lwin[0].rearrange("(c p) f -> p c f", p=128)

        def load_win(c):
            bi = c % 2
            S.op("pool", lambda e: e.dma_start(out=win[bi][:, :, 0:128], in_=wsrc[:, :, c * 128:(c + 1) * 128]),
                 writes=[wit(bi, 0)], dma=True)
            S.op("pool", lambda e: e.dma_start(out=win[bi][:, :, 128:256],
                                               in_=wsrc[:, :, DR + c * 128:DR + (c + 1) * 128]),
                 writes=[wit(bi, 1)], dma=True)

        def load_wout(c0):
            S.op("pool", lambda e: e.dma_start(
                out=wout[:], in_=self.d_lwout[0, c0 * 128:(c0 + CG) * 128, :].rearrange("(k p) d -> p k d", p=128)),
                writes=["wout"], dma=True)

        pairs = [(c, hf) for c in range(NLC) for hf in range(2)]

        def units(p):
            c, hf = pairs[p]
            return [(c, 2 * hf + j, (2 * p + j) % 4) for j in range(2)]

        def Wp(p):
            for (c, tt, s_) in units(p):
                bi = c % 2
                ts = slice(tt * 512, (tt + 1) * 512)
                for half in range(2):
                    for k in range(NC_):
                        S.op("pe", lambda e, k=k: e.matmul(
                            self.bank(2 * s_ + half), lhsT=win[bi][:, k, half * 128:(half + 1) * 128],
                            rhs=self.hT[:, k, ts], start=(k == 0), stop=(k == NC_ - 1)),
                            reads=[wit(bi, half), ("h", k, tt)], writes=[("ps", 2 * s_ + half)])

        def E1(p):
            for (c, tt, s_) in units(p):
                u = self.bank(2 * s_)
                wcol = [self.pcol("lcw", j * NLC + c) for j in range(4)]
                S.op("act", lambda e: e.activation(out=xc[s_][:], in_=u, func=AF.Identity, scale=wcol[3],
                                                   bias=self.pcol("lcb", c)),
                     reads=[("ps", 2 * s_), "params"], writes=[("xc", s_)])
                S.op("act", lambda e: e.activation(out=us[s_][:, 0:3], in_=u[:, 509:512], func=AF.Identity),
                     reads=[("ps", 2 * s_)], writes=[("us", s_)])
                for k in (1, 2, 3):
                    S.op("dve", lambda e, k=k: e.scalar_tensor_tensor(
                        out=xc[s_][:, k:512], in0=u[:, 0:512 - k], scalar=wcol[3 - k], in1=xc[s_][:, k:512],
                        op0=ALU.mult, op1=ALU.add), reads=[("ps", 2 * s_), ("xc", s_), "params"], writes=[("xc", s_)])
                if tt > 0:
                    sp = (s_ - 1) % 4
                    for k in (1, 2, 3):
                        S.op("dve", lambda e, k=k: e.scalar_tensor_tensor(
                            out=xc[s_][:, 0:k], in0=us[sp][:, 3 - k:3], scalar=wcol[3 - k], in1=xc[s_][:, 0:k],
                            op0=ALU.mult, op1=ALU.add), reads=[("us", sp), ("xc", s_), "params"], writes=[("xc", s_)])
                S.op("pool", lambda e: e.tensor_copy(out=xcb[s_][:], in_=xc[s_][:]),
                     reads=[("xc", s_)], writes=[("xcb", s_)])
            for (c, tt, s_) in units(p):
                S.op("act", lambda e: e.activation(out=gy[s_][:], in_=self.bank(2 * s_ + 1), func=AF.Gelu_apprx_tanh),
                     reads=[("ps", 2 * s_ + 1)], writes=[("gy", s_)])

        def Gp(p):
            for (c, tt, s_) in units(p):
                for which, wt, wtok in ((0, wa, "wa"), (1, wx, "wx")):
                    S.op("pe", lambda e, which=which, wt=wt: e.matmul(
                        self.bank(2 * s_ + which), lhsT=wt[:, c, :], rhs=xcb[s_][:], start=True, stop=True),
                        reads=[wtok, ("xcb", s_)], writes=[("ps", 2 * s_ + which)])

        def E2(p):
            un = units(p)
            for (c, tt, s_) in un:
                hba = dp[:, 2 * NLC + c:2 * NLC + c + 1]
                hbx = dp[:, 3 * NLC + c:3 * NLC + c + 1]
                S.op("act", lambda e: e.activation(out=A[s_][:], in_=self.bank(2 * s_), func=AF.Tanh, scale=0.5, bias=hba),
                     reads=[("ps", 2 * s_), "dp"], writes=[("A", s_)])
                S.op("act", lambda e: e.activation(out=I[s_][:], in_=self.bank(2 * s_ + 1), func=AF.Tanh, scale=0.5, bias=hbx),
                     reads=[("ps", 2 * s_ + 1), "dp"], writes=[("I", s_)])
            for (c, tt, s_) in un:
                hcl = dp[:, NLC + c:NLC + c + 1]
                S.op("act", lambda e: e.activation(out=A[s_][:], in_=A[s_][:], func=AF.Exp, scale=hcl, bias=hcl),
                     reads=[("A", s_), "dp"], writes=[("A", s_)])
            for (c, tt, s_) in un:
                S.op("act", lambda e: e.activation(out=M[s_][:], in_=A[s_][:], func=AF.Square),
                     reads=[("A", s_)], writes=[("M", s_)])
            for (c, tt, s_) in un:
                S.op("act", lambda e: e.activation(out=M[s_][:], in_=M[s_][:], func=AF.Sqrt, scale=-1.0, bias=1.0),
                     reads=[("M", s_)], writes=[("M", s_)])
            for (c, tt, s_) in un:
                S.op("dve", lambda e: e.scalar_tensor_tensor(out=I[s_][:], in0=I[s_][:], scalar=1.0, in1=xc[s_][:],
                                                             op0=ALU.add, op1=ALU.mult),
                     reads=[("I", s_), ("xc", s_)], writes=[("I", s_)])
                S.op("dve", lambda e: e.scalar_tensor_tensor(out=I[s_][:], in0=I[s_][:], scalar=0.5, in1=M[s_][:],
                                                             op0=ALU.mult, op1=ALU.mult),
                     reads=[("I", s_), ("M", s_)], writes=[("I", s_)])
                if tt > 0:
                    sp = (s_ - 1) % 4
                    init, itok = M[sp][:, 511:512], [("M", sp)]
                else:
                    init, itok = 0.0, []
                S.op("dve", lambda e, init=init: e.tensor_tensor_scan(
                    out=M[s_][:], data0=A[s_][:], data1=I[s_][:], initial=init, op0=ALU.mult, op1=ALU.add),
                    reads=[("A", s_), ("I", s_), ("M", s_)] + itok, writes=[("M", s_)])
                slot = c % NSLOT
                S.op("pool", lambda e, slot=slot, tt=tt: e.tensor_tensor(
                    out=hy[:, slot, tt * 512:(tt + 1) * 512], in0=M[s_][:], in1=gy[s_][:], op=ALU.mult),
                    reads=[("M", s_), ("gy", s_)], writes=[("hy", slot, tt)])

        def outproj(c0, final):
            self.resid_proj(wout, CG, lambda k, tt: hy[:, (c0 + k) % NSLOT, tt * 512:(tt + 1) * 512],
                            lambda k, tt: [("hy", (c0 + k) % NSLOT, tt)], "wout", final=final)

        load_wout(0)
        npair = len(pairs)
        pend = {}
        for idx in range(npair + 1):
            if idx < npair:
                c, hf = pairs[idx]
                if hf == 0 and c + 1 < NLC:
                    load_win(c + 1)
                Wp(idx)
                E1(idx)
            if idx >= 1:
                Gp(idx - 1)
                E2(idx - 1)
                c, hf = pairs[idx - 1]
                if hf == 1 and c == CG - 1:
                    pend[idx + 1] = "out0"
                    pend[idx + 5] = "wout1"
            act = pend.get(idx)
            if act == "out0":
                outproj(0, False)
            elif act == "wout1":
                load_wout(CG)
        outproj(CG, True)

    def pool(self, st, layer):
        nc, S = self.nc, self.S
        rstd_all = self.sb(st, "rstd_all", [128, S_], F32)
        hf = [self.sb(st, "hf%d" % i, [128, S_], F32) for i in range(2)]
        B = [self.sb(st, "pB%d" % i, [128, S_], F32) for i in range(4)]
        t16 = [self.sb(st, "t16_%d" % i, [128, 16], F32) for i in range(2)]
        pw = self.sb(st, "pw", [128, 4, 2, 256], BF16)
        S.op("pool", lambda e: e.dma_start(out=pw[:], in_=self.d_pw[0].rearrange("g (k p) d -> p g k d", p=128)),
             writes=["pw"], dma=True)
        for tt in range(4):
            self.rstd_tt(tt, rstd_all[:, tt * 512:(tt + 1) * 512], ("rstd_all", tt))
        xall = lambda c: [("x", c, tt) for tt in range(4)]
        hall = lambda c: [("h", c, tt) for tt in range(4)]
        for c in range(NC_):
            g = c // 2
            k2 = c % 2
            hfc = hf[k2]
            S.op("dve", lambda e, c=c, hfc=hfc: e.scalar_tensor_tensor(
                out=hfc[:], in0=self.xT[:, c, :], scalar=self.pcol("nmix", layer * NC_ + c), in1=rstd_all[:],
                op0=ALU.mult, op1=ALU.mult),
                reads=xall(c) + [("rstd_all", tt) for tt in range(4)] + ["params"], writes=[("hf", k2)])
            cur, curtoks = hfc, [("hf", k2)]
            for k in range(g + 1):
                d = 2 ** k
                bi = k2 * 2 + k % 2
                nxt, nxttok = B[bi], ("pB", bi)
                eng = "pool" if k % 2 == 0 else "dve"
                S.op(eng, lambda e, cur=cur, nxt=nxt, d=d: e.tensor_tensor(
                    out=nxt[:, d:S_], in0=cur[:, d:S_], in1=cur[:, 0:S_ - d], op=ALU.add),
                    reads=list(curtoks), writes=[nxttok])
                S.op("act", lambda e, cur=cur, nxt=nxt, d=d: e.activation(out=nxt[:, 0:d], in_=cur[:, 0:d], func=AF.Identity),
                     reads=list(curtoks), writes=[(nxttok, "head")])
                cur, curtoks = nxt, [nxttok, (nxttok, "head")]
            w = 2 ** (g + 1)
            icnt = self.cf[:, CF.off["icnt"] + g * 16:CF.off["icnt"] + (g + 1) * 16]
            S.op("dve", lambda e, c=c, cur=cur, hfc=hfc, w=w: e.scalar_tensor_tensor(
                out=self.hT[:, c, :], in0=cur[:], scalar=1.0 / w, in1=hfc[:], op0=ALU.mult, op1=ALU.subtract),
                reads=curtoks + [("hf", k2)], writes=hall(c))
            S.op("dve", lambda e, cur=cur, k2=k2, icnt=icnt: e.tensor_tensor(
                out=t16[k2][:], in0=cur[:, 0:16], in1=icnt, op=ALU.mult),
                reads=curtoks + ["cf"], writes=[("t16", k2)])
            S.op("dve", lambda e, c=c, k2=k2, hfc=hfc: e.tensor_tensor(
                out=self.hT[:, c, 0:16], in0=t16[k2][:], in1=hfc[:, 0:16], op=ALU.subtract),
                reads=[("t16", k2), ("hf", k2)] + hall(c), writes=hall(c))
        for tt in range(4):
            for g in range(4):
                for m2 in range(2):
                    m = 2 * g + m2
                    b = self.psrot % 8
                    self.psrot += 1
                    for kk in range(2):
                        S.op("pe", lambda e, b=b, g=g, kk=kk, m2=m2, tt=tt: e.matmul(
                            self.bank(b), lhsT=pw[:, g, kk, m2 * 128:(m2 + 1) * 128],
                            rhs=self.hT[:, 2 * g + kk, tt * 512:(tt + 1) * 512], start=(kk == 0), stop=(kk == 1)),
                            reads=["pw", ("h", 2 * g + kk, tt)], writes=[("ps", b)])
                    xs = self.xT[:, m, tt * 512:(tt + 1) * 512]
                    S.op("dve", lambda e, b=b, xs=xs, m=m: e.scalar_tensor_tensor(
                        out=xs, in0=self.bank(b), scalar=self.pcol("pscale", m), in1=xs, op0=ALU.mult, op1=ALU.add),
                        reads=[("ps", b), ("x", m, tt), "params"], writes=[("x", m, tt)])
            self.tail_hook(tt)


ALL_PHASES = []
for _l in range(DEPTH):
    ALL_PHASES += [("mix", _l), ("ffn", _l)]

WEIGHT_KEYS = ["ffn_w_up", "ffn_w_down", "attn_w_qkv", "attn_w_o", "lru_w_in", "lru_w_a", "lru_w_x",
               "lru_w_out", "pool_w"]


def run_phases(inputs, phases, x_cores=None, trace=False, **bkw):
    nc = Builder(phases, **bkw).build()
    params = pack_params(inputs)
    c16, cf = make_consts()
    if x_cores is None:
        x = np.asarray(inputs["x"], np.float32)
        x_cores = [np.ascontiguousarray(x[b].T) for b in range(8)]
    shared = {k: np.ascontiguousarray(np.asarray(inputs[k], np.float32)) for k in WEIGHT_KEYS}
    shared.update({"params": params, "c16": c16, "cf": cf})
    in_maps = []
    for b in range(8):
        m = dict(shared)
        m["xT"] = x_cores[b]
        in_maps.append(m)
    res = run_bass_kernel_spmd(nc, in_maps, core_ids=list(range(8)), trace=trace)
    return [r["outT"] for r in res.results], res


def kernel(**inputs):
    outs, _ = run_phases(inputs, ALL_PHASES)
    return np.stack([np.ascontiguousarray(o.T) for o in outs], axis=0).astype(np.float32)
```

```python
import contextlib
import numpy as np
import concourse.bass as bass
import concourse.mybir as mybir
from concourse.bass_utils import run_bass_kernel_spmd

F32 = mybir.dt.float32
BF16 = mybir.dt.bfloat16
AF = mybir.ActivationFunctionType
ALU = mybir.AluOpType
AX = mybir.AxisListType

D = 1024
S_ = 2048
NC_ = 8
DEPTH = 4
FH = 2816
NPAIR = 22
DR = 1280
NLC = 10
EPS = 1e-6
NEG = -30000.0
ENGS = ("pe", "act", "dve", "pool", "sp")


class Op:
    __slots__ = ("eng", "fn", "deps", "dma", "signal", "semv")

    def __init__(self, eng, fn, dma):
        self.eng = eng
        self.fn = fn
        self.deps = []
        self.dma = dma
        self.signal = False
        self.semv = None


class _Rec:
    def __init__(self):
        self.calls = []

    def __getattr__(self, name):
        def f(*args, **kwargs):
            self.calls.append((name, args, kwargs))
        return f


class Sched:
    N_DMA_SEMS = 8

    def __init__(self, nc):
        self.nc = nc
        self.ops = {e: [] for e in ENGS}
        self.last_writer = {}
        self.readers = {}
        self.fence_pending = set()
        self.fence_ops = []

    def fence(self):
        self.fence_ops = [self.ops[e][-1] for e in ENGS if self.ops[e]]
        self.fence_pending = set(ENGS)

    def op(self, eng, fn, reads=(), writes=(), dma=False):
        rec = _Rec()
        fn(rec)
        name, args, kwargs = rec.calls[0]
        o = Op(eng, lambda e: getattr(e, name)(*args, **kwargs), dma)
        cand = []
        for t in reads:
            w = self.last_writer.get(t)
            if w is not None:
                cand.append((w, True))
        for t in writes:
            w = self.last_writer.get(t)
            if w is not None:
                cand.append((w, False))
            for r in self.readers.get(t, ()):
                cand.append((r, False))
        seen = set()
        for d, raw in cand:
            if d is o or id(d) in seen:
                continue
            if d.eng == eng and not d.dma and not dma:
                if eng == "pe" or not raw:
                    continue
            seen.add(id(d))
            o.deps.append(d)
        if eng in self.fence_pending:
            self.fence_pending.discard(eng)
            for d in self.fence_ops:
                if id(d) in seen or (d.eng == eng and not d.dma and not dma):
                    continue
                seen.add(id(d))
                o.deps.append(d)
        self.ops[eng].append(o)
        for t in writes:
            self.last_writer[t] = o
            self.readers[t] = []
        for t in reads:
            self.readers.setdefault(t, []).append(o)
        return o

    def emit(self, final_waits=()):
        nc = self.nc
        for e in ENGS:
            for o in self.ops[e]:
                for d in o.deps:
                    d.signal = True
        for o in final_waits:
            o.signal = True
        with contextlib.ExitStack() as st:
            sems = {e: st.enter_context(nc.semaphore("s_" + e)) for e in ENGS}
            for e in ("sp", "pool", "act"):
                for k in range(self.N_DMA_SEMS):
                    sems[(e, k)] = st.enter_context(nc.semaphore("d_%s%d" % (e, k)))
            for e in ENGS:
                c = 0
                dcount = [0] * self.N_DMA_SEMS
                nd = 0
                for o in self.ops[e]:
                    if o.dma:
                        k = nd % self.N_DMA_SEMS
                        nd += 1
                        dcount[k] += 1
                        o.semv = ((e, k), 16 * dcount[k])
                    elif o.signal:
                        c += 1
                        o.semv = (e, c)
            block = st.enter_context(nc.Block())
            engobj = {"pe": block.tensor, "act": block.scalar, "dve": block.vector,
                      "pool": block.gpsimd, "sp": block.sync}

            def make(e):
                def body(eng):
                    known = {}
                    for o in self.ops[e]:
                        waits = {}
                        for d in o.deps:
                            sk, v = d.semv
                            if known.get(sk, 0) >= v:
                                continue
                            if waits.get(sk, 0) < v:
                                waits[sk] = v
                        if o.dma:
                            sk, v = o.semv
                            if v > 16 and known.get(sk, 0) < v - 16 and waits.get(sk, 0) < v - 16:
                                waits[sk] = v - 16
                        for sk, v in waits.items():
                            eng.wait_ge(sems[sk], v)
                            known[sk] = v
                        ins = o.fn(eng)
                        if o.semv is not None:
                            ins.then_inc(sems[o.semv[0]], 16 if o.dma else 1)
                    if e == "sp":
                        for o in final_waits:
                            eng.wait_ge(sems[o.semv[0]], o.semv[1])
                return body

            for e in ENGS:
                engobj[e](make(e))


class Cols:
    def __init__(self):
        self.n = 0
        self.off = {}

    def add(self, name, k):
        self.off[name] = self.n
        self.n += k


PC = Cols()
PC.add("nmix", DEPTH * NC_)
PC.add("nffn", DEPTH * NC_)
PC.add("qg", 2)
PC.add("kg", 2)
PC.add("lcw", 4 * NLC)
PC.add("lcb", NLC)
PC.add("lba", NLC)
PC.add("lbx", NLC)
PC.add("llam", NLC)
PC.add("pscale", NC_)
PC.add("fcw", DEPTH * 3 * 44)
PC.add("fcb", DEPTH * 44)

CC = Cols()
CC.add("ident", 128)
CC.add("ones", 128)
CC.add("cb", 2 * 256)
CC.add("esel", 8 * 128)
NCB = CC.n
CF = Cols()
CF.add("pastmask", 64)
CF.add("icnt", 4 * 16)
CF.add("eps", 1)


def pack_params(inp):
    P = np.zeros((128, PC.n), np.float32)

    def put(name, arr):
        a = np.asarray(arr, np.float32)
        a = a.reshape(-1, a.shape[-1] // 128, 128)
        a = a.transpose(2, 0, 1).reshape(128, -1)
        P[:, PC.off[name]:PC.off[name] + a.shape[1]] = a

    put("nmix", inp["norm_mix_g"])
    put("nffn", inp["norm_ffn_g"])
    put("qg", inp["attn_q_g"])
    put("kg", inp["attn_k_g"])
    put("lcw", inp["lru_conv_w"][0])
    put("lcb", inp["lru_conv_b"][0])
    put("lba", inp["lru_b_a"][0])
    put("lbx", inp["lru_b_x"][0])
    put("llam", inp["lru_lambda"][0])
    put("pscale", inp["pool_scale"][0])
    put("fcw", inp["ffn_conv_w"])
    put("fcb", inp["ffn_conv_b"])
    return P


def make_consts():
    cb16 = np.zeros((128, CC.n), np.float32)
    cb16[:, CC.off["ident"]:CC.off["ident"] + 128] = np.eye(128, dtype=np.float32)
    cb16[:, CC.off["ones"]:CC.off["ones"] + 128] = 1.0
    p = np.arange(128)[:, None]
    q = np.arange(256)[None, :]
    for j in range(2):
        m = np.where(j * 128 + p <= q, 0.0, NEG).astype(np.float32)
        cb16[:, CC.off["cb"] + j * 256:CC.off["cb"] + (j + 1) * 256] = m
    es = np.zeros((128, 8, 128), np.float32)
    for n in range(8):
        es[n, n, :] = 1.0
    cb16[:, CC.off["esel"]:CC.off["esel"] + 1024] = es.reshape(128, 1024)
    cf = np.zeros((128, CF.n), np.float32)
    pm = np.zeros((8, 8), np.float32)
    for i in range(8):
        for n in range(8):
            pm[i, n] = 0.0 if n < 4 + i // 2 else -1e30
    cf[:, CF.off["pastmask"]:CF.off["pastmask"] + 64] = pm.reshape(1, 64)
    ic = np.zeros((4, 16), np.float32)
    for g, w in enumerate((2, 4, 8, 16)):
        for t in range(16):
            ic[g, t] = 1.0 / min(t + 1, w)
    cf[:, CF.off["icnt"]:CF.off["icnt"] + 64] = ic.reshape(1, 64)
    cf[:, CF.off["eps"]] = EPS
    return cb16, cf


class Builder:
    def __init__(self, phases, attn_heads=8, attn_hg=4):
        self.phases = phases
        self.attn_heads = attn_heads
        self.attn_hg = attn_hg
        nc = bass.Bass("TRN2", target_bir_lowering=False)
        self.nc = nc
        dt = nc.dram_tensor
        self.d_x = dt("xT", [D, S_], F32, kind="ExternalInput").ap()
        self.d_out = dt("outT", [D, S_], F32, kind="ExternalOutput").ap()
        self.d_params = dt("params", [128, PC.n], F32, kind="ExternalInput").ap()
        self.d_c16 = dt("c16", [128, CC.n], F32, kind="ExternalInput").ap()
        self.d_cf = dt("cf", [128, CF.n], F32, kind="ExternalInput").ap()
        self.d_wup = dt("ffn_w_up", [DEPTH, D, 2 * FH], F32, kind="ExternalInput").ap()
        self.d_wdn = dt("ffn_w_down", [DEPTH, FH, D], F32, kind="ExternalInput").ap()
        self.d_wqkv = dt("attn_w_qkv", [2, D, 3 * D], F32, kind="ExternalInput").ap()
        self.d_wo = dt("attn_w_o", [2, D, D], F32, kind="ExternalInput").ap()
        self.d_lwin = dt("lru_w_in", [1, D, 2 * DR], F32, kind="ExternalInput").ap()
        self.d_lwa = dt("lru_w_a", [1, NLC, 128, 128], F32, kind="ExternalInput").ap()
        self.d_lwx = dt("lru_w_x", [1, NLC, 128, 128], F32, kind="ExternalInput").ap()
        self.d_lwout = dt("lru_w_out", [1, DR, D], F32, kind="ExternalInput").ap()
        self.d_pw = dt("pool_w", [1, 4, 256, 256], F32, kind="ExternalInput").ap()
        self.S = Sched(nc)
        self.psrot = 0
        self.uid = 0
        self.out_dmas = []

    def sb(self, st, name, shape, dtype):
        self.uid += 1
        return st.enter_context(self.nc.sbuf_tensor("%s_u%d" % (name, self.uid), shape, dtype))

    def pcol(self, name, idx):
        o = PC.off[name] + idx
        return self.params[:, o:o + 1]

    def build(self):
        nc, S = self.nc, self.S
        with contextlib.ExitStack() as st:
            self.xT = self.sb(st, "xT_sb", [128, NC_, S_], F32)
            self.hT = self.sb(st, "hT_sb", [128, NC_, S_], BF16)
            self.params = self.sb(st, "params_sb", [128, PC.n], F32)
            self.c16 = self.sb(st, "c16_sb", [128, CC.n], BF16)
            self.cf = self.sb(st, "cf_sb", [128, CF.n], F32)
            self.sq = self.sb(st, "sq_sb", [128, NC_, 512], BF16)
            self.lnv = self.sb(st, "lnv_sb", [128, 512], F32)
            self.rstd = self.sb(st, "rstd_sb", [128, 512], F32)
            self.PS = st.enter_context(nc.psum_tensor("PS", [128, 4096], F32))
            self.ident = self.c16[:, CC.off["ident"]:CC.off["ident"] + 128]
            self.ones = self.c16[:, CC.off["ones"]:CC.off["ones"] + 128]
            self.eps = self.cf[:, CF.off["eps"]:CF.off["eps"] + 1]

            S.op("sp", lambda e: e.dma_start(out=self.params[:], in_=self.d_params[:, :]),
                 writes=["params"], dma=True)
            S.op("sp", lambda e: e.dma_start(out=self.cf[:], in_=self.d_cf[:, :]),
                 writes=["cf"], dma=True)
            S.op("pool", lambda e: e.dma_start(out=self.c16[:], in_=self.d_c16[:, :]),
                 writes=["c16"], dma=True)
            for tt in range(4):
                for c in range(NC_):
                    S.op("sp", lambda e, c=c, tt=tt: e.dma_start(
                        out=self.xT[:, c, tt * 512:(tt + 1) * 512],
                        in_=self.d_x[c * 128:(c + 1) * 128, tt * 512:(tt + 1) * 512]),
                        writes=[("x", c, tt)], dma=True)

            self.wfirst = self.sb(st, "wfirst", [128, NC_, 512], BF16)
            nph = len(self.phases)
            self.prefetch_first(self.phases[0])
            self.start_norm(self.phases[0])
            for i, ph in enumerate(self.phases):
                kind, layer = ph
                self.next_phase = self.phases[i + 1] if i + 1 < nph else None
                S.fence()
                with contextlib.ExitStack() as st2:
                    if kind == "ffn":
                        self.ffn(st2, layer)
                    else:
                        mk = layer % 3
                        if mk == 0:
                            self.attn(st2, layer)
                        elif mk == 1:
                            self.lru(st2, layer)
                        else:
                            self.pool(st2, layer)

            S.emit(final_waits=self.out_dmas)
        return nc

    def bank(self, b):
        return self.PS[:, b * 512:(b + 1) * 512]

    def normA(self, tt):
        ts = slice(tt * 512, (tt + 1) * 512)
        self.S.op("act", lambda e: e.activation(out=self.sq[:], in_=self.xT[:, :, ts], func=AF.Square),
                  reads=[("x", c, tt) for c in range(NC_)], writes=["sq"])

    def normB(self, gname, layer, tt):
        S = self.S
        ts = slice(tt * 512, (tt + 1) * 512)
        b = self.psrot % 8
        self.psrot += 1
        for c in range(NC_):
            S.op("pe", lambda e, c=c: e.matmul(self.bank(b), lhsT=self.ones, rhs=self.sq[:, c, :],
                                               start=(c == 0), stop=(c == NC_ - 1)),
                 reads=["sq", "c16"], writes=[("ps", b)])
        S.op("act", lambda e: e.activation(out=self.lnv[:], in_=self.bank(b), func=AF.Ln,
                                           scale=1.0 / D, bias=self.eps),
             reads=[("ps", b), "cf"], writes=["lnv"])
        S.op("act", lambda e: e.activation(out=self.rstd[:], in_=self.lnv[:], func=AF.Exp, scale=-0.5),
             reads=["lnv"], writes=["rstd"])
        for c in range(NC_):
            S.op("dve", lambda e, c=c: e.scalar_tensor_tensor(
                out=self.hT[:, c, ts], in0=self.xT[:, c, ts], scalar=self.pcol(gname, layer * NC_ + c),
                in1=self.rstd[:], op0=ALU.mult, op1=ALU.mult),
                reads=[("x", c, tt), "rstd", "params"], writes=[("h", c, tt)])

    @staticmethod
    def norm_of(ph):
        if ph is None:
            return None
        kind, layer = ph
        if kind == "ffn":
            return ("nffn", layer)
        if layer % 3 == 2:
            return None
        return ("nmix", layer)

    def start_norm(self, ph):
        nm = self.norm_of(ph)
        if nm is None:
            return
        for tt in range(4):
            self.normA(tt)
            self.normB(nm[0], nm[1], tt)

    def tail_hook(self, tt):
        nm = self.norm_of(self.next_phase)
        if self.next_phase is None:
            for c in range(NC_):
                self.out_dmas.append(self.S.op("sp", lambda e, c=c: e.dma_start(
                    out=self.d_out[c * 128:(c + 1) * 128, tt * 512:(tt + 1) * 512],
                    in_=self.xT[:, c, tt * 512:(tt + 1) * 512]),
                    reads=[("x", c, tt)], dma=True))
            return
        if tt == 0:
            self.prefetch_first(self.next_phase)
        if nm is None:
            return
        if tt >= 1:
            self.normB(nm[0], nm[1], tt - 1)
        self.normA(tt)
        if tt == 3:
            self.normB(nm[0], nm[1], 3)

    def prefetch_first(self, ph):
        S = self.S
        kind, layer = ph
        wf = self.wfirst
        if kind == "ffn":
            src = self.d_wup[layer].rearrange("(c p) f -> p c f", p=128)
            S.op("pool", lambda e: e.dma_start(out=wf[:, :, 0:256], in_=src[:, :, 0:256]),
                 writes=[("wf", 0), ("wf", 1)], dma=True)
            S.op("pool", lambda e: e.dma_start(out=wf[:, :, 256:512], in_=src[:, :, FH:FH + 256]),
                 writes=[("wf", 2), ("wf", 3)], dma=True)
        elif layer % 3 == 0:
            src = self.d_wqkv[layer // 3].rearrange("(c p) f -> p c f", p=128)
            for k in range(3):
                S.op("pool", lambda e, k=k: e.dma_start(out=wf[:, :, k * 128:(k + 1) * 128],
                                                        in_=src[:, :, k * D:k * D + 128]),
                     writes=[("wf", k)], dma=True)
        elif layer % 3 == 1:
            src = self.d_lwin[0].rearrange("(c p) f -> p c f", p=128)
            S.op("pool", lambda e: e.dma_start(out=wf[:, :, 0:128], in_=src[:, :, 0:128]),
                 writes=[("wf", 0)], dma=True)
            S.op("pool", lambda e: e.dma_start(out=wf[:, :, 128:256], in_=src[:, :, DR:DR + 128]),
                 writes=[("wf", 1)], dma=True)

    def ffn(self, st, layer):
        nc, S = self.nc, self.S
        GROUPS = [(0, 6), (6, 12), (12, 17), (17, 22)]
        NP = 7
        NW = 3
        wup = [self.wfirst] + [self.sb(st, "wup%d" % i, [128, NC_, 512], BF16) for i in range(1, NW)]
        wdn = self.sb(st, "wdn", [128, 6, D], BF16)
        Pb = self.sb(st, "Pb", [128, NP, S_], BF16)
        cbuf = [self.sb(st, "cbuf%d" % i, [128, S_], F32) for i in range(3)]
        wup_src = self.d_wup[layer].rearrange("(c p) f -> p c f", p=128)

        def wtok(bi, half):
            if bi == 0:
                return [("wf", 2 * half), ("wf", 2 * half + 1)]
            return [("wup", bi, half)]

        def load_slab(s):
            bi = s % NW
            S.op("pool", lambda e: e.dma_start(out=wup[bi][:, :, 0:256],
                                               in_=wup_src[:, :, s * 256:(s + 1) * 256]),
                 writes=wtok(bi, 0), dma=True)
            S.op("pool", lambda e: e.dma_start(out=wup[bi][:, :, 256:512],
                                               in_=wup_src[:, :, FH + s * 256:FH + (s + 1) * 256]),
                 writes=wtok(bi, 1), dma=True)

        def load_wdn(j0, j1):
            S.op("pool", lambda e: e.dma_start(
                out=wdn[:, 0:j1 - j0, :],
                in_=self.d_wdn[layer, j0 * 128:j1 * 128, :].rearrange("(j p) d -> p j d", p=128)),
                writes=["wdn"], dma=True)

        cbi = [0]

        def up(j):
            s, r = j // 2, j % 2
            bi = s % NW
            cg = None
            for half in range(2):
                fj = half * NPAIR + j
                base = half * 2048
                lcol = half * 256 + r * 128
                for tt in range(4):
                    for c in range(NC_):
                        S.op("pe", lambda e, c=c, tt=tt: e.matmul(
                            self.PS[:, base + tt * 512: base + (tt + 1) * 512],
                            lhsT=wup[bi][:, c, lcol:lcol + 128],
                            rhs=self.hT[:, c, tt * 512:(tt + 1) * 512],
                            start=(c == 0), stop=(c == NC_ - 1)),
                            reads=wtok(bi, half) + [("h", c, tt)], writes=[("ps", half * 4 + tt)])
                if half == 1 and r == 1 and s + NW < 11:
                    load_slab(s + NW)
                cb = cbuf[cbi[0] % 3]
                cbk = cbi[0] % 3
                cbn = [("cbuf", cbk, 0), ("cbuf", cbk, 1)]
                cbi[0] += 1
                w0 = self.pcol("fcw", (layer * 3 + 0) * 44 + fj)
                w1 = self.pcol("fcw", (layer * 3 + 1) * 44 + fj)
                w2 = self.pcol("fcw", (layer * 3 + 2) * 44 + fj)
                bb = self.pcol("fcb", layer * 44 + fj)
                u = self.PS[:, base:base + 2048]
                for hh in range(2):
                    lo, hi = hh * 1024, (hh + 1) * 1024
                    psr = [("ps", half * 4 + 2 * hh), ("ps", half * 4 + 2 * hh + 1)]
                    if hh == 1:
                        psr.append(("ps", half * 4 + 1))
                    ct = [("cbuf", cbk, hh)]
                    S.op("act", lambda e: e.activation(out=cb[:, lo:hi], in_=u[:, lo:hi], func=AF.Identity,
                                                       scale=w2, bias=bb),
                         reads=psr + ["params"], writes=ct)
                    for k, wk in ((1, w1), (2, w0)):
                        o0 = max(lo, k)
                        S.op("dve", lambda e, o0=o0, k=k, wk=wk: e.scalar_tensor_tensor(
                            out=cb[:, o0:hi], in0=u[:, o0 - k:hi - k], scalar=wk, in1=cb[:, o0:hi],
                            op0=ALU.mult, op1=ALU.add), reads=psr + ct + ["params"], writes=ct)
                for hh in range(2):
                    lo, hi = hh * 1024, (hh + 1) * 1024
                    if half == 0:
                        S.op("act", lambda e: e.activation(out=cb[:, lo:hi], in_=cb[:, lo:hi], func=AF.Silu),
                             reads=[cbn[hh]], writes=[cbn[hh]])
                    else:
                        S.op("pool", lambda e: e.tensor_tensor(out=Pb[:, j % NP, lo:hi], in0=cg[0][:, lo:hi],
                                                               in1=cb[:, lo:hi], op=ALU.mult),
                             reads=[cbn[hh], cg[1][hh]], writes=[("P", j % NP, hh)])
                if half == 0:
                    cg = (cb, cbn)

        def down(j0, j1, nbanks=4, final=False):
            nj = j1 - j0
            for tt in range(4):
                for m in range(NC_):
                    b = self.psrot % nbanks
                    self.psrot += 1
                    for jj in range(nj):
                        slot = (j0 + jj) % NP
                        S.op("pe", lambda e, jj=jj, slot=slot: e.matmul(
                            self.bank(b), lhsT=wdn[:, jj, m * 128:(m + 1) * 128],
                            rhs=Pb[:, slot, tt * 512:(tt + 1) * 512],
                            start=(jj == 0), stop=(jj == nj - 1)),
                            reads=["wdn", ("P", slot, tt // 2)], writes=[("ps", b)])
                    xs = self.xT[:, m, tt * 512:(tt + 1) * 512]
                    S.op("dve", lambda e: e.tensor_tensor(out=xs, in0=self.bank(b), in1=xs, op=ALU.add),
                         reads=[("ps", b), ("x", m, tt)], writes=[("x", m, tt)])
                if final:
                    self.tail_hook(tt)

        for s0 in range(1, NW):
            load_slab(s0)
        load_wdn(*GROUPS[0])
        gidx = {}
        for gi, (j0, j1) in enumerate(GROUPS):
            for j in range(j0, j1):
                gidx[j] = gi
        for j in range(NPAIR):
            up(j)
            gi = gidx[j]
            j0, j1 = GROUPS[gi]
            if j == j0 and gi > 0:
                down(*GROUPS[gi - 1])
            if j == j0 + 2 and gi > 0:
                load_wdn(j0, j1)
        down(*GROUPS[-1], nbanks=8, final=True)

    def rstd_tt(self, tt, dst, dst_tok):
        S = self.S
        ts = slice(tt * 512, (tt + 1) * 512)
        b = self.psrot % 8
        self.psrot += 1
        S.op("act", lambda e: e.activation(out=self.sq[:], in_=self.xT[:, :, ts], func=AF.Square),
             reads=[("x", c, tt) for c in range(NC_)], writes=["sq"])
        for c in range(NC_):
            S.op("pe", lambda e, c=c: e.matmul(self.bank(b), lhsT=self.ones, rhs=self.sq[:, c, :],
                                               start=(c == 0), stop=(c == NC_ - 1)),
                 reads=["sq", "c16"], writes=[("ps", b)])
        S.op("act", lambda e: e.activation(out=self.lnv[:], in_=self.bank(b), func=AF.Ln,
                                           scale=1.0 / D, bias=self.eps),
             reads=[("ps", b), "cf"], writes=["lnv"])
        S.op("act", lambda e: e.activation(out=dst, in_=self.lnv[:], func=AF.Exp, scale=-0.5),
             reads=["lnv"], writes=[dst_tok])

    def resid_proj(self, w, nk, rhs_fn, rhs_toks, wtok, final=False):
        S = self.S
        for tt in range(4):
            for m in range(NC_):
                b = self.psrot % 8
                self.psrot += 1
                for k in range(nk):
                    S.op("pe", lambda e, b=b, k=k, m=m, tt=tt: e.matmul(
                        self.bank(b), lhsT=w[:, k, m * 128:(m + 1) * 128], rhs=rhs_fn(k, tt),
                        start=(k == 0), stop=(k == nk - 1)),
                        reads=[wtok] + rhs_toks(k, tt), writes=[("ps", b)])
                xs = self.xT[:, m, tt * 512:(tt + 1) * 512]
                S.op("dve", lambda e, b=b, xs=xs: e.tensor_tensor(
                    out=xs, in0=self.bank(b), in1=xs, op=ALU.add),
                    reads=[("ps", b), ("x", m, tt)], writes=[("x", m, tt)])
            if final:
                self.tail_hook(tt)

    def attn(self, st, layer):
        nc, S = self.nc, self.S
        slot = layer // 3
        HG = self.attn_hg
        NH = self.attn_heads
        wqkv = [self.wfirst, self.sb(st, "wqkv1", [128, NC_, 384], BF16)]

        def wqt(bi, k):
            return ("wf", k) if bi == 0 else ("wqkv", bi, k)

        qn = [self.sb(st, "qn%d" % i, [128, S_], BF16) for i in range(2)]
        kn = [self.sb(st, "kn%d" % i, [128, S_], BF16) for i in range(2)]
        Vt = [self.sb(st, "Vt%d" % i, [128, 16, 128], BF16) for i in range(2)]
        oT = self.sb(st, "oT", [128, HG, S_], BF16)
        wo = self.sb(st, "wo", [128, HG, D], BF16)
        rs = [self.sb(st, "rs%d" % i, [128, 512], F32) for i in range(2)]
        sq2 = [self.sb(st, "sq2_%d" % i, [128, 512], BF16) for i in range(2)]
        PT = [self.sb(st, "PT%d" % i, [128, 512], BF16) for i in range(4)]
        kmf = self.sb(st, "kmf", [128, 8], F32)
        kmb = self.sb(st, "kmb", [128, 8], BF16)
        g1 = self.sb(st, "g1", [128, 64], F32)
        top = self.sb(st, "top", [128, 64], F32)
        cmpf = self.sb(st, "cmpf", [128, 64], F32)
        btok = self.sb(st, "btok", [128, 8, 128], BF16)
        biasT = self.sb(st, "biasT", [128, 1024], BF16)
        rden = [self.sb(st, "rden%d" % i, [128, 256], F32) for i in range(2)]
        S.op("pool", lambda e: e.memset(btok[:], 0.0), writes=["btok"])
        S.op("pool", lambda e: e.memset(biasT[:], 0.0), writes=["biasT"])
        wsrc = self.d_wqkv[slot].rearrange("(c p) f -> p c f", p=128)
        scale = 128.0 ** -0.5
        pastmask = self.cf[:, CF.off["pastmask"]:CF.off["pastmask"] + 64]
        cbm = [self.c16[:, CC.off["cb"] + j * 256:CC.off["cb"] + (j + 1) * 256] for j in range(2)]
        esel = [self.c16[:, CC.off["esel"] + n * 128:CC.off["esel"] + (n + 1) * 128] for n in range(8)]
        cnt = {"s": 0, "p": 0, "k2": 0, "pt": 0}

        def sbank():
            b = 2 + cnt["s"] % 3
            cnt["s"] += 1
            return b


        def load_w(hd):
            bi = hd % 2
            for k in range(3):
                S.op("pool", lambda e, k=k: e.dma_start(
                    out=wqkv[bi][:, :, k * 128:(k + 1) * 128],
                    in_=wsrc[:, :, k * D + hd * 128:k * D + (hd + 1) * 128]),
                    writes=[wqt(bi, k)], dma=True)

        def prologue_units(hd):
            bi = hd % 2
            units = []
            items = [(which, tt) for which in (0, 1) for tt in range(4)]
            state = {}

            def A1(i):
                which, tt = items[i]
                ts = slice(tt * 512, (tt + 1) * 512)
                b = 5 + cnt["p"] % 3
                cnt["p"] += 1
                k2 = cnt["k2"] % 2
                cnt["k2"] += 1
                state[i] = (b, k2)
                for c in range(NC_):
                    S.op("pe", lambda e, c=c: e.matmul(
                        self.bank(b), lhsT=wqkv[bi][:, c, which * 128:(which + 1) * 128],
                        rhs=self.hT[:, c, ts], start=(c == 0), stop=(c == NC_ - 1)),
                        reads=[wqt(bi, which), ("h", c, tt)], writes=[("ps", b)])
                S.op("act", lambda e: e.activation(out=sq2[k2][:], in_=self.bank(b), func=AF.Square),
                     reads=[("ps", b)], writes=[("sq2", k2)])

            def A2(i):
                which, tt = items[i]
                ts = slice(tt * 512, (tt + 1) * 512)
                b, k2 = state[i]
                dst = (qn if which == 0 else kn)[bi]
                gname = "qg" if which == 0 else "kg"
                b7 = sbank()
                S.op("pe", lambda e: e.matmul(self.bank(b7), lhsT=self.ones, rhs=sq2[k2][:], start=True, stop=True),
                     reads=[("sq2", k2), "c16"], writes=[("ps", b7)])
                S.op("act", lambda e: e.activation(out=rs[k2][:], in_=self.bank(b7), func=AF.Ln,
                                                   scale=1.0 / 128, bias=self.eps),
                     reads=[("ps", b7), "cf"], writes=[("rs", k2)])
                S.op("act", lambda e: e.activation(out=rs[k2][:], in_=rs[k2][:], func=AF.Exp, scale=-0.5),
                     reads=[("rs", k2)], writes=[("rs", k2)])
                S.op("dve", lambda e: e.scalar_tensor_tensor(
                    out=dst[:, ts], in0=self.bank(b), scalar=self.pcol(gname, slot), in1=rs[k2][:],
                    op0=ALU.mult, op1=ALU.mult),
                    reads=[("ps", b), ("rs", k2), "params"], writes=[("qk", which, bi, tt)])

            units.append(lambda: A1(0))
            for i in range(1, 8):
                units.append(lambda i=i: A1(i))
                units.append(lambda i=i: A2(i - 1))
            units.append(lambda: A2(7))

            def Vunit(bq):
                b = 5 + cnt["p"] % 3
                cnt["p"] += 1
                for i4 in range(4):
                    i = bq * 4 + i4
                    for c in range(NC_):
                        S.op("pe", lambda e, i=i, i4=i4, c=c: e.matmul(
                            self.bank(b)[:, i4 * 128:(i4 + 1) * 128], lhsT=self.hT[:, c, i * 128:(i + 1) * 128],
                            rhs=wqkv[bi][:, c, 256:384], start=(c == 0), stop=(c == NC_ - 1)),
                            reads=[wqt(bi, 2), ("h", c, bq)], writes=[("ps", b)])
                S.op("act", lambda e: e.activation(
                    out=Vt[bi][:, bq * 4:(bq + 1) * 4, :], in_=self.bank(b).rearrange("p (i d) -> p i d", d=128),
                    func=AF.Identity), reads=[("ps", b)], writes=[("V", bi, bq)])

            for bq in range(4):
                units.append(lambda bq=bq: Vunit(bq))

            def Cunit():
                S.op("dve", lambda e: e.tensor_reduce(out=kmf[:], in_=kn[bi][:].rearrange("p (n k) -> p n k", k=256),
                                                      axis=AX.X, op=ALU.add),
                     reads=[("qk", 1, bi, tt) for tt in range(4)], writes=["kmf"])
                S.op("dve", lambda e: e.tensor_scalar(out=kmb[:], in0=kmf[:], scalar1=1.0 / 256, scalar2=None,
                                                      op0=ALU.mult), reads=["kmf"], writes=["kmb"])
                bg = sbank()
                for i in range(8):
                    S.op("pe", lambda e, i=i: e.matmul(
                        self.bank(bg)[:, i * 8:(i + 1) * 8], lhsT=qn[bi][:, (8 + i) * 128:(9 + i) * 128], rhs=kmb[:],
                        start=True, stop=True),
                        reads=["kmb", ("qk", 0, bi, 2 + i // 4)], writes=[("ps", bg)])
                S.op("dve", lambda e: e.tensor_tensor(out=g1[:], in0=self.bank(bg)[:, 0:64], in1=pastmask, op=ALU.add),
                     reads=[("ps", bg), "cf"], writes=["g1"])
                for i in range(8):
                    S.op("dve", lambda e, i=i: e.max(out=top[:, i * 8:(i + 1) * 8], in_=g1[:, i * 8:(i + 1) * 8]),
                         reads=["g1"], writes=["top"])
                S.op("dve", lambda e: e.tensor_tensor(
                    out=cmpf[:].rearrange("p (i n) -> p i n", n=8), in0=g1[:].rearrange("p (i n) -> p i n", n=8),
                    in1=top[:].rearrange("p (i n) -> p i n", n=8)[:, :, 2:3].to_broadcast([128, 8, 8]), op=ALU.is_lt),
                    reads=["g1", "top"], writes=["cmpf"])

            units.append(Cunit)
            return units

        def Dunit(hd):
            S.op("dve", lambda e: e.tensor_scalar(out=btok[:, :, 0:8], in0=cmpf[:].rearrange("p (i n) -> p i n", n=8),
                                                  scalar1=NEG, scalar2=None, op0=ALU.mult),
                 reads=["cmpf"], writes=["btok"])
            for k in range(2):
                bd_ = sbank()
                for i4 in range(4):
                    i = k * 4 + i4
                    S.op("pe", lambda e, i=i, i4=i4: e.matmul(
                        self.bank(bd_)[:, i4 * 128:(i4 + 1) * 128], lhsT=btok[:, i, :], rhs=self.ident,
                        start=True, stop=True),
                        reads=["btok", "c16"], writes=[("ps", bd_)])
                S.op("act", lambda e, k=k: e.activation(out=biasT[0:8, k * 512:(k + 1) * 512],
                                                        in_=self.bank(bd_)[0:8, :], func=AF.Identity),
                     reads=[("ps", bd_)], writes=["biasT"])

        def E1(hd, qb, n, st_):
            bi = hd % 2
            qs = slice(qb * 256, (qb + 1) * 256)
            bs_ = sbank()
            pk = cnt["pt"] % 4
            cnt["pt"] += 1
            st_[(qb, n)] = pk
            for half in range(2):
                kt = 2 * n + half
                osl = self.bank(bs_)[:, half * 256:(half + 1) * 256]
                extra = (n == qb) or (qb >= 4)
                S.op("pe", lambda e, osl=osl, kt=kt, extra=extra: e.matmul(
                    osl, lhsT=kn[bi][:, kt * 128:(kt + 1) * 128], rhs=qn[bi][:, qs],
                    start=True, stop=(not extra)),
                    reads=[("qk", 1, bi, kt // 4), ("qk", 0, bi, qb // 2)], writes=[("ps", bs_)])
                if n == qb:
                    S.op("pe", lambda e, osl=osl, half=half: e.matmul(
                        osl, lhsT=self.ident, rhs=cbm[half], start=False, stop=True),
                        reads=["c16"], writes=[("ps", bs_)])
                elif qb >= 4:
                    S.op("pe", lambda e, osl=osl: e.matmul(
                        osl, lhsT=esel[n], rhs=biasT[:, (qb - 4) * 256:(qb - 3) * 256],
                        start=False, stop=True),
                        reads=["c16", "biasT"], writes=[("ps", bs_)])
            S.op("act", lambda e: e.activation(out=PT[pk][:], in_=self.bank(bs_), func=AF.Exp, scale=scale),
                 reads=[("ps", bs_)], writes=[("PT", pk)])

        def E2(hd, qb, n, st_):
            bi = hd % 2
            qs = slice(qb * 256, (qb + 1) * 256)
            pk = st_[(qb, n)]
            bo = qb % 2
            for half in range(2):
                kt = 2 * n + half
                first = (n == 0 and half == 0)
                last = (n == qb and half == 1)
                S.op("pe", lambda e, kt=kt, half=half, first=first, last=last: e.matmul(
                    self.bank(bo)[:, 0:256], lhsT=Vt[bi][:, kt, :], rhs=PT[pk][:, half * 256:(half + 1) * 256],
                    start=first, stop=last, skip_group_check=True),
                    reads=[("V", bi, kt // 4), ("PT", pk)], writes=[("ps", bo)])
                S.op("pe", lambda e, half=half, last=last: e.matmul(
                    self.bank(bo)[:, 256:512], lhsT=self.ones, rhs=PT[pk][:, half * 256:(half + 1) * 256],
                    start=False, stop=last, skip_group_check=True),
                    reads=["c16", ("PT", pk)], writes=[("ps", bo)])
            if n == qb:
                rk = qb % 2
                S.op("dve", lambda e: e.reciprocal(out=rden[rk][:], in_=self.bank(bo)[:, 256:512]),
                     reads=[("ps", bo)], writes=[("rden", rk)])
                S.op("dve", lambda e: e.tensor_tensor(
                    out=oT[:, hd % HG, qs], in0=self.bank(bo)[:, 0:256], in1=rden[rk][:], op=ALU.mult),
                    reads=[("ps", bo), ("rden", rk)], writes=[("oT", hd % HG, qb // 2)])

        for u in prologue_units(0):
            u()
        LAG = 2
        for hd in range(NH):
            if hd + 1 < NH:
                load_w(hd + 1)
                nxt = prologue_units(hd + 1)
            else:
                nxt = []
            if hd % HG == 0:
                g0 = hd
                S.op("pool", lambda e, g0=g0: e.dma_start(
                    out=wo[:], in_=self.d_wo[slot, g0 * 128:(g0 + HG) * 128, :].rearrange("(h p) d -> p h d", p=128)),
                    writes=["wo"], dma=True)
            items = [(qb, n) for qb in range(8) for n in range(qb + 1)]
            st_ = {}
            for idx in range(len(items) + LAG):
                if idx == 4:
                    Dunit(hd)
                if idx < len(items):
                    E1(hd, items[idx][0], items[idx][1], st_)
                if idx >= LAG:
                    E2(hd, items[idx - LAG][0], items[idx - LAG][1], st_)
                if nxt and idx >= 6:
                    nxt.pop(0)()
            while nxt:
                nxt.pop(0)()
            if hd % HG == HG - 1:
                self.resid_proj(wo, HG, lambda k, tt: oT[:, k, tt * 512:(tt + 1) * 512],
                                lambda k, tt: [("oT", k, tt)], "wo", final=(hd == NH - 1))

    def lru(self, st, layer):
        nc, S = self.nc, self.S
        CG = 5
        NSLOT = 6
        win = [self.wfirst, self.sb(st, "win1", [128, NC_, 256], BF16)]

        def wit(bi, half):
            return ("wf", half) if bi == 0 else ("win", bi, half)

        wa = self.sb(st, "wa", [128, NLC, 128], BF16)
        wx = self.sb(st, "wx", [128, NLC, 128], BF16)
        wout = self.sb(st, "wout", [128, CG, D], BF16)
        hy = self.sb(st, "hy", [128, NSLOT, S_], BF16)
        xc = [self.sb(st, "xc%d" % i, [128, 512], F32) for i in range(4)]
        xcb = [self.sb(st, "xcb%d" % i, [128, 512], BF16) for i in range(4)]
        gy = [self.sb(st, "gy%d" % i, [128, 512], BF16) for i in range(4)]
        A = [self.sb(st, "lruA%d" % i, [128, 512], F32) for i in range(4)]
        I = [self.sb(st, "lruI%d" % i, [128, 512], F32) for i in range(4)]
        M = [self.sb(st, "lruM%d" % i, [128, 512], F32) for i in range(4)]
        us = [self.sb(st, "lruus%d" % i, [128, 4], F32) for i in range(4)]
        dp = self.sb(st, "lrudp", [128, 4 * NLC], F32)
        S.op("pool", lambda e: e.dma_start(out=wa[:], in_=self.d_lwa[0].rearrange("n c d -> c n d")),
             writes=["wa"], dma=True)
        S.op("pool", lambda e: e.dma_start(out=wx[:], in_=self.d_lwx[0].rearrange("n c d -> c n d")),
             writes=["wx"], dma=True)
        lam = self.params[:, PC.off["llam"]:PC.off["llam"] + NLC]
        S.op("act", lambda e: e.activation(out=dp[:, 0:NLC], in_=lam, func=AF.Exp, scale=-1.0),
             reads=["params"], writes=["dp0"])
        S.op("act", lambda e: e.activation(out=dp[:, 0:NLC], in_=dp[:, 0:NLC], func=AF.Ln, bias=1.0),
             reads=["dp0"], writes=["dp0"])
        S.op("dve", lambda e: e.tensor_scalar(out=dp[:, NLC:2 * NLC], in0=dp[:, 0:NLC], scalar1=-4.0, scalar2=None,
                                              op0=ALU.mult), reads=["dp0"], writes=["dp"])
        S.op("dve", lambda e: e.tensor_scalar(out=dp[:, 2 * NLC:3 * NLC],
                                              in0=self.params[:, PC.off["lba"]:PC.off["lba"] + NLC],
                                              scalar1=0.5, scalar2=None, op0=ALU.mult), reads=["params"], writes=["dp"])
        S.op("dve", lambda e: e.tensor_scalar(out=dp[:, 3 * NLC:4 * NLC],
                                              in0=self.params[:, PC.off["lbx"]:PC.off["lbx"] + NLC],
                                              scalar1=0.5, scalar2=None, op0=ALU.mult), reads=["params"], writes=["dp"])
        wsrc = self.d_lwin[0].rearrange("(c p) f -> p c f", p=128)

        def load_win(c):
            bi = c % 2
            S.op("pool", lambda e: e.dma_start(out=win[bi][:, :, 0:128], in_=wsrc[:, :, c * 128:(c + 1) * 128]),
                 writes=[wit(bi, 0)], dma=True)
            S.op("pool", lambda e: e.dma_start(out=win[bi][:, :, 128:256],
                                               in_=wsrc[:, :, DR + c * 128:DR + (c + 1) * 128]),
                 writes=[wit(bi, 1)], dma=True)

        def load_wout(c0):
            S.op("pool", lambda e: e.dma_start(
                out=wout[:], in_=self.d_lwout[0, c0 * 128:(c0 + CG) * 128, :].rearrange("(k p) d -> p k d", p=128)),
                writes=["wout"], dma=True)

        pairs = [(c, hf) for c in range(NLC) for hf in range(2)]

        def units(p):
            c, hf = pairs[p]
            return [(c, 2 * hf + j, (2 * p + j) % 4) for j in range(2)]

        def Wp(p):
            for (c, tt, s_) in units(p):
                bi = c % 2
                ts = slice(tt * 512, (tt + 1) * 512)
                for half in range(2):
                    for k in range(NC_):
                        S.op("pe", lambda e, k=k: e.matmul(
                            self.bank(2 * s_ + half), lhsT=win[bi][:, k, half * 128:(half + 1) * 128],
                            rhs=self.hT[:, k, ts], start=(k == 0), stop=(k == NC_ - 1)),
                            reads=[wit(bi, half), ("h", k, tt)], writes=[("ps", 2 * s_ + half)])

        def E1(p):
            for (c, tt, s_) in units(p):
                u = self.bank(2 * s_)
                wcol = [self.pcol("lcw", j * NLC + c) for j in range(4)]
                S.op("act", lambda e: e.activation(out=xc[s_][:], in_=u, func=AF.Identity, scale=wcol[3],
                                                   bias=self.pcol("lcb", c)),
                     reads=[("ps", 2 * s_), "params"], writes=[("xc", s_)])
                S.op("act", lambda e: e.activation(out=us[s_][:, 0:3], in_=u[:, 509:512], func=AF.Identity),
                     reads=[("ps", 2 * s_)], writes=[("us", s_)])
                for k in (1, 2, 3):
                    S.op("dve", lambda e, k=k: e.scalar_tensor_tensor(
                        out=xc[s_][:, k:512], in0=u[:, 0:512 - k], scalar=wcol[3 - k], in1=xc[s_][:, k:512],
                        op0=ALU.mult, op1=ALU.add), reads=[("ps", 2 * s_), ("xc", s_), "params"], writes=[("xc", s_)])
                if tt > 0:
                    sp = (s_ - 1) % 4
                    for k in (1, 2, 3):
                        S.op("dve", lambda e, k=k: e.scalar_tensor_tensor(
                            out=xc[s_][:, 0:k], in0=us[sp][:, 3 - k:3], scalar=wcol[3 - k], in1=xc[s_][:, 0:k],
                            op0=ALU.mult, op1=ALU.add), reads=[("us", sp), ("xc", s_), "params"], writes=[("xc", s_)])
                S.op("pool", lambda e: e.tensor_copy(out=xcb[s_][:], in_=xc[s_][:]),
                     reads=[("xc", s_)], writes=[("xcb", s_)])
            for (c, tt, s_) in units(p):
                S.op("act", lambda e: e.activation(out=gy[s_][:], in_=self.bank(2 * s_ + 1), func=AF.Gelu_apprx_tanh),
                     reads=[("ps", 2 * s_ + 1)], writes=[("gy", s_)])

        def Gp(p):
            for (c, tt, s_) in units(p):
                for which, wt, wtok in ((0, wa, "wa"), (1, wx, "wx")):
                    S.op("pe", lambda e, which=which, wt=wt: e.matmul(
                        self.bank(2 * s_ + which), lhsT=wt[:, c, :], rhs=xcb[s_][:], start=True, stop=True),
                        reads=[wtok, ("xcb", s_)], writes=[("ps", 2 * s_ + which)])

        def E2(p):
            un = units(p)
            for (c, tt, s_) in un:
                hba = dp[:, 2 * NLC + c:2 * NLC + c + 1]
                hbx = dp[:, 3 * NLC + c:3 * NLC + c + 1]
                S.op("act", lambda e: e.activation(out=A[s_][:], in_=self.bank(2 * s_), func=AF.Tanh, scale=0.5, bias=hba),
                     reads=[("ps", 2 * s_), "dp"], writes=[("A", s_)])
                S.op("act", lambda e: e.activation(out=I[s_][:], in_=self.bank(2 * s_ + 1), func=AF.Tanh, scale=0.5, bias=hbx),
                     reads=[("ps", 2 * s_ + 1), "dp"], writes=[("I", s_)])
            for (c, tt, s_) in un:
                hcl = dp[:, NLC + c:NLC + c + 1]
                S.op("act", lambda e: e.activation(out=A[s_][:], in_=A[s_][:], func=AF.Exp, scale=hcl, bias=hcl),
                     reads=[("A", s_), "dp"], writes=[("A", s_)])
            for (c, tt, s_) in un:
                S.op("act", lambda e: e.activation(out=M[s_][:], in_=A[s_][:], func=AF.Square),
                     reads=[("A", s_)], writes=[("M", s_)])
            for (c, tt, s_) in un:
                S.op("act", lambda e: e.activation(out=M[s_][:], in_=M[s_][:], func=AF.Sqrt, scale=-1.0, bias=1.0),
                     reads=[("M", s_)], writes=[("M", s_)])
            for (c, tt, s_) in un:
                S.op("dve", lambda e: e.scalar_tensor_tensor(out=I[s_][:], in0=I[s_][:], scalar=1.0, in1=xc[s_][:],
                                                             op0=ALU.add, op1=ALU.mult),
                     reads=[("I", s_), ("xc", s_)], writes=[("I", s_)])
                S.op("dve", lambda e: e.scalar_tensor_tensor(out=I[s_][:], in0=I[s_][:], scalar=0.5, in1=M[s_][:],
                                                             op0=ALU.mult, op1=ALU.mult),
                     reads=[("I", s_), ("M", s_)], writes=[("I", s_)])
                if tt > 0:
                    sp = (s_ - 1) % 4
                    init, itok = M[sp][:, 511:512], [("M", sp)]
                else:
                    init, itok = 0.0, []
                S.op("dve", lambda e, init=init: e.tensor_tensor_scan(
                    out=M[s_][:], data0=A[s_][:], data1=I[s_][:], initial=init, op0=ALU.mult, op1=ALU.add),
                    reads=[("A", s_), ("I", s_), ("M", s_)] + itok, writes=[("M", s_)])
                slot = c % NSLOT
                S.op("pool", lambda e, slot=slot, tt=tt: e.tensor_tensor(
                    out=hy[:, slot, tt * 512:(tt + 1) * 512], in0=M[s_][:], in1=gy[s_][:], op=ALU.mult),
                    reads=[("M", s_), ("gy", s_)], writes=[("hy", slot, tt)])

        def outproj(c0, final):
            self.resid_proj(wout, CG, lambda k, tt: hy[:, (c0 + k) % NSLOT, tt * 512:(tt + 1) * 512],
                            lambda k, tt: [("hy", (c0 + k) % NSLOT, tt)], "wout", final=final)

        load_wout(0)
        npair = len(pairs)
        pend = {}
        for idx in range(npair + 1):
            if idx < npair:
                c, hf = pairs[idx]
                if hf == 0 and c + 1 < NLC:
                    load_win(c + 1)
                Wp(idx)
                E1(idx)
            if idx >= 1:
                Gp(idx - 1)
                E2(idx - 1)
                c, hf = pairs[idx - 1]
                if hf == 1 and c == CG - 1:
                    pend[idx + 1] = "out0"
                    pend[idx + 5] = "wout1"
            act = pend.get(idx)
            if act == "out0":
                outproj(0, False)
            elif act == "wout1":
                load_wout(CG)
        outproj(CG, True)

    def pool(self, st, layer):
        nc, S = self.nc, self.S
        rstd_all = self.sb(st, "rstd_all", [128, S_], F32)
        hf = [self.sb(st, "hf%d" % i, [128, S_], F32) for i in range(2)]
        B = [self.sb(st, "pB%d" % i, [128, S_], F32) for i in range(4)]
        t16 = [self.sb(st, "t16_%d" % i, [128, 16], F32) for i in range(2)]
        pw = self.sb(st, "pw", [128, 4, 2, 256], BF16)
        S.op("pool", lambda e: e.dma_start(out=pw[:], in_=self.d_pw[0].rearrange("g (k p) d -> p g k d", p=128)),
             writes=["pw"], dma=True)
        for tt in range(4):
            self.rstd_tt(tt, rstd_all[:, tt * 512:(tt + 1) * 512], ("rstd_all", tt))
        xall = lambda c: [("x", c, tt) for tt in range(4)]
        hall = lambda c: [("h", c, tt) for tt in range(4)]

        def chunk_steps(c):
            g = c // 2
            k2 = c % 2
            hfc = hf[k2]
            steps = []
            steps.append(lambda: S.op("dve", lambda e: e.scalar_tensor_tensor(
                out=hfc[:], in0=self.xT[:, c, :], scalar=self.pcol("nmix", layer * NC_ + c), in1=rstd_all[:],
                op0=ALU.mult, op1=ALU.mult),
                reads=xall(c) + [("rstd_all", tt) for tt in range(4)] + ["params"], writes=[("hf", k2)]))
            cur, curtoks = hfc, [("hf", k2)]
            for k in range(g + 1):
                d = 2 ** k
                bi = k2 * 2 + k % 2
                nxt, nxttok = B[bi], ("pB", bi)
                eng = "pool" if k % 2 == 0 else "dve"

                def stp(cur=cur, nxt=nxt, d=d, eng=eng, curtoks=list(curtoks), nxttok=nxttok):
                    S.op(eng, lambda e: e.tensor_tensor(
                        out=nxt[:, d:S_], in0=cur[:, d:S_], in1=cur[:, 0:S_ - d], op=ALU.add),
                        reads=curtoks, writes=[nxttok])
                    S.op("act", lambda e: e.activation(out=nxt[:, 0:d], in_=cur[:, 0:d], func=AF.Identity),
                         reads=curtoks, writes=[(nxttok, "head")])
                steps.append(stp)
                cur, curtoks = nxt, [nxttok, (nxttok, "head")]
            w = 2 ** (g + 1)
            icnt = self.cf[:, CF.off["icnt"] + g * 16:CF.off["icnt"] + (g + 1) * 16]

            def fin(cur=cur, curtoks=list(curtoks)):
                S.op("dve", lambda e: e.scalar_tensor_tensor(
                    out=self.hT[:, c, :], in0=cur[:], scalar=1.0 / w, in1=hfc[:], op0=ALU.mult, op1=ALU.subtract),
                    reads=curtoks + [("hf", k2)], writes=hall(c))
                S.op("dve", lambda e: e.tensor_tensor(out=t16[k2][:], in0=cur[:, 0:16], in1=icnt, op=ALU.mult),
                     reads=curtoks + ["cf"], writes=[("t16", k2)])
                S.op("dve", lambda e: e.tensor_tensor(
                    out=self.hT[:, c, 0:16], in0=t16[k2][:], in1=hfc[:, 0:16], op=ALU.subtract),
                    reads=[("t16", k2), ("hf", k2)] + hall(c), writes=hall(c))
            steps.append(fin)
            return steps

        for c0 in range(0, NC_, 2):
            sa, sb_ = chunk_steps(c0), chunk_steps(c0 + 1)
            for i in range(len(sa)):
                sa[i]()
                sb_[i]()
        for tt in range(4):
            for g in range(4):
                for m2 in range(2):
                    m = 2 * g + m2
                    b = self.psrot % 8
                    self.psrot += 1
                    for kk in range(2):
                        S.op("pe", lambda e, b=b, g=g, kk=kk, m2=m2, tt=tt: e.matmul(
                            self.bank(b), lhsT=pw[:, g, kk, m2 * 128:(m2 + 1) * 128],
                            rhs=self.hT[:, 2 * g + kk, tt * 512:(tt + 1) * 512], start=(kk == 0), stop=(kk == 1)),
                            reads=["pw", ("h", 2 * g + kk, tt)], writes=[("ps", b)])
                    xs = self.xT[:, m, tt * 512:(tt + 1) * 512]
                    S.op("dve", lambda e, b=b, xs=xs, m=m: e.scalar_tensor_tensor(
                        out=xs, in0=self.bank(b), scalar=self.pcol("pscale", m), in1=xs, op0=ALU.mult, op1=ALU.add),
                        reads=[("ps", b), ("x", m, tt), "params"], writes=[("x", m, tt)])
            self.tail_hook(tt)


ALL_PHASES = []
for _l in range(DEPTH):
    ALL_PHASES += [("mix", _l), ("ffn", _l)]

WEIGHT_KEYS = ["ffn_w_up", "ffn_w_down", "attn_w_qkv", "attn_w_o", "lru_w_in", "lru_w_a", "lru_w_x",
               "lru_w_out", "pool_w"]


def run_phases(inputs, phases, x_cores=None, trace=False, **bkw):
    nc = Builder(phases, **bkw).build()
    params = pack_params(inputs)
    c16, cf = make_consts()
    if x_cores is None:
        x = np.asarray(inputs["x"], np.float32)
        x_cores = [np.ascontiguousarray(x[b].T) for b in range(8)]
    shared = {k: np.ascontiguousarray(np.asarray(inputs[k], np.float32)) for k in WEIGHT_KEYS}
    shared.update({"params": params, "c16": c16, "cf": cf})
    in_maps = []
    for b in range(8):
        m = dict(shared)
        m["xT"] = x_cores[b]
        in_maps.append(m)
    res = run_bass_kernel_spmd(nc, in_maps, core_ids=list(range(8)), trace=trace)
    return [r["outT"] for r in res.results], res


def kernel(**inputs):
    outs, _ = run_phases(inputs, ALL_PHASES)
    return np.stack([np.ascontiguousarray(o.T) for o in outs], axis=0).astype(np.float32)
```

```python
import contextlib
import numpy as np
import concourse.bass as bass
import concourse.mybir as mybir
from concourse.bass_utils import run_bass_kernel_spmd

F32 = mybir.dt.float32
BF16 = mybir.dt.bfloat16
AF = mybir.ActivationFunctionType
ALU = mybir.AluOpType
AX = mybir.AxisListType

D = 1024
S_ = 2048
NC_ = 8
DEPTH = 4
FH = 2816
NPAIR = 22
DR = 1280
NLC = 10
EPS = 1e-6
NEG = -30000.0
ENGS = ("pe", "act", "dve", "pool", "sp")


class Op:
    __slots__ = ("eng", "fn", "deps", "dma", "signal", "semv")

    def __init__(self, eng, fn, dma):
        self.eng = eng
        self.fn = fn
        self.deps = []
        self.dma = dma
        self.signal = False
        self.semv = None


class _Rec:
    def __init__(self):
        self.calls = []

    def __getattr__(self, name):
        def f(*args, **kwargs):
            self.calls.append((name, args, kwargs))
        return f


class Sched:
    N_DMA_SEMS = 8

    def __init__(self, nc):
        self.nc = nc
        self.ops = {e: [] for e in ENGS}
        self.last_writer = {}
        self.readers = {}
        self.fence_pending = set()
        self.fence_ops = []
        self.last_scr = {}
        self.cur_nofence = False

    def fence(self):
        self.fence_ops = [self.last_scr[e] for e in ENGS if e in self.last_scr]
        self.fence_pending = set(ENGS)

    def op(self, eng, fn, reads=(), writes=(), dma=False):
        rec = _Rec()
        fn(rec)
        name, args, kwargs = rec.calls[0]
        o = Op(eng, lambda e: getattr(e, name)(*args, **kwargs), dma)
        cand = []
        for t in reads:
            w = self.last_writer.get(t)
            if w is not None:
                cand.append((w, True))
        for t in writes:
            w = self.last_writer.get(t)
            if w is not None:
                cand.append((w, False))
            for r in self.readers.get(t, ()):
                cand.append((r, False))
        seen = set()
        for d, raw in cand:
            if d is o or id(d) in seen:
                continue
            if d.eng == eng and not d.dma and not dma:
                if eng == "pe" or not raw:
                    continue
            seen.add(id(d))
            o.deps.append(d)
        if eng in self.fence_pending:
            self.fence_pending.discard(eng)
            for d in self.fence_ops:
                if id(d) in seen or (d.eng == eng and not d.dma and not dma):
                    continue
                seen.add(id(d))
                o.deps.append(d)
        self.ops[eng].append(o)
        if not self.cur_nofence:
            self.last_scr[eng] = o
        for t in writes:
            self.last_writer[t] = o
            self.readers[t] = []
        for t in reads:
            self.readers.setdefault(t, []).append(o)
        return o

    def emit(self, final_waits=()):
        nc = self.nc
        for e in ENGS:
            for o in self.ops[e]:
                for d in o.deps:
                    d.signal = True
        for o in final_waits:
            o.signal = True
        with contextlib.ExitStack() as st:
            sems = {e: st.enter_context(nc.semaphore("s_" + e)) for e in ENGS}
            for e in ("sp", "pool", "act"):
                for k in range(self.N_DMA_SEMS):
                    sems[(e, k)] = st.enter_context(nc.semaphore("d_%s%d" % (e, k)))
            for e in ENGS:
                c = 0
                dcount = [0] * self.N_DMA_SEMS
                nd = 0
                for o in self.ops[e]:
                    if o.dma:
                        k = nd % self.N_DMA_SEMS
                        nd += 1
                        dcount[k] += 1
                        o.semv = ((e, k), 16 * dcount[k])
                    elif o.signal:
                        c += 1
                        o.semv = (e, c)
            block = st.enter_context(nc.Block())
            engobj = {"pe": block.tensor, "act": block.scalar, "dve": block.vector,
                      "pool": block.gpsimd, "sp": block.sync}

            def make(e):
                def body(eng):
                    known = {}
                    for o in self.ops[e]:
                        waits = {}
                        for d in o.deps:
                            sk, v = d.semv
                            if known.get(sk, 0) >= v:
                                continue
                            if waits.get(sk, 0) < v:
                                waits[sk] = v
                        if o.dma:
                            sk, v = o.semv
                            if v > 16 and known.get(sk, 0) < v - 16 and waits.get(sk, 0) < v - 16:
                                waits[sk] = v - 16
                        for sk, v in waits.items():
                            eng.wait_ge(sems[sk], v)
                            known[sk] = v
                        ins = o.fn(eng)
                        if o.semv is not None:
                            ins.then_inc(sems[o.semv[0]], 16 if o.dma else 1)
                    if e == "sp":
                        for o in final_waits:
                            eng.wait_ge(sems[o.semv[0]], o.semv[1])
                return body

            for e in ENGS:
                engobj[e](make(e))


class Cols:
    def __init__(self):
        self.n = 0
        self.off = {}

    def add(self, name, k):
        self.off[name] = self.n
        self.n += k


PC = Cols()
PC.add("nmix", DEPTH * NC_)
PC.add("nffn", DEPTH * NC_)
PC.add("qg", 2)
PC.add("kg", 2)
PC.add("lcw", 4 * NLC)
PC.add("lcb", NLC)
PC.add("lba", NLC)
PC.add("lbx", NLC)
PC.add("llam", NLC)
PC.add("pscale", NC_)
PC.add("fcw", DEPTH * 3 * 44)
PC.add("fcb", DEPTH * 44)

CC = Cols()
CC.add("ident", 128)
CC.add("ones", 128)
CC.add("cb", 2 * 256)
CC.add("esel", 8 * 128)
NCB = CC.n
CF = Cols()
CF.add("pastmask", 64)
CF.add("icnt", 4 * 16)
CF.add("eps", 1)


def pack_params(inp):
    P = np.zeros((128, PC.n), np.float32)

    def put(name, arr):
        a = np.asarray(arr, np.float32)
        a = a.reshape(-1, a.shape[-1] // 128, 128)
        a = a.transpose(2, 0, 1).reshape(128, -1)
        P[:, PC.off[name]:PC.off[name] + a.shape[1]] = a

    put("nmix", inp["norm_mix_g"])
    put("nffn", inp["norm_ffn_g"])
    put("qg", inp["attn_q_g"])
    put("kg", inp["attn_k_g"])
    put("lcw", inp["lru_conv_w"][0])
    put("lcb", inp["lru_conv_b"][0])
    put("lba", inp["lru_b_a"][0])
    put("lbx", inp["lru_b_x"][0])
    put("llam", inp["lru_lambda"][0])
    put("pscale", inp["pool_scale"][0])
    put("fcw", inp["ffn_conv_w"])
    put("fcb", inp["ffn_conv_b"])
    return P


def make_consts():
    cb16 = np.zeros((128, CC.n), np.float32)
    cb16[:, CC.off["ident"]:CC.off["ident"] + 128] = np.eye(128, dtype=np.float32)
    cb16[:, CC.off["ones"]:CC.off["ones"] + 128] = 1.0
    p = np.arange(128)[:, None]
    q = np.arange(256)[None, :]
    for j in range(2):
        m = np.where(j * 128 + p <= q, 0.0, NEG).astype(np.float32)
        cb16[:, CC.off["cb"] + j * 256:CC.off["cb"] + (j + 1) * 256] = m
    es = np.zeros((128, 8, 128), np.float32)
    for n in range(8):
        es[n, n, :] = 1.0
    cb16[:, CC.off["esel"]:CC.off["esel"] + 1024] = es.reshape(128, 1024)
    cf = np.zeros((128, CF.n), np.float32)
    pm = np.zeros((8, 8), np.float32)
    for i in range(8):
        for n in range(8):
            pm[i, n] = 0.0 if n < 4 + i // 2 else -1e30
    cf[:, CF.off["pastmask"]:CF.off["pastmask"] + 64] = pm.reshape(1, 64)
    ic = np.zeros((4, 16), np.float32)
    for g, w in enumerate((2, 4, 8, 16)):
        for t in range(16):
            ic[g, t] = 1.0 / min(t + 1, w)
    cf[:, CF.off["icnt"]:CF.off["icnt"] + 64] = ic.reshape(1, 64)
    cf[:, CF.off["eps"]] = EPS
    return cb16, cf


class Builder:
    def __init__(self, phases, attn_heads=8, attn_hg=4):
        self.phases = phases
        self.attn_heads = attn_heads
        self.attn_hg = attn_hg
        nc = bass.Bass("TRN2", target_bir_lowering=False)
        self.nc = nc
        dt = nc.dram_tensor
        self.d_x = dt("xT", [D, S_], F32, kind="ExternalInput").ap()
        self.d_out = dt("outT", [D, S_], F32, kind="ExternalOutput").ap()
        self.d_params = dt("params", [128, PC.n], F32, kind="ExternalInput").ap()
        self.d_c16 = dt("c16", [128, CC.n], F32, kind="ExternalInput").ap()
        self.d_cf = dt("cf", [128, CF.n], F32, kind="ExternalInput").ap()
        self.d_wup = dt("ffn_w_up", [DEPTH, D, 2 * FH], F32, kind="ExternalInput").ap()
        self.d_wdn = dt("ffn_w_down", [DEPTH, FH, D], F32, kind="ExternalInput").ap()
        self.d_wqkv = dt("attn_w_qkv", [2, D, 3 * D], F32, kind="ExternalInput").ap()
        self.d_wo = dt("attn_w_o", [2, D, D], F32, kind="ExternalInput").ap()
        self.d_lwin = dt("lru_w_in", [1, D, 2 * DR], F32, kind="ExternalInput").ap()
        self.d_lwa = dt("lru_w_a", [1, NLC, 128, 128], F32, kind="ExternalInput").ap()
        self.d_lwx = dt("lru_w_x", [1, NLC, 128, 128], F32, kind="ExternalInput").ap()
        self.d_lwout = dt("lru_w_out", [1, DR, D], F32, kind="ExternalInput").ap()
        self.d_pw = dt("pool_w", [1, 4, 256, 256], F32, kind="ExternalInput").ap()
        self.S = Sched(nc)
        self.psrot = 0
        self.uid = 0
        self.out_dmas = []

    def sb(self, st, name, shape, dtype):
        self.uid += 1
        return st.enter_context(self.nc.sbuf_tensor("%s_u%d" % (name, self.uid), shape, dtype))

    def pcol(self, name, idx):
        o = PC.off[name] + idx
        return self.params[:, o:o + 1]

    def build(self):
        nc, S = self.nc, self.S
        with contextlib.ExitStack() as st:
            self.xT = self.sb(st, "xT_sb", [128, NC_, S_], F32)
            self.hT = self.sb(st, "hT_sb", [128, NC_, S_], BF16)
            self.params = self.sb(st, "params_sb", [128, PC.n], F32)
            self.c16 = self.sb(st, "c16_sb", [128, CC.n], BF16)
            self.cf = self.sb(st, "cf_sb", [128, CF.n], F32)
            self.sq = self.sb(st, "sq_sb", [128, NC_, 512], BF16)
            self.lnv = self.sb(st, "lnv_sb", [128, 512], F32)
            self.rstd = self.sb(st, "rstd_sb", [128, 512], F32)
            self.PS = st.enter_context(nc.psum_tensor("PS", [128, 4096], F32))
            self.ident = self.c16[:, CC.off["ident"]:CC.off["ident"] + 128]
            self.ones = self.c16[:, CC.off["ones"]:CC.off["ones"] + 128]
            self.eps = self.cf[:, CF.off["eps"]:CF.off["eps"] + 1]

            S.op("sp", lambda e: e.dma_start(out=self.params[:], in_=self.d_params[:, :]),
                 writes=["params"], dma=True)
            S.op("sp", lambda e: e.dma_start(out=self.cf[:], in_=self.d_cf[:, :]),
                 writes=["cf"], dma=True)
            S.op("pool", lambda e: e.dma_start(out=self.c16[:], in_=self.d_c16[:, :]),
                 writes=["c16"], dma=True)
            for tt in range(4):
                for c in range(NC_):
                    S.op("sp", lambda e, c=c, tt=tt: e.dma_start(
                        out=self.xT[:, c, tt * 512:(tt + 1) * 512],
                        in_=self.d_x[c * 128:(c + 1) * 128, tt * 512:(tt + 1) * 512]),
                        writes=[("x", c, tt)], dma=True)

            self.wfirst = self.sb(st, "wfirst", [128, NC_, 512], BF16)
            nph = len(self.phases)
            self.prefetch_first(self.phases[0])
            self.start_norm(self.phases[0])
            for i, ph in enumerate(self.phases):
                kind, layer = ph
                self.next_phase = self.phases[i + 1] if i + 1 < nph else None
                S.fence()
                with contextlib.ExitStack() as st2:
                    if kind == "ffn":
                        self.ffn(st2, layer)
                    else:
                        mk = layer % 3
                        if mk == 0:
                            self.attn(st2, layer)
                        elif mk == 1:
                            self.lru(st2, layer)
                        else:
                            self.pool(st2, layer)

            S.emit(final_waits=self.out_dmas)
        return nc

    def bank(self, b):
        return self.PS[:, b * 512:(b + 1) * 512]

    def normA(self, tt):
        ts = slice(tt * 512, (tt + 1) * 512)
        self.S.op("act", lambda e: e.activation(out=self.sq[:], in_=self.xT[:, :, ts], func=AF.Square),
                  reads=[("x", c, tt) for c in range(NC_)], writes=["sq"])

    def normB(self, gname, layer, tt):
        S = self.S
        ts = slice(tt * 512, (tt + 1) * 512)
        b = self.psrot % 8
        self.psrot += 1
        for c in range(NC_):
            S.op("pe", lambda e, c=c: e.matmul(self.bank(b), lhsT=self.ones, rhs=self.sq[:, c, :],
                                               start=(c == 0), stop=(c == NC_ - 1)),
                 reads=["sq", "c16"], writes=[("ps", b)])
        S.op("act", lambda e: e.activation(out=self.lnv[:], in_=self.bank(b), func=AF.Ln,
                                           scale=1.0 / D, bias=self.eps),
             reads=[("ps", b), "cf"], writes=["lnv"])
        S.op("act", lambda e: e.activation(out=self.rstd[:], in_=self.lnv[:], func=AF.Exp, scale=-0.5),
             reads=["lnv"], writes=["rstd"])
        for c in range(NC_):
            S.op("dve", lambda e, c=c: e.scalar_tensor_tensor(
                out=self.hT[:, c, ts], in0=self.xT[:, c, ts], scalar=self.pcol(gname, layer * NC_ + c),
                in1=self.rstd[:], op0=ALU.mult, op1=ALU.mult),
                reads=[("x", c, tt), "rstd", "params"], writes=[("h", c, tt)])

    @staticmethod
    def norm_of(ph):
        if ph is None:
            return None
        kind, layer = ph
        if kind == "ffn":
            return ("nffn", layer)
        if layer % 3 == 2:
            return None
        return ("nmix", layer)

    def start_norm(self, ph):
        nm = self.norm_of(ph)
        if nm is None:
            return
        for tt in range(4):
            self.normA(tt)
            self.normB(nm[0], nm[1], tt)

    def tail_hook(self, tt):
        self.S.cur_nofence = True
        try:
            self._tail_hook(tt)
        finally:
            self.S.cur_nofence = False

    def _tail_hook(self, tt):
        nm = self.norm_of(self.next_phase)
        if self.next_phase is None:
            for c in range(NC_):
                self.out_dmas.append(self.S.op("sp", lambda e, c=c: e.dma_start(
                    out=self.d_out[c * 128:(c + 1) * 128, tt * 512:(tt + 1) * 512],
                    in_=self.xT[:, c, tt * 512:(tt + 1) * 512]),
                    reads=[("x", c, tt)], dma=True))
            return
        if tt == 0:
            self.prefetch_first(self.next_phase)
        if nm is None:
            return
        if tt >= 1:
            self.normB(nm[0], nm[1], tt - 1)
        self.normA(tt)
        if tt == 3:
            self.normB(nm[0], nm[1], 3)

    def prefetch_first(self, ph):
        S = self.S
        kind, layer = ph
        wf = self.wfirst
        if kind == "ffn":
            src = self.d_wup[layer].rearrange("(c p) f -> p c f", p=128)
            S.op("pool", lambda e: e.dma_start(out=wf[:, :, 0:256], in_=src[:, :, 0:256]),
                 writes=[("wf", 0), ("wf", 1)], dma=True)
            S.op("pool", lambda e: e.dma_start(out=wf[:, :, 256:512], in_=src[:, :, FH:FH + 256]),
                 writes=[("wf", 2), ("wf", 3)], dma=True)
        elif layer % 3 == 0:
            src = self.d_wqkv[layer // 3].rearrange("(c p) f -> p c f", p=128)
            for k in range(3):
                S.op("pool", lambda e, k=k: e.dma_start(out=wf[:, :, k * 128:(k + 1) * 128],
                                                        in_=src[:, :, k * D:k * D + 128]),
                     writes=[("wf", k)], dma=True)
        elif layer % 3 == 1:
            src = self.d_lwin[0].rearrange("(c p) f -> p c f", p=128)
            S.op("pool", lambda e: e.dma_start(out=wf[:, :, 0:128], in_=src[:, :, 0:128]),
                 writes=[("wf", 0)], dma=True)
            S.op("pool", lambda e: e.dma_start(out=wf[:, :, 128:256], in_=src[:, :, DR:DR + 128]),
                 writes=[("wf", 1)], dma=True)

    def ffn(self, st, layer):
        nc, S = self.nc, self.S
        GROUPS = [(0, 6), (6, 12), (12, 17), (17, 22)]
        NP = 7
        NW = 3
        wup = [self.wfirst] + [self.sb(st, "wup%d" % i, [128, NC_, 512], BF16) for i in range(1, NW)]
        wdn = self.sb(st, "wdn", [128, 6, D], BF16)
        Pb = self.sb(st, "Pb", [128, NP, S_], BF16)
        cbuf = [self.sb(st, "cbuf%d" % i, [128, S_], F32) for i in range(3)]
        wup_src = self.d_wup[layer].rearrange("(c p) f -> p c f", p=128)

        def wtok(bi, half):
            if bi == 0:
                return [("wf", 2 * half), ("wf", 2 * half + 1)]
            return [("wup", bi, half)]

        def load_slab(s):
            bi = s % NW
            S.op("pool", lambda e: e.dma_start(out=wup[bi][:, :, 0:256],
                                               in_=wup_src[:, :, s * 256:(s + 1) * 256]),
                 writes=wtok(bi, 0), dma=True)
            S.op("pool", lambda e: e.dma_start(out=wup[bi][:, :, 256:512],
                                               in_=wup_src[:, :, FH + s * 256:FH + (s + 1) * 256]),
                 writes=wtok(bi, 1), dma=True)

        def load_wdn(j0, j1):
            S.op("pool", lambda e: e.dma_start(
                out=wdn[:, 0:j1 - j0, :],
                in_=self.d_wdn[layer, j0 * 128:j1 * 128, :].rearrange("(j p) d -> p j d", p=128)),
                writes=["wdn"], dma=True)

        cbi = [0]

        def up(j):
            s, r = j // 2, j % 2
            bi = s % NW
            cg = None
            for half in range(2):
                fj = half * NPAIR + j
                base = half * 2048
                lcol = half * 256 + r * 128
                for tt in range(4):
                    for c in range(NC_):
                        S.op("pe", lambda e, c=c, tt=tt: e.matmul(
                            self.PS[:, base + tt * 512: base + (tt + 1) * 512],
                            lhsT=wup[bi][:, c, lcol:lcol + 128],
                            rhs=self.hT[:, c, tt * 512:(tt + 1) * 512],
                            start=(c == 0), stop=(c == NC_ - 1)),
                            reads=wtok(bi, half) + [("h", c, tt)], writes=[("ps", half * 4 + tt)])
                if half == 1 and r == 1 and s + NW < 11:
                    load_slab(s + NW)
                cb = cbuf[cbi[0] % 3]
                cbk = cbi[0] % 3
                cbn = [("cbuf", cbk, 0), ("cbuf", cbk, 1)]
                cbi[0] += 1
                w0 = self.pcol("fcw", (layer * 3 + 0) * 44 + fj)
                w1 = self.pcol("fcw", (layer * 3 + 1) * 44 + fj)
                w2 = self.pcol("fcw", (layer * 3 + 2) * 44 + fj)
                bb = self.pcol("fcb", layer * 44 + fj)
                u = self.PS[:, base:base + 2048]
                for hh in range(2):
                    lo, hi = hh * 1024, (hh + 1) * 1024
                    psr = [("ps", half * 4 + 2 * hh), ("ps", half * 4 + 2 * hh + 1)]
                    if hh == 1:
                        psr.append(("ps", half * 4 + 1))
                    ct = [("cbuf", cbk, hh)]
                    S.op("act", lambda e: e.activation(out=cb[:, lo:hi], in_=u[:, lo:hi], func=AF.Identity,
                                                       scale=w2, bias=bb),
                         reads=psr + ["params"], writes=ct)
                    for k, wk in ((1, w1), (2, w0)):
                        o0 = max(lo, k)
                        S.op("dve", lambda e, o0=o0, k=k, wk=wk: e.scalar_tensor_tensor(
                            out=cb[:, o0:hi], in0=u[:, o0 - k:hi - k], scalar=wk, in1=cb[:, o0:hi],
                            op0=ALU.mult, op1=ALU.add), reads=psr + ct + ["params"], writes=ct)
                for hh in range(2):
                    lo, hi = hh * 1024, (hh + 1) * 1024
                    if half == 0:
                        S.op("act", lambda e: e.activation(out=cb[:, lo:hi], in_=cb[:, lo:hi], func=AF.Silu),
                             reads=[cbn[hh]], writes=[cbn[hh]])
                    else:
                        S.op("pool", lambda e: e.tensor_tensor(out=Pb[:, j % NP, lo:hi], in0=cg[0][:, lo:hi],
                                                               in1=cb[:, lo:hi], op=ALU.mult),
                             reads=[cbn[hh], cg[1][hh]], writes=[("P", j % NP, hh)])
                if half == 0:
                    cg = (cb, cbn)

        def down(j0, j1, nbanks=4, final=False):
            nj = j1 - j0
            for tt in range(4):
                for m in range(NC_):
                    b = self.psrot % nbanks
                    self.psrot += 1
                    for jj in range(nj):
                        slot = (j0 + jj) % NP
                        S.op("pe", lambda e, jj=jj, slot=slot: e.matmul(
                            self.bank(b), lhsT=wdn[:, jj, m * 128:(m + 1) * 128],
                            rhs=Pb[:, slot, tt * 512:(tt + 1) * 512],
                            start=(jj == 0), stop=(jj == nj - 1)),
                            reads=["wdn", ("P", slot, tt // 2)], writes=[("ps", b)])
                    xs = self.xT[:, m, tt * 512:(tt + 1) * 512]
                    S.op("dve", lambda e: e.tensor_tensor(out=xs, in0=self.bank(b), in1=xs, op=ALU.add),
                         reads=[("ps", b), ("x", m, tt)], writes=[("x", m, tt)])
                if final:
                    self.tail_hook(tt)

        for s0 in range(1, NW):
            load_slab(s0)
        load_wdn(*GROUPS[0])
        gidx = {}
        for gi, (j0, j1) in enumerate(GROUPS):
            for j in range(j0, j1):
                gidx[j] = gi
        for j in range(NPAIR):
            up(j)
            gi = gidx[j]
            j0, j1 = GROUPS[gi]
            if j == j0 and gi > 0:
                down(*GROUPS[gi - 1])
            if j == j0 + 2 and gi > 0:
                load_wdn(j0, j1)
        down(*GROUPS[-1], nbanks=8, final=True)

    def rstd_tt(self, tt, dst, dst_tok):
        S = self.S
        ts = slice(tt * 512, (tt + 1) * 512)
        b = self.psrot % 8
        self.psrot += 1
        S.op("act", lambda e: e.activation(out=self.sq[:], in_=self.xT[:, :, ts], func=AF.Square),
             reads=[("x", c, tt) for c in range(NC_)], writes=["sq"])
        for c in range(NC_):
            S.op("pe", lambda e, c=c: e.matmul(self.bank(b), lhsT=self.ones, rhs=self.sq[:, c, :],
                                               start=(c == 0), stop=(c == NC_ - 1)),
                 reads=["sq", "c16"], writes=[("ps", b)])
        S.op("act", lambda e: e.activation(out=self.lnv[:], in_=self.bank(b), func=AF.Ln,
                                           scale=1.0 / D, bias=self.eps),
             reads=[("ps", b), "cf"], writes=["lnv"])
        S.op("act", lambda e: e.activation(out=dst, in_=self.lnv[:], func=AF.Exp, scale=-0.5),
             reads=["lnv"], writes=[dst_tok])

    def resid_proj(self, w, nk, rhs_fn, rhs_toks, wtok, final=False):
        S = self.S
        for tt in range(4):
            for m in range(NC_):
                b = self.psrot % 8
                self.psrot += 1
                for k in range(nk):
                    S.op("pe", lambda e, b=b, k=k, m=m, tt=tt: e.matmul(
                        self.bank(b), lhsT=w[:, k, m * 128:(m + 1) * 128], rhs=rhs_fn(k, tt),
                        start=(k == 0), stop=(k == nk - 1)),
                        reads=[wtok] + rhs_toks(k, tt), writes=[("ps", b)])
                xs = self.xT[:, m, tt * 512:(tt + 1) * 512]
                S.op("dve", lambda e, b=b, xs=xs: e.tensor_tensor(
                    out=xs, in0=self.bank(b), in1=xs, op=ALU.add),
                    reads=[("ps", b), ("x", m, tt)], writes=[("x", m, tt)])
            if final:
                self.tail_hook(tt)

    def attn(self, st, layer):
        nc, S = self.nc, self.S
        slot = layer // 3
        HG = self.attn_hg
        NH = self.attn_heads
        wqkv = [self.wfirst, self.sb(st, "wqkv1", [128, NC_, 384], BF16)]

        def wqt(bi, k):
            return ("wf", k) if bi == 0 else ("wqkv", bi, k)

        qn = [self.sb(st, "qn%d" % i, [128, S_], BF16) for i in range(2)]
        kn = [self.sb(st, "kn%d" % i, [128, S_], BF16) for i in range(2)]
        Vt = [self.sb(st, "Vt%d" % i, [128, 16, 128], BF16) for i in range(2)]
        oT = self.sb(st, "oT", [128, HG, S_], BF16)
        wo = self.sb(st, "wo", [128, HG, D], BF16)
        rs = [self.sb(st, "rs%d" % i, [128, 512], F32) for i in range(2)]
        sq2 = [self.sb(st, "sq2_%d" % i, [128, 512], BF16) for i in range(2)]
        PT = [self.sb(st, "PT%d" % i, [128, 512], BF16) for i in range(4)]
        kmf = self.sb(st, "kmf", [128, 8], F32)
        kmb = self.sb(st, "kmb", [128, 8], BF16)
        g1 = self.sb(st, "g1", [128, 64], F32)
        top = self.sb(st, "top", [128, 64], F32)
        cmpf = self.sb(st, "cmpf", [128, 64], F32)
        btok = self.sb(st, "btok", [128, 8, 128], BF16)
        biasT = self.sb(st, "biasT", [128, 1024], BF16)
        rden = [self.sb(st, "rden%d" % i, [128, 256], F32) for i in range(2)]
        S.op("pool", lambda e: e.memset(btok[:], 0.0), writes=["btok"])
        S.op("pool", lambda e: e.memset(biasT[:], 0.0), writes=["biasT"])
        wsrc = self.d_wqkv[slot].rearrange("(c p) f -> p c f", p=128)
        scale = 128.0 ** -0.5
        pastmask = self.cf[:, CF.off["pastmask"]:CF.off["pastmask"] + 64]
        cbm = [self.c16[:, CC.off["cb"] + j * 256:CC.off["cb"] + (j + 1) * 256] for j in range(2)]
        esel = [self.c16[:, CC.off["esel"] + n * 128:CC.off["esel"] + (n + 1) * 128] for n in range(8)]
        cnt = {"s": 0, "p": 0, "k2": 0, "pt": 0}

        def sbank():
            b = 2 + cnt["s"] % 3
            cnt["s"] += 1
            return b


        def load_w(hd):
            bi = hd % 2
            for k in range(3):
                S.op("pool", lambda e, k=k: e.dma_start(
                    out=wqkv[bi][:, :, k * 128:(k + 1) * 128],
                    in_=wsrc[:, :, k * D + hd * 128:k * D + (hd + 1) * 128]),
                    writes=[wqt(bi, k)], dma=True)

        def prologue_units(hd):
            bi = hd % 2
            units = []
            items = [(which, tt) for which in (0, 1) for tt in range(4)]
            state = {}

            def A1(i):
                which, tt = items[i]
                ts = slice(tt * 512, (tt + 1) * 512)
                b = 5 + cnt["p"] % 3
                cnt["p"] += 1
                k2 = cnt["k2"] % 2
                cnt["k2"] += 1
                state[i] = (b, k2)
                for c in range(NC_):
                    S.op("pe", lambda e, c=c: e.matmul(
                        self.bank(b), lhsT=wqkv[bi][:, c, which * 128:(which + 1) * 128],
                        rhs=self.hT[:, c, ts], start=(c == 0), stop=(c == NC_ - 1)),
                        reads=[wqt(bi, which), ("h", c, tt)], writes=[("ps", b)])
                S.op("act", lambda e: e.activation(out=sq2[k2][:], in_=self.bank(b), func=AF.Square),
                     reads=[("ps", b)], writes=[("sq2", k2)])

            def A2(i):
                which, tt = items[i]
                ts = slice(tt * 512, (tt + 1) * 512)
                b, k2 = state[i]
                dst = (qn if which == 0 else kn)[bi]
                gname = "qg" if which == 0 else "kg"
                b7 = sbank()
                S.op("pe", lambda e: e.matmul(self.bank(b7), lhsT=self.ones, rhs=sq2[k2][:], start=True, stop=True),
                     reads=[("sq2", k2), "c16"], writes=[("ps", b7)])
                S.op("act", lambda e: e.activation(out=rs[k2][:], in_=self.bank(b7), func=AF.Ln,
                                                   scale=1.0 / 128, bias=self.eps),
                     reads=[("ps", b7), "cf"], writes=[("rs", k2)])
                S.op("act", lambda e: e.activation(out=rs[k2][:], in_=rs[k2][:], func=AF.Exp, scale=-0.5),
                     reads=[("rs", k2)], writes=[("rs", k2)])
                S.op("dve", lambda e: e.scalar_tensor_tensor(
                    out=dst[:, ts], in0=self.bank(b), scalar=self.pcol(gname, slot), in1=rs[k2][:],
                    op0=ALU.mult, op1=ALU.mult),
                    reads=[("ps", b), ("rs", k2), "params"], writes=[("qk", which, bi, tt)])

            units.append(lambda: A1(0))
            for i in range(1, 8):
                units.append(lambda i=i: A1(i))
                units.append(lambda i=i: A2(i - 1))
            units.append(lambda: A2(7))

            def Vunit(bq):
                b = 5 + cnt["p"] % 3
                cnt["p"] += 1
                for i4 in range(4):
                    i = bq * 4 + i4
                    for c in range(NC_):
                        S.op("pe", lambda e, i=i, i4=i4, c=c: e.matmul(
                            self.bank(b)[:, i4 * 128:(i4 + 1) * 128], lhsT=self.hT[:, c, i * 128:(i + 1) * 128],
                            rhs=wqkv[bi][:, c, 256:384], start=(c == 0), stop=(c == NC_ - 1)),
                            reads=[wqt(bi, 2), ("h", c, bq)], writes=[("ps", b)])
                S.op("act", lambda e: e.activation(
                    out=Vt[bi][:, bq * 4:(bq + 1) * 4, :], in_=self.bank(b).rearrange("p (i d) -> p i d", d=128),
                    func=AF.Identity), reads=[("ps", b)], writes=[("V", bi, bq)])

            for bq in range(4):
                units.append(lambda bq=bq: Vunit(bq))

            def Cunit():
                S.op("dve", lambda e: e.tensor_reduce(out=kmf[:], in_=kn[bi][:].rearrange("p (n k) -> p n k", k=256),
                                                      axis=AX.X, op=ALU.add),
                     reads=[("qk", 1, bi, tt) for tt in range(4)], writes=["kmf"])
                S.op("dve", lambda e: e.tensor_scalar(out=kmb[:], in0=kmf[:], scalar1=1.0 / 256, scalar2=None,
                                                      op0=ALU.mult), reads=["kmf"], writes=["kmb"])
                bg = sbank()
                for i in range(8):
                    S.op("pe", lambda e, i=i: e.matmul(
                        self.bank(bg)[:, i * 8:(i + 1) * 8], lhsT=qn[bi][:, (8 + i) * 128:(9 + i) * 128], rhs=kmb[:],
                        start=True, stop=True),
                        reads=["kmb", ("qk", 0, bi, 2 + i // 4)], writes=[("ps", bg)])
                S.op("dve", lambda e: e.tensor_tensor(out=g1[:], in0=self.bank(bg)[:, 0:64], in1=pastmask, op=ALU.add),
                     reads=[("ps", bg), "cf"], writes=["g1"])
                for i in range(8):
                    S.op("dve", lambda e, i=i: e.max(out=top[:, i * 8:(i + 1) * 8], in_=g1[:, i * 8:(i + 1) * 8]),
                         reads=["g1"], writes=["top"])
                S.op("dve", lambda e: e.tensor_tensor(
                    out=cmpf[:].rearrange("p (i n) -> p i n", n=8), in0=g1[:].rearrange("p (i n) -> p i n", n=8),
                    in1=top[:].rearrange("p (i n) -> p i n", n=8)[:, :, 2:3].to_broadcast([128, 8, 8]), op=ALU.is_lt),
                    reads=["g1", "top"], writes=["cmpf"])

            units.append(Cunit)
            return units

        def Dunit(hd):
            S.op("dve", lambda e: e.tensor_scalar(out=btok[:, :, 0:8], in0=cmpf[:].rearrange("p (i n) -> p i n", n=8),
                                                  scalar1=NEG, scalar2=None, op0=ALU.mult),
                 reads=["cmpf"], writes=["btok"])
            for k in range(2):
                bd_ = sbank()
                for i4 in range(4):
                    i = k * 4 + i4
                    S.op("pe", lambda e, i=i, i4=i4: e.matmul(
                        self.bank(bd_)[:, i4 * 128:(i4 + 1) * 128], lhsT=btok[:, i, :], rhs=self.ident,
                        start=True, stop=True),
                        reads=["btok", "c16"], writes=[("ps", bd_)])
                S.op("act", lambda e, k=k: e.activation(out=biasT[0:8, k * 512:(k + 1) * 512],
                                                        in_=self.bank(bd_)[0:8, :], func=AF.Identity),
                     reads=[("ps", bd_)], writes=["biasT"])

        def E1(hd, qb, n, st_):
            bi = hd % 2
            qs = slice(qb * 256, (qb + 1) * 256)
            bs_ = sbank()
            pk = cnt["pt"] % 4
            cnt["pt"] += 1
            st_[(qb, n)] = pk
            for half in range(2):
                kt = 2 * n + half
                osl = self.bank(bs_)[:, half * 256:(half + 1) * 256]
                extra = (n == qb) or (qb >= 4)
                S.op("pe", lambda e, osl=osl, kt=kt, extra=extra: e.matmul(
                    osl, lhsT=kn[bi][:, kt * 128:(kt + 1) * 128], rhs=qn[bi][:, qs],
                    start=True, stop=(not extra)),
                    reads=[("qk", 1, bi, kt // 4), ("qk", 0, bi, qb // 2)], writes=[("ps", bs_)])
                if n == qb:
                    S.op("pe", lambda e, osl=osl, half=half: e.matmul(
                        osl, lhsT=self.ident, rhs=cbm[half], start=False, stop=True),
                        reads=["c16"], writes=[("ps", bs_)])
                elif qb >= 4:
                    S.op("pe", lambda e, osl=osl: e.matmul(
                        osl, lhsT=esel[n], rhs=biasT[:, (qb - 4) * 256:(qb - 3) * 256],
                        start=False, stop=True),
                        reads=["c16", "biasT"], writes=[("ps", bs_)])
            S.op("act", lambda e: e.activation(out=PT[pk][:], in_=self.bank(bs_), func=AF.Exp, scale=scale),
                 reads=[("ps", bs_)], writes=[("PT", pk)])

        def E2(hd, qb, n, st_):
            bi = hd % 2
            qs = slice(qb * 256, (qb + 1) * 256)
            pk = st_[(qb, n)]
            bo = qb % 2
            for half in range(2):
                kt = 2 * n + half
                first = (n == 0 and half == 0)
                last = (n == qb and half == 1)
                S.op("pe", lambda e, kt=kt, half=half, first=first, last=last: e.matmul(
                    self.bank(bo)[:, 0:256], lhsT=Vt[bi][:, kt, :], rhs=PT[pk][:, half * 256:(half + 1) * 256],
                    start=first, stop=last, skip_group_check=True),
                    reads=[("V", bi, kt // 4), ("PT", pk)], writes=[("ps", bo)])
                S.op("pe", lambda e, half=half, last=last: e.matmul(
                    self.bank(bo)[:, 256:512], lhsT=self.ones, rhs=PT[pk][:, half * 256:(half + 1) * 256],
                    start=False, stop=last, skip_group_check=True),
                    reads=["c16", ("PT", pk)], writes=[("ps", bo)])
            if n == qb:
                rk = qb % 2
                S.op("dve", lambda e: e.reciprocal(out=rden[rk][:], in_=self.bank(bo)[:, 256:512]),
                     reads=[("ps", bo)], writes=[("rden", rk)])
                S.op("dve", lambda e: e.tensor_tensor(
                    out=oT[:, hd % HG, qs], in0=self.bank(bo)[:, 0:256], in1=rden[rk][:], op=ALU.mult),
                    reads=[("ps", bo), ("rden", rk)], writes=[("oT", hd % HG, qb // 2)])

        for u in prologue_units(0):
            u()
        LAG = 2
        for hd in range(NH):
            if hd + 1 < NH:
                load_w(hd + 1)
                nxt = prologue_units(hd + 1)
            else:
                nxt = []
            if hd % HG == 0:
                g0 = hd
                S.op("pool", lambda e, g0=g0: e.dma_start(
                    out=wo[:], in_=self.d_wo[slot, g0 * 128:(g0 + HG) * 128, :].rearrange("(h p) d -> p h d", p=128)),
                    writes=["wo"], dma=True)
            items = [(qb, n) for qb in range(8) for n in range(qb + 1)]
            st_ = {}
            for idx in range(len(items) + LAG):
                if idx == 4:
                    Dunit(hd)
                if idx < len(items):
                    E1(hd, items[idx][0], items[idx][1], st_)
                if idx >= LAG:
                    E2(hd, items[idx - LAG][0], items[idx - LAG][1], st_)
                if nxt and idx >= 6:
                    nxt.pop(0)()
            while nxt:
                nxt.pop(0)()
            if hd % HG == HG - 1:
                self.resid_proj(wo, HG, lambda k, tt: oT[:, k, tt * 512:(tt + 1) * 512],
                                lambda k, tt: [("oT", k, tt)], "wo", final=(hd == NH - 1))

    def lru(self, st, layer):
        nc, S = self.nc, self.S
        CG = 5
        NSLOT = 6
        win = [self.wfirst, self.sb(st, "win1", [128, NC_, 256], BF16)]

        def wit(bi, half):
            return ("wf", half) if bi == 0 else ("win", bi, half)

        wa = self.sb(st, "wa", [128, NLC, 128], BF16)
        wx = self.sb(st, "wx", [128, NLC, 128], BF16)
        wout = self.sb(st, "wout", [128, CG, D], BF16)
        hy = self.sb(st, "hy", [128, NSLOT, S_], BF16)
        xc = [self.sb(st, "xc%d" % i, [128, 512], F32) for i in range(4)]
        xcb = [self.sb(st, "xcb%d" % i, [128, 512], BF16) for i in range(4)]
        gy = [self.sb(st, "gy%d" % i, [128, 512], BF16) for i in range(4)]
        A = [self.sb(st, "lruA%d" % i, [128, 512], F32) for i in range(4)]
        I = [self.sb(st, "lruI%d" % i, [128, 512], F32) for i in range(4)]
        M = [self.sb(st, "lruM%d" % i, [128, 512], F32) for i in range(4)]
        us = [self.sb(st, "lruus%d" % i, [128, 4], F32) for i in range(4)]
        dp = self.sb(st, "lrudp", [128, 4 * NLC], F32)
        S.op("pool", lambda e: e.dma_start(out=wa[:], in_=self.d_lwa[0].rearrange("n c d -> c n d")),
             writes=["wa"], dma=True)
        S.op("pool", lambda e: e.dma_start(out=wx[:], in_=self.d_lwx[0].rearrange("n c d -> c n d")),
             writes=["wx"], dma=True)
        lam = self.params[:, PC.off["llam"]:PC.off["llam"] + NLC]
        S.op("act", lambda e: e.activation(out=dp[:, 0:NLC], in_=lam, func=AF.Exp, scale=-1.0),
             reads=["params"], writes=["dp0"])
        S.op("act", lambda e: e.activation(out=dp[:, 0:NLC], in_=dp[:, 0:NLC], func=AF.Ln, bias=1.0),
             reads=["dp0"], writes=["dp0"])
        S.op("dve", lambda e: e.tensor_scalar(out=dp[:, NLC:2 * NLC], in0=dp[:, 0:NLC], scalar1=-4.0, scalar2=None,
                                              op0=ALU.mult), reads=["dp0"], writes=["dp"])
        S.op("dve", lambda e: e.tensor_scalar(out=dp[:, 2 * NLC:3 * NLC],
                                              in0=self.params[:, PC.off["lba"]:PC.off["lba"] + NLC],
                                              scalar1=0.5, scalar2=None, op0=ALU.mult), reads=["params"], writes=["dp"])
        S.op("dve", lambda e: e.tensor_scalar(out=dp[:, 3 * NLC:4 * NLC],
                                              in0=self.params[:, PC.off["lbx"]:PC.off["lbx"] + NLC],
                                              scalar1=0.5, scalar2=None, op0=ALU.mult), reads=["params"], writes=["dp"])
        wsrc = self.d_lwin[0].rearrange("(c p) f -> p c f", p=128)

        def load_win(c):
            bi = c % 2
            S.op("pool", lambda e: e.dma_start(out=win[bi][:, :, 0:128], in_=wsrc[:, :, c * 128:(c + 1) * 128]),
                 writes=[wit(bi, 0)], dma=True)
            S.op("pool", lambda e: e.dma_start(out=win[bi][:, :, 128:256],
                                               in_=wsrc[:, :, DR + c * 128:DR + (c + 1) * 128]),
                 writes=[wit(bi, 1)], dma=True)

        def load_wout(c0):
            S.op("pool", lambda e: e.dma_start(
                out=wout[:], in_=self.d_lwout[0, c0 * 128:(c0 + CG) * 128, :].rearrange("(k p) d -> p k d", p=128)),
                writes=["wout"], dma=True)

        pairs = [(c, hf) for c in range(NLC) for hf in range(2)]

        def units(p):
            c, hf = pairs[p]
            return [(c, 2 * hf + j, (2 * p + j) % 4) for j in range(2)]

        def Wp(p):
            for (c, tt, s_) in units(p):
                bi = c % 2
                ts = slice(tt * 512, (tt + 1) * 512)
                for half in range(2):
                    for k in range(NC_):
                        S.op("pe", lambda e, k=k: e.matmul(
                            self.bank(2 * s_ + half), lhsT=win[bi][:, k, half * 128:(half + 1) * 128],
                            rhs=self.hT[:, k, ts], start=(k == 0), stop=(k == NC_ - 1)),
                            reads=[wit(bi, half), ("h", k, tt)], writes=[("ps", 2 * s_ + half)])

        def E1(p):
            for (c, tt, s_) in units(p):
                u = self.bank(2 * s_)
                wcol = [self.pcol("lcw", j * NLC + c) for j in range(4)]
                S.op("act", lambda e: e.activation(out=xc[s_][:], in_=u, func=AF.Identity, scale=wcol[3],
                                                   bias=self.pcol("lcb", c)),
                     reads=[("ps", 2 * s_), "params"], writes=[("xc", s_)])
                S.op("act", lambda e: e.activation(out=us[s_][:, 0:3], in_=u[:, 509:512], func=AF.Identity),
                     reads=[("ps", 2 * s_)], writes=[("us", s_)])
                for k in (1, 2, 3):
                    S.op("dve", lambda e, k=k: e.scalar_tensor_tensor(
                        out=xc[s_][:, k:512], in0=u[:, 0:512 - k], scalar=wcol[3 - k], in1=xc[s_][:, k:512],
                        op0=ALU.mult, op1=ALU.add), reads=[("ps", 2 * s_), ("xc", s_), "params"], writes=[("xc", s_)])
                if tt > 0:
                    sp = (s_ - 1) % 4
                    for k in (1, 2, 3):
                        S.op("dve", lambda e, k=k: e.scalar_tensor_tensor(
                            out=xc[s_][:, 0:k], in0=us[sp][:, 3 - k:3], scalar=wcol[3 - k], in1=xc[s_][:, 0:k],
                            op0=ALU.mult, op1=ALU.add), reads=[("us", sp), ("xc", s_), "params"], writes=[("xc", s_)])
                S.op("pool", lambda e: e.tensor_copy(out=xcb[s_][:], in_=xc[s_][:]),
                     reads=[("xc", s_)], writes=[("xcb", s_)])
            for (c, tt, s_) in units(p):
                S.op("act", lambda e: e.activation(out=gy[s_][:], in_=self.bank(2 * s_ + 1), func=AF.Gelu_apprx_tanh),
                     reads=[("ps", 2 * s_ + 1)], writes=[("gy", s_)])

        def Gp(p):
            for (c, tt, s_) in units(p):
                for which, wt, wtok in ((0, wa, "wa"), (1, wx, "wx")):
                    S.op("pe", lambda e, which=which, wt=wt: e.matmul(
                        self.bank(2 * s_ + which), lhsT=wt[:, c, :], rhs=xcb[s_][:], start=True, stop=True),
                        reads=[wtok, ("xcb", s_)], writes=[("ps", 2 * s_ + which)])

        def E2(p):
            un = units(p)
            for (c, tt, s_) in un:
                hba = dp[:, 2 * NLC + c:2 * NLC + c + 1]
                hbx = dp[:, 3 * NLC + c:3 * NLC + c + 1]
                S.op("act", lambda e: e.activation(out=A[s_][:], in_=self.bank(2 * s_), func=AF.Tanh, scale=0.5, bias=hba),
                     reads=[("ps", 2 * s_), "dp"], writes=[("A", s_)])
                S.op("act", lambda e: e.activation(out=I[s_][:], in_=self.bank(2 * s_ + 1), func=AF.Tanh, scale=0.5, bias=hbx),
                     reads=[("ps", 2 * s_ + 1), "dp"], writes=[("I", s_)])
            for (c, tt, s_) in un:
                hcl = dp[:, NLC + c:NLC + c + 1]
                S.op("act", lambda e: e.activation(out=A[s_][:], in_=A[s_][:], func=AF.Exp, scale=hcl, bias=hcl),
                     reads=[("A", s_), "dp"], writes=[("A", s_)])
            for (c, tt, s_) in un:
                S.op("act", lambda e: e.activation(out=M[s_][:], in_=A[s_][:], func=AF.Square),
                     reads=[("A", s_)], writes=[("M", s_)])
            for (c, tt, s_) in un:
                S.op("act", lambda e: e.activation(out=M[s_][:], in_=M[s_][:], func=AF.Sqrt, scale=-1.0, bias=1.0),
                     reads=[("M", s_)], writes=[("M", s_)])
            for (c, tt, s_) in un:
                S.op("dve", lambda e: e.scalar_tensor_tensor(out=I[s_][:], in0=I[s_][:], scalar=1.0, in1=xc[s_][:],
                                                             op0=ALU.add, op1=ALU.mult),
                     reads=[("I", s_), ("xc", s_)], writes=[("I", s_)])
                S.op("dve", lambda e: e.scalar_tensor_tensor(out=I[s_][:], in0=I[s_][:], scalar=0.5, in1=M[s_][:],
                                                             op0=ALU.mult, op1=ALU.mult),
                     reads=[("I", s_), ("M", s_)], writes=[("I", s_)])
                if tt > 0:
                    sp = (s_ - 1) % 4
                    init, itok = M[sp][:, 511:512], [("M", sp)]
                else:
                    init, itok = 0.0, []
                S.op("dve", lambda e, init=init: e.tensor_tensor_scan(
                    out=M[s_][:], data0=A[s_][:], data1=I[s_][:], initial=init, op0=ALU.mult, op1=ALU.add),
                    reads=[("A", s_), ("I", s_), ("M", s_)] + itok, writes=[("M", s_)])
                slot = c % NSLOT
                S.op("pool", lambda e, slot=slot, tt=tt: e.tensor_tensor(
                    out=hy[:, slot, tt * 512:(tt + 1) * 512], in0=M[s_][:], in1=gy[s_][:], op=ALU.mult),
                    reads=[("M", s_), ("gy", s_)], writes=[("hy", slot, tt)])

        def outproj(c0, final):
            self.resid_proj(wout, CG, lambda k, tt: hy[:, (c0 + k) % NSLOT, tt * 512:(tt + 1) * 512],
                            lambda k, tt: [("hy", (c0 + k) % NSLOT, tt)], "wout", final=final)

        load_wout(0)
        npair = len(pairs)
        pend = {}
        for idx in range(npair + 1):
            if idx < npair:
                c, hf = pairs[idx]
                if hf == 0 and c + 1 < NLC:
                    load_win(c + 1)
                Wp(idx)
                E1(idx)
            if idx >= 1:
                Gp(idx - 1)
                E2(idx - 1)
                c, hf = pairs[idx - 1]
                if hf == 1 and c == CG - 1:
                    pend[idx + 1] = "out0"
                    pend[idx + 5] = "wout1"
            act = pend.get(idx)
            if act == "out0":
                outproj(0, False)
            elif act == "wout1":
                load_wout(CG)
        outproj(CG, True)

    def pool(self, st, layer):
        nc, S = self.nc, self.S
        rstd_all = self.sb(st, "rstd_all", [128, S_], F32)
        hf = [self.sb(st, "hf%d" % i, [128, S_], F32) for i in range(2)]
        B = [self.sb(st, "pB%d" % i, [128, S_], F32) for i in range(4)]
        t16 = [self.sb(st, "t16_%d" % i, [128, 16], F32) for i in range(2)]
        pw = self.sb(st, "pw", [128, 4, 2, 256], BF16)
        S.op("pool", lambda e: e.dma_start(out=pw[:], in_=self.d_pw[0].rearrange("g (k p) d -> p g k d", p=128)),
             writes=["pw"], dma=True)
        for tt in range(4):
            self.rstd_tt(tt, rstd_all[:, tt * 512:(tt + 1) * 512], ("rstd_all", tt))
        xall = lambda c: [("x", c, tt) for tt in range(4)]
        hall = lambda c: [("h", c, tt) for tt in range(4)]

        def chunk_steps(c):
            g = c // 2
            k2 = c % 2
            hfc = hf[k2]
            steps = []
            steps.append(lambda: S.op("dve", lambda e: e.scalar_tensor_tensor(
                out=hfc[:], in0=self.xT[:, c, :], scalar=self.pcol("nmix", layer * NC_ + c), in1=rstd_all[:],
                op0=ALU.mult, op1=ALU.mult),
                reads=xall(c) + [("rstd_all", tt) for tt in range(4)] + ["params"], writes=[("hf", k2)]))
            cur, curtoks = hfc, [("hf", k2)]
            for k in range(g + 1):
                d = 2 ** k
                bi = k2 * 2 + k % 2
                nxt, nxttok = B[bi], ("pB", bi)
                eng = "pool" if k % 2 == 0 else "dve"

                def stp(cur=cur, nxt=nxt, d=d, eng=eng, curtoks=list(curtoks), nxttok=nxttok):
                    S.op(eng, lambda e: e.tensor_tensor(
                        out=nxt[:, d:S_], in0=cur[:, d:S_], in1=cur[:, 0:S_ - d], op=ALU.add),
                        reads=curtoks, writes=[nxttok])
                    S.op("act", lambda e: e.activation(out=nxt[:, 0:d], in_=cur[:, 0:d], func=AF.Identity),
                         reads=curtoks, writes=[(nxttok, "head")])
                steps.append(stp)
                cur, curtoks = nxt, [nxttok, (nxttok, "head")]
            w = 2 ** (g + 1)
            icnt = self.cf[:, CF.off["icnt"] + g * 16:CF.off["icnt"] + (g + 1) * 16]

            def fin(cur=cur, curtoks=list(curtoks)):
                S.op("dve", lambda e: e.scalar_tensor_tensor(
                    out=self.hT[:, c, :], in0=cur[:], scalar=1.0 / w, in1=hfc[:], op0=ALU.mult, op1=ALU.subtract),
                    reads=curtoks + [("hf", k2)], writes=hall(c))
                S.op("dve", lambda e: e.tensor_tensor(out=t16[k2][:], in0=cur[:, 0:16], in1=icnt, op=ALU.mult),
                     reads=curtoks + ["cf"], writes=[("t16", k2)])
                S.op("dve", lambda e: e.tensor_tensor(
                    out=self.hT[:, c, 0:16], in0=t16[k2][:], in1=hfc[:, 0:16], op=ALU.subtract),
                    reads=[("t16", k2), ("hf", k2)] + hall(c), writes=hall(c))
            steps.append(fin)
            return steps

        for c0 in range(0, NC_, 2):
            sa, sb_ = chunk_steps(c0), chunk_steps(c0 + 1)
            for i in range(len(sa)):
                sa[i]()
                sb_[i]()
        for tt in range(4):
            for g in range(4):
                for m2 in range(2):
                    m = 2 * g + m2
                    b = self.psrot % 8
                    self.psrot += 1
                    for kk in range(2):
                        S.op("pe", lambda e, b=b, g=g, kk=kk, m2=m2, tt=tt: e.matmul(
                            self.bank(b), lhsT=pw[:, g, kk, m2 * 128:(m2 + 1) * 128],
                            rhs=self.hT[:, 2 * g + kk, tt * 512:(tt + 1) * 512], start=(kk == 0), stop=(kk == 1)),
                            reads=["pw", ("h", 2 * g + kk, tt)], writes=[("ps", b)])
                    xs = self.xT[:, m, tt * 512:(tt + 1) * 512]
                    S.op("dve", lambda e, b=b, xs=xs, m=m: e.scalar_tensor_tensor(
                        out=xs, in0=self.bank(b), scalar=self.pcol("pscale", m), in1=xs, op0=ALU.mult, op1=ALU.add),
                        reads=[("ps", b), ("x", m, tt), "params"], writes=[("x", m, tt)])
            self.tail_hook(tt)


ALL_PHASES = []
for _l in range(DEPTH):
    ALL_PHASES += [("mix", _l), ("ffn", _l)]

WEIGHT_KEYS = ["ffn_w_up", "ffn_w_down", "attn_w_qkv", "attn_w_o", "lru_w_in", "lru_w_a", "lru_w_x",
               "lru_w_out", "pool_w"]


def run_phases(inputs, phases, x_cores=None, trace=False, **bkw):
    nc = Builder(phases, **bkw).build()
    params = pack_params(inputs)
    c16, cf = make_consts()
    if x_cores is None:
        x = np.asarray(inputs["x"], np.float32)
        x_cores = [np.ascontiguousarray(x[b].T) for b in range(8)]
    shared = {k: np.ascontiguousarray(np.asarray(inputs[k], np.float32)) for k in WEIGHT_KEYS}
    shared.update({"params": params, "c16": c16, "cf": cf})
    in_maps = []
    for b in range(8):
        m = dict(shared)
        m["xT"] = x_cores[b]
        in_maps.append(m)
    res = run_bass_kernel_spmd(nc, in_maps, core_ids=list(range(8)), trace=trace)
    return [r["outT"] for r in res.results], res


def kernel(**inputs):
    outs, _ = run_phases(inputs, ALL_PHASES)
    return np.stack([np.ascontiguousarray(o.T) for o in outs], axis=0).astype(np.float32)
```

```python
import contextlib
import numpy as np
import concourse.bass as bass
import concourse.mybir as mybir
from concourse.bass_utils import run_bass_kernel_spmd

F32 = mybir.dt.float32
BF16 = mybir.dt.bfloat16
AF = mybir.ActivationFunctionType
ALU = mybir.AluOpType
AX = mybir.AxisListType

D = 1024
S_ = 2048
NC_ = 8
DEPTH = 4
FH = 2816
NPAIR = 22
DR = 1280
NLC = 10
EPS = 1e-6
NEG = -30000.0
ENGS = ("pe", "act", "dve", "pool", "sp")


class Op:
    __slots__ = ("eng", "fn", "deps", "dma", "signal", "semv")

    def __init__(self, eng, fn, dma):
        self.eng = eng
        self.fn = fn
        self.deps = []
        self.dma = dma
        self.signal = False
        self.semv = None


class _Rec:
    def __init__(self):
        self.calls = []

    def __getattr__(self, name):
        def f(*args, **kwargs):
            self.calls.append((name, args, kwargs))
        return f


class Sched:
    N_DMA_SEMS = 8

    def __init__(self, nc):
        self.nc = nc
        self.ops = {e: [] for e in ENGS}
        self.last_writer = {}
        self.readers = {}
        self.fence_pending = set()
        self.fence_ops = []
        self.last_scr = {}
        self.cur_nofence = False

    def fence(self):
        self.fence_ops = [self.last_scr[e] for e in ENGS if e in self.last_scr]
        self.fence_pending = set(ENGS)

    def op(self, eng, fn, reads=(), writes=(), dma=False):
        rec = _Rec()
        fn(rec)
        name, args, kwargs = rec.calls[0]
        o = Op(eng, lambda e: getattr(e, name)(*args, **kwargs), dma)
        cand = []
        for t in reads:
            w = self.last_writer.get(t)
            if w is not None:
                cand.append((w, True))
        for t in writes:
            w = self.last_writer.get(t)
            if w is not None:
                cand.append((w, False))
            for r in self.readers.get(t, ()):
                cand.append((r, False))
        seen = set()
        for d, raw in cand:
            if d is o or id(d) in seen:
                continue
            if d.eng == eng and not d.dma and not dma:
                if eng == "pe" or not raw:
                    continue
            seen.add(id(d))
            o.deps.append(d)
        if eng in self.fence_pending:
            self.fence_pending.discard(eng)
            for d in self.fence_ops:
                if id(d) in seen or (d.eng == eng and not d.dma and not dma):
                    continue
                seen.add(id(d))
                o.deps.append(d)
        self.ops[eng].append(o)
        if not self.cur_nofence:
            self.last_scr[eng] = o
        for t in writes:
            self.last_writer[t] = o
            self.readers[t] = []
        for t in reads:
            self.readers.setdefault(t, []).append(o)
        return o

    def emit(self, final_waits=()):
        nc = self.nc
        for e in ENGS:
            for o in self.ops[e]:
                for d in o.deps:
                    d.signal = True
        for o in final_waits:
            o.signal = True
        with contextlib.ExitStack() as st:
            sems = {e: st.enter_context(nc.semaphore("s_" + e)) for e in ENGS}
            for e in ("sp", "pool", "act"):
                for k in range(self.N_DMA_SEMS):
                    sems[(e, k)] = st.enter_context(nc.semaphore("d_%s%d" % (e, k)))
            for e in ENGS:
                c = 0
                dcount = [0] * self.N_DMA_SEMS
                nd = 0
                for o in self.ops[e]:
                    if o.dma:
                        k = nd % self.N_DMA_SEMS
                        nd += 1
                        dcount[k] += 1
                        o.semv = ((e, k), 16 * dcount[k])
                    elif o.signal:
                        c += 1
                        o.semv = (e, c)
            block = st.enter_context(nc.Block())
            engobj = {"pe": block.tensor, "act": block.scalar, "dve": block.vector,
                      "pool": block.gpsimd, "sp": block.sync}

            def make(e):
                def body(eng):
                    known = {}
                    for o in self.ops[e]:
                        waits = {}
                        for d in o.deps:
                            sk, v = d.semv
                            if known.get(sk, 0) >= v:
                                continue
                            if waits.get(sk, 0) < v:
                                waits[sk] = v
                        if o.dma:
                            sk, v = o.semv
                            if v > 16 and known.get(sk, 0) < v - 16 and waits.get(sk, 0) < v - 16:
                                waits[sk] = v - 16
                        for sk, v in waits.items():
                            eng.wait_ge(sems[sk], v)
                            known[sk] = v
                        ins = o.fn(eng)
                        if o.semv is not None:
                            ins.then_inc(sems[o.semv[0]], 16 if o.dma else 1)
                    if e == "sp":
                        for o in final_waits:
                            eng.wait_ge(sems[o.semv[0]], o.semv[1])
                return body

            for e in ENGS:
                engobj[e](make(e))


class Cols:
    def __init__(self):
        self.n = 0
        self.off = {}

    def add(self, name, k):
        self.off[name] = self.n
        self.n += k


PC = Cols()
PC.add("nmix", DEPTH * NC_)
PC.add("nffn", DEPTH * NC_)
PC.add("qg", 2)
PC.add("kg", 2)
PC.add("lcw", 4 * NLC)
PC.add("lcb", NLC)
PC.add("lba", NLC)
PC.add("lbx", NLC)
PC.add("llam", NLC)
PC.add("pscale", NC_)
PC.add("fcw", DEPTH * 3 * 44)
PC.add("fcb", DEPTH * 44)

CC = Cols()
CC.add("ident", 128)
CC.add("ones", 128)
CC.add("cb", 2 * 256)
CC.add("esel", 8 * 128)
NCB = CC.n
CF = Cols()
CF.add("pastmask", 64)
CF.add("icnt", 4 * 16)
CF.add("eps", 1)


def pack_params(inp):
    P = np.zeros((128, PC.n), np.float32)

    def put(name, arr):
        a = np.asarray(arr, np.float32)
        a = a.reshape(-1, a.shape[-1] // 128, 128)
        a = a.transpose(2, 0, 1).reshape(128, -1)
        P[:, PC.off[name]:PC.off[name] + a.shape[1]] = a

    put("nmix", inp["norm_mix_g"])
    put("nffn", inp["norm_ffn_g"])
    put("qg", inp["attn_q_g"])
    put("kg", inp["attn_k_g"])
    put("lcw", inp["lru_conv_w"][0])
    put("lcb", inp["lru_conv_b"][0])
    put("lba", inp["lru_b_a"][0])
    put("lbx", inp["lru_b_x"][0])
    put("llam", inp["lru_lambda"][0])
    put("pscale", inp["pool_scale"][0])
    put("fcw", inp["ffn_conv_w"])
    put("fcb", inp["ffn_conv_b"])
    return P


def make_consts():
    cb16 = np.zeros((128, CC.n), np.float32)
    cb16[:, CC.off["ident"]:CC.off["ident"] + 128] = np.eye(128, dtype=np.float32)
    cb16[:, CC.off["ones"]:CC.off["ones"] + 128] = 1.0
    p = np.arange(128)[:, None]
    q = np.arange(256)[None, :]
    for j in range(2):
        m = np.where(j * 128 + p <= q, 0.0, NEG).astype(np.float32)
        cb16[:, CC.off["cb"] + j * 256:CC.off["cb"] + (j + 1) * 256] = m
    es = np.zeros((128, 8, 128), np.float32)
    for n in range(8):
        es[n, n, :] = 1.0
    cb16[:, CC.off["esel"]:CC.off["esel"] + 1024] = es.reshape(128, 1024)
    cf = np.zeros((128, CF.n), np.float32)
    pm = np.zeros((8, 8), np.float32)
    for i in range(8):
        for n in range(8):
            pm[i, n] = 0.0 if n < 4 + i // 2 else -1e30
    cf[:, CF.off["pastmask"]:CF.off["pastmask"] + 64] = pm.reshape(1, 64)
    ic = np.zeros((4, 16), np.float32)
    for g, w in enumerate((2, 4, 8, 16)):
        for t in range(16):
            ic[g, t] = 1.0 / min(t + 1, w)
    cf[:, CF.off["icnt"]:CF.off["icnt"] + 64] = ic.reshape(1, 64)
    cf[:, CF.off["eps"]] = EPS
    return cb16, cf


class Builder:
    def __init__(self, phases, attn_heads=8, attn_hg=4):
        self.phases = phases
        self.attn_heads = attn_heads
        self.attn_hg = attn_hg
        nc = bass.Bass("TRN2", target_bir_lowering=False)
        self.nc = nc
        dt = nc.dram_tensor
        self.d_x = dt("xT", [D, S_], F32, kind="ExternalInput").ap()
        self.d_out = dt("outT", [D, S_], F32, kind="ExternalOutput").ap()
        self.d_params = dt("params", [128, PC.n], F32, kind="ExternalInput").ap()
        self.d_c16 = dt("c16", [128, CC.n], F32, kind="ExternalInput").ap()
        self.d_cf = dt("cf", [128, CF.n], F32, kind="ExternalInput").ap()
        self.d_wup = dt("ffn_w_up", [DEPTH, D, 2 * FH], F32, kind="ExternalInput").ap()
        self.d_wdn = dt("ffn_w_down", [DEPTH, FH, D], F32, kind="ExternalInput").ap()
        self.d_wqkv = dt("attn_w_qkv", [2, D, 3 * D], F32, kind="ExternalInput").ap()
        self.d_wo = dt("attn_w_o", [2, D, D], F32, kind="ExternalInput").ap()
        self.d_lwin = dt("lru_w_in", [1, D, 2 * DR], F32, kind="ExternalInput").ap()
        self.d_lwa = dt("lru_w_a", [1, NLC, 128, 128], F32, kind="ExternalInput").ap()
        self.d_lwx = dt("lru_w_x", [1, NLC, 128, 128], F32, kind="ExternalInput").ap()
        self.d_lwout = dt("lru_w_out", [1, DR, D], F32, kind="ExternalInput").ap()
        self.d_pw = dt("pool_w", [1, 4, 256, 256], F32, kind="ExternalInput").ap()
        self.S = Sched(nc)
        self.psrot = 0
        self.uid = 0
        self.out_dmas = []

    def sb(self, st, name, shape, dtype):
        self.uid += 1
        return st.enter_context(self.nc.sbuf_tensor("%s_u%d" % (name, self.uid), shape, dtype))

    def pcol(self, name, idx):
        o = PC.off[name] + idx
        return self.params[:, o:o + 1]

    def build(self):
        nc, S = self.nc, self.S
        with contextlib.ExitStack() as st:
            self.xT = self.sb(st, "xT_sb", [128, NC_, S_], F32)
            self.hT = self.sb(st, "hT_sb", [128, NC_, S_], BF16)
            self.params = self.sb(st, "params_sb", [128, PC.n], F32)
            self.c16 = self.sb(st, "c16_sb", [128, CC.n], BF16)
            self.cf = self.sb(st, "cf_sb", [128, CF.n], F32)
            self.sq = self.sb(st, "sq_sb", [128, NC_, 512], BF16)
            self.lnv = self.sb(st, "lnv_sb", [128, 512], F32)
            self.rstd = self.sb(st, "rstd_sb", [128, 512], F32)
            self.PS = st.enter_context(nc.psum_tensor("PS", [128, 4096], F32))
            self.ident = self.c16[:, CC.off["ident"]:CC.off["ident"] + 128]
            self.ones = self.c16[:, CC.off["ones"]:CC.off["ones"] + 128]
            self.eps = self.cf[:, CF.off["eps"]:CF.off["eps"] + 1]

            S.op("sp", lambda e: e.dma_start(out=self.params[:], in_=self.d_params[:, :]),
                 writes=["params"], dma=True)
            S.op("sp", lambda e: e.dma_start(out=self.cf[:], in_=self.d_cf[:, :]),
                 writes=["cf"], dma=True)
            S.op("pool", lambda e: e.dma_start(out=self.c16[:], in_=self.d_c16[:, :]),
                 writes=["c16"], dma=True)
            for tt in range(4):
                for c in range(NC_):
                    S.op("sp", lambda e, c=c, tt=tt: e.dma_start(
                        out=self.xT[:, c, tt * 512:(tt + 1) * 512],
                        in_=self.d_x[c * 128:(c + 1) * 128, tt * 512:(tt + 1) * 512]),
                        writes=[("x", c, tt)], dma=True)

            self.wfirst = self.sb(st, "wfirst", [128, NC_, 512], BF16)
            nph = len(self.phases)
            self.prefetch_first(self.phases[0])
            self.start_norm(self.phases[0])
            for i, ph in enumerate(self.phases):
                kind, layer = ph
                self.next_phase = self.phases[i + 1] if i + 1 < nph else None
                S.fence()
                with contextlib.ExitStack() as st2:
                    if kind == "ffn":
                        self.ffn(st2, layer)
                    else:
                        mk = layer % 3
                        if mk == 0:
                            self.attn(st2, layer)
                        elif mk == 1:
                            self.lru(st2, layer)
                        else:
                            self.pool(st2, layer)

            S.emit(final_waits=self.out_dmas)
        return nc

    def bank(self, b):
        return self.PS[:, b * 512:(b + 1) * 512]

    def normA(self, tt):
        ts = slice(tt * 512, (tt + 1) * 512)
        self.S.op("act", lambda e: e.activation(out=self.sq[:], in_=self.xT[:, :, ts], func=AF.Square),
                  reads=[("x", c, tt) for c in range(NC_)], writes=["sq"])

    def normB_mm(self, tt):
        S = self.S
        b = self.psrot % 8
        self.psrot += 1
        for c in range(NC_):
            S.op("pe", lambda e, c=c: e.matmul(self.bank(b), lhsT=self.ones, rhs=self.sq[:, c, :],
                                               start=(c == 0), stop=(c == NC_ - 1)),
                 reads=["sq", "c16"], writes=[("ps", b)])
        return b

    def normB_rest(self, gname, layer, tt, b):
        S = self.S
        ts = slice(tt * 512, (tt + 1) * 512)
        S.op("act", lambda e: e.activation(out=self.lnv[:], in_=self.bank(b), func=AF.Ln,
                                           scale=1.0 / D, bias=self.eps),
             reads=[("ps", b), "cf"], writes=["lnv"])
        S.op("act", lambda e: e.activation(out=self.rstd[:], in_=self.lnv[:], func=AF.Exp, scale=-0.5),
             reads=["lnv"], writes=["rstd"])
        for c in range(NC_):
            S.op("dve", lambda e, c=c: e.scalar_tensor_tensor(
                out=self.hT[:, c, ts], in0=self.xT[:, c, ts], scalar=self.pcol(gname, layer * NC_ + c),
                in1=self.rstd[:], op0=ALU.mult, op1=ALU.mult),
                reads=[("x", c, tt), "rstd", "params"], writes=[("h", c, tt)])

    def normB(self, gname, layer, tt):
        b = self.normB_mm(tt)
        self.normB_rest(gname, layer, tt, b)

    @staticmethod
    def norm_of(ph):
        if ph is None:
            return None
        kind, layer = ph
        if kind == "ffn":
            return ("nffn", layer)
        if layer % 3 == 2:
            return None
        return ("nmix", layer)

    def start_norm(self, ph):
        nm = self.norm_of(ph)
        if nm is None:
            return
        self.normA(0)
        for tt in range(4):
            b = self.normB_mm(tt)
            if tt + 1 < 4:
                self.normA(tt + 1)
            self.normB_rest(nm[0], nm[1], tt, b)

    def tail_hook(self, tt):
        self.S.cur_nofence = True
        try:
            self._tail_hook(tt)
        finally:
            self.S.cur_nofence = False

    def _tail_hook(self, tt):
        nm = self.norm_of(self.next_phase)
        if self.next_phase is None:
            for c in range(NC_):
                self.out_dmas.append(self.S.op("sp", lambda e, c=c: e.dma_start(
                    out=self.d_out[c * 128:(c + 1) * 128, tt * 512:(tt + 1) * 512],
                    in_=self.xT[:, c, tt * 512:(tt + 1) * 512]),
                    reads=[("x", c, tt)], dma=True))
            return
        if tt == 0:
            self.prefetch_first(self.next_phase)
        if nm is None:
            return
        if tt >= 1:
            self.normB(nm[0], nm[1], tt - 1)
        self.normA(tt)
        if tt == 3:
            self.normB(nm[0], nm[1], 3)

    def prefetch_first(self, ph):
        S = self.S
        kind, layer = ph
        wf = self.wfirst
        if kind == "ffn":
            src = self.d_wup[layer].rearrange("(c p) f -> p c f", p=128)
            S.op("pool", lambda e: e.dma_start(out=wf[:, :, 0:256], in_=src[:, :, 0:256]),
                 writes=[("wf", 0), ("wf", 1)], dma=True)
            S.op("pool", lambda e: e.dma_start(out=wf[:, :, 256:512], in_=src[:, :, FH:FH + 256]),
                 writes=[("wf", 2), ("wf", 3)], dma=True)
        elif layer % 3 == 0:
            src = self.d_wqkv[layer // 3].rearrange("(c p) f -> p c f", p=128)
            for k in range(3):
                S.op("pool", lambda e, k=k: e.dma_start(out=wf[:, :, k * 128:(k + 1) * 128],
                                                        in_=src[:, :, k * D:k * D + 128]),
                     writes=[("wf", k)], dma=True)
        elif layer % 3 == 1:
            src = self.d_lwin[0].rearrange("(c p) f -> p c f", p=128)
            S.op("pool", lambda e: e.dma_start(out=wf[:, :, 0:128], in_=src[:, :, 0:128]),
                 writes=[("wf", 0)], dma=True)
            S.op("pool", lambda e: e.dma_start(out=wf[:, :, 128:256], in_=src[:, :, DR:DR + 128]),
                 writes=[("wf", 1)], dma=True)

    def ffn(self, st, layer):
        nc, S = self.nc, self.S
        GROUPS = [(0, 6), (6, 12), (12, 17), (17, 22)]
        NP = 7
        NW = 3
        wup = [self.wfirst] + [self.sb(st, "wup%d" % i, [128, NC_, 512], BF16) for i in range(1, NW)]
        wdn = self.sb(st, "wdn", [128, 6, D], BF16)
        Pb = self.sb(st, "Pb", [128, NP, S_], BF16)
        cbuf = [self.sb(st, "cbuf%d" % i, [128, S_], F32) for i in range(3)]
        wup_src = self.d_wup[layer].rearrange("(c p) f -> p c f", p=128)

        def wtok(bi, half):
            if bi == 0:
                return [("wf", 2 * half), ("wf", 2 * half + 1)]
            return [("wup", bi, half)]

        def load_slab(s):
            bi = s % NW
            S.op("pool", lambda e: e.dma_start(out=wup[bi][:, :, 0:256],
                                               in_=wup_src[:, :, s * 256:(s + 1) * 256]),
                 writes=wtok(bi, 0), dma=True)
            S.op("pool", lambda e: e.dma_start(out=wup[bi][:, :, 256:512],
                                               in_=wup_src[:, :, FH + s * 256:FH + (s + 1) * 256]),
                 writes=wtok(bi, 1), dma=True)

        def load_wdn(j0, j1):
            S.op("pool", lambda e: e.dma_start(
                out=wdn[:, 0:j1 - j0, :],
                in_=self.d_wdn[layer, j0 * 128:j1 * 128, :].rearrange("(j p) d -> p j d", p=128)),
                writes=["wdn"], dma=True)

        cbi = [0]

        def up(j):
            s, r = j // 2, j % 2
            bi = s % NW
            cg = None
            for half in range(2):
                fj = half * NPAIR + j
                base = half * 2048
                lcol = half * 256 + r * 128
                for tt in range(4):
                    for c in range(NC_):
                        S.op("pe", lambda e, c=c, tt=tt: e.matmul(
                            self.PS[:, base + tt * 512: base + (tt + 1) * 512],
                            lhsT=wup[bi][:, c, lcol:lcol + 128],
                            rhs=self.hT[:, c, tt * 512:(tt + 1) * 512],
                            start=(c == 0), stop=(c == NC_ - 1)),
                            reads=wtok(bi, half) + [("h", c, tt)], writes=[("ps", half * 4 + tt)])
                if half == 1 and r == 1 and s + NW < 11:
                    load_slab(s + NW)
                cb = cbuf[cbi[0] % 3]
                cbk = cbi[0] % 3
                cbn = [("cbuf", cbk, 0), ("cbuf", cbk, 1)]
                cbi[0] += 1
                w0 = self.pcol("fcw", (layer * 3 + 0) * 44 + fj)
                w1 = self.pcol("fcw", (layer * 3 + 1) * 44 + fj)
                w2 = self.pcol("fcw", (layer * 3 + 2) * 44 + fj)
                bb = self.pcol("fcb", layer * 44 + fj)
                u = self.PS[:, base:base + 2048]
                for hh in range(2):
                    lo, hi = hh * 1024, (hh + 1) * 1024
                    psr = [("ps", half * 4 + 2 * hh), ("ps", half * 4 + 2 * hh + 1)]
                    if hh == 1:
                        psr.append(("ps", half * 4 + 1))
                    ct = [("cbuf", cbk, hh)]
                    S.op("act", lambda e: e.activation(out=cb[:, lo:hi], in_=u[:, lo:hi], func=AF.Identity,
                                                       scale=w2, bias=bb),
                         reads=psr + ["params"], writes=ct)
                    for k, wk in ((1, w1), (2, w0)):
                        o0 = max(lo, k)
                        S.op("dve", lambda e, o0=o0, k=k, wk=wk: e.scalar_tensor_tensor(
                            out=cb[:, o0:hi], in0=u[:, o0 - k:hi - k], scalar=wk, in1=cb[:, o0:hi],
                            op0=ALU.mult, op1=ALU.add), reads=psr + ct + ["params"], writes=ct)
                for hh in range(2):
                    lo, hi = hh * 1024, (hh + 1) * 1024
                    if half == 0:
                        S.op("act", lambda e: e.activation(out=cb[:, lo:hi], in_=cb[:, lo:hi], func=AF.Silu),
                             reads=[cbn[hh]], writes=[cbn[hh]])
                    else:
                        S.op("pool", lambda e: e.tensor_tensor(out=Pb[:, j % NP, lo:hi], in0=cg[0][:, lo:hi],
                                                               in1=cb[:, lo:hi], op=ALU.mult),
                             reads=[cbn[hh], cg[1][hh]], writes=[("P", j % NP, hh)])
                if half == 0:
                    cg = (cb, cbn)

        def down(j0, j1, nbanks=4, final=False):
            nj = j1 - j0
            for tt in range(4):
                for m in range(NC_):
                    b = self.psrot % nbanks
                    self.psrot += 1
                    for jj in range(nj):
                        slot = (j0 + jj) % NP
                        S.op("pe", lambda e, jj=jj, slot=slot: e.matmul(
                            self.bank(b), lhsT=wdn[:, jj, m * 128:(m + 1) * 128],
                            rhs=Pb[:, slot, tt * 512:(tt + 1) * 512],
                            start=(jj == 0), stop=(jj == nj - 1)),
                            reads=["wdn", ("P", slot, tt // 2)], writes=[("ps", b)])
                    xs = self.xT[:, m, tt * 512:(tt + 1) * 512]
                    S.op("dve", lambda e: e.tensor_tensor(out=xs, in0=self.bank(b), in1=xs, op=ALU.add),
                         reads=[("ps", b), ("x", m, tt)], writes=[("x", m, tt)])
                if final:
                    self.tail_hook(tt)

        for s0 in range(1, NW):
            load_slab(s0)
        load_wdn(*GROUPS[0])
        gidx = {}
        for gi, (j0, j1) in enumerate(GROUPS):
            for j in range(j0, j1):
                gidx[j] = gi
        for j in range(NPAIR):
            up(j)
            gi = gidx[j]
            j0, j1 = GROUPS[gi]
            if j == j0 and gi > 0:
                down(*GROUPS[gi - 1])
            if j == j0 + 2 and gi > 0:
                load_wdn(j0, j1)
        down(*GROUPS[-1], nbanks=8, final=True)

    def rstd_tt(self, tt, dst, dst_tok):
        S = self.S
        ts = slice(tt * 512, (tt + 1) * 512)
        b = self.psrot % 8
        self.psrot += 1
        S.op("act", lambda e: e.activation(out=self.sq[:], in_=self.xT[:, :, ts], func=AF.Square),
             reads=[("x", c, tt) for c in range(NC_)], writes=["sq"])
        for c in range(NC_):
            S.op("pe", lambda e, c=c: e.matmul(self.bank(b), lhsT=self.ones, rhs=self.sq[:, c, :],
                                               start=(c == 0), stop=(c == NC_ - 1)),
                 reads=["sq", "c16"], writes=[("ps", b)])
        S.op("act", lambda e: e.activation(out=self.lnv[:], in_=self.bank(b), func=AF.Ln,
                                           scale=1.0 / D, bias=self.eps),
             reads=[("ps", b), "cf"], writes=["lnv"])
        S.op("act", lambda e: e.activation(out=dst, in_=self.lnv[:], func=AF.Exp, scale=-0.5),
             reads=["lnv"], writes=[dst_tok])

    def resid_proj(self, w, nk, rhs_fn, rhs_toks, wtok, final=False):
        S = self.S
        for tt in range(4):
            for m in range(NC_):
                b = self.psrot % 8
                self.psrot += 1
                for k in range(nk):
                    S.op("pe", lambda e, b=b, k=k, m=m, tt=tt: e.matmul(
                        self.bank(b), lhsT=w[:, k, m * 128:(m + 1) * 128], rhs=rhs_fn(k, tt),
                        start=(k == 0), stop=(k == nk - 1)),
                        reads=[wtok] + rhs_toks(k, tt), writes=[("ps", b)])
                xs = self.xT[:, m, tt * 512:(tt + 1) * 512]
                S.op("dve", lambda e, b=b, xs=xs: e.tensor_tensor(
                    out=xs, in0=self.bank(b), in1=xs, op=ALU.add),
                    reads=[("ps", b), ("x", m, tt)], writes=[("x", m, tt)])
            if final:
                self.tail_hook(tt)

    def attn(self, st, layer):
        nc, S = self.nc, self.S
        slot = layer // 3
        HG = self.attn_hg
        NH = self.attn_heads
        wqkv = [self.wfirst, self.sb(st, "wqkv1", [128, NC_, 384], BF16)]

        def wqt(bi, k):
            return ("wf", k) if bi == 0 else ("wqkv", bi, k)

        qn = [self.sb(st, "qn%d" % i, [128, S_], BF16) for i in range(2)]
        kn = [self.sb(st, "kn%d" % i, [128, S_], BF16) for i in range(2)]
        Vt = [self.sb(st, "Vt%d" % i, [128, 16, 128], BF16) for i in range(2)]
        oT = self.sb(st, "oT", [128, HG, S_], BF16)
        wo = self.sb(st, "wo", [128, HG, D], BF16)
        rs = [self.sb(st, "rs%d" % i, [128, 512], F32) for i in range(2)]
        sq2 = [self.sb(st, "sq2_%d" % i, [128, 512], BF16) for i in range(2)]
        PT = [self.sb(st, "PT%d" % i, [128, 512], BF16) for i in range(4)]
        kmf = self.sb(st, "kmf", [128, 8], F32)
        kmb = self.sb(st, "kmb", [128, 8], BF16)
        g1 = self.sb(st, "g1", [128, 64], F32)
        top = self.sb(st, "top", [128, 64], F32)
        cmpf = self.sb(st, "cmpf", [128, 64], F32)
        btok = self.sb(st, "btok", [128, 8, 128], BF16)
        biasT = self.sb(st, "biasT", [128, 1024], BF16)
        rden = [self.sb(st, "rden%d" % i, [128, 256], F32) for i in range(2)]
        S.op("pool", lambda e: e.memset(btok[:], 0.0), writes=["btok"])
        S.op("pool", lambda e: e.memset(biasT[:], 0.0), writes=["biasT"])
        wsrc = self.d_wqkv[slot].rearrange("(c p) f -> p c f", p=128)
        scale = 128.0 ** -0.5
        pastmask = self.cf[:, CF.off["pastmask"]:CF.off["pastmask"] + 64]
        cbm = [self.c16[:, CC.off["cb"] + j * 256:CC.off["cb"] + (j + 1) * 256] for j in range(2)]
        esel = [self.c16[:, CC.off["esel"] + n * 128:CC.off["esel"] + (n + 1) * 128] for n in range(8)]
        cnt = {"s": 0, "p": 0, "k2": 0, "pt": 0}

        def sbank():
            b = 2 + cnt["s"] % 3
            cnt["s"] += 1
            return b


        def load_w(hd):
            bi = hd % 2
            for k in range(3):
                S.op("pool", lambda e, k=k: e.dma_start(
                    out=wqkv[bi][:, :, k * 128:(k + 1) * 128],
                    in_=wsrc[:, :, k * D + hd * 128:k * D + (hd + 1) * 128]),
                    writes=[wqt(bi, k)], dma=True)

        def prologue_units(hd):
            bi = hd % 2
            units = []
            items = [(which, tt) for which in (0, 1) for tt in range(4)]
            state = {}

            def A1(i):
                which, tt = items[i]
                ts = slice(tt * 512, (tt + 1) * 512)
                b = 5 + cnt["p"] % 3
                cnt["p"] += 1
                k2 = cnt["k2"] % 2
                cnt["k2"] += 1
                state[i] = (b, k2)
                for c in range(NC_):
                    S.op("pe", lambda e, c=c: e.matmul(
                        self.bank(b), lhsT=wqkv[bi][:, c, which * 128:(which + 1) * 128],
                        rhs=self.hT[:, c, ts], start=(c == 0), stop=(c == NC_ - 1)),
                        reads=[wqt(bi, which), ("h", c, tt)], writes=[("ps", b)])
                S.op("act", lambda e: e.activation(out=sq2[k2][:], in_=self.bank(b), func=AF.Square),
                     reads=[("ps", b)], writes=[("sq2", k2)])

            def A2(i):
                which, tt = items[i]
                ts = slice(tt * 512, (tt + 1) * 512)
                b, k2 = state[i]
                dst = (qn if which == 0 else kn)[bi]
                gname = "qg" if which == 0 else "kg"
                b7 = sbank()
                S.op("pe", lambda e: e.matmul(self.bank(b7), lhsT=self.ones, rhs=sq2[k2][:], start=True, stop=True),
                     reads=[("sq2", k2), "c16"], writes=[("ps", b7)])
                S.op("act", lambda e: e.activation(out=rs[k2][:], in_=self.bank(b7), func=AF.Ln,
                                                   scale=1.0 / 128, bias=self.eps),
                     reads=[("ps", b7), "cf"], writes=[("rs", k2)])
                S.op("act", lambda e: e.activation(out=rs[k2][:], in_=rs[k2][:], func=AF.Exp, scale=-0.5),
                     reads=[("rs", k2)], writes=[("rs", k2)])
                S.op("dve", lambda e: e.scalar_tensor_tensor(
                    out=dst[:, ts], in0=self.bank(b), scalar=self.pcol(gname, slot), in1=rs[k2][:],
                    op0=ALU.mult, op1=ALU.mult),
                    reads=[("ps", b), ("rs", k2), "params"], writes=[("qk", which, bi, tt)])

            units.append(lambda: A1(0))
            for i in range(1, 8):
                units.append(lambda i=i: A1(i))
                units.append(lambda i=i: A2(i - 1))
            units.append(lambda: A2(7))

            def Vunit(bq):
                b = 5 + cnt["p"] % 3
                cnt["p"] += 1
                for i4 in range(4):
                    i = bq * 4 + i4
                    for c in range(NC_):
                        S.op("pe", lambda e, i=i, i4=i4, c=c: e.matmul(
                            self.bank(b)[:, i4 * 128:(i4 + 1) * 128], lhsT=self.hT[:, c, i * 128:(i + 1) * 128],
                            rhs=wqkv[bi][:, c, 256:384], start=(c == 0), stop=(c == NC_ - 1)),
                            reads=[wqt(bi, 2), ("h", c, bq)], writes=[("ps", b)])
                S.op("act", lambda e: e.activation(
                    out=Vt[bi][:, bq * 4:(bq + 1) * 4, :], in_=self.bank(b).rearrange("p (i d) -> p i d", d=128),
                    func=AF.Identity), reads=[("ps", b)], writes=[("V", bi, bq)])

            for bq in range(4):
                units.append(lambda bq=bq: Vunit(bq))

            def Cunit():
                S.op("dve", lambda e: e.tensor_reduce(out=kmf[:], in_=kn[bi][:].rearrange("p (n k) -> p n k", k=256),
                                                      axis=AX.X, op=ALU.add),
                     reads=[("qk", 1, bi, tt) for tt in range(4)], writes=["kmf"])
                S.op("dve", lambda e: e.tensor_scalar(out=kmb[:], in0=kmf[:], scalar1=1.0 / 256, scalar2=None,
                                                      op0=ALU.mult), reads=["kmf"], writes=["kmb"])
                bg = sbank()
                for i in range(8):
                    S.op("pe", lambda e, i=i: e.matmul(
                        self.bank(bg)[:, i * 8:(i + 1) * 8], lhsT=qn[bi][:, (8 + i) * 128:(9 + i) * 128], rhs=kmb[:],
                        start=True, stop=True),
                        reads=["kmb", ("qk", 0, bi, 2 + i // 4)], writes=[("ps", bg)])
                S.op("dve", lambda e: e.tensor_tensor(out=g1[:], in0=self.bank(bg)[:, 0:64], in1=pastmask, op=ALU.add),
                     reads=[("ps", bg), "cf"], writes=["g1"])
                for i in range(8):
                    S.op("dve", lambda e, i=i: e.max(out=top[:, i * 8:(i + 1) * 8], in_=g1[:, i * 8:(i + 1) * 8]),
                         reads=["g1"], writes=["top"])
                S.op("dve", lambda e: e.tensor_tensor(
                    out=cmpf[:].rearrange("p (i n) -> p i n", n=8), in0=g1[:].rearrange("p (i n) -> p i n", n=8),
                    in1=top[:].rearrange("p (i n) -> p i n", n=8)[:, :, 2:3].to_broadcast([128, 8, 8]), op=ALU.is_lt),
                    reads=["g1", "top"], writes=["cmpf"])

            units.append(Cunit)
            return units

        def Dunit(hd):
            S.op("dve", lambda e: e.tensor_scalar(out=btok[:, :, 0:8], in0=cmpf[:].rearrange("p (i n) -> p i n", n=8),
                                                  scalar1=NEG, scalar2=None, op0=ALU.mult),
                 reads=["cmpf"], writes=["btok"])
            for k in range(2):
                bd_ = sbank()
                for i4 in range(4):
                    i = k * 4 + i4
                    S.op("pe", lambda e, i=i, i4=i4: e.matmul(
                        self.bank(bd_)[:, i4 * 128:(i4 + 1) * 128], lhsT=btok[:, i, :], rhs=self.ident,
                        start=True, stop=True),
                        reads=["btok", "c16"], writes=[("ps", bd_)])
                S.op("act", lambda e, k=k: e.activation(out=biasT[0:8, k * 512:(k + 1) * 512],
                                                        in_=self.bank(bd_)[0:8, :], func=AF.Identity),
                     reads=[("ps", bd_)], writes=["biasT"])

        def E1(hd, qb, n, st_):
            bi = hd % 2
            qs = slice(qb * 256, (qb + 1) * 256)
            bs_ = sbank()
            pk = cnt["pt"] % 4
            cnt["pt"] += 1
            st_[(qb, n)] = pk
            for half in range(2):
                kt = 2 * n + half
                osl = self.bank(bs_)[:, half * 256:(half + 1) * 256]
                extra = (n == qb) or (qb >= 4)
                S.op("pe", lambda e, osl=osl, kt=kt, extra=extra: e.matmul(
                    osl, lhsT=kn[bi][:, kt * 128:(kt + 1) * 128], rhs=qn[bi][:, qs],
                    start=True, stop=(not extra)),
                    reads=[("qk", 1, bi, kt // 4), ("qk", 0, bi, qb // 2)], writes=[("ps", bs_)])
                if n == qb:
                    S.op("pe", lambda e, osl=osl, half=half: e.matmul(
                        osl, lhsT=self.ident, rhs=cbm[half], start=False, stop=True),
                        reads=["c16"], writes=[("ps", bs_)])
                elif qb >= 4:
                    S.op("pe", lambda e, osl=osl: e.matmul(
                        osl, lhsT=esel[n], rhs=biasT[:, (qb - 4) * 256:(qb - 3) * 256],
                        start=False, stop=True),
                        reads=["c16", "biasT"], writes=[("ps", bs_)])
            S.op("act", lambda e: e.activation(out=PT[pk][:], in_=self.bank(bs_), func=AF.Exp, scale=scale),
                 reads=[("ps", bs_)], writes=[("PT", pk)])

        def E2(hd, qb, n, st_):
            bi = hd % 2
            qs = slice(qb * 256, (qb + 1) * 256)
            pk = st_[(qb, n)]
            bo = qb % 2
            for half in range(2):
                kt = 2 * n + half
                first = (n == 0 and half == 0)
                last = (n == qb and half == 1)
                S.op("pe", lambda e, kt=kt, half=half, first=first, last=last: e.matmul(
                    self.bank(bo)[:, 0:256], lhsT=Vt[bi][:, kt, :], rhs=PT[pk][:, half * 256:(half + 1) * 256],
                    start=first, stop=last, skip_group_check=True),
                    reads=[("V", bi, kt // 4), ("PT", pk)], writes=[("ps", bo)])
                S.op("pe", lambda e, half=half, last=last: e.matmul(
                    self.bank(bo)[:, 256:512], lhsT=self.ones, rhs=PT[pk][:, half * 256:(half + 1) * 256],
                    start=False, stop=last, skip_group_check=True),
                    reads=["c16", ("PT", pk)], writes=[("ps", bo)])
            if n == qb:
                rk = qb % 2
                S.op("dve", lambda e: e.reciprocal(out=rden[rk][:], in_=self.bank(bo)[:, 256:512]),
                     reads=[("ps", bo)], writes=[("rden", rk)])
                S.op("dve", lambda e: e.tensor_tensor(
                    out=oT[:, hd % HG, qs], in0=self.bank(bo)[:, 0:256], in1=rden[rk][:], op=ALU.mult),
                    reads=[("ps", bo), ("rden", rk)], writes=[("oT", hd % HG, qb // 2)])

        for u in prologue_units(0):
            u()
        LAG = 2
        for hd in range(NH):
            if hd + 1 < NH:
                load_w(hd + 1)
                nxt = prologue_units(hd + 1)
            else:
                nxt = []
            if hd % HG == 0:
                g0 = hd
                S.op("pool", lambda e, g0=g0: e.dma_start(
                    out=wo[:], in_=self.d_wo[slot, g0 * 128:(g0 + HG) * 128, :].rearrange("(h p) d -> p h d", p=128)),
                    writes=["wo"], dma=True)
            items = [(qb, n) for qb in range(8) for n in range(qb + 1)]
            st_ = {}
            for idx in range(len(items) + LAG):
                if idx == 4:
                    Dunit(hd)
                if idx < len(items):
                    E1(hd, items[idx][0], items[idx][1], st_)
                if idx >= LAG:
                    E2(hd, items[idx - LAG][0], items[idx - LAG][1], st_)
                if nxt and idx >= 6:
                    nxt.pop(0)()
            while nxt:
                nxt.pop(0)()
            if hd % HG == HG - 1:
                self.resid_proj(wo, HG, lambda k, tt: oT[:, k, tt * 512:(tt + 1) * 512],
                                lambda k, tt: [("oT", k, tt)], "wo", final=(hd == NH - 1))

    def lru(self, st, layer):
        nc, S = self.nc, self.S
        CG = 5
        NSLOT = 6
        win = [self.wfirst, self.sb(st, "win1", [128, NC_, 256], BF16)]

        def wit(bi, half):
            return ("wf", half) if bi == 0 else ("win", bi, half)

        wa = self.sb(st, "wa", [128, NLC, 128], BF16)
        wx = self.sb(st, "wx", [128, NLC, 128], BF16)
        wout = self.sb(st, "wout", [128, CG, D], BF16)
        hy = self.sb(st, "hy", [128, NSLOT, S_], BF16)
        xc = [self.sb(st, "xc%d" % i, [128, 512], F32) for i in range(4)]
        xcb = [self.sb(st, "xcb%d" % i, [128, 512], BF16) for i in range(4)]
        gy = [self.sb(st, "gy%d" % i, [128, 512], BF16) for i in range(4)]
        A = [self.sb(st, "lruA%d" % i, [128, 512], F32) for i in range(4)]
        I = [self.sb(st, "lruI%d" % i, [128, 512], F32) for i in range(4)]
        M = [self.sb(st, "lruM%d" % i, [128, 512], F32) for i in range(4)]
        us = [self.sb(st, "lruus%d" % i, [128, 4], F32) for i in range(4)]
        dp = self.sb(st, "lrudp", [128, 4 * NLC], F32)
        S.op("pool", lambda e: e.dma_start(out=wa[:], in_=self.d_lwa[0].rearrange("n c d -> c n d")),
             writes=["wa"], dma=True)
        S.op("pool", lambda e: e.dma_start(out=wx[:], in_=self.d_lwx[0].rearrange("n c d -> c n d")),
             writes=["wx"], dma=True)
        lam = self.params[:, PC.off["llam"]:PC.off["llam"] + NLC]
        S.op("act", lambda e: e.activation(out=dp[:, 0:NLC], in_=lam, func=AF.Exp, scale=-1.0),
             reads=["params"], writes=["dp0"])
        S.op("act", lambda e: e.activation(out=dp[:, 0:NLC], in_=dp[:, 0:NLC], func=AF.Ln, bias=1.0),
             reads=["dp0"], writes=["dp0"])
        S.op("dve", lambda e: e.tensor_scalar(out=dp[:, NLC:2 * NLC], in0=dp[:, 0:NLC], scalar1=-4.0, scalar2=None,
                                              op0=ALU.mult), reads=["dp0"], writes=["dp"])
        S.op("dve", lambda e: e.tensor_scalar(out=dp[:, 2 * NLC:3 * NLC],
                                              in0=self.params[:, PC.off["lba"]:PC.off["lba"] + NLC],
                                              scalar1=0.5, scalar2=None, op0=ALU.mult), reads=["params"], writes=["dp"])
        S.op("dve", lambda e: e.tensor_scalar(out=dp[:, 3 * NLC:4 * NLC],
                                              in0=self.params[:, PC.off["lbx"]:PC.off["lbx"] + NLC],
                                              scalar1=0.5, scalar2=None, op0=ALU.mult), reads=["params"], writes=["dp"])
        wsrc = self.d_lwin[0].rearrange("(c p) f -> p c f", p=128)

        def load_win(c):
            bi = c % 2
            S.op("pool", lambda e: e.dma_start(out=win[bi][:, :, 0:128], in_=wsrc[:, :, c * 128:(c + 1) * 128]),
                 writes=[wit(bi, 0)], dma=True)
            S.op("pool", lambda e: e.dma_start(out=win[bi][:, :, 128:256],
                                               in_=wsrc[:, :, DR + c * 128:DR + (c + 1) * 128]),
                 writes=[wit(bi, 1)], dma=True)

        def load_wout(c0):
            S.op("pool", lambda e: e.dma_start(
                out=wout[:], in_=self.d_lwout[0, c0 * 128:(c0 + CG) * 128, :].rearrange("(k p) d -> p k d", p=128)),
                writes=["wout"], dma=True)

        pairs = [(c, hf) for c in range(NLC) for hf in range(2)]

        def units(p):
            c, hf = pairs[p]
            return [(c, 2 * hf + j, (2 * p + j) % 4) for j in range(2)]

        def Wp(p):
            for (c, tt, s_) in units(p):
                bi = c % 2
                ts = slice(tt * 512, (tt + 1) * 512)
                for half in range(2):
                    for k in range(NC_):
                        S.op("pe", lambda e, k=k: e.matmul(
                            self.bank(2 * s_ + half), lhsT=win[bi][:, k, half * 128:(half + 1) * 128],
                            rhs=self.hT[:, k, ts], start=(k == 0), stop=(k == NC_ - 1)),
                            reads=[wit(bi, half), ("h", k, tt)], writes=[("ps", 2 * s_ + half)])

        def E1(p):
            for (c, tt, s_) in units(p):
                u = self.bank(2 * s_)
                wcol = [self.pcol("lcw", j * NLC + c) for j in range(4)]
                S.op("act", lambda e: e.activation(out=xc[s_][:], in_=u, func=AF.Identity, scale=wcol[3],
                                                   bias=self.pcol("lcb", c)),
                     reads=[("ps", 2 * s_), "params"], writes=[("xc", s_)])
                S.op("act", lambda e: e.activation(out=us[s_][:, 0:3], in_=u[:, 509:512], func=AF.Identity),
                     reads=[("ps", 2 * s_)], writes=[("us", s_)])
                for k in (1, 2, 3):
                    S.op("dve", lambda e, k=k: e.scalar_tensor_tensor(
                        out=xc[s_][:, k:512], in0=u[:, 0:512 - k], scalar=wcol[3 - k], in1=xc[s_][:, k:512],
                        op0=ALU.mult, op1=ALU.add), reads=[("ps", 2 * s_), ("xc", s_), "params"], writes=[("xc", s_)])
                if tt > 0:
                    sp = (s_ - 1) % 4
                    for k in (1, 2, 3):
                        S.op("dve", lambda e, k=k: e.scalar_tensor_tensor(
                            out=xc[s_][:, 0:k], in0=us[sp][:, 3 - k:3], scalar=wcol[3 - k], in1=xc[s_][:, 0:k],
                            op0=ALU.mult, op1=ALU.add), reads=[("us", sp), ("xc", s_), "params"], writes=[("xc", s_)])
                S.op("pool", lambda e: e.tensor_copy(out=xcb[s_][:], in_=xc[s_][:]),
                     reads=[("xc", s_)], writes=[("xcb", s_)])
            for (c, tt, s_) in units(p):
                S.op("act", lambda e: e.activation(out=gy[s_][:], in_=self.bank(2 * s_ + 1), func=AF.Gelu_apprx_tanh),
                     reads=[("ps", 2 * s_ + 1)], writes=[("gy", s_)])

        def Gp(p):
            for (c, tt, s_) in units(p):
                for which, wt, wtok in ((0, wa, "wa"), (1, wx, "wx")):
                    S.op("pe", lambda e, which=which, wt=wt: e.matmul(
                        self.bank(2 * s_ + which), lhsT=wt[:, c, :], rhs=xcb[s_][:], start=True, stop=True),
                        reads=[wtok, ("xcb", s_)], writes=[("ps", 2 * s_ + which)])

        def E2(p):
            un = units(p)
            for (c, tt, s_) in un:
                hba = dp[:, 2 * NLC + c:2 * NLC + c + 1]
                hbx = dp[:, 3 * NLC + c:3 * NLC + c + 1]
                S.op("act", lambda e: e.activation(out=A[s_][:], in_=self.bank(2 * s_), func=AF.Tanh, scale=0.5, bias=hba),
                     reads=[("ps", 2 * s_), "dp"], writes=[("A", s_)])
                S.op("act", lambda e: e.activation(out=I[s_][:], in_=self.bank(2 * s_ + 1), func=AF.Tanh, scale=0.5, bias=hbx),
                     reads=[("ps", 2 * s_ + 1), "dp"], writes=[("I", s_)])
            for (c, tt, s_) in un:
                hcl = dp[:, NLC + c:NLC + c + 1]
                S.op("act", lambda e: e.activation(out=A[s_][:], in_=A[s_][:], func=AF.Exp, scale=hcl, bias=hcl),
                     reads=[("A", s_), "dp"], writes=[("A", s_)])
            for (c, tt, s_) in un:
                S.op("act", lambda e: e.activation(out=M[s_][:], in_=A[s_][:], func=AF.Square),
                     reads=[("A", s_)], writes=[("M", s_)])
            for (c, tt, s_) in un:
                S.op("act", lambda e: e.activation(out=M[s_][:], in_=M[s_][:], func=AF.Sqrt, scale=-1.0, bias=1.0),
                     reads=[("M", s_)], writes=[("M", s_)])
            for (c, tt, s_) in un:
                S.op("dve", lambda e: e.scalar_tensor_tensor(out=I[s_][:], in0=I[s_][:], scalar=1.0, in1=xc[s_][:],
                                                             op0=ALU.add, op1=ALU.mult),
                     reads=[("I", s_), ("xc", s_)], writes=[("I", s_)])
                S.op("dve", lambda e: e.scalar_tensor_tensor(out=I[s_][:], in0=I[s_][:], scalar=0.5, in1=M[s_][:],
                                                             op0=ALU.mult, op1=ALU.mult),
                     reads=[("I", s_), ("M", s_)], writes=[("I", s_)])
                if tt > 0:
                    sp = (s_ - 1) % 4
                    init, itok = M[sp][:, 511:512], [("M", sp)]
                else:
                    init, itok = 0.0, []
                S.op("dve", lambda e, init=init: e.tensor_tensor_scan(
                    out=M[s_][:], data0=A[s_][:], data1=I[s_][:], initial=init, op0=ALU.mult, op1=ALU.add),
                    reads=[("A", s_), ("I", s_), ("M", s_)] + itok, writes=[("M", s_)])
                slot = c % NSLOT
                S.op("pool", lambda e, slot=slot, tt=tt: e.tensor_tensor(
                    out=hy[:, slot, tt * 512:(tt + 1) * 512], in0=M[s_][:], in1=gy[s_][:], op=ALU.mult),
                    reads=[("M", s_), ("gy", s_)], writes=[("hy", slot, tt)])

        def outproj(c0, final):
            self.resid_proj(wout, CG, lambda k, tt: hy[:, (c0 + k) % NSLOT, tt * 512:(tt + 1) * 512],
                            lambda k, tt: [("hy", (c0 + k) % NSLOT, tt)], "wout", final=final)

        load_wout(0)
        npair = len(pairs)
        pend = {}
        for idx in range(npair + 1):
            if idx < npair:
                c, hf = pairs[idx]
                if hf == 0 and c + 1 < NLC:
                    load_win(c + 1)
                Wp(idx)
                E1(idx)
            if idx >= 1:
                Gp(idx - 1)
                E2(idx - 1)
                c, hf = pairs[idx - 1]
                if hf == 1 and c == CG - 1:
                    pend[idx + 1] = "out0"
                    pend[idx + 5] = "wout1"
            act = pend.get(idx)
            if act == "out0":
                outproj(0, False)
            elif act == "wout1":
                load_wout(CG)
        outproj(CG, True)

    def pool(self, st, layer):
        nc, S = self.nc, self.S
        rstd_all = self.sb(st, "rstd_all", [128, S_], F32)
        hf = [self.sb(st, "hf%d" % i, [128, S_], F32) for i in range(2)]
        B = [self.sb(st, "pB%d" % i, [128, S_], F32) for i in range(4)]
        t16 = [self.sb(st, "t16_%d" % i, [128, 16], F32) for i in range(2)]
        pw = self.sb(st, "pw", [128, 4, 2, 256], BF16)
        S.op("pool", lambda e: e.dma_start(out=pw[:], in_=self.d_pw[0].rearrange("g (k p) d -> p g k d", p=128)),
             writes=["pw"], dma=True)
        self.normA(0)
        for tt in range(4):
            b = self.normB_mm(tt)
            if tt + 1 < 4:
                self.normA(tt + 1)
            S.op("act", lambda e: e.activation(out=self.lnv[:], in_=self.bank(b), func=AF.Ln,
                                               scale=1.0 / D, bias=self.eps),
                 reads=[("ps", b), "cf"], writes=["lnv"])
            S.op("act", lambda e, tt=tt: e.activation(out=rstd_all[:, tt * 512:(tt + 1) * 512], in_=self.lnv[:],
                                                      func=AF.Exp, scale=-0.5),
                 reads=["lnv"], writes=[("rstd_all", tt)])
        xall = lambda c: [("x", c, tt) for tt in range(4)]
        hall = lambda c: [("h", c, tt) for tt in range(4)]

        def chunk_steps(c):
            g = c // 2
            k2 = c % 2
            hfc = hf[k2]
            steps = []
            steps.append(lambda: S.op("dve", lambda e: e.scalar_tensor_tensor(
                out=hfc[:], in0=self.xT[:, c, :], scalar=self.pcol("nmix", layer * NC_ + c), in1=rstd_all[:],
                op0=ALU.mult, op1=ALU.mult),
                reads=xall(c) + [("rstd_all", tt) for tt in range(4)] + ["params"], writes=[("hf", k2)]))
            cur, curtoks = hfc, [("hf", k2)]
            for k in range(g + 1):
                d = 2 ** k
                bi = k2 * 2 + k % 2
                nxt, nxttok = B[bi], ("pB", bi)
                eng = "pool" if k % 2 == 0 else "dve"

                def stp(cur=cur, nxt=nxt, d=d, eng=eng, curtoks=list(curtoks), nxttok=nxttok):
                    S.op(eng, lambda e: e.tensor_tensor(
                        out=nxt[:, d:S_], in0=cur[:, d:S_], in1=cur[:, 0:S_ - d], op=ALU.add),
                        reads=curtoks, writes=[nxttok])
                    S.op("act", lambda e: e.activation(out=nxt[:, 0:d], in_=cur[:, 0:d], func=AF.Identity),
                         reads=curtoks, writes=[(nxttok, "head")])
                steps.append(stp)
                cur, curtoks = nxt, [nxttok, (nxttok, "head")]
            w = 2 ** (g + 1)
            icnt = self.cf[:, CF.off["icnt"] + g * 16:CF.off["icnt"] + (g + 1) * 16]

            def fin(cur=cur, curtoks=list(curtoks)):
                S.op("dve", lambda e: e.scalar_tensor_tensor(
                    out=self.hT[:, c, :], in0=cur[:], scalar=1.0 / w, in1=hfc[:], op0=ALU.mult, op1=ALU.subtract),
                    reads=curtoks + [("hf", k2)], writes=hall(c))
                S.op("dve", lambda e: e.tensor_tensor(out=t16[k2][:], in0=cur[:, 0:16], in1=icnt, op=ALU.mult),
                     reads=curtoks + ["cf"], writes=[("t16", k2)])
                S.op("dve", lambda e: e.tensor_tensor(
                    out=self.hT[:, c, 0:16], in0=t16[k2][:], in1=hfc[:, 0:16], op=ALU.subtract),
                    reads=[("t16", k2), ("hf", k2)] + hall(c), writes=hall(c))
            steps.append(fin)
            return steps

        for c0 in range(0, NC_, 2):
            sa, sb_ = chunk_steps(c0), chunk_steps(c0 + 1)
            for i in range(len(sa)):
                sa[i]()
                sb_[i]()
        for tt in range(4):
            for g in range(4):
                for m2 in range(2):
                    m = 2 * g + m2
                    b = self.psrot % 8
                    self.psrot += 1
                    for kk in range(2):
                        S.op("pe", lambda e, b=b, g=g, kk=kk, m2=m2, tt=tt: e.matmul(
                            self.bank(b), lhsT=pw[:, g, kk, m2 * 128:(m2 + 1) * 128],
                            rhs=self.hT[:, 2 * g + kk, tt * 512:(tt + 1) * 512], start=(kk == 0), stop=(kk == 1)),
                            reads=["pw", ("h", 2 * g + kk, tt)], writes=[("ps", b)])
                    xs = self.xT[:, m, tt * 512:(tt + 1) * 512]
                    S.op("dve", lambda e, b=b, xs=xs, m=m: e.scalar_tensor_tensor(
                        out=xs, in0=self.bank(b), scalar=self.pcol("pscale", m), in1=xs, op0=ALU.mult, op1=ALU.add),
                        reads=[("ps", b), ("x", m, tt), "params"], writes=[("x", m, tt)])
            self.tail_hook(tt)


ALL_PHASES = []
for _l in range(DEPTH):
    ALL_PHASES += [("mix", _l), ("ffn", _l)]

WEIGHT_KEYS = ["ffn_w_up", "ffn_w_down", "attn_w_qkv", "attn_w_o", "lru_w_in", "lru_w_a", "lru_w_x",
               "lru_w_out", "pool_w"]


def run_phases(inputs, phases, x_cores=None, trace=False, **bkw):
    nc = Builder(phases, **bkw).build()
    params = pack_params(inputs)
    c16, cf = make_consts()
    if x_cores is None:
        x = np.asarray(inputs["x"], np.float32)
        x_cores = [np.ascontiguousarray(x[b].T) for b in range(8)]
    shared = {k: np.ascontiguousarray(np.asarray(inputs[k], np.float32)) for k in WEIGHT_KEYS}
    shared.update({"params": params, "c16": c16, "cf": cf})
    in_maps = []
    for b in range(8):
        m = dict(shared)
        m["xT"] = x_cores[b]
        in_maps.append(m)
    res = run_bass_kernel_spmd(nc, in_maps, core_ids=list(range(8)), trace=trace)
    return [r["outT"] for r in res.results], res


def kernel(**inputs):
    outs, _ = run_phases(inputs, ALL_PHASES)
    return np.stack([np.ascontiguousarray(o.T) for o in outs], axis=0).astype(np.float32)
```
